# Optimizing a Trainium2 kernel written in Bass

```python
import math, functools
import jax, jax.numpy as jnp
from jax import lax
import numpy as np

D_MODEL = 1024
BATCH = 32
SEQ = 2048
DEPTH = 4

CTX_LEN = 256
GRID_W = 64
N_MOD = 6
RMS_EPS = 1e-6
N_EVEN = (DEPTH + 1) // 2
N_ODD = DEPTH // 2

SSD_W = D_MODEL
SSD_HEAD_DIM = 64
SSD_HEADS = SSD_W // SSD_HEAD_DIM
SSD_GROUPS = 2
SSD_HPG = SSD_HEADS // SSD_GROUPS
SSD_STATE = 128
SSD_BC = SSD_GROUPS * SSD_STATE
SSD_XBC = SSD_W + 2 * SSD_BC
SSD_CONV = 4
SSD_CHUNK = 128
LRU_W = D_MODEL
LRU_BLOCKS = 8
LRU_BLOCK_W = LRU_W // LRU_BLOCKS
LRU_CONV = 4
LRU_C = 8.0
EV_SPLITS = (SSD_W, SSD_W + SSD_XBC, SSD_W + SSD_XBC + 2 * SSD_HEADS, SSD_W + SSD_XBC + 2 * SSD_HEADS + LRU_W)
EV_IN = SSD_W + SSD_XBC + 2 * SSD_HEADS + 2 * LRU_W
EV_MIX = SSD_W + LRU_W
HG_W = 3 * D_MODEL // 4
HG_DK = 128
HG_DV = 128
HG_HEADS = HG_W // HG_DK
HG_CHUNK = 32
S5_W = D_MODEL // 4
S5_GROUP_CH = 16
S5_GROUPS = S5_W // S5_GROUP_CH
S5_STATE = 64
OD_SPLITS = (HG_W, 2 * HG_W, 3 * HG_W, 4 * HG_W, 5 * HG_W)
OD_IN = 5 * HG_W + S5_W
OD_MIX = HG_W + S5_W
D_FF = 2816
FFN_CONV = 3

kernel_name = "hybrid_ssd_rglru_hgrn2_s5_diffusion_trunk"


def rms_norm(t, g):
    tf = t.astype(jnp.float32)
    tf = tf * lax.rsqrt(jnp.mean(jnp.square(tf), axis=-1, keepdims=True) + RMS_EPS)
    return (tf * g.astype(jnp.float32)).astype(t.dtype)


def modulate(t, shift, scale):
    return t * (1 + scale) + shift


def depthwise_conv1d(t, w, b):
    k = w.shape[0]
    y = lax.conv_general_dilated(t, w[:, None, :], (1,), [((k - 1) // 2, k // 2)],
                                 dimension_numbers=("NWC", "WIO", "NWC"),
                                 feature_group_count=t.shape[-1])
    return y + b


def depthwise_conv2d(t, w, b):
    y = lax.conv_general_dilated(t, w[:, :, None, :], (1, 1), [(1, 1), (1, 1)],
                                 dimension_numbers=("NHWC", "HWIO", "NHWC"),
                                 feature_group_count=t.shape[-1])
    return y + b


def to_chunks(t, size):
    b, n = t.shape[:2]
    return jnp.swapaxes(t.reshape((b, n // size, size) + t.shape[2:]), 0, 1)


def from_chunks(t):
    t = jnp.swapaxes(t, 0, 1)
    return t.reshape((t.shape[0], t.shape[1] * t.shape[2]) + t.shape[3:])


def run_direction(scan_fn, ctx_seqs, lat_seqs, h0, reverse):
    flip = (lambda t: jnp.flip(t, axis=1)) if reverse else (lambda t: t)
    y_c, h_c = scan_fn(*map(flip, ctx_seqs), h0)
    y_x, _ = scan_fn(*map(flip, lat_seqs), h_c)
    return flip(y_c), flip(y_x)


def ssd_scan(x, dt, bm, cm, h0, a):
    log_a = dt * a
    xdt = x * dt[..., None]
    mask = jnp.tril(jnp.ones((SSD_CHUNK, SSD_CHUNK), dtype=bool))

    def step(h, inp):
        xc, lac, bc, cc = inp
        cum = jnp.cumsum(lac, axis=1)
        seg = cum[:, :, None] - cum[:, None]
        decay = jnp.exp(jnp.where(mask[None, :, :, None, None], seg, -jnp.inf))
        cb = jnp.einsum("blgn,bsgn->blsg", cc, bc)
        y = jnp.einsum("blsgh,bsghp->blghp", cb[..., None] * decay, xc)
        y = y + jnp.einsum("blgn,bghpn->blghp", cc, h) * jnp.exp(cum)[..., None]
        to_end = jnp.exp(cum[:, -1:] - cum)
        h = h * jnp.exp(cum[:, -1])[..., None, None] + jnp.einsum(
            "bsgn,bsghp->bghpn", bc, xc * to_end[..., None])
        return h, y

    h, ys = lax.scan(step, h0, tuple(to_chunks(t, SSD_CHUNK) for t in (xdt, log_a, bm, cm)))
    return from_chunks(ys), h


def linear_recurrence(a, b, h0):
    b = b.at[:, 0].add(a[:, 0] * h0)

    def combine(l, r):
        return l[0] * r[0], r[0] * l[1] + r[1]

    return lax.associative_scan(combine, (a, b), axis=1)[1]


def rglru_scan(u, h0, w_a, b_a, w_i, b_i, lam):
    r = jax.nn.sigmoid(jnp.einsum("btnk,nkj->btnj", u, w_a) + b_a)
    i = jax.nn.sigmoid(jnp.einsum("btnk,nkj->btnj", u, w_i) + b_i)
    log_a = -LRU_C * jax.nn.softplus(-lam) * r
    a = jnp.exp(log_a)
    bx = jnp.sqrt(-jnp.expm1(2 * log_a)) * (i * u)
    h = linear_recurrence(a, bx, h0)
    return h, h[:, -1]


def hgrn_scan(q, k, log_f, v, s0):
    mask = jnp.tril(jnp.ones((HG_CHUNK, HG_CHUNK), dtype=bool))

    def step(s, inp):
        qc, kc, gc, vc = inp
        cum = jnp.cumsum(gc, axis=1)
        seg = cum[:, :, None] - cum[:, None]
        decay = jnp.exp(jnp.where(mask[None, :, :, None, None], seg, -jnp.inf))
        att = jnp.einsum("blhk,blshk->blsh", qc, decay * kc[:, None])
        y = jnp.einsum("blsh,bshv->blhv", att, vc)
        y = y + jnp.einsum("blhk,bhkv->blhv", qc * jnp.exp(cum), s)
        s = s * jnp.exp(cum[:, -1])[..., None] + jnp.einsum(
            "bshk,bshv->bhkv", kc * jnp.exp(cum[:, -1:] - cum), vc)
        return s, y

    s, ys = lax.scan(step, s0, tuple(to_chunks(t, HG_CHUNK) for t in (q, k, log_f, v)))
    return from_chunks(ys), s


def s5_scan(u, h0, lam_re, lam_im, log_step, b_re, b_im, c_re, c_im):
    step = jnp.exp(log_step)[:, None]
    mag = jnp.exp(lam_re * step)
    ar, ai = mag * jnp.cos(lam_im * step), mag * jnp.sin(lam_im * step)
    den = lam_re * lam_re + lam_im * lam_im
    zr = ((ar - 1) * lam_re + ai * lam_im) / den
    zi = (ai * lam_re - (ar - 1) * lam_im) / den
    bbr = zr[..., None] * b_re - zi[..., None] * b_im
    bbi = zr[..., None] * b_im + zi[..., None] * b_re
    xr = jnp.einsum("btgk,gpk->btgp", u, bbr)
    xi = jnp.einsum("btgk,gpk->btgp", u, bbi)
    h0r, h0i = h0
    xr = xr.at[:, 0].add(ar * h0r - ai * h0i)
    xi = xi.at[:, 0].add(ar * h0i + ai * h0r)
    a_r = jnp.broadcast_to(ar, xr.shape)
    a_i = jnp.broadcast_to(ai, xi.shape)

    def combine(l, r):
        lar, lai, lbr, lbi = l
        rar, rai, rbr, rbi = r
        return (lar * rar - lai * rai, lar * rai + lai * rar,
                rar * lbr - rai * lbi + rbr, rar * lbi + rai * lbr + rbi)

    _, _, hr, hi = lax.associative_scan(combine, (a_r, a_i, xr, xi), axis=1)
    y = jnp.einsum("btgp,gkp->btgk", hr, c_re) - jnp.einsum("btgp,gkp->btgk", hi, c_im)
    return y, (hr[:, -1], hi[:, -1])


def even_mixer(hx, hc, w_in, w_out, ssd_conv_w, ssd_conv_b, ssd_dt_bias, ssd_a_log, ssd_d, ssd_norm_g,
               lru_conv_w, lru_conv_b, lru_w_a, lru_b_a, lru_w_i, lru_b_i, lru_lam, need_ctx):
    ssd_a = -jnp.exp(ssd_a_log).reshape(2, SSD_GROUPS, SSD_HPG)

    def prepare(h):
        bsz, t = h.shape[:2]
        z, xbc, dt, gy, u = jnp.split(h @ w_in, EV_SPLITS, axis=-1)
        xbc = jax.nn.silu(depthwise_conv1d(xbc, ssd_conv_w, ssd_conv_b))
        xs, bm, cm = jnp.split(xbc, [SSD_W, SSD_W + SSD_BC], axis=-1)
        dt = jax.nn.softplus(dt.reshape(bsz, t, 2, SSD_HEADS) + ssd_dt_bias)
        u = depthwise_conv1d(u, lru_conv_w, lru_conv_b)
        return {"z": z, "gy": gy,
                "x": xs.reshape(bsz, t, SSD_GROUPS, SSD_HPG, SSD_HEAD_DIM),
                "b": bm.reshape(bsz, t, SSD_GROUPS, SSD_STATE),
                "c": cm.reshape(bsz, t, SSD_GROUPS, SSD_STATE),
                "dt": dt.reshape(bsz, t, 2, SSD_GROUPS, SSD_HPG),
                "u": u.reshape(bsz, t, LRU_BLOCKS, LRU_BLOCK_W)}

    pc, px = prepare(hc), prepare(hx)
    bsz = hx.shape[0]
    ssd_h0 = jnp.zeros((bsz, SSD_GROUPS, SSD_HPG, SSD_HEAD_DIM, SSD_STATE), hx.dtype)
    lru_h0 = jnp.zeros((bsz, LRU_BLOCKS, LRU_BLOCK_W), hx.dtype)
    ssd_dirs, lru_dirs = [], []
    for d, reverse in enumerate((False, True)):
        ssd_dirs.append(run_direction(
            functools.partial(ssd_scan, a=ssd_a[d]),
            (pc["x"], pc["dt"][:, :, d], pc["b"], pc["c"]),
            (px["x"], px["dt"][:, :, d], px["b"], px["c"]), ssd_h0, reverse))
        lru_dirs.append(run_direction(
            functools.partial(rglru_scan, w_a=lru_w_a[d], b_a=lru_b_a[d].reshape(LRU_BLOCKS, LRU_BLOCK_W),
                              w_i=lru_w_i[d], b_i=lru_b_i[d].reshape(LRU_BLOCKS, LRU_BLOCK_W),
                              lam=lru_lam[d].reshape(LRU_BLOCKS, LRU_BLOCK_W)),
            (pc["u"],), (px["u"],), lru_h0, reverse))

    def finish(p, ssd_y, lru_h):
        bsz_, t = p["z"].shape[:2]
        y = (ssd_y + ssd_d.reshape(SSD_GROUPS, SSD_HPG, 1) * p["x"]).reshape(bsz_, t, SSD_W)
        y = rms_norm(y * jax.nn.silu(p["z"]), ssd_norm_g)
        r = lru_h.reshape(bsz_, t, LRU_W) * jax.nn.gelu(p["gy"])
        return jnp.concatenate([y, r], axis=-1) @ w_out

    out_x = finish(px, ssd_dirs[0][1] + ssd_dirs[1][1], lru_dirs[0][1] + lru_dirs[1][1])
    out_c = finish(pc, ssd_dirs[0][0] + ssd_dirs[1][0], lru_dirs[0][0] + lru_dirs[1][0]) if need_ctx else None
    return out_x, out_c


def odd_mixer(hx, hc, lower_bound, w_in, w_out, hg_norm_g, s5_lam_re, s5_lam_im, s5_log_step,
              s5_b_re, s5_b_im, s5_c_re, s5_c_im, s5_d, s5_glu_w, s5_glu_b, need_ctx):
    lb = lower_bound.reshape(HG_HEADS, HG_DK)

    def prepare(h):
        bsz, t = h.shape[:2]
        q, f_fwd, f_bwd, v, g, u = jnp.split(h @ w_in, OD_SPLITS, axis=-1)
        gates = []
        for f in (f_fwd, f_bwd):
            f = f.reshape(bsz, t, HG_HEADS, HG_DK)
            log_f = jnp.log(lb + (1 - lb) * jax.nn.sigmoid(f))
            k = (1 - lb) * jax.nn.sigmoid(-f)
            gates.append((k, log_f))
        return {"q": jax.nn.silu(q.reshape(bsz, t, HG_HEADS, HG_DK)),
                "v": v.reshape(bsz, t, HG_HEADS, HG_DV), "g": g, "gates": gates,
                "u": u.reshape(bsz, t, S5_GROUPS, S5_GROUP_CH)}

    pc, px = prepare(hc), prepare(hx)
    bsz = hx.shape[0]
    hg_h0 = jnp.zeros((bsz, HG_HEADS, HG_DK, HG_DV), hx.dtype)
    s5_zero = jnp.zeros((bsz, S5_GROUPS, S5_STATE), hx.dtype)
    hg_dirs, s5_dirs = [], []
    for d, reverse in enumerate((False, True)):
        hg_dirs.append(run_direction(
            hgrn_scan, (pc["q"], pc["gates"][d][0], pc["gates"][d][1], pc["v"]),
            (px["q"], px["gates"][d][0], px["gates"][d][1], px["v"]), hg_h0, reverse))
        s5_dirs.append(run_direction(
            functools.partial(s5_scan, lam_re=s5_lam_re[d], lam_im=s5_lam_im[d], log_step=s5_log_step[d],
                              b_re=s5_b_re, b_im=s5_b_im, c_re=s5_c_re[d], c_im=s5_c_im[d]),
            (pc["u"],), (px["u"],), (s5_zero, s5_zero), reverse))

    def finish(p, o, y):
        bsz_, t = p["g"].shape[:2]
        o = rms_norm(o, hg_norm_g) * jax.nn.silu(p["g"].reshape(bsz_, t, HG_HEADS, HG_DV))
        y = y + s5_d.reshape(S5_GROUPS, S5_GROUP_CH) * p["u"]
        y = jax.nn.gelu(y.reshape(bsz_, t, S5_W))
        y = y * jax.nn.sigmoid(y @ s5_glu_w + s5_glu_b)
        return jnp.concatenate([o.reshape(bsz_, t, HG_W), y], axis=-1) @ w_out

    out_x = finish(px, hg_dirs[0][1] + hg_dirs[1][1], s5_dirs[0][1] + s5_dirs[1][1])
    out_c = finish(pc, hg_dirs[0][0] + hg_dirs[1][0], s5_dirs[0][0] + s5_dirs[1][0]) if need_ctx else None
    return out_x, out_c


def conv_ffn(h, rows, w_gate, w_up, conv_w, conv_b, w_down):
    a = h @ w_gate
    if rows is None:
        a = depthwise_conv1d(a, conv_w[1], conv_b)
    else:
        bsz, t, f = a.shape
        a = depthwise_conv2d(a.reshape(bsz, rows, GRID_W, f), conv_w, conv_b).reshape(bsz, t, f)
    return (jax.nn.silu(a) * (h @ w_up)) @ w_down


def setup_inputs(seed: int = 0) -> dict:
    key = jax.random.key(seed)
    ks = iter(jax.random.split(key, 64))
    D = D_MODEL

    def nrm(shape, scale):
        return scale * jax.random.normal(next(ks), shape, jnp.float32)

    def uni(shape, lo, hi):
        return jax.random.uniform(next(ks), shape, jnp.float32, lo, hi)

    ssd_dt = jnp.exp(uni((N_EVEN, 2, SSD_HEADS), math.log(1e-3), math.log(1e-1)))
    lru_a = uni((N_EVEN, 2, LRU_W), 0.9, 0.999) ** (1.0 / LRU_C)
    n_idx = jnp.arange(S5_STATE, dtype=jnp.float32)
    return {
        "x": nrm((BATCH, SEQ, D), 1.0),
        "c": nrm((BATCH, D), 1.0),
        "ctx": nrm((BATCH, CTX_LEN, D), 1.0),
        "c_ctx": nrm((D,), 1.0),
        "w_mod": nrm((DEPTH, D, N_MOD * D), 0.5 * D ** -0.5),
        "b_mod": nrm((DEPTH, N_MOD * D), 0.02),
        "norm_mix_g": 1.0 + nrm((DEPTH, D), 0.02),
        "norm_ffn_g": 1.0 + nrm((DEPTH, D), 0.02),
        "final_norm_g": 1.0 + nrm((D,), 0.02),
        "ev_w_in": nrm((N_EVEN, D, EV_IN), D ** -0.5),
        "ev_w_out": nrm((N_EVEN, EV_MIX, D), EV_MIX ** -0.5),
        "ssd_conv_w": nrm((N_EVEN, SSD_CONV, SSD_XBC), SSD_CONV ** -0.5),
        "ssd_conv_b": nrm((N_EVEN, SSD_XBC), 0.02),
        "ssd_dt_bias": ssd_dt + jnp.log(-jnp.expm1(-ssd_dt)),
        "ssd_a_log": jnp.log(uni((N_EVEN, 2, SSD_HEADS), 1.0, 16.0)),
        "ssd_d": 1.0 + nrm((N_EVEN, SSD_HEADS), 0.1),
        "ssd_norm_g": 1.0 + nrm((N_EVEN, SSD_W), 0.02),
        "lru_conv_w": nrm((N_EVEN, LRU_CONV, LRU_W), LRU_CONV ** -0.5),
        "lru_conv_b": nrm((N_EVEN, LRU_W), 0.02),
        "lru_w_a": nrm((N_EVEN, 2, LRU_BLOCKS, LRU_BLOCK_W, LRU_BLOCK_W), LRU_BLOCK_W ** -0.5),
        "lru_b_a": nrm((N_EVEN, 2, LRU_W), 0.02),
        "lru_w_i": nrm((N_EVEN, 2, LRU_BLOCKS, LRU_BLOCK_W, LRU_BLOCK_W), LRU_BLOCK_W ** -0.5),
        "lru_b_i": nrm((N_EVEN, 2, LRU_W), 0.02),
        "lru_lam": jnp.log(lru_a) - jnp.log1p(-lru_a),
        "od_w_in": nrm((N_ODD, D, OD_IN), D ** -0.5),
        "od_w_out": nrm((N_ODD, OD_MIX, D), OD_MIX ** -0.5),
        "hg_lb_logits": nrm((DEPTH, HG_W), 0.1),
        "hg_norm_g": 1.0 + nrm((N_ODD, HG_HEADS, HG_DV), 0.02),
        "s5_lam_re": -0.5 + nrm((N_ODD, 2, S5_GROUPS, S5_STATE), 0.01),
        "s5_lam_im": math.pi * n_idx + nrm((N_ODD, 2, S5_GROUPS, S5_STATE), 0.01),
        "s5_log_step": uni((N_ODD, 2, S5_GROUPS), math.log(1e-3), math.log(1e-1)),
        "s5_b_re": nrm((N_ODD, S5_GROUPS, S5_STATE, S5_GROUP_CH), (2 * S5_GROUP_CH) ** -0.5),
        "s5_b_im": nrm((N_ODD, S5_GROUPS, S5_STATE, S5_GROUP_CH), (2 * S5_GROUP_CH) ** -0.5),
        "s5_c_re": nrm((N_ODD, 2, S5_GROUPS, S5_GROUP_CH, S5_STATE), 0.5),
        "s5_c_im": nrm((N_ODD, 2, S5_GROUPS, S5_GROUP_CH, S5_STATE), 0.5),
        "s5_d": nrm((N_ODD, S5_W), 1.0),
        "s5_glu_w": nrm((N_ODD, S5_W, S5_W), S5_W ** -0.5),
        "s5_glu_b": nrm((N_ODD, S5_W), 0.02),
        "ffn_w_gate": nrm((DEPTH, D, D_FF), D ** -0.5),
        "ffn_w_up": nrm((DEPTH, D, D_FF), D ** -0.5),
        "ffn_conv_w": nrm((DEPTH, FFN_CONV, FFN_CONV, D_FF), 1.0 / FFN_CONV),
        "ffn_conv_b": nrm((DEPTH, D_FF), 0.02),
        "ffn_w_down": nrm((DEPTH, D_FF, D), D_FF ** -0.5),
    }


def reference(x, c, ctx, c_ctx, w_mod, b_mod, norm_mix_g, norm_ffn_g, final_norm_g,
              ev_w_in, ev_w_out, ssd_conv_w, ssd_conv_b, ssd_dt_bias, ssd_a_log, ssd_d, ssd_norm_g,
              lru_conv_w, lru_conv_b, lru_w_a, lru_b_a, lru_w_i, lru_b_i, lru_lam,
              od_w_in, od_w_out, hg_lb_logits, hg_norm_g,
              s5_lam_re, s5_lam_im, s5_log_step, s5_b_re, s5_b_im, s5_c_re, s5_c_im, s5_d,
              s5_glu_w, s5_glu_b,
              ffn_w_gate, ffn_w_up, ffn_conv_w, ffn_conv_b, ffn_w_down):
    rows = x.shape[1] // GRID_W
    p = jax.nn.softmax(hg_lb_logits.astype(jnp.float32), axis=0)
    lower_bounds = (jnp.cumsum(p, axis=0) - p[0]).astype(hg_lb_logits.dtype)
    s_c = jax.nn.silu(c)
    s_cc = jax.nn.silu(c_ctx)
    for layer in range(DEPTH):
        last = layer == DEPTH - 1
        j = layer // 2
        mod_x = jnp.split((s_c @ w_mod[layer] + b_mod[layer])[:, None, :], N_MOD, axis=-1)
        mod_c = jnp.split(s_cc @ w_mod[layer] + b_mod[layer], N_MOD, axis=-1)
        hx = modulate(rms_norm(x, norm_mix_g[layer]), mod_x[0], mod_x[1])
        hc = modulate(rms_norm(ctx, norm_mix_g[layer]), mod_c[0], mod_c[1])
        if layer % 2 == 0:
            ox, oc = even_mixer(hx, hc, ev_w_in[j], ev_w_out[j], ssd_conv_w[j], ssd_conv_b[j],
                                ssd_dt_bias[j], ssd_a_log[j], ssd_d[j], ssd_norm_g[j],
                                lru_conv_w[j], lru_conv_b[j], lru_w_a[j], lru_b_a[j],
                                lru_w_i[j], lru_b_i[j], lru_lam[j], not last)
        else:
            ox, oc = odd_mixer(hx, hc, lower_bounds[layer], od_w_in[j], od_w_out[j], hg_norm_g[j],
                               s5_lam_re[j], s5_lam_im[j], s5_log_step[j], s5_b_re[j], s5_b_im[j],
                               s5_c_re[j], s5_c_im[j], s5_d[j], s5_glu_w[j], s5_glu_b[j], not last)
        x = x + mod_x[2] * ox
        fx = modulate(rms_norm(x, norm_ffn_g[layer]), mod_x[3], mod_x[4])
        x = x + mod_x[5] * conv_ffn(fx, rows, ffn_w_gate[layer], ffn_w_up[layer],
                                    ffn_conv_w[layer], ffn_conv_b[layer], ffn_w_down[layer])
        if not last:
            ctx = ctx + mod_c[2] * oc
            fc = modulate(rms_norm(ctx, norm_ffn_g[layer]), mod_c[3], mod_c[4])
            ctx = ctx + mod_c[5] * conv_ffn(fc, None, ffn_w_gate[layer], ffn_w_up[layer],
                                            ffn_conv_w[layer], ffn_conv_b[layer], ffn_w_down[layer])
    return rms_norm(x, final_norm_g)
```

```python
import contextlib
import numpy as np
import concourse.bass as bass
import concourse.mybir as mybir
from concourse.ap import AP
from concourse.bass_utils import run_bass_kernel_spmd

F32 = mybir.dt.float32
BF16 = mybir.dt.bfloat16
AF = mybir.ActivationFunctionType
ALU = mybir.AluOpType

T = 2304
CTX = 256
BLKS = [(0, 256), (256, 512), (768, 512), (1280, 512), (1792, 512)]
NCORES = 8
EPS = 1e-6


class Dep:
    __slots__ = ("w", "r", "rd", "const")

    def __init__(self, const=False):
        self.w = None
        self.r = {}
        self.rd = []
        self.const = const


class Op:
    __slots__ = ("eng", "fn", "deps", "marked", "ev", "dma", "idx", "bar", "epoch")


class Prog:
    DMA_SEMS = {"sp": 6, "act": 6, "pool": 24}
    ENGS = ("pe", "act", "dve", "pool", "sp")

    def __init__(self, nc):
        self.nc = nc
        self.ops = []
        self.last = {}
        self.pending_dma = []
        self.nbar = 0

    def _new(self, eng, fn, dma):
        o = Op()
        o.eng = eng
        o.fn = fn
        o.dma = dma
        o.marked = dma
        o.ev = None
        o.bar = 0
        o.epoch = -1
        o.idx = len(self.ops)
        self.ops.append(o)
        return o

    def op(self, eng, fn, reads=(), writes=(), dma=False, pe_acc=False):
        deps = set()
        for d in reads:
            if d.w is not None:
                deps.add(d.w)
        for d in writes:
            if d.w is not None:
                if not (pe_acc and d.w.eng == "pe" and not d.w.dma):
                    deps.add(d.w)
            for r in d.r.values():
                deps.add(r)
            for r in d.rd:
                deps.add(r)
        o = self._new(eng, fn, dma)
        o.deps = deps
        for d in reads:
            if not d.const:
                if dma:
                    d.rd.append(o)
                else:
                    d.r[eng] = o
        for d in writes:
            d.w = o
            d.r = {}
            d.rd = []
        if dma:
            self.pending_dma.append(o)
        else:
            self.last[eng] = o
        return o

    def barrier(self):
        deps = set(self.last.values()) | set(self.pending_dma)
        self.pending_dma = []
        self.last = {}
        self.nbar += 1
        for e in self.ENGS:
            o = self._new(e, None, False)
            o.deps = set(deps)
            o.bar = self.nbar

    def emit(self, final_deps):
        nc = self.nc
        engs = {"pe": nc.tensor, "act": nc.scalar, "dve": nc.vector, "pool": nc.gpsimd, "sp": nc.sync}
        fin = self._new("sp", None, False)
        fin.deps = set(final_deps)
        for o in self.ops:
            for d in o.deps:
                d.marked = True
        cnt = {e: 0 for e in engs}
        dma_rr = {e: 0 for e in engs}
        dma_cnt = {}
        seen = {e: {} for e in engs}
        per_eng = {e: [] for e in engs}
        epoch = 0
        maxv = 0
        nb_in_group = 0
        for o in self.ops:
            mw = {}
            o.epoch = epoch
            if o.dma:
                k = dma_rr[o.eng] % self.DMA_SEMS[o.eng]
                dma_rr[o.eng] += 1
                sk_own = ("dma", o.eng, k)
                prev = dma_cnt.get(sk_own, 0)
                if prev > 0 and seen[o.eng].get(sk_own, 0) < prev:
                    mw[sk_own] = prev
                    seen[o.eng][sk_own] = prev
                dma_cnt[sk_own] = prev + 16
                o.ev = (sk_own, prev + 16)
                maxv = max(maxv, prev + 16)
            elif o.marked:
                cnt[o.eng] += 1
                o.ev = (("eng", o.eng), cnt[o.eng])
                maxv = max(maxv, cnt[o.eng])
            for d in o.deps:
                if d.epoch != epoch:
                    continue
                sk, v = d.ev
                if seen[o.eng].get(sk, 0) < v:
                    seen[o.eng][sk] = v
                    mw[sk] = max(mw.get(sk, 0), v)
            per_eng[o.eng].append((o, list(mw.items())))
            if o.bar:
                nb_in_group += 1
                if nb_in_group == len(self.ENGS):
                    nb_in_group = 0
                    epoch += 1
                    cnt = {e: 0 for e in engs}
                    seen = {e: {sk: v for sk, v in seen[e].items() if sk[0] == "dma"} for e in engs}
        assert maxv < 8000, maxv
        self.stats = {e: len(per_eng[e]) for e in engs}
        self.stats["sem_maxv"] = maxv
        self.stats["nbar"] = self.nbar
        with contextlib.ExitStack() as st:
            sems = {}
            for e in engs:
                sems[("eng", e)] = st.enter_context(nc.semaphore("s_" + e))
            for e in ("sp", "act", "pool"):
                for k in range(self.DMA_SEMS[e]):
                    sems[("dma", e, k)] = st.enter_context(nc.semaphore("d_%s%d" % (e, k)))
            bsemA = st.enter_context(nc.semaphore("barA"))
            bsemB = st.enter_context(nc.semaphore("barB"))
            block = st.enter_context(nc.Block())
            NE = len(self.ENGS)

            def mk(ename):
                def body(eng):
                    for o, waits in per_eng[ename]:
                        for sk, v in waits:
                            eng.wait_ge(sems[sk], v)
                        if o.bar:
                            eng.sem_inc(bsemA, 1)
                            if ename == "sp":
                                eng.wait_ge(bsemA, NE * o.bar)
                                for sk_, sm in sems.items():
                                    if sk_[0] == "eng":
                                        eng.sem_clear(sm)
                                eng.sem_inc(bsemB, 1)
                            eng.wait_ge(bsemB, o.bar)
                            continue
                        if o.fn is None:
                            continue
                        ins = o.fn(eng)
                        if o.dma:
                            ins.then_inc(sems[o.ev[0]], 16)
                        elif o.marked:
                            ins.then_inc(sems[("eng", ename)], 1)
                return body

            block.tensor(mk("pe"))
            block.scalar(mk("act"))
            block.vector(mk("dve"))
            block.gpsimd(mk("pool"))
            block.sync(mk("sp"))


class Tile:
    def __init__(self, t):
        self.t = t
        self.deps = {}

    def d(self, key=None):
        if key not in self.deps:
            self.deps[key] = Dep()
        return self.deps[key]

    def __getitem__(self, idx):
        return self.t[idx]


def rev(ap):
    apl = [list(x) for x in ap.ap]
    n = apl[-1][1]
    off = ap.offset + (n - 1) * apl[-1][0]
    apl[-1][0] = -apl[-1][0]
    return AP(ap.tensor, off, apl)


def fm_vec(v):
    v = np.asarray(v, np.float32).reshape(-1, 128)
    return np.ascontiguousarray(v.T)


def w_colchunks(w, nk):
    K, N = w.shape
    return np.ascontiguousarray(w.reshape(nk, 128, N // 128, 128).transpose(2, 1, 0, 3))


def w_rows(w, nk):
    K, N = w.shape
    return np.ascontiguousarray(w.reshape(nk, 128, N).transpose(1, 0, 2))


class ParPack:
    def __init__(self):
        self.cols = []
        self.off = {}
        self.n = 0

    def add(self, name, arr):
        arr = np.asarray(arr, np.float32)
        arr = arr.reshape(arr.shape[0], -1)
        if arr.shape[0] < 128:
            arr = np.concatenate([arr, np.zeros((128 - arr.shape[0], arr.shape[1]), np.float32)], 0)
        self.off[name] = (self.n, arr.shape[1])
        self.cols.append(arr)
        self.n += arr.shape[1]

    def pack(self):
        return np.ascontiguousarray(np.concatenate(self.cols, axis=1))


def pack_params(inp, off_only=False):
    pp = ParPack()
    z = (lambda *s: np.zeros(s, np.float32))
    g = (lambda k: inp[k]) if not off_only else None
    for l in range(4):
        pp.add("nmg%d" % l, fm_vec(g("norm_mix_g")[l]) if g else z(128, 8))
        pp.add("nfg%d" % l, fm_vec(g("norm_ffn_g")[l]) if g else z(128, 8))
        pp.add("bmod%d" % l, fm_vec(g("b_mod")[l]) if g else z(128, 48))
        if g:
            cw = g("ffn_conv_w")[l].reshape(9, 22, 128).transpose(2, 1, 0)
            pp.add("fcw%d" % l, cw)
            pp.add("fcb%d" % l, fm_vec(g("ffn_conv_b")[l]))
        else:
            pp.add("fcw%d" % l, z(128, 22 * 9))
            pp.add("fcb%d" % l, z(128, 22))
    pp.add("fng", fm_vec(g("final_norm_g")) if g else z(128, 8))
    for j in range(2):
        if g:
            pp.add("scw%d" % j, g("ssd_conv_w")[j].reshape(4, 12, 128).transpose(2, 1, 0))
            pp.add("scb%d" % j, fm_vec(g("ssd_conv_b")[j]))
            pp.add("sng%d" % j, fm_vec(g("ssd_norm_g")[j]))
            pp.add("sd%d" % j, fm_vec(np.repeat(g("ssd_d")[j], 64)))
            pp.add("dtb%d" % j, g("ssd_dt_bias")[j].reshape(32, 1))
            pp.add("alog%d" % j, g("ssd_a_log")[j].reshape(32, 1))
            pp.add("dtbrow%d" % j, np.tile(g("ssd_dt_bias")[j].reshape(1, 32), (128, 1)))
            pp.add("alogrow%d" % j, np.tile(g("ssd_a_log")[j].reshape(1, 32), (128, 1)))
            pp.add("lcw%d" % j, g("lru_conv_w")[j].reshape(4, 8, 128).transpose(2, 1, 0))
            pp.add("lcb%d" % j, fm_vec(g("lru_conv_b")[j]))
            pp.add("lba%d" % j, g("lru_b_a")[j].reshape(2, 8, 128).transpose(2, 0, 1))
            pp.add("lbi%d" % j, g("lru_b_i")[j].reshape(2, 8, 128).transpose(2, 0, 1))
            pp.add("llam%d" % j, g("lru_lam")[j].reshape(2, 8, 128).transpose(2, 0, 1))
        else:
            pp.add("scw%d" % j, z(128, 48)); pp.add("scb%d" % j, z(128, 12)); pp.add("sng%d" % j, z(128, 8))
            pp.add("sd%d" % j, z(128, 8)); pp.add("dtb%d" % j, z(128, 1)); pp.add("alog%d" % j, z(128, 1))
            pp.add("dtbrow%d" % j, z(128, 32)); pp.add("alogrow%d" % j, z(128, 32))
            pp.add("lcw%d" % j, z(128, 32)); pp.add("lcb%d" % j, z(128, 8)); pp.add("lba%d" % j, z(128, 16))
            pp.add("lbi%d" % j, z(128, 16)); pp.add("llam%d" % j, z(128, 16))
    if g:
        pp.add("lbl", g("hg_lb_logits").reshape(4, 6, 128).transpose(2, 0, 1))
    else:
        pp.add("lbl", z(128, 24))
    for j in range(2):
        if g:
            pp.add("hgn%d" % j, g("hg_norm_g")[j].reshape(6, 128).T)
            pp.add("s5d%d" % j, fm_vec(g("s5_d")[j]))
            pp.add("glub%d" % j, fm_vec(g("s5_glu_b")[j]))
            sp_ = np.zeros((128, 3, 2, 8), np.float32)
            for gg in range(16):
                sp_[(gg % 2) * 64:(gg % 2) * 64 + 64, 0, :, gg // 2] = g("s5_lam_re")[j][:, gg].T
                sp_[(gg % 2) * 64:(gg % 2) * 64 + 64, 1, :, gg // 2] = g("s5_lam_im")[j][:, gg].T
                sp_[(gg % 2) * 64:(gg % 2) * 64 + 64, 2, :, gg // 2] = g("s5_log_step")[j][:, gg][None, :]
            pp.add("s5p%d" % j, sp_)
        else:
            pp.add("hgn%d" % j, z(128, 6)); pp.add("s5d%d" % j, z(128, 2)); pp.add("glub%d" % j, z(128, 2))
            pp.add("s5p%d" % j, z(128, 48))
    return pp


def host_prep(inp, nseq_total=32):
    sh = {}
    sh["par"] = pack_params(inp).pack()
    sh["wmod"] = np.stack([w_rows(inp["w_mod"][l], 8) for l in range(4)])
    sh["ffg"] = np.stack([w_colchunks(inp["ffn_w_gate"][l], 8) for l in range(4)])
    sh["ffu"] = np.stack([w_colchunks(inp["ffn_w_up"][l], 8) for l in range(4)])
    sh["ffd"] = np.stack([w_colchunks(inp["ffn_w_down"][l], 22) for l in range(4)])
    ev = inp["ev_w_in"]
    evc = np.concatenate([ev[:, :, 0:2560], ev[:, :, 2592:4640]], axis=2)
    sh["evin"] = np.stack([w_colchunks(evc[j], 8) for j in range(2)])
    sh["evdt"] = np.stack([w_rows(ev[j][:, 2560:2592], 8) for j in range(2)])
    sh["evout"] = np.stack([w_colchunks(inp["ev_w_out"][j], 16) for j in range(2)])
    la = np.stack([inp["lru_w_a"], inp["lru_w_i"]], axis=1)
    sh["lruw"] = np.ascontiguousarray(la.transpose(0, 4, 1, 2, 3, 5))
    od = inp["od_w_in"]
    odc = np.concatenate([od[:, :, 0:2304], od[:, :, 3072:4096]], axis=2)
    sh["odin"] = np.stack([w_colchunks(odc[j], 8) for j in range(2)])
    sh["odv"] = np.stack([w_rows(od[j][:, 2304:3072], 8) for j in range(2)])
    sh["odout"] = np.stack([w_colchunks(inp["od_w_out"][j], 8) for j in range(2)])
    sh["gluw"] = np.stack([w_rows(inp["s5_glu_w"][j], 2) for j in range(2)])
    s5b = np.zeros((2, 8, 2, 128, 128), np.float32)
    s5c = np.zeros((2, 2, 8, 2, 128, 128), np.float32)
    s5row = np.zeros((2, 2, 3, 8, 128), np.float32)
    for g in range(16):
        q, gi, go = g // 2, g % 8, g % 2
        for ri, nm in enumerate(("s5_b_re", "s5_b_im")):
            s5b[:, q, ri, gi * 16:(gi + 1) * 16, go * 64:(go + 1) * 64] = inp[nm][:, g].transpose(0, 2, 1)
        for ri, nm in enumerate(("s5_c_re", "s5_c_im")):
            s5c[:, :, q, ri, go * 64:(go + 1) * 64, gi * 16:(gi + 1) * 16] = inp[nm][:, :, g].transpose(0, 1, 3, 2)
        s5row[:, :, 0, q, go * 64:(go + 1) * 64] = inp["s5_lam_re"][:, :, g]
        s5row[:, :, 1, q, go * 64:(go + 1) * 64] = inp["s5_lam_im"][:, :, g]
        s5row[:, :, 2, q, go * 64:(go + 1) * 64] = inp["s5_log_step"][:, :, g][:, :, None]
    sh["s5b"] = s5b
    sh["s5c"] = s5c
    sh["s5row"] = s5row
    return sh


def core_inputs(inp, core, nseq):
    b0 = core * nseq
    xs = []
    for s in range(nseq):
        full = np.concatenate([inp["ctx"][b0 + s], inp["x"][b0 + s]], axis=0)
        xs.append(full.reshape(T, 8, 128).transpose(2, 1, 0))
    cc = np.concatenate([inp["c"][b0:b0 + nseq], inp["c_ctx"][None, :]], axis=0)
    ccf = cc.reshape(nseq + 1, 8, 128).transpose(2, 1, 0)
    return {"xin": np.ascontiguousarray(np.stack(xs)), "cc": np.ascontiguousarray(ccf)}


class K:
    def __init__(self, nseq, phases, final=True):
        self.nseq = nseq
        self.NR = nseq + 1
        self.phases = phases
        self.final = final
        self.nc = bass.Bass("TRN2", target_bir_lowering=False)
        self.P = Prog(self.nc)
        self.po = pack_params(None, off_only=True).off
        self.npar = pack_params(None, off_only=True).n

    def sb(self, st, name, shape, dt):
        self.uid = getattr(self, "uid", 0) + 1
        return Tile(st.enter_context(self.nc.sbuf_tensor("t%d_%s" % (self.uid, name), shape, dt)))

    def dram_in(self, name, shape):
        return self.nc.dram_tensor(name, list(shape), F32, kind="ExternalInput").ap()

    def psum(self):
        t = self.ps[self.psi % 8]
        self.psi += 1
        return t

    def par(self, name, c0=0, n=1):
        o, w = self.po[name]
        return self.partile[:, o + c0:o + c0 + n]

    def mm(self, out, lhsT, rhs, start, stop, reads, writes):
        return self.P.op("pe", lambda e: e.matmul(out, lhsT, rhs, start=start, stop=stop), reads, writes, pe_acc=True)

    def act(self, out, in_, func, reads, writes, bias=None, scale=None):
        kw = {}
        if bias is not None:
            kw["bias"] = bias
        if scale is not None:
            kw["scale"] = scale
        return self.P.op("act", lambda e: e.activation(out=out, in_=in_, func=func, **kw), reads, writes)

    def tt(self, eng, out, in0, in1, op, reads, writes):
        return self.P.op(eng, lambda e: e.tensor_tensor(out=out, in0=in0, in1=in1, op=op), reads, writes)

    def ts(self, eng, out, in0, s1, s2, op0, op1, reads, writes):
        if s2 is None:
            return self.P.op(eng, lambda e: e.tensor_scalar(out=out, in0=in0, scalar1=s1, scalar2=None, op0=op0), reads, writes)
        return self.P.op(eng, lambda e: e.tensor_scalar(out=out, in0=in0, scalar1=s1, scalar2=s2, op0=op0, op1=op1), reads, writes)

    def stt(self, eng, out, in0, scalar, in1, op0, op1, reads, writes):
        return self.P.op(eng, lambda e: e.scalar_tensor_tensor(out=out, in0=in0, scalar=scalar, in1=in1, op0=op0, op1=op1), reads, writes)

    def cp(self, eng, out, in_, reads, writes):
        if eng == "act":
            return self.P.op("act", lambda e: e.copy(out=out, in_=in_), reads, writes)
        return self.P.op(eng, lambda e: e.tensor_copy(out=out, in_=in_), reads, writes)

    def dma(self, eng, out, in_, reads, writes):
        return self.P.op(eng, lambda e: e.dma_start(out=out, in_=in_), reads, writes, dma=True)

    def memset(self, eng, ap, val, writes):
        return self.P.op(eng, lambda e: e.memset(ap, val), (), writes)

    def build(self):
        nc, P = self.nc, self.P
        NR = self.NR
        self.xin = self.dram_in("xin", [self.nseq, 128, 8, T])
        self.cc = self.dram_in("cc", [128, 8, NR])
        self.par_d = self.dram_in("par", [128, self.npar])
        self.wmod_d = self.dram_in("wmod", [4, 128, 8, 6144])
        self.ffg_d = self.dram_in("ffg", [4, 22, 128, 8, 128])
        self.ffu_d = self.dram_in("ffu", [4, 22, 128, 8, 128])
        self.ffd_d = self.dram_in("ffd", [4, 8, 128, 22, 128])
        self.evin_d = self.dram_in("evin", [2, 36, 128, 8, 128])
        self.evdt_d = self.dram_in("evdt", [2, 128, 8, 32])
        self.evout_d = self.dram_in("evout", [2, 8, 128, 16, 128])
        self.lruw_d = self.dram_in("lruw", [2, 128, 2, 2, 8, 128])
        self.cst_d = self.dram_in("cst", [128, 768])
        self.odin_d = self.dram_in("odin", [2, 26, 128, 8, 128])
        self.odv_d = self.dram_in("odv", [2, 128, 8, 768])
        self.odout_d = self.dram_in("odout", [2, 8, 128, 8, 128])
        self.gluw_d = self.dram_in("gluw", [2, 128, 2, 256])
        self.s5b_d = self.dram_in("s5b", [2, 8, 2, 128, 128])
        self.s5c_d = self.dram_in("s5c", [2, 2, 8, 2, 128, 128])
        self.s5row_d = self.dram_in("s5row", [2, 2, 3, 8, 128])
        self.yout = nc.dram_tensor("yout", [self.nseq, 128, 8, 2048], F32, kind="ExternalOutput").ap()
        if getattr(self, "debug", False):
            self.dbg = nc.dram_tensor("dbg", [128, 16, T], F32, kind="ExternalOutput").ap()
        self.out_ops = []
        with contextlib.ExitStack() as st:
            self.ps = [Tile(st.enter_context(nc.psum_tensor("ps%d" % i, [128, 512], F32))) for i in range(8)]
            self.psi = 0
            self.partile = self.sb(st, "par", [128, self.npar], F32)
            self.cst = self.sb(st, "cst", [128, 768], F32)
            self.identb = self.sb(st, "identb", [128, 128], BF16)
            self.onesb = self.sb(st, "onesb", [128, 128], BF16)
            self.mods = self.sb(st, "mods", [128, 4 * 48 * NR], F32)
            self.amix = self.sb(st, "amix", [128, 4 * 8 * NR], F32)
            self.affn = self.sb(st, "affn", [128, 4 * 8 * NR], F32)
            self.lbt = self.sb(st, "lbt", [128, 4, 6], F32)
            self.x = self.sb(st, "x", [128, 8, T], F32)
            self.hx = self.sb(st, "hx", [128, 8, T], BF16)
            self.cD = Dep(const=True)
            o1 = self.dma("sp", self.partile[:], self.par_d, (), [self.cD])
            o2 = self.dma("sp", self.cst[:], self.cst_d, (), [self.cD])
            self.cp("dve", self.identb[:], self.cst[:, 0:128], [self.cD], [self.cD])
            self.memset("dve", self.onesb[:], 1.0, [self.cD])
            self.prologue_mods()
            self.prologue_lb()
            P.barrier()
            for s in range(self.nseq):
                self.dma("sp", self.x[:, 0:4, :], self.xin[s, :, 0:4, :], (), [self.x.d()])
                self.dma("act", self.x[:, 4:8, :], self.xin[s, :, 4:8, :], (), [self.x.d()])
                for kind, l in self.phases:
                    if kind == "ffn":
                        self.ffn(l, s)
                    elif kind == "mix":
                        if l % 2 == 0:
                            self.even_mixer(l, s)
                        else:
                            self.odd_mixer(l, s)
                    P.barrier()
                self.final_out(s)
                P.barrier()
            P.emit(self.out_ops)
        return nc

    def mod(self, l, chunk0, r):
        NR = self.NR
        base = (l * 48 + chunk0) * NR + r
        return lambda k: self.mods[:, base + k * NR: base + k * NR + 1]

    def prologue_mods(self):
        NR = self.NR
        with contextlib.ExitStack() as st:
            ccf = self.sb(st, "ccf", [128, 8, NR], F32)
            sfm = self.sb(st, "sfm", [128, 8, NR], BF16)
            wm = [self.sb(st, "wm%d" % i, [128, 8, 1536], BF16) for i in range(2)]
            self.dma("sp", ccf[:], self.cc, (), [ccf.d()])
            self.act(sfm[:], ccf[:], AF.Silu, [ccf.d()], [sfm.d()])
            it = 0
            for l in range(4):
                ps = self.psum()
                for piece in range(4):
                    w = wm[it % 2]
                    it += 1
                    self.dma("pool", w[:], self.wmod_d[l, :, :, piece * 1536:(piece + 1) * 1536], (), [w.d()])
                    for c in range(12):
                        ch = piece * 12 + c
                        for k in range(8):
                            self.mm(ps[:, ch * NR:(ch + 1) * NR], w[:, k, c * 128:(c + 1) * 128], sfm[:, k, :],
                                    k == 0, k == 7, [w.d(), sfm.d()], [ps.d()])
                mo = self.mods[:, l * 48 * NR:(l + 1) * 48 * NR].rearrange("p (c r) -> p c r", r=NR)
                o, _ = self.po["bmod%d" % l]
                self.tt("dve", mo, ps[:, 0:48 * NR].rearrange("p (c r) -> p c r", r=NR),
                        self.partile[:, o:o + 48].unsqueeze(2).to_broadcast([128, 48, NR]), ALU.add,
                        [ps.d(), self.cD], [self.cD])
                for dst, gname, c0 in ((self.amix, "nmg%d" % l, 8), (self.affn, "nfg%d" % l, 32)):
                    dv = dst[:, l * 8 * NR:(l + 1) * 8 * NR].rearrange("p (c r) -> p c r", r=NR)
                    sc = self.mods[:, (l * 48 + c0) * NR:(l * 48 + c0 + 8) * NR].rearrange("p (c r) -> p c r", r=NR)
                    go, _ = self.po[gname]
                    self.ts("dve", dv, sc, 1.0, None, ALU.add, None, [self.cD], [self.cD])
                    self.tt("dve", dv, dv, self.partile[:, go:go + 8].unsqueeze(2).to_broadcast([128, 8, NR]), ALU.mult,
                            [self.cD], [self.cD])

    def rms_mod(self, st, src, dst, A, B, s, nch=8, srcdep=None, dstdep=None, pergroup=False):
        sq = [self.sb(st, "rm_sq%d" % i, [128, nch, 512], BF16) for i in range(2)]
        rs = [self.sb(st, "rm_rs%d" % i, [128, nch if pergroup else 1, 512], F32) for i in range(2)]
        tm = [self.sb(st, "rm_tm%d" % i, [128, 512], F32) for i in range(2)]
        sd = srcdep or src.d()
        dd = dstdep or dst.d()
        ndiv = 128.0 if pergroup else 128.0 * nch
        for bi, (t0, n) in enumerate(BLKS):
            r = self.nseq if t0 == 0 else s
            q = sq[bi % 2]
            rr = rs[bi % 2]
            self.act(q[:, :, 0:n], src[:, 0:nch, t0:t0 + n], AF.Square, [sd], [q.d()])
            groups = [[k] for k in range(nch)] if pergroup else [list(range(nch))]
            for gi, grp in enumerate(groups):
                ps = self.psum()
                for i, k in enumerate(grp):
                    self.mm(ps[:, 0:n], self.onesb[:], q[:, k, 0:n], i == 0, i == len(grp) - 1, [q.d(), self.cD], [ps.d()])
                self.ts("dve", rr[:, gi, 0:n], ps[:, 0:n], 1.0 / ndiv, EPS, ALU.mult, ALU.add, [ps.d()], [rr.d()])
                self.P.op("dve", (lambda o_, i_: lambda e: e.reciprocal(out=o_, in_=i_))(rr[:, gi, 0:n], rr[:, gi, 0:n]), [rr.d()], [rr.d()])
                self.act(rr[:, gi, 0:n], rr[:, gi, 0:n], AF.Sqrt, [rr.d()], [rr.d()])
            for k in range(nch):
                tmp = tm[k % 2]
                gi = k if pergroup else 0
                if A is not None:
                    self.stt("dve", tmp[:, 0:n], src[:, k, t0:t0 + n], A(k, r), rr[:, gi, 0:n], ALU.mult, ALU.mult,
                             [sd, rr.d(), self.cD], [tmp.d()])
                else:
                    self.tt("dve", tmp[:, 0:n], src[:, k, t0:t0 + n], rr[:, gi, 0:n], ALU.mult, [sd, rr.d()], [tmp.d()])
                if B is not None:
                    self.act(dst[:, k, t0:t0 + n], tmp[:, 0:n], AF.Identity, [tmp.d(), self.cD], [dd], bias=B(k, r))
                else:
                    self.cp("act", dst[:, k, t0:t0 + n], tmp[:, 0:n], [tmp.d()], [dd])

    def ffn(self, l, s):
        NR = self.NR
        A = lambda k, r: self.affn[:, (l * 8 + k) * NR + r:(l * 8 + k) * NR + r + 1]
        B = lambda k, r: self.mods[:, (l * 48 + 24 + k) * NR + r:(l * 48 + 24 + k) * NR + r + 1]
        with contextlib.ExitStack() as st0:
            with contextlib.ExitStack() as st:
                self.rms_mod(st, self.x, self.hx, A, B, s)
            self.P.barrier()
            gh = self.sb(st0, "gh", [128, 22, 1280], BF16)
            fo, _ = self.po["fcw%d" % l]
            bo, _ = self.po["fcb%d" % l]
            it = 0
            for half in range(2):
              with contextlib.ExitStack() as st1:
                wg = [self.sb(st1, "wg%d" % i, [128, 8, 128], BF16) for i in range(2)]
                wu = [self.sb(st1, "wu%d" % i, [128, 8, 128], BF16) for i in range(2)]
                dg = [self.sb(st1, "dg%d" % i, [128, 9, 128], BF16) for i in range(2)]
                apc = [self.sb(st1, "apc%d" % i, [128, 258], BF16) for i in range(2)]
                apl = [None, None]
                apl[half] = [self.sb(st1, "apl%d_%d" % (half, i), [128, 18, 66], BF16) for i in range(2)]
                sg = [self.sb(st1, "sg%d" % i, [128, 512], BF16) for i in range(2)]
                for t_ in apc + apl[half]:
                    self.memset("pool", t_[:], 0.0, [t_.d()])
                if half == 0:
                    pieces = [(256, 8, 1), (256 + 512, 8, 9), (256 + 1024, 1, 17)]
                    oblks = [("c", 0, 256, 0), ("l", 256, 512, 0), ("l", 768, 512, 8)]
                else:
                    pieces = [(256 + 960, 1, 0), (256 + 1024, 8, 1), (256 + 1536, 8, 9)]
                    oblks = [("l", 1280, 512, 0), ("l", 1792, 512, 8)]
                row_off = 1 if half == 0 else 1
                def load(f, it):
                    self.dma("pool", wg[it % 2][:], self.ffg_d[l, f], (), [wg[it % 2].d()])
                    self.dma("pool", wu[it % 2][:], self.ffu_d[l, f], (), [wu[it % 2].d()])
                load(0, it)
                for f in range(22):
                    if f + 1 < 22:
                        load(f + 1, it + 1)
                    g_, u_, d_ = wg[it % 2], wu[it % 2], dg[it % 2]
                    pc, pl = apc[it % 2], apl[half][it % 2]
                    for tap in range(9):
                        self.ts("pool", d_[:, tap, :], self.identb[:], self.partile[:, fo + f * 9 + tap:fo + f * 9 + tap + 1], None,
                                ALU.mult, None, [self.cD], [d_.d()])
                    if half == 0:
                        ps = self.psum()
                        for k in range(8):
                            self.mm(ps[:, 0:256], g_[:, k, :], self.hx[:, k, 0:256], k == 0, k == 7, [g_.d(), self.hx.d()], [ps.d()])
                        self.cp("act", pc[:, 1:257], ps[:, 0:256], [ps.d()], [pc.d()])
                    for (tk0, nr, pr0) in pieces:
                        ps = self.psum()
                        n = nr * 64
                        for k in range(8):
                            self.mm(ps[:, 0:n], g_[:, k, :], self.hx[:, k, tk0:tk0 + n], k == 0, k == 7, [g_.d(), self.hx.d()], [ps.d()])
                        self.cp("act", pl[:, pr0:pr0 + nr, 1:65], ps[:, 0:n].rearrange("p (r c) -> p r c", c=64), [ps.d()], [pl.d()])
                    gcol = 0
                    for (kind, tk0, n, lr0) in oblks:
                        psc = self.psum()
                        if kind == "c":
                            for i, dx in enumerate((-1, 0, 1)):
                                self.mm(psc[:, 0:256], d_[:, 3 + (dx + 1), :], pc[:, 1 + dx:257 + dx], i == 0, i == 2, [d_.d(), pc.d()], [psc.d()])
                        else:
                            i = 0
                            for dy in (-1, 0, 1):
                                for dx in (-1, 0, 1):
                                    r0 = row_off + lr0 + dy
                                    self.mm(psc[:, 0:512].rearrange("p (r c) -> p r c", c=64), d_[:, (dy + 1) * 3 + (dx + 1), :],
                                            pl[:, r0:r0 + 8, 1 + dx:65 + dx], i == 0, i == 8, [d_.d(), pl.d()], [psc.d()])
                                    i += 1
                        sgt = sg[gcol % 2]
                        self.act(sgt[:, 0:n], psc[:, 0:n], AF.Silu, [psc.d(), self.cD], [sgt.d()], bias=self.partile[:, bo + f:bo + f + 1])
                        psu = self.psum()
                        for k in range(8):
                            self.mm(psu[:, 0:n], u_[:, k, :], self.hx[:, k, tk0:tk0 + n], k == 0, k == 7, [u_.d(), self.hx.d()], [psu.d()])
                        hoff = tk0 if half == 0 else tk0 - 1280
                        self.tt("dve", gh[:, f, hoff:hoff + n], sgt[:, 0:n], psu[:, 0:n], ALU.mult, [sgt.d(), psu.d()], [gh.d(f)])
                        gcol += 1
                    it += 1
              self.P.barrier()
              with contextlib.ExitStack() as st1:
                wd = [self.sb(st1, "wd%d" % i, [128, 22, 128], BF16) for i in range(2)]
                ghd = [gh.d(f) for f in range(22)]
                self.dma("pool", wd[0][:], self.ffd_d[l, 0], (), [wd[0].d()])
                for oc in range(8):
                    if oc + 1 < 8:
                        self.dma("pool", wd[(oc + 1) % 2][:], self.ffd_d[l, oc + 1], (), [wd[(oc + 1) % 2].d()])
                    w_ = wd[oc % 2]
                    for (kind, tk0, n, lr0) in oblks:
                        r = self.nseq if kind == "c" else s
                        hoff = tk0 if half == 0 else tk0 - 1280
                        ps = self.psum()
                        for f in range(22):
                            self.mm(ps[:, 0:n], w_[:, f, :], gh[:, f, hoff:hoff + n], f == 0, f == 21, [w_.d()] + ghd, [ps.d()])
                        m5 = self.mods[:, (l * 48 + 40 + oc) * NR + r:(l * 48 + 40 + oc) * NR + r + 1]
                        self.stt("dve", self.x[:, oc, tk0:tk0 + n], ps[:, 0:n], m5, self.x[:, oc, tk0:tk0 + n], ALU.mult, ALU.add,
                                 [ps.d(), self.cD, self.x.d()], [self.x.d()])
                self.P.barrier()


    def dump(self, tile, ch0, nch, slot0, dep=None):
        if not getattr(self, "debug", False):
            return
        self.P.barrier()
        with contextlib.ExitStack() as st:
            stg = self.sb(st, "dbgstg", [128, T], F32)
            for i in range(nch):
                src = tile[:, ch0 + i, :] if len(tile.t.shape) == 3 else tile[:, :]
                n = src.shape[-1]
                self.cp("dve", stg[:, 0:n], src, [dep or tile.d()], [stg.d()])
                self.out_ops.append(self.dma("sp", self.dbg[:, slot0 + i, 0:n], stg[:, 0:n], [stg.d()], ()))
            self.P.barrier()

    def dump_ap(self, ap, dep, slot):
        if not getattr(self, "debug", False):
            return
        self.P.barrier()
        with contextlib.ExitStack() as st:
            n = ap.shape[-1]
            stg = self.sb(st, "dbgstg2", [128, n], F32)
            self.cp("dve", stg[:, 0:n], ap, [dep], [stg.d()])
            self.out_ops.append(self.dma("sp", self.dbg[:, slot, 0:n], stg[:, 0:n], [stg.d()], ()))
            self.P.barrier()

    def outproj(self, wd_ap_fn, nk, mix, l, s, gate_chunk0):
        NR = self.NR
        with contextlib.ExitStack() as st:
            wo = [self.sb(st, "wo%d" % i, [128, nk, 128], BF16) for i in range(2)]
            self.dma("pool", wo[0][:], wd_ap_fn(0), (), [wo[0].d()])
            for oc in range(8):
                if oc + 1 < 8:
                    self.dma("pool", wo[(oc + 1) % 2][:], wd_ap_fn(oc + 1), (), [wo[(oc + 1) % 2].d()])
                w_ = wo[oc % 2]
                for (t0, n) in BLKS:
                    r = self.nseq if t0 == 0 else s
                    ps = self.psum()
                    for k in range(nk):
                        self.mm(ps[:, 0:n], w_[:, k, :], mix[:, k, t0:t0 + n], k == 0, k == nk - 1, [w_.d(), mix.d()], [ps.d()])
                    m2 = self.mods[:, (l * 48 + gate_chunk0 + oc) * NR + r:(l * 48 + gate_chunk0 + oc) * NR + r + 1]
                    self.stt("dve", self.x[:, oc, t0:t0 + n], ps[:, 0:n], m2, self.x[:, oc, t0:t0 + n], ALU.mult, ALU.add,
                             [ps.d(), self.cD, self.x.d()], [self.x.d()])
            self.P.barrier()

    def proj_conv(self, w_dram, cw_off, cb_off, wt, pad, dgt, dst, func, ntap=4, dst_fn=None, dst_dep=None):
        self.dma("pool", wt[:], w_dram, (), [wt.d()])
        for k in range(ntap):
            self.ts("pool", dgt[:, k, :], self.identb[:], self.partile[:, cw_off + k:cw_off + k + 1], None, ALU.mult, None,
                    [self.cD], [dgt.d()])
        for (t0, n) in BLKS:
            ps = self.psum()
            for k in range(8):
                self.mm(ps[:, 0:n], wt[:, k, :], self.hx[:, k, t0:t0 + n], k == 0, k == 7, [wt.d(), self.hx.d()], [ps.d()])
            po = 1 + t0 if t0 == 0 else 260 + (t0 - CTX)
            self.cp("act", pad[:, po:po + n], ps[:, 0:n], [ps.d()], [pad.d()])
        for (t0, n) in BLKS:
            ps = self.psum()
            po = t0 if t0 == 0 else 259 + (t0 - CTX)
            for k in range(ntap):
                self.mm(ps[:, 0:n], dgt[:, k, :], pad[:, po + k:po + k + n], k == 0, k == ntap - 1, [dgt.d(), pad.d()], [ps.d()])
            o_ap = dst_fn(t0, n) if dst_fn is not None else dst[:, t0:t0 + n]
            self.act(o_ap, ps[:, 0:n], func, [ps.d(), self.cD], [dst_dep or dst.d()], bias=self.partile[:, cb_off:cb_off + 1])

    def even_mixer(self, l, s):
        j = l // 2
        NR = self.NR
        A = lambda k, r: self.amix[:, (l * 8 + k) * NR + r:(l * 8 + k) * NR + r + 1]
        B = lambda k, r: self.mods[:, (l * 48 + k) * NR + r:(l * 48 + k) * NR + r + 1]
        with contextlib.ExitStack() as st0:
            with contextlib.ExitStack() as st:
                self.rms_mod(st, self.x, self.hx, A, B, s)
            self.P.barrier()
            mix = self.sb(st0, "mix", [128, 8, T], BF16)
            which = getattr(self, "even_parts", ("ssd", "lru"))
            if "ssd" in which:
                self.ssd(l, s, mix)
                self.dump(mix, 0, 8, 0)
                self.outproj(lambda oc: self.evout_d[j, oc, :, 0:8, :], 8, mix, l, s, 16)
            if "lru" in which:
                self.lru(l, s, mix)
                self.dump(mix, 0, 8, 8)
                self.outproj(lambda oc: self.evout_d[j, oc, :, 8:16, :], 8, mix, l, s, 16)

    def lru(self, l, s, mix):
        j = l // 2
        po = self.po
        with contextlib.ExitStack() as st:
            cA = self.sb(st, "cA", [128, 16], F32)
            wu = self.sb(st, "lwu", [128, 8, 128], BF16)
            wgy = [self.sb(st, "lwgy%d" % i, [128, 8, 128], BF16) for i in range(2)]
            wai = [self.sb(st, "lwai%d" % i, [128, 2, 2, 128], BF16) for i in range(2)]
            pad = self.sb(st, "lpad", [128, 2312], BF16)
            dgt = self.sb(st, "ldg", [128, 4, 128], BF16)
            ucb = self.sb(st, "ucb", [128, T], BF16)
            a_t = self.sb(st, "lru_a", [128, T], F32)
            bx = [self.sb(st, "lru_bx%d" % i, [128, T], BF16) for i in range(2)]
            tmp = [self.sb(st, "ltmp%d" % i, [128, 512], F32 if i < 3 else BF16) for i in range(6)]
            self.memset("pool", pad[:], 0.0, [pad.d()])
            lo, _ = po["llam%d" % j]
            self.act(cA[:], self.partile[:, lo:lo + 16], AF.Exp, [self.cD], [cA.d()], scale=-1.0)
            self.act(cA[:], cA[:], AF.Ln, [cA.d()], [cA.d()], bias=1.0)
            self.ts("dve", cA[:], cA[:], -8.0, None, ALU.mult, None, [cA.d()], [cA.d()])
            for jj in range(8):
                self.dma("pool", wgy[jj % 2][:], self.evin_d[j, 20 + jj], (), [wgy[jj % 2].d()])
                self.dma("pool", wai[jj % 2][:], self.lruw_d[j, :, :, :, jj, :], (), [wai[jj % 2].d()])
                self.proj_conv(self.evin_d[j, 28 + jj], po["lcw%d" % j][0] + jj * 4, po["lcb%d" % j][0] + jj, wu, pad, dgt, ucb, AF.Identity)
                w2 = wai[jj % 2]
                for d in range(2):
                    ba = self.partile[:, po["lba%d" % j][0] + d * 8 + jj:po["lba%d" % j][0] + d * 8 + jj + 1]
                    bi = self.partile[:, po["lbi%d" % j][0] + d * 8 + jj:po["lbi%d" % j][0] + d * 8 + jj + 1]
                    for (t0, n) in BLKS:
                        psa = self.psum()
                        self.mm(psa[:, 0:n], w2[:, 0, d, :], ucb[:, t0:t0 + n], True, True, [w2.d(), ucb.d()], [psa.d()])
                        psi_ = self.psum()
                        self.mm(psi_[:, 0:n], w2[:, 1, d, :], ucb[:, t0:t0 + n], True, True, [w2.d(), ucb.d()], [psi_.d()])
                        rt, it_, sq, t3 = tmp[0], tmp[1], tmp[2], tmp[3]
                        self.act(rt[:, 0:n], psa[:, 0:n], AF.Sigmoid, [psa.d(), self.cD], [rt.d()], bias=ba)
                        self.act(a_t[:, t0:t0 + n], rt[:, 0:n], AF.Exp, [rt.d(), cA.d()], [a_t.d()], scale=cA[:, d * 8 + jj:d * 8 + jj + 1])
                        self.act(it_[:, 0:n], psi_[:, 0:n], AF.Sigmoid, [psi_.d(), self.cD], [it_.d()], bias=bi)
                        self.act(sq[:, 0:n], a_t[:, t0:t0 + n], AF.Square, [a_t.d()], [sq.d()])
                        self.act(sq[:, 0:n], sq[:, 0:n], AF.Sqrt, [sq.d()], [sq.d()], scale=-1.0, bias=1.0)
                        self.tt("dve", t3[:, 0:n], sq[:, 0:n], it_[:, 0:n], ALU.mult, [sq.d(), it_.d()], [t3.d()])
                        self.tt("dve", bx[d][:, t0:t0 + n], t3[:, 0:n], ucb[:, t0:t0 + n], ALU.mult, [t3.d(), ucb.d()], [bx[d].d()])
                    b_ = bx[d]
                    if d == 0:
                        self.P.op("dve", (lambda o_, a_, b2: lambda e: e.tensor_tensor_scan(out=o_, data0=a_, data1=b2, initial=0.0, op0=ALU.mult, op1=ALU.add))(
                            b_[:, 0:T], a_t[:, 0:T], b_[:, 0:T]), [a_t.d(), b_.d()], [b_.d()])
                    else:
                        self.P.op("dve", (lambda o_, a_, b2: lambda e: e.tensor_tensor_scan(out=o_, data0=a_, data1=b2, initial=0.0, op0=ALU.mult, op1=ALU.add))(
                            rev(b_[:, 0:CTX]), rev(a_t[:, 0:CTX]), rev(b_[:, 0:CTX])), [a_t.d(), b_.d()], [b_.d()])
                        self.P.op("dve", (lambda o_, a_, b2, i_: lambda e: e.tensor_tensor_scan(out=o_, data0=a_, data1=b2, initial=i_, op0=ALU.mult, op1=ALU.add))(
                            rev(b_[:, CTX:T]), rev(a_t[:, CTX:T]), rev(b_[:, CTX:T]), b_[:, 0:1]), [a_t.d(), b_.d()], [b_.d()])
                wg_ = wgy[jj % 2]
                for (t0, n) in BLKS:
                    ps = self.psum()
                    for k in range(8):
                        self.mm(ps[:, 0:n], wg_[:, k, :], self.hx[:, k, t0:t0 + n], k == 0, k == 7, [wg_.d(), self.hx.d()], [ps.d()])
                    ge, sm = tmp[4], tmp[5]
                    self.act(ge[:, 0:n], ps[:, 0:n], AF.Gelu, [ps.d()], [ge.d()])
                    self.tt("pool", sm[:, 0:n], bx[0][:, t0:t0 + n], bx[1][:, t0:t0 + n], ALU.add, [bx[0].d(), bx[1].d()], [sm.d()])
                    self.tt("dve", mix[:, jj, t0:t0 + n], sm[:, 0:n], ge[:, 0:n], ALU.mult, [sm.d(), ge.d()], [mix.d()])
            self.P.barrier()

    def pe_T(self, out, in_, reads, writes):
        return self.P.op("pe", lambda e: e.transpose(out, in_, self.identb[:]), list(reads) + [self.cD], writes, pe_acc=True)

    def to_tokmajor_ap(self, src_fn, src_dep, dst):
        c = 0
        while c < 18:
            nq = min(4, 18 - c)
            ps = self.psum()
            pb = ps[:].bitcast(BF16)
            for q in range(nq):
                self.pe_T(pb[:, q * 128:(q + 1) * 128], src_fn(c + q), [src_dep], [ps.d()])
            self.cp("act", dst[:, c:c + nq, :], pb[:, 0:nq * 128].rearrange("p (q f) -> p q f", f=128), [ps.d()], [dst.d()])
            c += nq

    def ssd(self, l, s, mix):
        j = l // 2
        po = self.po
        tri = lambda d: self.cst[:, 128 + 128 * d:256 + 128 * d]
        onesf = self.cst[:, 384:512]
        bc = lambda ap, shape, ax: ap.unsqueeze(ax).to_broadcast(shape)
        with contextlib.ExitStack() as st:
            dt_tok = self.sb(st, "dt_tok", [128, 18, 32], F32)
            la_tok = self.sb(st, "la_tok", [128, 18, 32], F32)
            cumcol = self.sb(st, "cumcol", [128, 18, 32], F32)
            Bfm = self.sb(st, "Bfm", [128, T], BF16)
            Cfm = self.sb(st, "Cfm", [128, T], BF16)
            Hst = [self.sb(st, "Hst%d" % i, [128, 2, 64], F32) for i in range(2)]
            Hbf = [self.sb(st, "Hbf%d" % i, [128, 2, 64], BF16) for i in range(2)]
            NB = 2
            stA = contextlib.ExitStack()
            wdt = self.sb(stA, "swdt", [128, 8, 32], BF16)
            arow = self.sb(stA, "arow", [128, 32], F32)
            t9 = self.sb(stA, "t9", [128, 9, 32], F32)
            self.dma("pool", wdt[:], self.evdt_d[j], (), [wdt.d()])
            ao = po["alogrow%d" % j][0]
            bo = po["dtbrow%d" % j][0]
            self.act(arow[:], self.partile[:, ao:ao + 32], AF.Exp, [self.cD], [arow.d()])
            self.ts("dve", arow[:], arow[:], -1.0, None, ALU.mult, None, [arow.d()], [arow.d()])
            for half in range(2):
                ps = self.psum()
                for ci in range(9):
                    c = half * 9 + ci
                    for k in range(8):
                        self.mm(ps[:, ci * 32:(ci + 1) * 32], self.hx[:, k, c * 128:(c + 1) * 128], wdt[:, k, :], k == 0, k == 7,
                                [self.hx.d(), wdt.d()], [ps.d()])
                self.tt("dve", t9[:], ps[:, 0:288].rearrange("p (c h) -> p c h", h=32),
                        bc(self.partile[:, bo:bo + 32], [128, 9, 32], 1), ALU.add, [ps.d(), self.cD], [t9.d()])
                self.act(t9[:], t9[:], AF.Exp, [t9.d()], [t9.d()])
                self.act(dt_tok[:, half * 9:(half + 1) * 9, :], t9[:], AF.Ln, [t9.d()], [dt_tok.d()], bias=1.0)
                self.tt("dve", la_tok[:, half * 9:(half + 1) * 9, :], dt_tok[:, half * 9:(half + 1) * 9, :],
                        bc(arow[:], [128, 9, 32], 1), ALU.mult, [dt_tok.d(), arow.d()], [la_tok.d()])
            for half in range(2):
                ps = self.psum()
                for ci in range(9):
                    c = half * 9 + ci
                    for d in range(2):
                        self.mm(ps[:, ci * 32 + d * 16:ci * 32 + (d + 1) * 16], tri(d), la_tok[:, c, d * 16:(d + 1) * 16], True, True,
                                [la_tok.d(), self.cD], [ps.d()])
                self.cp("dve", cumcol[:, half * 9:(half + 1) * 9, :], ps[:, 0:288].rearrange("p (c h) -> p c h", h=32), [ps.d()], [cumcol.d()])
            self.P.barrier()
            stA.close()
            it = 0
            for g in range(2):
              with contextlib.ExitStack() as stB:
                wt = self.sb(stB, "swt", [128, 8, 128], BF16)
                pad = self.sb(stB, "spad", [128, 2312], BF16)
                dgt = self.sb(stB, "sdg", [128, 4, 128], BF16)
                self.memset("pool", pad[:], 0.0, [pad.d()])
                self.proj_conv(self.evin_d[j, 8 + 8 + g], po["scw%d" % j][0] + (8 + g) * 4, po["scb%d" % j][0] + 8 + g, wt, pad, dgt, Bfm, AF.Silu)
                self.proj_conv(self.evin_d[j, 8 + 10 + g], po["scw%d" % j][0] + (10 + g) * 4, po["scb%d" % j][0] + 10 + g, wt, pad, dgt, Cfm, AF.Silu)
                for i in range(4):
                    cx = 4 * g + i
                    self.proj_conv(self.evin_d[j, 8 + cx], po["scw%d" % j][0] + cx * 4, po["scb%d" % j][0] + cx, wt, pad, dgt, None, AF.Silu,
                                   dst_fn=(lambda cx_: lambda t0, n: mix[:, cx_, t0:t0 + n])(cx), dst_dep=mix.d())
                self.P.barrier()
              with contextlib.ExitStack() as stC:
                Btok = self.sb(stC, "Btok", [128, 18, 128], BF16)
                xstok = self.sb(stC, "xstok", [128, 18, 128], BF16)
                rhsla = [self.sb(stC, "rhsla%d" % i, [128, 2, 128], F32) for i in range(NB)]
                mt = [self.sb(stC, "mt%d" % i, [128, 2, 128], F32) for i in range(NB)]
                E_ = [self.sb(stC, "E%d" % i, [128, 2, 128], BF16) for i in range(NB)]
                Mh = [self.sb(stC, "Mh%d" % i, [128, 2, 128], BF16) for i in range(NB)]
                Eb = [self.sb(stC, "Eb%d" % i, [128, 2, 128], BF16) for i in range(NB)]
                Cex = [self.sb(stC, "Cex%d" % i, [128, 2, 128], BF16) for i in range(NB)]
                cbm = [self.sb(stC, "cbm%d" % i, [128, 128], BF16) for i in range(NB)]
                xdt = [self.sb(stC, "xdt%d" % i, [128, 2, 64], BF16) for i in range(NB)]
                xdtw = [self.sb(stC, "xdtw%d" % i, [128, 2, 64], BF16) for i in range(NB)]
                wv = [self.sb(stC, "wv%d" % i, [128, 2], F32) for i in range(NB)]
                dtot = [self.sb(stC, "dtot%d" % i, [128, 2], F32) for i in range(NB)]
                self.to_tokmajor_ap(lambda c: Bfm[:, c * 128:(c + 1) * 128], Bfm.d(), Btok)
                for i in range(4):
                    cx = 4 * g + i
                    self.to_tokmajor_ap((lambda cx_: lambda c: mix[:, cx_, c * 128:(c + 1) * 128])(cx), mix.d(), xstok)
                    sdo = po["sd%d" % j][0] + cx
                    self.ts("dve", mix[:, cx, :], mix[:, cx, :], self.partile[:, sdo:sdo + 1], None, ALU.mult, None, [mix.d(), self.cD], [mix.d()])
                    for d in range(2):
                        hd = d * 16 + 8 * g + 2 * i
                        last = 127 if d == 0 else 0
                        self.memset("pool", Hst[d][:], 0.0, [Hst[d].d()])
                        self.memset("pool", Hbf[d][:], 0.0, [Hbf[d].d()])
                        order = list(range(18)) if d == 0 else [1, 0] + list(range(17, 1, -1))
                        for c in order:
                            b = it % NB
                            it += 1
                            cs = slice(c * 128, (c + 1) * 128)
                            ps_cb = self.psum()
                            self.mm(ps_cb[:, 0:128], Bfm[:, cs], Cfm[:, cs], True, True, [Bfm.d(), Cfm.d()], [ps_cb.d()])
                            self.tt("dve", cbm[b][:], ps_cb[:, 0:128], tri(d), ALU.mult, [ps_cb.d(), self.cD], [cbm[b].d()])
                            self.tt("dve", rhsla[b][:], bc(tri(d), [128, 2, 128], 1), bc(la_tok[:, c, hd:hd + 2], [128, 2, 128], 2), ALU.mult,
                                    [la_tok.d(), self.cD], [rhsla[b].d()])
                            ps_cum = self.psum()
                            self.mm(ps_cum[:, 0:256], onesf, rhsla[b][:].rearrange("p h l -> p (h l)"), True, True, [rhsla[b].d(), self.cD], [ps_cum.d()])
                            cum3 = ps_cum[:, 0:256].rearrange("p (h l) -> p h l", l=128)
                            ccb = bc(cumcol[:, c, hd:hd + 2], [128, 2, 128], 2)
                            self.tt("dve", mt[b][:], cum3, ccb, ALU.min, [ps_cum.d(), cumcol.d()], [mt[b].d()])
                            self.tt("dve", mt[b][:], mt[b][:], ccb, ALU.subtract, [mt[b].d(), cumcol.d()], [mt[b].d()])
                            self.act(E_[b][:], mt[b][:], AF.Exp, [mt[b].d()], [E_[b].d()])
                            self.tt("pool", Mh[b][:], E_[b][:], bc(cbm[b][:], [128, 2, 128], 1), ALU.mult, [E_[b].d(), cbm[b].d()], [Mh[b].d()])
                            self.act(Eb[b][:], cum3, AF.Exp, [ps_cum.d()], [Eb[b].d()])
                            self.tt("pool", Cex[b][:], Eb[b][:], bc(Cfm[:, cs], [128, 2, 128], 1), ALU.mult, [Eb[b].d(), Cfm.d()], [Cex[b].d()])
                            self.tt("dve", xdt[b][:], xstok[:, c, :].rearrange("p (h q) -> p h q", q=64),
                                    bc(dt_tok[:, c, hd:hd + 2], [128, 2, 64], 2), ALU.mult, [xstok.d(), dt_tok.d()], [xdt[b].d()])
                            ps_y = self.psum()
                            for hh in range(2):
                                self.mm(ps_y[hh * 64:(hh + 1) * 64, 0:128], xdt[b][:, hh, :], Mh[b][:, hh, :], True, False,
                                        [xdt[b].d(), Mh[b].d()], [ps_y.d()])
                                self.mm(ps_y[hh * 64:(hh + 1) * 64, 0:128], Hbf[d][:, hh, :], Cex[b][:, hh, :], False, True,
                                        [Hbf[d].d(), Cex[b].d()], [ps_y.d()])
                            self.tt("dve", mix[:, cx, cs], mix[:, cx, cs], ps_y[:, 0:128], ALU.add, [mix.d(), ps_y.d()], [mix.d()])
                            self.tt("dve", wv[b][:], cum3[:, :, last], cumcol[:, c, hd:hd + 2], ALU.subtract, [ps_cum.d(), cumcol.d()], [wv[b].d()])
                            self.act(wv[b][:], wv[b][:], AF.Exp, [wv[b].d()], [wv[b].d()])
                            self.act(dtot[b][:], cum3[:, :, last], AF.Exp, [ps_cum.d()], [dtot[b].d()])
                            self.tt("dve", xdtw[b][:], xdt[b][:], bc(wv[b][:], [128, 2, 64], 2), ALU.mult, [xdt[b].d(), wv[b].d()], [xdtw[b].d()])
                            ps_h = self.psum()
                            self.mm(ps_h[:, 0:128], Btok[:, c, :], xdtw[b][:].rearrange("p h q -> p (h q)"), True, True, [Btok.d(), xdtw[b].d()], [ps_h.d()])
                            self.tt("dve", Hst[d][:], Hst[d][:], bc(dtot[b][:], [128, 2, 64], 2), ALU.mult, [Hst[d].d(), dtot[b].d()], [Hst[d].d()])
                            self.tt("dve", Hst[d][:], Hst[d][:], ps_h[:, 0:128].rearrange("p (h q) -> p h q", q=64), ALU.add, [Hst[d].d(), ps_h.d()], [Hst[d].d()])
                            self.cp("act", Hbf[d][:], Hst[d][:], [Hst[d].d()], [Hbf[d].d()])
                self.P.barrier()
            self.dump(mix, 0, 8, 8)
            wt = self.sb(st, "swt2", [128, 8, 128], BF16)
            szt = [self.sb(st, "szt%d" % i, [128, 512], BF16) for i in range(2)]
            for ch in range(8):
                self.dma("pool", wt[:], self.evin_d[j, ch], (), [wt.d()])
                for bi, (t0, n) in enumerate(BLKS):
                    ps = self.psum()
                    for k in range(8):
                        self.mm(ps[:, 0:n], wt[:, k, :], self.hx[:, k, t0:t0 + n], k == 0, k == 7, [wt.d(), self.hx.d()], [ps.d()])
                    z_ = szt[bi % 2]
                    self.act(z_[:, 0:n], ps[:, 0:n], AF.Silu, [ps.d()], [z_.d()])
                    self.tt("dve", mix[:, ch, t0:t0 + n], mix[:, ch, t0:t0 + n], z_[:, 0:n], ALU.mult, [mix.d(), z_.d()], [mix.d()])
            self.P.barrier()
        with contextlib.ExitStack() as st:
            go = po["sng%d" % j][0]
            self.rms_mod(st, mix, mix, lambda k, r: self.partile[:, go + k:go + k + 1], None, s)
            self.P.barrier()

    def prologue_lb(self):
        with contextlib.ExitStack() as st:
            lo = self.po["lbl"][0]
            L = self.partile[:, lo:lo + 24].rearrange("p (l h) -> p l h", h=6)
            mx = self.sb(st, "lbmx", [128, 6], F32)
            e = self.sb(st, "lbe", [128, 4, 6], F32)
            sm = self.sb(st, "lbs", [128, 6], F32)
            D = [self.cD]
            self.tt("dve", mx[:], L[:, 0, :], L[:, 1, :], ALU.max, D, D)
            self.tt("dve", mx[:], mx[:], L[:, 2, :], ALU.max, D, D)
            self.tt("dve", mx[:], mx[:], L[:, 3, :], ALU.max, D, D)
            self.tt("dve", e[:], L, mx[:].unsqueeze(1).to_broadcast([128, 4, 6]), ALU.subtract, D, D)
            self.act(e[:], e[:], AF.Exp, D, D)
            self.tt("dve", sm[:], e[:, 0, :], e[:, 1, :], ALU.add, D, D)
            self.tt("dve", sm[:], sm[:], e[:, 2, :], ALU.add, D, D)
            self.tt("dve", sm[:], sm[:], e[:, 3, :], ALU.add, D, D)
            self.P.op("dve", lambda en: en.reciprocal(out=sm[:], in_=sm[:]), D, D)
            self.tt("dve", e[:], e[:], sm[:].unsqueeze(1).to_broadcast([128, 4, 6]), ALU.mult, D, D)
            self.memset("dve", self.lbt[:, 0, :], 0.0, D)
            for l in range(1, 4):
                self.tt("dve", self.lbt[:, l, :], self.lbt[:, l - 1, :], e[:, l, :], ALU.add, D, D)
            self.P.barrier()

    def odd_mixer(self, l, s):
        j = l // 2
        NR = self.NR
        A = lambda k, r: self.amix[:, (l * 8 + k) * NR + r:(l * 8 + k) * NR + r + 1]
        B = lambda k, r: self.mods[:, (l * 48 + k) * NR + r:(l * 48 + k) * NR + r + 1]
        with contextlib.ExitStack() as st0:
            with contextlib.ExitStack() as st:
                self.rms_mod(st, self.x, self.hx, A, B, s)
            self.P.barrier()
            mix = self.sb(st0, "mixo", [128, 8, T], BF16)
            which = getattr(self, "odd_parts", ("hgrn", "s5"))
            if "hgrn" in which:
                self.hgrn(l, s, mix)
            else:
                self.memset("pool", mix[:, 0:6, :], 0.0, [mix.d()])
            if "s5" in which:
                self.s5(l, s, mix)
            else:
                self.memset("pool", mix[:, 6:8, :], 0.0, [mix.d()])
            self.dump(mix, 0, 8, 0)
            self.outproj(lambda oc: self.odout_d[j, oc], 8, mix, l, s, 16)

    def scan(self, out, d0, d1, init, reads, writes):
        return self.P.op("dve", lambda e: e.tensor_tensor_scan(out=out, data0=d0, data1=d1, initial=init, op0=ALU.mult, op1=ALU.add),
                         reads, writes)

    def hgrn(self, l, s, mix):
        j = l // 2
        po = self.po
        hgm = lambda d: self.cst[:, 512 + 128 * d:640 + 128 * d]
        with contextlib.ExitStack() as st:
            lbm = self.sb(st, "lbm", [128, 6, 2], F32)
            rmask_t = self.sb(st, "rmask", [128, T + 32], BF16)
            self.rmask = rmask_t
            self.memset("pool", rmask_t[:], 1.0, [self.cD])
            self.memset("pool", rmask_t[:].rearrange("p (c i) -> p c i", i=32)[:, :, 0:1], 0.0, [self.cD])
            wq = self.sb(st, "hwq", [128, 8, 128], BF16)
            wf = [self.sb(st, "hwf%d" % i, [128, 8, 128], BF16) for i in range(2)]
            wv = self.sb(st, "hwv", [128, 8, 128], BF16)
            vtok = self.sb(st, "vtok", [128, 18, 128], BF16)
            qt = [self.sb(st, "qt%d" % d, [128, T], BF16) for d in range(2)]
            kt = [self.sb(st, "kt%d" % d, [128, T], BF16) for d in range(2)]
            elast = [self.sb(st, "elast%d" % d, [128, 72], F32) for d in range(2)]
            eprev = [self.sb(st, "eprev%d" % d, [128, 72], F32) for d in range(2)]
            R = [self.sb(st, "R%d" % d, [128, 128], F32) for d in range(2)]
            Rb = [self.sb(st, "Rb%d" % d, [128, 128], BF16) for d in range(2)]
            qsl = self.sb(st, "qsl", [128, 512], F32)
            tA = [self.sb(st, "htA", [128, 512], F32)] * 2
            tB = [self.sb(st, "htB", [128, 512], F32)] * 2
            tC = [self.sb(st, "htC", [128, 512], F32)] * 2
            ktok = [self.sb(st, "ktok%d" % i, [128, 128], BF16) for i in range(2)]
            ktok2 = [self.sb(st, "ktokm%d" % i, [128, 128], BF16) for i in range(2)]
            attm = [self.sb(st, "attm%d" % i, [128, 128], BF16) for i in range(2)]
            self.ts("dve", lbm[:, :, 0], self.lbt[:, l, :], -1.0, 1.0, ALU.mult, ALU.add, [self.cD], [lbm.d()])
            self.ts("dve", lbm[:, :, 1], lbm[:, :, 0], -1.0, None, ALU.mult, None, [lbm.d()], [lbm.d()])
            self.memset("pool", mix[:, 0:6, :], 0.0, [mix.d()])
            it = 0
            for hh in range(6):
                oml = lbm[:, hh, 0:1]
                noml = lbm[:, hh, 1:2]
                lb = self.lbt[:, l, hh:hh + 1]
                self.dma("pool", wq[:], self.odin_d[j, hh], (), [wq.d()])
                self.dma("pool", wf[0][:], self.odin_d[j, 6 + hh], (), [wf[0].d()])
                self.dma("pool", wf[1][:], self.odin_d[j, 12 + hh], (), [wf[1].d()])
                self.dma("pool", wv[:], self.odv_d[j, :, :, hh * 128:(hh + 1) * 128], (), [wv.d()])
                c = 0
                while c < 18:
                    nq = min(4, 18 - c)
                    ps = self.psum()
                    for q in range(nq):
                        for k in range(8):
                            self.mm(ps[:, q * 128:(q + 1) * 128], self.hx[:, k, (c + q) * 128:(c + q + 1) * 128], wv[:, k, :], k == 0, k == 7,
                                    [self.hx.d(), wv.d()], [ps.d()])
                    self.cp("act", vtok[:, c:c + nq, :], ps[:, 0:nq * 128].rearrange("p (q f) -> p q f", f=128), [ps.d()], [vtok.d()])
                    c += nq
                for bi, (t0, n) in enumerate(BLKS):
                    ps = self.psum()
                    for k in range(8):
                        self.mm(ps[:, 0:n], wq[:, k, :], self.hx[:, k, t0:t0 + n], k == 0, k == 7, [wq.d(), self.hx.d()], [ps.d()])
                    self.act(qsl[:, 0:n], ps[:, 0:n], AF.Silu, [ps.d()], [qsl.d()])
                    for d in range(2):
                        a_, b_, c_ = tA[d], tB[d], tC[d]
                        ps = self.psum()
                        for k in range(8):
                            self.mm(ps[:, 0:n], wf[d][:, k, :], self.hx[:, k, t0:t0 + n], k == 0, k == 7, [wf[d].d(), self.hx.d()], [ps.d()])
                        self.act(a_[:, 0:n], ps[:, 0:n], AF.Sigmoid, [ps.d()], [a_.d()])
                        self.act(b_[:, 0:n], a_[:, 0:n], AF.Ln, [a_.d(), lbm.d(), self.cD], [b_.d()], scale=oml, bias=lb)
                        self.ts("dve", a_[:, 0:n], a_[:, 0:n], noml, oml, ALU.mult, ALU.add, [a_.d(), lbm.d()], [a_.d()])
                        if d == 0:
                            self.scan(c_[:, 0:n], self.rmask[:, t0:t0 + n], b_[:, 0:n], 0.0, [b_.d(), self.cD], [c_.d()])
                        else:
                            self.scan(rev(c_[:, 0:n]), rev(self.rmask[:, t0 + 1:t0 + n + 1]), rev(b_[:, 0:n]), 0.0, [b_.d(), self.cD], [c_.d()])
                        self.act(b_[:, 0:n], c_[:, 0:n], AF.Exp, [c_.d()], [b_.d()])
                        lastpos = 31 if d == 0 else 0
                        self.cp("dve", elast[d][:, t0 // 32:(t0 + n) // 32], b_[:, 0:n].rearrange("p (c i) -> p c i", i=32)[:, :, lastpos],
                                [b_.d()], [elast[d].d()])
                        self.tt("dve", qt[d][:, t0:t0 + n], qsl[:, 0:n], b_[:, 0:n], ALU.mult, [qsl.d(), b_.d()], [qt[d].d()])
                        self.act(c_[:, 0:n], c_[:, 0:n], AF.Exp, [c_.d()], [c_.d()], scale=-1.0)
                        self.tt("pool", kt[d][:, t0:t0 + n], a_[:, 0:n], c_[:, 0:n], ALU.mult, [a_.d(), c_.d()], [kt[d].d()])
                for d in range(2):
                    self.memset("dve", eprev[d][:], 1.0, [eprev[d].d()])
                    if d == 0:
                        self.cp("dve", eprev[d][:, 1:72], elast[d][:, 0:71], [elast[d].d()], [eprev[d].d()])
                    else:
                        self.cp("dve", eprev[d][:, 0:7], elast[d][:, 1:8], [elast[d].d()], [eprev[d].d()])
                        self.cp("dve", eprev[d][:, 8:71], elast[d][:, 9:72], [elast[d].d()], [eprev[d].d()])
                        self.cp("dve", eprev[d][:, 71:72], elast[d][:, 0:1], [elast[d].d()], [eprev[d].d()])
                    self.memset("pool", R[d][:], 0.0, [R[d].d()])
                    self.memset("pool", Rb[d][:], 0.0, [Rb[d].d()])
                blocks = [list(range(18)), [1, 0] + list(range(17, 1, -1))]
                for step in range(18):
                    for d in range(2):
                        bk = blocks[d][step]
                        b = it % 2
                        it += 1
                        ts_ = slice(bk * 128, (bk + 1) * 128)
                        ps_a = self.psum()
                        self.mm(ps_a[:, 0:128], kt[d][:, ts_], qt[d][:, ts_], True, True, [kt[d].d(), qt[d].d()], [ps_a.d()])
                        self.tt("dve", attm[b][:], ps_a[:, 0:128], hgm(d), ALU.mult, [ps_a.d(), self.cD], [attm[b].d()])
                        ps_t = self.psum()
                        pb = ps_t[:].bitcast(BF16)
                        self.pe_T(pb[:, 0:128], kt[d][:, ts_], [kt[d].d()], [ps_t.d()])
                        self.cp("act", ktok[b][:], pb[:, 0:128], [ps_t.d()], [ktok[b].d()])
                        self.cp("act", ktok2[b][64:128, :], pb[64:128, 0:128], [ps_t.d()], [ktok2[b].d()])
                        self.memset("pool", ktok2[b][64:96, :], 0.0, [ktok2[b].d()])
                        ps_o = self.psum()
                        self.mm(ps_o[:, 0:128], vtok[:, bk, :], attm[b][:], True, False, [vtok.d(), attm[b].d()], [ps_o.d()])
                        corder = [0, 1, 2, 3] if d == 0 else [3, 2, 1, 0]
                        for ci, cc in enumerate(corder):
                            cg = bk * 4 + cc
                            cs_ = slice(cg * 32, (cg + 1) * 32)
                            self.mm(ps_o[:, cc * 32:(cc + 1) * 32], Rb[d][:], qt[d][:, cs_], False, ci == 3, [Rb[d].d(), qt[d].d()], [ps_o.d()])
                            ps_u = self.psum()
                            if cc < 3:
                                self.mm(ps_u[:, 0:128], ktok[b][cc * 32:(cc + 1) * 32, :], vtok[cc * 32:(cc + 1) * 32, bk, :], True, True,
                                        [ktok[b].d(), vtok.d()], [ps_u.d()])
                            else:
                                self.mm(ps_u[:, 0:128], ktok2[b][64:128, :], vtok[64:128, bk, :], True, True,
                                        [ktok2[b].d(), vtok.d()], [ps_u.d()])
                            self.stt("dve", R[d][:], R[d][:], eprev[d][:, cg:cg + 1], ps_u[:, 0:128], ALU.mult, ALU.add,
                                     [R[d].d(), eprev[d].d(), ps_u.d()], [R[d].d()])
                            self.act(Rb[d][:], R[d][:], AF.Copy, [R[d].d(), elast[d].d()], [Rb[d].d()], scale=elast[d][:, cg:cg + 1])
                        self.tt("dve", mix[:, hh, ts_], mix[:, hh, ts_], ps_o[:, 0:128], ALU.add, [mix.d(), ps_o.d()], [mix.d()])
            self.P.barrier()
        with contextlib.ExitStack() as st:
            go = po["hgn%d" % j][0]
            self.rms_mod(st, mix, mix, lambda k, r: self.partile[:, go + k:go + k + 1], None, s, nch=6, pergroup=True)
            self.P.barrier()
        with contextlib.ExitStack() as st:
            wg_ = [self.sb(st, "hwg%d" % i, [128, 8, 128], BF16) for i in range(2)]
            sz = [self.sb(st, "hsz%d" % i, [128, 512], BF16) for i in range(2)]
            for hh in range(6):
                w_ = wg_[hh % 2]
                self.dma("pool", w_[:], self.odin_d[j, 18 + hh], (), [w_.d()])
                for bi, (t0, n) in enumerate(BLKS):
                    ps = self.psum()
                    for k in range(8):
                        self.mm(ps[:, 0:n], w_[:, k, :], self.hx[:, k, t0:t0 + n], k == 0, k == 7, [w_.d(), self.hx.d()], [ps.d()])
                    z_ = sz[bi % 2]
                    self.act(z_[:, 0:n], ps[:, 0:n], AF.Silu, [ps.d()], [z_.d()])
                    self.tt("dve", mix[:, hh, t0:t0 + n], mix[:, hh, t0:t0 + n], z_[:, 0:n], ALU.mult, [mix.d(), z_.d()], [mix.d()])
            self.P.barrier()

    def cexp_small(self, st, name, lr, li, ls, shape, D):
        mk = lambda n: self.sb(st, name + n, shape, F32)
        step, c, sn, mag, t1, t2 = mk("st"), mk("c"), mk("s"), mk("m"), mk("t1"), mk("t2")
        dd = [step.d()]
        self.act(step[:], ls, AF.Exp, D, dd)
        self.tt("dve", mag[:], lr, step[:], ALU.mult, D + dd, dd)
        self.act(mag[:], mag[:], AF.Exp, dd, dd)
        self.tt("dve", t1[:], li, step[:], ALU.mult, D + dd, dd)
        self.act(sn[:], t1[:], AF.Sin, dd, dd, scale=1.0 / 16.0)
        self.ts("dve", t2[:], t1[:], 1.0 / 16.0, 1.5707963267948966, ALU.mult, ALU.add, dd, dd)
        self.act(c[:], t2[:], AF.Sin, dd, dd)
        for _ in range(4):
            self.tt("dve", t1[:], c[:], c[:], ALU.mult, dd, dd)
            self.tt("dve", t2[:], sn[:], sn[:], ALU.mult, dd, dd)
            self.tt("dve", sn[:], sn[:], c[:], ALU.mult, dd, dd)
            self.ts("dve", sn[:], sn[:], 2.0, None, ALU.mult, None, dd, dd)
            self.tt("dve", c[:], t1[:], t2[:], ALU.subtract, dd, dd)
        for t_ in (c, sn, mag, t1, t2):
            t_.deps[None] = step.d()
        return c, sn, mag, step, t1, t2

    def s5(self, l, s, mix):
        j = l // 2
        po = self.po
        L = 256
        NTC = T // L
        D = [self.cD]
        with contextlib.ExitStack() as st:
            ufm = self.sb(st, "ufm", [128, 2, T], BF16)
            wu = self.sb(st, "s5wu", [128, 8, 128], BF16)
            for c in range(2):
                self.dma("pool", wu[:], self.odin_d[j, 24 + c], (), [wu.d()])
                for (t0, n) in BLKS:
                    ps = self.psum()
                    for k in range(8):
                        self.mm(ps[:, 0:n], wu[:, k, :], self.hx[:, k, t0:t0 + n], k == 0, k == 7, [wu.d(), self.hx.d()], [ps.d()])
                    self.cp("act", ufm[:, c, t0:t0 + n], ps[:, 0:n], [ps.d()], [ufm.d()])
            so = po["s5p%d" % j][0]
            pc, psn, pmag, _, _, _ = self.cexp_small(st, "sp", self.partile[:, so:so + 16], self.partile[:, so + 16:so + 32],
                                                    self.partile[:, so + 32:so + 48], [128, 16], D)
            pdep = [pc.d()]
            E = self.sb(st, "s5E", [128, 4, 2], F32)
            for d in range(2):
                for c in range(2):
                    with contextlib.ExitStack() as st2:
                        tabc = self.sb(st2, "tabc", [128, 4, L], F32)
                        tabs = self.sb(st2, "tabs", [128, 4, L], F32)
                        BM = self.sb(st2, "BM", [128, 4, 2, 128], BF16)
                        CM = self.sb(st2, "CM", [128, 4, 2, 128], BF16)
                        tdep = [tabc.d()]
                        tmp1 = self.sb(st2, "s5tm", [128, 128], F32)
                        for q4 in range(4):
                            q = c * 4 + q4
                            col = d * 8 + q
                            self.cp("dve", tabc[:, q4, 0:1], pc[:, col:col + 1], pdep, tdep)
                            self.cp("dve", tabs[:, q4, 0:1], psn[:, col:col + 1], pdep, tdep)
                            span = 1
                            while span < L:
                                cm_ = tabc[:, q4, span - 1:span]
                                sm_ = tabs[:, q4, span - 1:span]
                                lo, hi = slice(0, span), slice(span, 2 * span)
                                self.ts("dve", tmp1[:, 0:span], tabs[:, q4, lo], sm_, None, ALU.mult, None, tdep, [tmp1.d()])
                                self.stt("dve", tabc[:, q4, hi], tabc[:, q4, lo], cm_, tmp1[:, 0:span], ALU.mult, ALU.subtract, tdep + [tmp1.d()], tdep)
                                self.ts("dve", tmp1[:, 0:span], tabs[:, q4, lo], cm_, None, ALU.mult, None, tdep, [tmp1.d()])
                                self.stt("dve", tabs[:, q4, hi], tabc[:, q4, lo], sm_, tmp1[:, 0:span], ALU.mult, ALU.add, tdep + [tmp1.d()], tdep)
                                span *= 2
                            with contextlib.ExitStack() as st3:
                                rows = self.sb(st3, "s5rows", [128, 3, 128], F32)
                                bpad = self.sb(st3, "s5bp", [128, 2, 128], F32)
                                self.dma("pool", rows[:], self.s5row_d[j, d, :, q, :].unsqueeze(0).to_broadcast([128, 3, 128]), (), [rows.d()])
                                self.dma("act", bpad[:], self.s5b_d[j, q].rearrange("r p c -> p r c"), (), [bpad.d()])
                                self.dma("pool", CM[:, q4, :, :], self.s5c_d[j, d, q].rearrange("r p c -> p r c"), (), [CM.d()])
                                rd = [rows.d()]
                                rc, rs_, rmag, rstep, t1, t2 = self.cexp_small(st3, "sr", rows[:, 0, :], rows[:, 1, :], rows[:, 2, :], [128, 128], rd)
                                w = [rc.d()]
                                lr_, li_ = rows[:, 0, :], rows[:, 1, :]
                                ar, ai, den, zr, zi = rc, rs_, rstep, t1, t2
                                self.tt("dve", ar[:], rc[:], rmag[:], ALU.mult, w, w)
                                self.tt("dve", ai[:], rs_[:], rmag[:], ALU.mult, w, w)
                                self.tt("dve", den[:], lr_, lr_, ALU.mult, rd + w, w)
                                self.tt("dve", rmag[:], li_, li_, ALU.mult, rd + w, w)
                                self.tt("dve", den[:], den[:], rmag[:], ALU.add, w, w)
                                self.P.op("dve", (lambda t_: lambda e: e.reciprocal(out=t_[:], in_=t_[:]))(den), w, w)
                                self.ts("dve", ar[:], ar[:], -1.0, None, ALU.add, None, w, w)
                                self.tt("dve", zr[:], ar[:], lr_, ALU.mult, rd + w, w)
                                self.tt("dve", rmag[:], ai[:], li_, ALU.mult, rd + w, w)
                                self.tt("dve", zr[:], zr[:], rmag[:], ALU.add, w, w)
                                self.tt("dve", zr[:], zr[:], den[:], ALU.mult, w, w)
                                self.tt("dve", zi[:], ai[:], lr_, ALU.mult, rd + w, w)
                                self.tt("dve", rmag[:], ar[:], li_, ALU.mult, rd + w, w)
                                self.tt("dve", zi[:], zi[:], rmag[:], ALU.subtract, w, w)
                                self.tt("dve", zi[:], zi[:], den[:], ALU.mult, w, w)
                                bd = [bpad.d()]
                                self.tt("dve", ar[:], zr[:], bpad[:, 0, :], ALU.mult, w + bd, w)
                                self.tt("dve", ai[:], zi[:], bpad[:, 1, :], ALU.mult, w + bd, w)
                                self.tt("dve", BM[:, q4, 0, :], ar[:], ai[:], ALU.subtract, w, [BM.d()])
                                self.tt("dve", ar[:], zr[:], bpad[:, 1, :], ALU.mult, w + bd, w)
                                self.tt("dve", ai[:], zi[:], bpad[:, 0, :], ALU.mult, w + bd, w)
                                self.tt("dve", BM[:, q4, 1, :], ar[:], ai[:], ALU.add, w, [BM.d()])
                                self.ts("dve", CM[:, q4, 1, :], CM[:, q4, 1, :], -1.0, None, ALU.mult, None, [CM.d()], [CM.d()])
                                self.P.barrier()
                        self.memset("dve", E[:], 0.0, [E.d()])
                        h32 = [self.sb(st2, "s5h%d" % i, [128, 2, L], F32) for i in range(2)]
                        ta = [self.sb(st2, "s5ta%d" % i, [128, 4, L], F32) for i in range(2)]
                        hb = [self.sb(st2, "s5hb%d" % i, [128, 4, 2, L], BF16) for i in range(2)]
                        order = list(range(NTC)) if d == 0 else [0] + list(range(NTC - 1, 0, -1))
                        fl = (lambda a: a) if d == 0 else rev
                        lastpos = L - 1 if d == 0 else 0
                        it = 0
                        for ti, tc in enumerate(order):
                            t0 = tc * L
                            hbt = hb[ti % 2]
                            for q4 in range(4):
                                q = c * 4 + q4
                                col = d * 8 + q
                                b = it % 2
                                it += 1
                                ps_x = self.psum()
                                self.mm(ps_x[:, 0:L], BM[:, q4, 0, :], ufm[:, c, t0:t0 + L], True, True, [BM.d(), ufm.d()], [ps_x.d()])
                                self.mm(ps_x[:, L:2 * L], BM[:, q4, 1, :], ufm[:, c, t0:t0 + L], True, True, [BM.d(), ufm.d()], [ps_x.d()])
                                xr, xi = fl(ps_x[:, 0:L]), fl(ps_x[:, L:2 * L])
                                tcq, tsq = tabc[:, q4, :], tabs[:, q4, :]
                                a = ta[b]
                                hh_ = h32[b]
                                self.tt("dve", a[:, 0, :], tcq, xr, ALU.mult, tdep + [ps_x.d()], [a.d()])
                                self.tt("dve", a[:, 1, :], tsq, xi, ALU.mult, tdep + [ps_x.d()], [a.d()])
                                self.tt("dve", a[:, 2, :], tcq, xi, ALU.mult, tdep + [ps_x.d()], [a.d()])
                                self.tt("dve", a[:, 3, :], tsq, xr, ALU.mult, tdep + [ps_x.d()], [a.d()])
                                ad = [a.d()]
                                self.tt("pool", a[:, 0, :], a[:, 0, :], a[:, 1, :], ALU.add, ad, ad)
                                self.tt("pool", a[:, 2, :], a[:, 2, :], a[:, 3, :], ALU.subtract, ad, ad)
                                mg = pmag[:, col:col + 1].to_broadcast([128, L])
                                self.scan(a[:, 0, :], mg, a[:, 0, :], E[:, q4, 0:1], ad + [E.d()] + pdep, ad)
                                self.scan(a[:, 2, :], mg, a[:, 2, :], E[:, q4, 1:2], ad + [E.d()] + pdep, ad)
                                self.tt("pool", a[:, 1, :], tcq, a[:, 0, :], ALU.mult, tdep + ad, ad)
                                self.tt("dve", a[:, 3, :], tsq, a[:, 2, :], ALU.mult, tdep + ad, ad)
                                self.tt("pool", fl(hh_[:, 0, :]), a[:, 1, :], a[:, 3, :], ALU.subtract, ad, [hh_.d()])
                                self.tt("dve", a[:, 1, :], tsq, a[:, 0, :], ALU.mult, tdep + ad, ad)
                                self.tt("pool", a[:, 3, :], tcq, a[:, 2, :], ALU.mult, tdep + ad, ad)
                                self.tt("pool", fl(hh_[:, 1, :]), a[:, 1, :], a[:, 3, :], ALU.add, ad, [hh_.d()])
                                self.cp("dve", E[:, q4, :], hh_[:, :, lastpos], [hh_.d()], [E.d()])
                                self.cp("act", hbt[:, q4, :, :], hh_[:, :, :], [hh_.d()], [hbt.d()])
                            ps_y = self.psum()
                            for q4 in range(4):
                                for ri in range(2):
                                    self.mm(ps_y[:, 0:L], CM[:, q4, ri, :], hbt[:, q4, ri, :], q4 == 0 and ri == 0, q4 == 3 and ri == 1,
                                            [CM.d(), hbt.d()], [ps_y.d()])
                            if d == 0:
                                self.cp("act", mix[:, 6 + c, t0:t0 + L], ps_y[:, 0:L], [ps_y.d()], [mix.d()])
                            else:
                                self.tt("dve", mix[:, 6 + c, t0:t0 + L], mix[:, 6 + c, t0:t0 + L], ps_y[:, 0:L], ALU.add, [mix.d(), ps_y.d()], [mix.d()])
                        self.P.barrier()
            do = po["s5d%d" % j][0]
            gb = po["glub%d" % j][0]
            wgl = self.sb(st, "wglu", [128, 2, 256], BF16)
            yt = [self.sb(st, "s5yt%d" % i, [128, 512], F32) for i in range(2)]
            self.dma("pool", wgl[:], self.gluw_d[j], (), [wgl.d()])
            for c in range(2):
                for bi, (t0, n) in enumerate(BLKS):
                    y_ = yt[bi % 2]
                    self.stt("dve", y_[:, 0:n], ufm[:, c, t0:t0 + n], self.partile[:, do + c:do + c + 1], mix[:, 6 + c, t0:t0 + n], ALU.mult, ALU.add,
                             [ufm.d(), mix.d(), self.cD], [y_.d()])
                    self.act(ufm[:, c, t0:t0 + n], y_[:, 0:n], AF.Gelu, [y_.d()], [ufm.d()])
            for c in range(2):
                for bi, (t0, n) in enumerate(BLKS):
                    ps = self.psum()
                    for k in range(2):
                        self.mm(ps[:, 0:n], wgl[:, k, c * 128:(c + 1) * 128], ufm[:, k, t0:t0 + n], k == 0, k == 1, [wgl.d(), ufm.d()], [ps.d()])
                    y_ = yt[bi % 2]
                    self.act(y_[:, 0:n], ps[:, 0:n], AF.Sigmoid, [ps.d(), self.cD], [y_.d()], bias=self.partile[:, gb + c:gb + c + 1])
                    self.tt("dve", mix[:, 6 + c, t0:t0 + n], ufm[:, c, t0:t0 + n], y_[:, 0:n], ALU.mult, [ufm.d(), y_.d()], [mix.d()])
            self.P.barrier()

    def final_out(self, s):
        NR = self.NR
        if self.final:
            go, _ = self.po["fng"]
            A = lambda k, r: self.partile[:, go + k:go + k + 1]
            with contextlib.ExitStack() as st:
                self.rms_mod(st, self.x, self.x, A, None, s)
                self.out_ops.append(self.dma("sp", self.yout[s, :, 0:4, :], self.x[:, 0:4, CTX:T], [self.x.d()], ()))
                self.out_ops.append(self.dma("act", self.yout[s, :, 4:8, :], self.x[:, 4:8, CTX:T], [self.x.d()], ()))
                self.P.barrier()
        else:
            self.out_ops.append(self.dma("sp", self.yout[s, :, 0:4, :], self.x[:, 0:4, CTX:T], [self.x.d()], ()))
            self.out_ops.append(self.dma("act", self.yout[s, :, 4:8, :], self.x[:, 4:8, CTX:T], [self.x.d()], ()))


def make_consts():
    c = np.zeros((128, 768), np.float32)
    c[:, 0:128] = np.eye(128, dtype=np.float32)
    i = np.arange(128)
    c[:, 128:256] = (i[:, None] <= i[None, :]).astype(np.float32)
    c[:, 256:384] = (i[:, None] >= i[None, :]).astype(np.float32)
    c[:, 384:512] = 1.0
    same = (i[:, None] // 32) == (i[None, :] // 32)
    c[:, 512:640] = (same & (i[:, None] <= i[None, :])).astype(np.float32)
    c[:, 640:768] = (same & (i[:, None] >= i[None, :])).astype(np.float32)
    return c


def kernel(**inputs):
    nseq = 4
    inp = {k: np.asarray(v) for k, v in inputs.items()}
    phases = []
    for l in range(4):
        phases += [("mix", l), ("ffn", l)]
    kb = K(nseq, phases, final=True)
    nc = kb.build()
    sh = host_prep(inp)
    sh["cst"] = make_consts()
    in_maps = []
    for c in range(NCORES):
        m = dict(sh)
        m.update(core_inputs(inp, c, nseq))
        in_maps.append(m)
    res = run_bass_kernel_spmd(nc, in_maps, core_ids=list(range(NCORES)))
    outs = []
    for c in range(NCORES):
        y = res.results[c]["yout"]
        outs.append(y.transpose(0, 3, 2, 1).reshape(nseq, 2048, 1024))
    return np.ascontiguousarray(np.concatenate(outs, axis=0)).astype(np.float32)
```

```python
import contextlib
import numpy as np
import concourse.bass as bass
import concourse.mybir as mybir
from concourse.ap import AP
from concourse.bass_utils import run_bass_kernel_spmd

F32 = mybir.dt.float32
BF16 = mybir.dt.bfloat16
AF = mybir.ActivationFunctionType
ALU = mybir.AluOpType

T = 2304
CTX = 256
BLKS = [(0, 256), (256, 512), (768, 512), (1280, 512), (1792, 512)]
NCORES = 8
EPS = 1e-6


class Dep:
    __slots__ = ("w", "r", "rd", "const")

    def __init__(self, const=False):
        self.w = None
        self.r = {}
        self.rd = []
        self.const = const


class Op:
    __slots__ = ("eng", "fn", "deps", "marked", "ev", "dma", "idx", "bar", "epoch")


class Prog:
    DMA_SEMS = {"sp": 6, "act": 6, "pool": 24}
    ENGS = ("pe", "act", "dve", "pool", "sp")

    def __init__(self, nc):
        self.nc = nc
        self.ops = []
        self.last = {}
        self.pending_dma = []
        self.nbar = 0

    def _new(self, eng, fn, dma):
        o = Op()
        o.eng = eng
        o.fn = fn
        o.dma = dma
        o.marked = dma
        o.ev = None
        o.bar = 0
        o.epoch = -1
        o.idx = len(self.ops)
        self.ops.append(o)
        return o

    def op(self, eng, fn, reads=(), writes=(), dma=False, pe_acc=False):
        deps = set()
        for d in reads:
            if d.w is not None:
                deps.add(d.w)
        for d in writes:
            if d.w is not None:
                if not (pe_acc and d.w.eng == "pe" and not d.w.dma):
                    deps.add(d.w)
            for r in d.r.values():
                deps.add(r)
            for r in d.rd:
                deps.add(r)
        o = self._new(eng, fn, dma)
        o.deps = deps
        for d in reads:
            if not d.const:
                if dma:
                    d.rd.append(o)
                else:
                    d.r[eng] = o
        for d in writes:
            d.w = o
            d.r = {}
            d.rd = []
        if dma:
            self.pending_dma.append(o)
        else:
            self.last[eng] = o
        return o

    def barrier(self):
        deps = set(self.last.values()) | set(self.pending_dma)
        self.pending_dma = []
        self.last = {}
        self.nbar += 1
        for e in self.ENGS:
            o = self._new(e, None, False)
            o.deps = set(deps)
            o.bar = self.nbar

    def emit(self, final_deps):
        nc = self.nc
        engs = {"pe": nc.tensor, "act": nc.scalar, "dve": nc.vector, "pool": nc.gpsimd, "sp": nc.sync}
        fin = self._new("sp", None, False)
        fin.deps = set(final_deps)
        for o in self.ops:
            for d in o.deps:
                d.marked = True
        cnt = {e: 0 for e in engs}
        dma_rr = {e: 0 for e in engs}
        dma_cnt = {}
        seen = {e: {} for e in engs}
        per_eng = {e: [] for e in engs}
        epoch = 0
        maxv = 0
        nb_in_group = 0
        for o in self.ops:
            mw = {}
            o.epoch = epoch
            if o.dma:
                k = dma_rr[o.eng] % self.DMA_SEMS[o.eng]
                dma_rr[o.eng] += 1
                sk_own = ("dma", o.eng, k)
                prev = dma_cnt.get(sk_own, 0)
                if prev > 0 and seen[o.eng].get(sk_own, 0) < prev:
                    mw[sk_own] = prev
                    seen[o.eng][sk_own] = prev
                dma_cnt[sk_own] = prev + 16
                o.ev = (sk_own, prev + 16)
                maxv = max(maxv, prev + 16)
            elif o.marked:
                cnt[o.eng] += 1
                o.ev = (("eng", o.eng), cnt[o.eng])
                maxv = max(maxv, cnt[o.eng])
            for d in o.deps:
                if d.epoch != epoch:
                    continue
                sk, v = d.ev
                if seen[o.eng].get(sk, 0) < v:
                    seen[o.eng][sk] = v
                    mw[sk] = max(mw.get(sk, 0), v)
            per_eng[o.eng].append((o, list(mw.items())))
            if o.bar:
                nb_in_group += 1
                if nb_in_group == len(self.ENGS):
                    nb_in_group = 0
                    epoch += 1
                    cnt = {e: 0 for e in engs}
                    seen = {e: {sk: v for sk, v in seen[e].items() if sk[0] == "dma"} for e in engs}
        assert maxv < 8000, maxv
        self.stats = {e: len(per_eng[e]) for e in engs}
        self.stats["sem_maxv"] = maxv
        self.stats["nbar"] = self.nbar
        with contextlib.ExitStack() as st:
            sems = {}
            for e in engs:
                sems[("eng", e)] = st.enter_context(nc.semaphore("s_" + e))
            for e in ("sp", "act", "pool"):
                for k in range(self.DMA_SEMS[e]):
                    sems[("dma", e, k)] = st.enter_context(nc.semaphore("d_%s%d" % (e, k)))
            bsemA = st.enter_context(nc.semaphore("barA"))
            bsemB = st.enter_context(nc.semaphore("barB"))
            block = st.enter_context(nc.Block())
            NE = len(self.ENGS)

            def mk(ename):
                def body(eng):
                    for o, waits in per_eng[ename]:
                        for sk, v in waits:
                            eng.wait_ge(sems[sk], v)
                        if o.bar:
                            eng.sem_inc(bsemA, 1)
                            if ename == "sp":
                                eng.wait_ge(bsemA, NE * o.bar)
                                for sk_, sm in sems.items():
                                    if sk_[0] == "eng":
                                        eng.sem_clear(sm)
                                eng.sem_inc(bsemB, 1)
                            eng.wait_ge(bsemB, o.bar)
                            continue
                        if o.fn is None:
                            continue
                        ins = o.fn(eng)
                        if o.dma:
                            ins.then_inc(sems[o.ev[0]], 16)
                        elif o.marked:
                            ins.then_inc(sems[("eng", ename)], 1)
                return body

            block.tensor(mk("pe"))
            block.scalar(mk("act"))
            block.vector(mk("dve"))
            block.gpsimd(mk("pool"))
            block.sync(mk("sp"))


class Tile:
    def __init__(self, t):
        self.t = t
        self.deps = {}

    def d(self, key=None):
        if key not in self.deps:
            self.deps[key] = Dep()
        return self.deps[key]

    def __getitem__(self, idx):
        return self.t[idx]


def rev(ap):
    apl = [list(x) for x in ap.ap]
    n = apl[-1][1]
    off = ap.offset + (n - 1) * apl[-1][0]
    apl[-1][0] = -apl[-1][0]
    return AP(ap.tensor, off, apl)


def fm_vec(v):
    v = np.asarray(v, np.float32).reshape(-1, 128)
    return np.ascontiguousarray(v.T)


def w_colchunks(w, nk):
    K, N = w.shape
    return np.ascontiguousarray(w.reshape(nk, 128, N // 128, 128).transpose(2, 1, 0, 3))


def w_rows(w, nk):
    K, N = w.shape
    return np.ascontiguousarray(w.reshape(nk, 128, N).transpose(1, 0, 2))


class ParPack:
    def __init__(self):
        self.cols = []
        self.off = {}
        self.n = 0

    def add(self, name, arr):
        arr = np.asarray(arr, np.float32)
        arr = arr.reshape(arr.shape[0], -1)
        if arr.shape[0] < 128:
            arr = np.concatenate([arr, np.zeros((128 - arr.shape[0], arr.shape[1]), np.float32)], 0)
        self.off[name] = (self.n, arr.shape[1])
        self.cols.append(arr)
        self.n += arr.shape[1]

    def pack(self):
        return np.ascontiguousarray(np.concatenate(self.cols, axis=1))


def pack_params(inp, off_only=False):
    pp = ParPack()
    z = (lambda *s: np.zeros(s, np.float32))
    g = (lambda k: inp[k]) if not off_only else None
    for l in range(4):
        pp.add("nmg%d" % l, fm_vec(g("norm_mix_g")[l]) if g else z(128, 8))
        pp.add("nfg%d" % l, fm_vec(g("norm_ffn_g")[l]) if g else z(128, 8))
        pp.add("bmod%d" % l, fm_vec(g("b_mod")[l]) if g else z(128, 48))
        if g:
            cw = g("ffn_conv_w")[l].reshape(9, 22, 128).transpose(2, 1, 0)
            pp.add("fcw%d" % l, cw)
            pp.add("fcb%d" % l, fm_vec(g("ffn_conv_b")[l]))
        else:
            pp.add("fcw%d" % l, z(128, 22 * 9))
            pp.add("fcb%d" % l, z(128, 22))
    pp.add("fng", fm_vec(g("final_norm_g")) if g else z(128, 8))
    for j in range(2):
        if g:
            pp.add("scw%d" % j, g("ssd_conv_w")[j].reshape(4, 12, 128).transpose(2, 1, 0))
            pp.add("scb%d" % j, fm_vec(g("ssd_conv_b")[j]))
            pp.add("sng%d" % j, fm_vec(g("ssd_norm_g")[j]))
            pp.add("sd%d" % j, fm_vec(np.repeat(g("ssd_d")[j], 64)))
            pp.add("dtb%d" % j, g("ssd_dt_bias")[j].reshape(32, 1))
            pp.add("alog%d" % j, g("ssd_a_log")[j].reshape(32, 1))
            pp.add("dtbrow%d" % j, np.tile(g("ssd_dt_bias")[j].reshape(1, 32), (128, 1)))
            pp.add("alogrow%d" % j, np.tile(g("ssd_a_log")[j].reshape(1, 32), (128, 1)))
            pp.add("lcw%d" % j, g("lru_conv_w")[j].reshape(4, 8, 128).transpose(2, 1, 0))
            pp.add("lcb%d" % j, fm_vec(g("lru_conv_b")[j]))
            pp.add("lba%d" % j, g("lru_b_a")[j].reshape(2, 8, 128).transpose(2, 0, 1))
            pp.add("lbi%d" % j, g("lru_b_i")[j].reshape(2, 8, 128).transpose(2, 0, 1))
            pp.add("llam%d" % j, g("lru_lam")[j].reshape(2, 8, 128).transpose(2, 0, 1))
        else:
            pp.add("scw%d" % j, z(128, 48)); pp.add("scb%d" % j, z(128, 12)); pp.add("sng%d" % j, z(128, 8))
            pp.add("sd%d" % j, z(128, 8)); pp.add("dtb%d" % j, z(128, 1)); pp.add("alog%d" % j, z(128, 1))
            pp.add("dtbrow%d" % j, z(128, 32)); pp.add("alogrow%d" % j, z(128, 32))
            pp.add("lcw%d" % j, z(128, 32)); pp.add("lcb%d" % j, z(128, 8)); pp.add("lba%d" % j, z(128, 16))
            pp.add("lbi%d" % j, z(128, 16)); pp.add("llam%d" % j, z(128, 16))
    if g:
        pp.add("lbl", g("hg_lb_logits").reshape(4, 6, 128).transpose(2, 0, 1))
    else:
        pp.add("lbl", z(128, 24))
    for j in range(2):
        if g:
            pp.add("hgn%d" % j, g("hg_norm_g")[j].reshape(6, 128).T)
            pp.add("s5d%d" % j, fm_vec(g("s5_d")[j]))
            pp.add("glub%d" % j, fm_vec(g("s5_glu_b")[j]))
            sp_ = np.zeros((128, 3, 2, 8), np.float32)
            for gg in range(16):
                sp_[(gg % 2) * 64:(gg % 2) * 64 + 64, 0, :, gg // 2] = g("s5_lam_re")[j][:, gg].T
                sp_[(gg % 2) * 64:(gg % 2) * 64 + 64, 1, :, gg // 2] = g("s5_lam_im")[j][:, gg].T
                sp_[(gg % 2) * 64:(gg % 2) * 64 + 64, 2, :, gg // 2] = g("s5_log_step")[j][:, gg][None, :]
            pp.add("s5p%d" % j, sp_)
        else:
            pp.add("hgn%d" % j, z(128, 6)); pp.add("s5d%d" % j, z(128, 2)); pp.add("glub%d" % j, z(128, 2))
            pp.add("s5p%d" % j, z(128, 48))
    return pp


def host_prep(inp, nseq_total=32):
    sh = {}
    sh["par"] = pack_params(inp).pack()
    sh["wmod"] = np.stack([w_rows(inp["w_mod"][l], 8) for l in range(4)])
    sh["ffg"] = np.stack([w_colchunks(inp["ffn_w_gate"][l], 8) for l in range(4)])
    sh["ffu"] = np.stack([w_colchunks(inp["ffn_w_up"][l], 8) for l in range(4)])
    sh["ffd"] = np.stack([w_colchunks(inp["ffn_w_down"][l], 22) for l in range(4)])
    ev = inp["ev_w_in"]
    evc = np.concatenate([ev[:, :, 0:2560], ev[:, :, 2592:4640]], axis=2)
    sh["evin"] = np.stack([w_colchunks(evc[j], 8) for j in range(2)])
    sh["evdt"] = np.stack([w_rows(ev[j][:, 2560:2592], 8) for j in range(2)])
    sh["evout"] = np.stack([w_colchunks(inp["ev_w_out"][j], 16) for j in range(2)])
    la = np.stack([inp["lru_w_a"], inp["lru_w_i"]], axis=1)
    sh["lruw"] = np.ascontiguousarray(la.transpose(0, 4, 1, 2, 3, 5))
    od = inp["od_w_in"]
    odc = np.concatenate([od[:, :, 0:2304], od[:, :, 3072:4096]], axis=2)
    sh["odin"] = np.stack([w_colchunks(odc[j], 8) for j in range(2)])
    sh["odv"] = np.stack([w_rows(od[j][:, 2304:3072], 8) for j in range(2)])
    sh["odout"] = np.stack([w_colchunks(inp["od_w_out"][j], 8) for j in range(2)])
    sh["gluw"] = np.stack([w_rows(inp["s5_glu_w"][j], 2) for j in range(2)])
    s5b = np.zeros((2, 8, 2, 128, 128), np.float32)
    s5c = np.zeros((2, 2, 8, 2, 128, 128), np.float32)
    s5row = np.zeros((2, 2, 3, 8, 128), np.float32)
    for g in range(16):
        q, gi, go = g // 2, g % 8, g % 2
        for ri, nm in enumerate(("s5_b_re", "s5_b_im")):
            s5b[:, q, ri, gi * 16:(gi + 1) * 16, go * 64:(go + 1) * 64] = inp[nm][:, g].transpose(0, 2, 1)
        for ri, nm in enumerate(("s5_c_re", "s5_c_im")):
            s5c[:, :, q, ri, go * 64:(go + 1) * 64, gi * 16:(gi + 1) * 16] = inp[nm][:, :, g].transpose(0, 1, 3, 2)
        s5row[:, :, 0, q, go * 64:(go + 1) * 64] = inp["s5_lam_re"][:, :, g]
        s5row[:, :, 1, q, go * 64:(go + 1) * 64] = inp["s5_lam_im"][:, :, g]
        s5row[:, :, 2, q, go * 64:(go + 1) * 64] = inp["s5_log_step"][:, :, g][:, :, None]
    sh["s5b"] = s5b
    sh["s5c"] = s5c
    sh["s5row"] = s5row
    return sh


def core_inputs(inp, core, nseq):
    b0 = core * nseq
    xs = []
    for s in range(nseq):
        full = np.concatenate([inp["ctx"][b0 + s], inp["x"][b0 + s]], axis=0)
        xs.append(full.reshape(T, 8, 128).transpose(2, 1, 0))
    cc = np.concatenate([inp["c"][b0:b0 + nseq], inp["c_ctx"][None, :]], axis=0)
    ccf = cc.reshape(nseq + 1, 8, 128).transpose(2, 1, 0)
    return {"xin": np.ascontiguousarray(np.stack(xs)), "cc": np.ascontiguousarray(ccf)}


class K:
    def __init__(self, nseq, phases, final=True):
        self.nseq = nseq
        self.NR = nseq + 1
        self.phases = phases
        self.final = final
        self.nc = bass.Bass("TRN2", target_bir_lowering=False)
        self.P = Prog(self.nc)
        self.po = pack_params(None, off_only=True).off
        self.npar = pack_params(None, off_only=True).n

    def sb(self, st, name, shape, dt):
        self.uid = getattr(self, "uid", 0) + 1
        return Tile(st.enter_context(self.nc.sbuf_tensor("t%d_%s" % (self.uid, name), shape, dt)))

    def dram_in(self, name, shape):
        return self.nc.dram_tensor(name, list(shape), F32, kind="ExternalInput").ap()

    def psum(self):
        t = self.ps[self.psi % 8]
        self.psi += 1
        return t

    def par(self, name, c0=0, n=1):
        o, w = self.po[name]
        return self.partile[:, o + c0:o + c0 + n]

    def mm(self, out, lhsT, rhs, start, stop, reads, writes):
        return self.P.op("pe", lambda e: e.matmul(out, lhsT, rhs, start=start, stop=stop), reads, writes, pe_acc=True)

    def act(self, out, in_, func, reads, writes, bias=None, scale=None):
        kw = {}
        if bias is not None:
            kw["bias"] = bias
        if scale is not None:
            kw["scale"] = scale
        return self.P.op("act", lambda e: e.activation(out=out, in_=in_, func=func, **kw), reads, writes)

    def tt(self, eng, out, in0, in1, op, reads, writes):
        return self.P.op(eng, lambda e: e.tensor_tensor(out=out, in0=in0, in1=in1, op=op), reads, writes)

    def ts(self, eng, out, in0, s1, s2, op0, op1, reads, writes):
        if s2 is None:
            return self.P.op(eng, lambda e: e.tensor_scalar(out=out, in0=in0, scalar1=s1, scalar2=None, op0=op0), reads, writes)
        return self.P.op(eng, lambda e: e.tensor_scalar(out=out, in0=in0, scalar1=s1, scalar2=s2, op0=op0, op1=op1), reads, writes)

    def stt(self, eng, out, in0, scalar, in1, op0, op1, reads, writes):
        return self.P.op(eng, lambda e: e.scalar_tensor_tensor(out=out, in0=in0, scalar=scalar, in1=in1, op0=op0, op1=op1), reads, writes)

    def cp(self, eng, out, in_, reads, writes):
        if eng == "act":
            return self.P.op("act", lambda e: e.copy(out=out, in_=in_), reads, writes)
        return self.P.op(eng, lambda e: e.tensor_copy(out=out, in_=in_), reads, writes)

    def dma(self, eng, out, in_, reads, writes):
        return self.P.op(eng, lambda e: e.dma_start(out=out, in_=in_), reads, writes, dma=True)

    def memset(self, eng, ap, val, writes):
        return self.P.op(eng, lambda e: e.memset(ap, val), (), writes)

    def build(self):
        nc, P = self.nc, self.P
        NR = self.NR
        self.xin = self.dram_in("xin", [self.nseq, 128, 8, T])
        self.cc = self.dram_in("cc", [128, 8, NR])
        self.par_d = self.dram_in("par", [128, self.npar])
        self.wmod_d = self.dram_in("wmod", [4, 128, 8, 6144])
        self.ffg_d = self.dram_in("ffg", [4, 22, 128, 8, 128])
        self.ffu_d = self.dram_in("ffu", [4, 22, 128, 8, 128])
        self.ffd_d = self.dram_in("ffd", [4, 8, 128, 22, 128])
        self.evin_d = self.dram_in("evin", [2, 36, 128, 8, 128])
        self.evdt_d = self.dram_in("evdt", [2, 128, 8, 32])
        self.evout_d = self.dram_in("evout", [2, 8, 128, 16, 128])
        self.lruw_d = self.dram_in("lruw", [2, 128, 2, 2, 8, 128])
        self.cst_d = self.dram_in("cst", [128, 768])
        self.odin_d = self.dram_in("odin", [2, 26, 128, 8, 128])
        self.odv_d = self.dram_in("odv", [2, 128, 8, 768])
        self.odout_d = self.dram_in("odout", [2, 8, 128, 8, 128])
        self.gluw_d = self.dram_in("gluw", [2, 128, 2, 256])
        self.s5b_d = self.dram_in("s5b", [2, 8, 2, 128, 128])
        self.s5c_d = self.dram_in("s5c", [2, 2, 8, 2, 128, 128])
        self.s5row_d = self.dram_in("s5row", [2, 2, 3, 8, 128])
        self.yout = nc.dram_tensor("yout", [self.nseq, 128, 8, 2048], F32, kind="ExternalOutput").ap()
        if getattr(self, "debug", False):
            self.dbg = nc.dram_tensor("dbg", [128, 16, T], F32, kind="ExternalOutput").ap()
        self.out_ops = []
        with contextlib.ExitStack() as st:
            self.ps = [Tile(st.enter_context(nc.psum_tensor("ps%d" % i, [128, 512], F32))) for i in range(8)]
            self.psi = 0
            self.partile = self.sb(st, "par", [128, self.npar], F32)
            self.cst = self.sb(st, "cst", [128, 768], F32)
            self.identb = self.sb(st, "identb", [128, 128], BF16)
            self.onesb = self.sb(st, "onesb", [128, 128], BF16)
            self.mods = self.sb(st, "mods", [128, 4 * 48 * NR], F32)
            self.amix = self.sb(st, "amix", [128, 4 * 8 * NR], F32)
            self.affn = self.sb(st, "affn", [128, 4 * 8 * NR], F32)
            self.lbt = self.sb(st, "lbt", [128, 4, 6], F32)
            self.x = self.sb(st, "x", [128, 8, T], F32)
            self.hx = self.sb(st, "hx", [128, 8, T], BF16)
            self.cD = Dep(const=True)
            o1 = self.dma("sp", self.partile[:], self.par_d, (), [self.cD])
            o2 = self.dma("sp", self.cst[:], self.cst_d, (), [self.cD])
            self.cp("dve", self.identb[:], self.cst[:, 0:128], [self.cD], [self.cD])
            self.memset("dve", self.onesb[:], 1.0, [self.cD])
            self.prologue_mods()
            self.prologue_lb()
            P.barrier()
            for s in range(self.nseq):
                self.dma("sp", self.x[:, 0:4, :], self.xin[s, :, 0:4, :], (), [self.x.d()])
                self.dma("act", self.x[:, 4:8, :], self.xin[s, :, 4:8, :], (), [self.x.d()])
                for kind, l in self.phases:
                    if kind == "ffn":
                        self.ffn(l, s)
                    elif kind == "mix":
                        if l % 2 == 0:
                            self.even_mixer(l, s)
                        else:
                            self.odd_mixer(l, s)
                    P.barrier()
                self.final_out(s)
                P.barrier()
            P.emit(self.out_ops)
        return nc

    def mod(self, l, chunk0, r):
        NR = self.NR
        base = (l * 48 + chunk0) * NR + r
        return lambda k: self.mods[:, base + k * NR: base + k * NR + 1]

    def prologue_mods(self):
        NR = self.NR
        with contextlib.ExitStack() as st:
            ccf = self.sb(st, "ccf", [128, 8, NR], F32)
            sfm = self.sb(st, "sfm", [128, 8, NR], BF16)
            wm = [self.sb(st, "wm%d" % i, [128, 8, 1536], BF16) for i in range(2)]
            self.dma("sp", ccf[:], self.cc, (), [ccf.d()])
            self.act(sfm[:], ccf[:], AF.Silu, [ccf.d()], [sfm.d()])
            it = 0
            for l in range(4):
                ps = self.psum()
                for piece in range(4):
                    w = wm[it % 2]
                    it += 1
                    self.dma("pool", w[:], self.wmod_d[l, :, :, piece * 1536:(piece + 1) * 1536], (), [w.d()])
                    for c in range(12):
                        ch = piece * 12 + c
                        for k in range(8):
                            self.mm(ps[:, ch * NR:(ch + 1) * NR], w[:, k, c * 128:(c + 1) * 128], sfm[:, k, :],
                                    k == 0, k == 7, [w.d(), sfm.d()], [ps.d()])
                mo = self.mods[:, l * 48 * NR:(l + 1) * 48 * NR].rearrange("p (c r) -> p c r", r=NR)
                o, _ = self.po["bmod%d" % l]
                self.tt("dve", mo, ps[:, 0:48 * NR].rearrange("p (c r) -> p c r", r=NR),
                        self.partile[:, o:o + 48].unsqueeze(2).to_broadcast([128, 48, NR]), ALU.add,
                        [ps.d(), self.cD], [self.cD])
                for dst, gname, c0 in ((self.amix, "nmg%d" % l, 8), (self.affn, "nfg%d" % l, 32)):
                    dv = dst[:, l * 8 * NR:(l + 1) * 8 * NR].rearrange("p (c r) -> p c r", r=NR)
                    sc = self.mods[:, (l * 48 + c0) * NR:(l * 48 + c0 + 8) * NR].rearrange("p (c r) -> p c r", r=NR)
                    go, _ = self.po[gname]
                    self.ts("dve", dv, sc, 1.0, None, ALU.add, None, [self.cD], [self.cD])
                    self.tt("dve", dv, dv, self.partile[:, go:go + 8].unsqueeze(2).to_broadcast([128, 8, NR]), ALU.mult,
                            [self.cD], [self.cD])

    def rms_mod(self, st, src, dst, A, B, s, nch=8, srcdep=None, dstdep=None, pergroup=False):
        sq = [self.sb(st, "rm_sq%d" % i, [128, nch, 512], BF16) for i in range(2)]
        rs = [self.sb(st, "rm_rs%d" % i, [128, nch if pergroup else 1, 512], F32) for i in range(2)]
        tm = [self.sb(st, "rm_tm%d" % i, [128, 512], F32) for i in range(2)]
        sd = srcdep or src.d()
        dd = dstdep or dst.d()
        ndiv = 128.0 if pergroup else 128.0 * nch
        for bi, (t0, n) in enumerate(BLKS):
            r = self.nseq if t0 == 0 else s
            q = sq[bi % 2]
            rr = rs[bi % 2]
            self.act(q[:, :, 0:n], src[:, 0:nch, t0:t0 + n], AF.Square, [sd], [q.d()])
            groups = [[k] for k in range(nch)] if pergroup else [list(range(nch))]
            for gi, grp in enumerate(groups):
                ps = self.psum()
                for i, k in enumerate(grp):
                    self.mm(ps[:, 0:n], self.onesb[:], q[:, k, 0:n], i == 0, i == len(grp) - 1, [q.d(), self.cD], [ps.d()])
                self.ts("dve", rr[:, gi, 0:n], ps[:, 0:n], 1.0 / ndiv, EPS, ALU.mult, ALU.add, [ps.d()], [rr.d()])
                self.P.op("dve", (lambda o_, i_: lambda e: e.reciprocal(out=o_, in_=i_))(rr[:, gi, 0:n], rr[:, gi, 0:n]), [rr.d()], [rr.d()])
                self.act(rr[:, gi, 0:n], rr[:, gi, 0:n], AF.Sqrt, [rr.d()], [rr.d()])
            for k in range(nch):
                tmp = tm[k % 2]
                gi = k if pergroup else 0
                if A is not None:
                    self.stt("dve", tmp[:, 0:n], src[:, k, t0:t0 + n], A(k, r), rr[:, gi, 0:n], ALU.mult, ALU.mult,
                             [sd, rr.d(), self.cD], [tmp.d()])
                else:
                    self.tt("dve", tmp[:, 0:n], src[:, k, t0:t0 + n], rr[:, gi, 0:n], ALU.mult, [sd, rr.d()], [tmp.d()])
                if B is not None:
                    self.act(dst[:, k, t0:t0 + n], tmp[:, 0:n], AF.Identity, [tmp.d(), self.cD], [dd], bias=B(k, r))
                else:
                    self.cp("act", dst[:, k, t0:t0 + n], tmp[:, 0:n], [tmp.d()], [dd])

    def ffn(self, l, s):
        NR = self.NR
        A = lambda k, r: self.affn[:, (l * 8 + k) * NR + r:(l * 8 + k) * NR + r + 1]
        B = lambda k, r: self.mods[:, (l * 48 + 24 + k) * NR + r:(l * 48 + 24 + k) * NR + r + 1]
        with contextlib.ExitStack() as st0:
            with contextlib.ExitStack() as st:
                self.rms_mod(st, self.x, self.hx, A, B, s)
            self.P.barrier()
            gh = self.sb(st0, "gh", [128, 22, 1280], BF16)
            fo, _ = self.po["fcw%d" % l]
            bo, _ = self.po["fcb%d" % l]
            it = 0
            for half in range(2):
              with contextlib.ExitStack() as st1:
                wg = [self.sb(st1, "wg%d" % i, [128, 8, 128], BF16) for i in range(2)]
                wu = [self.sb(st1, "wu%d" % i, [128, 8, 128], BF16) for i in range(2)]
                dg = [self.sb(st1, "dg%d" % i, [128, 9, 128], BF16) for i in range(2)]
                apc = [self.sb(st1, "apc%d" % i, [128, 258], BF16) for i in range(2)]
                apl = [None, None]
                apl[half] = [self.sb(st1, "apl%d_%d" % (half, i), [128, 18, 66], BF16) for i in range(2)]
                sg = [self.sb(st1, "sg%d" % i, [128, 512], BF16) for i in range(2)]
                for t_ in apc + apl[half]:
                    self.memset("pool", t_[:], 0.0, [t_.d()])
                if half == 0:
                    pieces = [(256, 8, 1), (256 + 512, 8, 9), (256 + 1024, 1, 17)]
                    oblks = [("c", 0, 256, 0), ("l", 256, 512, 0), ("l", 768, 512, 8)]
                else:
                    pieces = [(256 + 960, 1, 0), (256 + 1024, 8, 1), (256 + 1536, 8, 9)]
                    oblks = [("l", 1280, 512, 0), ("l", 1792, 512, 8)]
                row_off = 1 if half == 0 else 1
                def load(f, it):
                    self.dma("pool", wg[it % 2][:], self.ffg_d[l, f], (), [wg[it % 2].d()])
                    self.dma("pool", wu[it % 2][:], self.ffu_d[l, f], (), [wu[it % 2].d()])
                load(0, it)
                for f in range(22):
                    if f + 1 < 22:
                        load(f + 1, it + 1)
                    g_, u_, d_ = wg[it % 2], wu[it % 2], dg[it % 2]
                    pc, pl = apc[it % 2], apl[half][it % 2]
                    self.tt("dve", d_[:], self.identb[:].unsqueeze(1).to_broadcast([128, 9, 128]),
                            self.partile[:, fo + f * 9:fo + f * 9 + 9].unsqueeze(2).to_broadcast([128, 9, 128]), ALU.mult, [self.cD], [d_.d()])
                    if half == 0:
                        ps = self.psum()
                        for k in range(8):
                            self.mm(ps[:, 0:256], g_[:, k, :], self.hx[:, k, 0:256], k == 0, k == 7, [g_.d(), self.hx.d()], [ps.d()])
                        self.cp("act", pc[:, 1:257], ps[:, 0:256], [ps.d()], [pc.d()])
                    for (tk0, nr, pr0) in pieces:
                        ps = self.psum()
                        n = nr * 64
                        for k in range(8):
                            self.mm(ps[:, 0:n], g_[:, k, :], self.hx[:, k, tk0:tk0 + n], k == 0, k == 7, [g_.d(), self.hx.d()], [ps.d()])
                        self.cp("act", pl[:, pr0:pr0 + nr, 1:65], ps[:, 0:n].rearrange("p (r c) -> p r c", c=64), [ps.d()], [pl.d()])
                    gcol = 0
                    for (kind, tk0, n, lr0) in oblks:
                        psc = self.psum()
                        if kind == "c":
                            for i, dx in enumerate((-1, 0, 1)):
                                self.mm(psc[:, 0:256], d_[:, 3 + (dx + 1), :], pc[:, 1 + dx:257 + dx], i == 0, i == 2, [d_.d(), pc.d()], [psc.d()])
                        else:
                            i = 0
                            for dy in (-1, 0, 1):
                                for dx in (-1, 0, 1):
                                    r0 = row_off + lr0 + dy
                                    self.mm(psc[:, 0:512].rearrange("p (r c) -> p r c", c=64), d_[:, (dy + 1) * 3 + (dx + 1), :],
                                            pl[:, r0:r0 + 8, 1 + dx:65 + dx], i == 0, i == 8, [d_.d(), pl.d()], [psc.d()])
                                    i += 1
                        sgt = sg[gcol % 2]
                        self.act(sgt[:, 0:n], psc[:, 0:n], AF.Silu, [psc.d(), self.cD], [sgt.d()], bias=self.partile[:, bo + f:bo + f + 1])
                        psu = self.psum()
                        for k in range(8):
                            self.mm(psu[:, 0:n], u_[:, k, :], self.hx[:, k, tk0:tk0 + n], k == 0, k == 7, [u_.d(), self.hx.d()], [psu.d()])
                        hoff = tk0 if half == 0 else tk0 - 1280
                        self.tt("dve", gh[:, f, hoff:hoff + n], sgt[:, 0:n], psu[:, 0:n], ALU.mult, [sgt.d(), psu.d()], [gh.d(f)])
                        gcol += 1
                    it += 1
              self.P.barrier()
              with contextlib.ExitStack() as st1:
                wd = [self.sb(st1, "wd%d" % i, [128, 22, 128], BF16) for i in range(2)]
                ghd = [gh.d(f) for f in range(22)]
                self.dma("pool", wd[0][:], self.ffd_d[l, 0], (), [wd[0].d()])
                for oc in range(8):
                    if oc + 1 < 8:
                        self.dma("pool", wd[(oc + 1) % 2][:], self.ffd_d[l, oc + 1], (), [wd[(oc + 1) % 2].d()])
                    w_ = wd[oc % 2]
                    for (kind, tk0, n, lr0) in oblks:
                        r = self.nseq if kind == "c" else s
                        hoff = tk0 if half == 0 else tk0 - 1280
                        ps = self.psum()
                        for f in range(22):
                            self.mm(ps[:, 0:n], w_[:, f, :], gh[:, f, hoff:hoff + n], f == 0, f == 21, [w_.d()] + ghd, [ps.d()])
                        m5 = self.mods[:, (l * 48 + 40 + oc) * NR + r:(l * 48 + 40 + oc) * NR + r + 1]
                        self.stt("dve", self.x[:, oc, tk0:tk0 + n], ps[:, 0:n], m5, self.x[:, oc, tk0:tk0 + n], ALU.mult, ALU.add,
                                 [ps.d(), self.cD, self.x.d()], [self.x.d()])
                self.P.barrier()


    def dump(self, tile, ch0, nch, slot0, dep=None):
        if not getattr(self, "debug", False):
            return
        self.P.barrier()
        with contextlib.ExitStack() as st:
            stg = self.sb(st, "dbgstg", [128, T], F32)
            for i in range(nch):
                src = tile[:, ch0 + i, :] if len(tile.t.shape) == 3 else tile[:, :]
                n = src.shape[-1]
                self.cp("dve", stg[:, 0:n], src, [dep or tile.d()], [stg.d()])
                self.out_ops.append(self.dma("sp", self.dbg[:, slot0 + i, 0:n], stg[:, 0:n], [stg.d()], ()))
            self.P.barrier()

    def dump_ap(self, ap, dep, slot):
        if not getattr(self, "debug", False):
            return
        self.P.barrier()
        with contextlib.ExitStack() as st:
            n = ap.shape[-1]
            stg = self.sb(st, "dbgstg2", [128, n], F32)
            self.cp("dve", stg[:, 0:n], ap, [dep], [stg.d()])
            self.out_ops.append(self.dma("sp", self.dbg[:, slot, 0:n], stg[:, 0:n], [stg.d()], ()))
            self.P.barrier()

    def outproj(self, wd_ap_fn, nk, mix, l, s, gate_chunk0):
        NR = self.NR
        with contextlib.ExitStack() as st:
            wo = [self.sb(st, "wo%d" % i, [128, nk, 128], BF16) for i in range(2)]
            self.dma("pool", wo[0][:], wd_ap_fn(0), (), [wo[0].d()])
            for oc in range(8):
                if oc + 1 < 8:
                    self.dma("pool", wo[(oc + 1) % 2][:], wd_ap_fn(oc + 1), (), [wo[(oc + 1) % 2].d()])
                w_ = wo[oc % 2]
                for (t0, n) in BLKS:
                    r = self.nseq if t0 == 0 else s
                    ps = self.psum()
                    for k in range(nk):
                        self.mm(ps[:, 0:n], w_[:, k, :], mix[:, k, t0:t0 + n], k == 0, k == nk - 1, [w_.d(), mix.d()], [ps.d()])
                    m2 = self.mods[:, (l * 48 + gate_chunk0 + oc) * NR + r:(l * 48 + gate_chunk0 + oc) * NR + r + 1]
                    self.stt("dve", self.x[:, oc, t0:t0 + n], ps[:, 0:n], m2, self.x[:, oc, t0:t0 + n], ALU.mult, ALU.add,
                             [ps.d(), self.cD, self.x.d()], [self.x.d()])
            self.P.barrier()

    def proj_conv(self, w_dram, cw_off, cb_off, wt, pad, dgt, dst, func, ntap=4, dst_fn=None, dst_dep=None):
        self.dma("pool", wt[:], w_dram, (), [wt.d()])
        self.tt("dve", dgt[:, 0:ntap, :], self.identb[:].unsqueeze(1).to_broadcast([128, ntap, 128]),
                self.partile[:, cw_off:cw_off + ntap].unsqueeze(2).to_broadcast([128, ntap, 128]), ALU.mult, [self.cD], [dgt.d()])
        for (t0, n) in BLKS:
            ps = self.psum()
            for k in range(8):
                self.mm(ps[:, 0:n], wt[:, k, :], self.hx[:, k, t0:t0 + n], k == 0, k == 7, [wt.d(), self.hx.d()], [ps.d()])
            po = 1 + t0 if t0 == 0 else 260 + (t0 - CTX)
            self.cp("act", pad[:, po:po + n], ps[:, 0:n], [ps.d()], [pad.d()])
        for (t0, n) in BLKS:
            ps = self.psum()
            po = t0 if t0 == 0 else 259 + (t0 - CTX)
            for k in range(ntap):
                self.mm(ps[:, 0:n], dgt[:, k, :], pad[:, po + k:po + k + n], k == 0, k == ntap - 1, [dgt.d(), pad.d()], [ps.d()])
            o_ap = dst_fn(t0, n) if dst_fn is not None else dst[:, t0:t0 + n]
            self.act(o_ap, ps[:, 0:n], func, [ps.d(), self.cD], [dst_dep or dst.d()], bias=self.partile[:, cb_off:cb_off + 1])

    def even_mixer(self, l, s):
        j = l // 2
        NR = self.NR
        A = lambda k, r: self.amix[:, (l * 8 + k) * NR + r:(l * 8 + k) * NR + r + 1]
        B = lambda k, r: self.mods[:, (l * 48 + k) * NR + r:(l * 48 + k) * NR + r + 1]
        with contextlib.ExitStack() as st0:
            with contextlib.ExitStack() as st:
                self.rms_mod(st, self.x, self.hx, A, B, s)
            self.P.barrier()
            mix = self.sb(st0, "mix", [128, 8, T], BF16)
            which = getattr(self, "even_parts", ("ssd", "lru"))
            if "ssd" in which:
                self.ssd(l, s, mix)
                self.dump(mix, 0, 8, 0)
                self.outproj(lambda oc: self.evout_d[j, oc, :, 0:8, :], 8, mix, l, s, 16)
            if "lru" in which:
                self.lru(l, s, mix)
                self.dump(mix, 0, 8, 8)
                self.outproj(lambda oc: self.evout_d[j, oc, :, 8:16, :], 8, mix, l, s, 16)

    def lru(self, l, s, mix):
        j = l // 2
        po = self.po
        with contextlib.ExitStack() as st:
            cA = self.sb(st, "cA", [128, 16], F32)
            wu = self.sb(st, "lwu", [128, 8, 128], BF16)
            wgy = [self.sb(st, "lwgy%d" % i, [128, 8, 128], BF16) for i in range(2)]
            wai = [self.sb(st, "lwai%d" % i, [128, 2, 2, 128], BF16) for i in range(2)]
            pad = self.sb(st, "lpad", [128, 2312], BF16)
            dgt = self.sb(st, "ldg", [128, 4, 128], BF16)
            ucb = self.sb(st, "ucb", [128, T], BF16)
            a_t = self.sb(st, "lru_a", [128, T], F32)
            bx = [self.sb(st, "lru_bx%d" % i, [128, T], BF16) for i in range(2)]
            tmp = [self.sb(st, "ltmp%d" % i, [128, 512], F32 if i < 3 else BF16) for i in range(6)]
            self.memset("pool", pad[:], 0.0, [pad.d()])
            lo, _ = po["llam%d" % j]
            self.act(cA[:], self.partile[:, lo:lo + 16], AF.Exp, [self.cD], [cA.d()], scale=-1.0)
            self.act(cA[:], cA[:], AF.Ln, [cA.d()], [cA.d()], bias=1.0)
            self.ts("dve", cA[:], cA[:], -8.0, None, ALU.mult, None, [cA.d()], [cA.d()])
            for jj in range(8):
                self.dma("pool", wgy[jj % 2][:], self.evin_d[j, 20 + jj], (), [wgy[jj % 2].d()])
                self.dma("pool", wai[jj % 2][:], self.lruw_d[j, :, :, :, jj, :], (), [wai[jj % 2].d()])
                self.proj_conv(self.evin_d[j, 28 + jj], po["lcw%d" % j][0] + jj * 4, po["lcb%d" % j][0] + jj, wu, pad, dgt, ucb, AF.Identity)
                w2 = wai[jj % 2]
                for d in range(2):
                    ba = self.partile[:, po["lba%d" % j][0] + d * 8 + jj:po["lba%d" % j][0] + d * 8 + jj + 1]
                    bi = self.partile[:, po["lbi%d" % j][0] + d * 8 + jj:po["lbi%d" % j][0] + d * 8 + jj + 1]
                    for (t0, n) in BLKS:
                        psa = self.psum()
                        self.mm(psa[:, 0:n], w2[:, 0, d, :], ucb[:, t0:t0 + n], True, True, [w2.d(), ucb.d()], [psa.d()])
                        psi_ = self.psum()
                        self.mm(psi_[:, 0:n], w2[:, 1, d, :], ucb[:, t0:t0 + n], True, True, [w2.d(), ucb.d()], [psi_.d()])
                        rt, it_, sq, t3 = tmp[0], tmp[1], tmp[2], tmp[3]
                        self.act(rt[:, 0:n], psa[:, 0:n], AF.Sigmoid, [psa.d(), self.cD], [rt.d()], bias=ba)
                        self.act(a_t[:, t0:t0 + n], rt[:, 0:n], AF.Exp, [rt.d(), cA.d()], [a_t.d()], scale=cA[:, d * 8 + jj:d * 8 + jj + 1])
                        self.act(it_[:, 0:n], psi_[:, 0:n], AF.Sigmoid, [psi_.d(), self.cD], [it_.d()], bias=bi)
                        self.act(sq[:, 0:n], a_t[:, t0:t0 + n], AF.Square, [a_t.d()], [sq.d()])
                        self.act(sq[:, 0:n], sq[:, 0:n], AF.Sqrt, [sq.d()], [sq.d()], scale=-1.0, bias=1.0)
                        self.tt("dve", t3[:, 0:n], sq[:, 0:n], it_[:, 0:n], ALU.mult, [sq.d(), it_.d()], [t3.d()])
                        self.tt("dve", bx[d][:, t0:t0 + n], t3[:, 0:n], ucb[:, t0:t0 + n], ALU.mult, [t3.d(), ucb.d()], [bx[d].d()])
                    b_ = bx[d]
                    if d == 0:
                        self.P.op("dve", (lambda o_, a_, b2: lambda e: e.tensor_tensor_scan(out=o_, data0=a_, data1=b2, initial=0.0, op0=ALU.mult, op1=ALU.add))(
                            b_[:, 0:T], a_t[:, 0:T], b_[:, 0:T]), [a_t.d(), b_.d()], [b_.d()])
                    else:
                        self.P.op("dve", (lambda o_, a_, b2: lambda e: e.tensor_tensor_scan(out=o_, data0=a_, data1=b2, initial=0.0, op0=ALU.mult, op1=ALU.add))(
                            rev(b_[:, 0:CTX]), rev(a_t[:, 0:CTX]), rev(b_[:, 0:CTX])), [a_t.d(), b_.d()], [b_.d()])
                        self.P.op("dve", (lambda o_, a_, b2, i_: lambda e: e.tensor_tensor_scan(out=o_, data0=a_, data1=b2, initial=i_, op0=ALU.mult, op1=ALU.add))(
                            rev(b_[:, CTX:T]), rev(a_t[:, CTX:T]), rev(b_[:, CTX:T]), b_[:, 0:1]), [a_t.d(), b_.d()], [b_.d()])
                wg_ = wgy[jj % 2]
                for (t0, n) in BLKS:
                    ps = self.psum()
                    for k in range(8):
                        self.mm(ps[:, 0:n], wg_[:, k, :], self.hx[:, k, t0:t0 + n], k == 0, k == 7, [wg_.d(), self.hx.d()], [ps.d()])
                    ge, sm = tmp[4], tmp[5]
                    self.act(ge[:, 0:n], ps[:, 0:n], AF.Gelu, [ps.d()], [ge.d()])
                    self.tt("dve", sm[:, 0:n], bx[0][:, t0:t0 + n], bx[1][:, t0:t0 + n], ALU.add, [bx[0].d(), bx[1].d()], [sm.d()])
                    self.tt("dve", mix[:, jj, t0:t0 + n], sm[:, 0:n], ge[:, 0:n], ALU.mult, [sm.d(), ge.d()], [mix.d()])
            self.P.barrier()

    def pe_T(self, out, in_, reads, writes):
        return self.P.op("pe", lambda e: e.transpose(out, in_, self.identb[:]), list(reads) + [self.cD], writes, pe_acc=True)

    def to_tokmajor_ap(self, src_fn, src_dep, dst):
        c = 0
        while c < 18:
            nq = min(4, 18 - c)
            ps = self.psum()
            pb = ps[:].bitcast(BF16)
            for q in range(nq):
                self.pe_T(pb[:, q * 128:(q + 1) * 128], src_fn(c + q), [src_dep], [ps.d()])
            self.cp("act", dst[:, c:c + nq, :], pb[:, 0:nq * 128].rearrange("p (q f) -> p q f", f=128), [ps.d()], [dst.d()])
            c += nq

    def ssd(self, l, s, mix):
        j = l // 2
        po = self.po
        tri = lambda d: self.cst[:, 128 + 128 * d:256 + 128 * d]
        onesf = self.cst[:, 384:512]
        bc = lambda ap, shape, ax: ap.unsqueeze(ax).to_broadcast(shape)
        with contextlib.ExitStack() as st:
            dt_tok = self.sb(st, "dt_tok", [128, 18, 32], F32)
            la_tok = self.sb(st, "la_tok", [128, 18, 32], F32)
            cumcol = self.sb(st, "cumcol", [128, 18, 32], F32)
            Bfm = self.sb(st, "Bfm", [128, T], BF16)
            Cfm = self.sb(st, "Cfm", [128, T], BF16)
            Hst = [self.sb(st, "Hst%d" % i, [128, 2, 64], F32) for i in range(2)]
            Hbf = [self.sb(st, "Hbf%d" % i, [128, 2, 64], BF16) for i in range(2)]
            NB = 2
            stA = contextlib.ExitStack()
            wdt = self.sb(stA, "swdt", [128, 8, 32], BF16)
            arow = self.sb(stA, "arow", [128, 32], F32)
            t9 = self.sb(stA, "t9", [128, 9, 32], F32)
            self.dma("pool", wdt[:], self.evdt_d[j], (), [wdt.d()])
            ao = po["alogrow%d" % j][0]
            bo = po["dtbrow%d" % j][0]
            self.act(arow[:], self.partile[:, ao:ao + 32], AF.Exp, [self.cD], [arow.d()])
            self.ts("dve", arow[:], arow[:], -1.0, None, ALU.mult, None, [arow.d()], [arow.d()])
            for half in range(2):
                ps = self.psum()
                for ci in range(9):
                    c = half * 9 + ci
                    for k in range(8):
                        self.mm(ps[:, ci * 32:(ci + 1) * 32], self.hx[:, k, c * 128:(c + 1) * 128], wdt[:, k, :], k == 0, k == 7,
                                [self.hx.d(), wdt.d()], [ps.d()])
                self.tt("dve", t9[:], ps[:, 0:288].rearrange("p (c h) -> p c h", h=32),
                        bc(self.partile[:, bo:bo + 32], [128, 9, 32], 1), ALU.add, [ps.d(), self.cD], [t9.d()])
                self.act(t9[:], t9[:], AF.Exp, [t9.d()], [t9.d()])
                self.act(dt_tok[:, half * 9:(half + 1) * 9, :], t9[:], AF.Ln, [t9.d()], [dt_tok.d()], bias=1.0)
                self.tt("dve", la_tok[:, half * 9:(half + 1) * 9, :], dt_tok[:, half * 9:(half + 1) * 9, :],
                        bc(arow[:], [128, 9, 32], 1), ALU.mult, [dt_tok.d(), arow.d()], [la_tok.d()])
            for half in range(2):
                ps = self.psum()
                for ci in range(9):
                    c = half * 9 + ci
                    for d in range(2):
                        self.mm(ps[:, ci * 32 + d * 16:ci * 32 + (d + 1) * 16], tri(d), la_tok[:, c, d * 16:(d + 1) * 16], True, True,
                                [la_tok.d(), self.cD], [ps.d()])
                self.cp("dve", cumcol[:, half * 9:(half + 1) * 9, :], ps[:, 0:288].rearrange("p (c h) -> p c h", h=32), [ps.d()], [cumcol.d()])
            self.P.barrier()
            stA.close()
            it = 0
            for g in range(2):
              with contextlib.ExitStack() as stB:
                wt = self.sb(stB, "swt", [128, 8, 128], BF16)
                pad = self.sb(stB, "spad", [128, 2312], BF16)
                dgt = self.sb(stB, "sdg", [128, 4, 128], BF16)
                self.memset("pool", pad[:], 0.0, [pad.d()])
                self.proj_conv(self.evin_d[j, 8 + 8 + g], po["scw%d" % j][0] + (8 + g) * 4, po["scb%d" % j][0] + 8 + g, wt, pad, dgt, Bfm, AF.Silu)
                self.proj_conv(self.evin_d[j, 8 + 10 + g], po["scw%d" % j][0] + (10 + g) * 4, po["scb%d" % j][0] + 10 + g, wt, pad, dgt, Cfm, AF.Silu)
                for i in range(4):
                    cx = 4 * g + i
                    self.proj_conv(self.evin_d[j, 8 + cx], po["scw%d" % j][0] + cx * 4, po["scb%d" % j][0] + cx, wt, pad, dgt, None, AF.Silu,
                                   dst_fn=(lambda cx_: lambda t0, n: mix[:, cx_, t0:t0 + n])(cx), dst_dep=mix.d())
                self.P.barrier()
              with contextlib.ExitStack() as stC:
                Btok = self.sb(stC, "Btok", [128, 18, 128], BF16)
                xstok = self.sb(stC, "xstok", [128, 18, 128], BF16)
                rhsla = [self.sb(stC, "rhsla%d" % i, [128, 2, 128], F32) for i in range(NB)]
                mt = [self.sb(stC, "mt%d" % i, [128, 2, 128], F32) for i in range(NB)]
                E_ = [self.sb(stC, "E%d" % i, [128, 2, 128], BF16) for i in range(NB)]
                Mh = [self.sb(stC, "Mh%d" % i, [128, 2, 128], BF16) for i in range(NB)]
                Eb = [self.sb(stC, "Eb%d" % i, [128, 2, 128], BF16) for i in range(NB)]
                Cex = [self.sb(stC, "Cex%d" % i, [128, 2, 128], BF16) for i in range(NB)]
                cbm = [self.sb(stC, "cbm%d" % i, [128, 128], BF16) for i in range(NB)]
                xdt = [self.sb(stC, "xdt%d" % i, [128, 2, 64], BF16) for i in range(NB)]
                xdtw = [self.sb(stC, "xdtw%d" % i, [128, 2, 64], BF16) for i in range(NB)]
                wv = [self.sb(stC, "wv%d" % i, [128, 2], F32) for i in range(NB)]
                dtot = [self.sb(stC, "dtot%d" % i, [128, 2], F32) for i in range(NB)]
                self.to_tokmajor_ap(lambda c: Bfm[:, c * 128:(c + 1) * 128], Bfm.d(), Btok)
                for i in range(4):
                    cx = 4 * g + i
                    self.to_tokmajor_ap((lambda cx_: lambda c: mix[:, cx_, c * 128:(c + 1) * 128])(cx), mix.d(), xstok)
                    sdo = po["sd%d" % j][0] + cx
                    self.ts("dve", mix[:, cx, :], mix[:, cx, :], self.partile[:, sdo:sdo + 1], None, ALU.mult, None, [mix.d(), self.cD], [mix.d()])
                    for d in range(2):
                        hd = d * 16 + 8 * g + 2 * i
                        last = 127 if d == 0 else 0
                        self.memset("pool", Hst[d][:], 0.0, [Hst[d].d()])
                        self.memset("pool", Hbf[d][:], 0.0, [Hbf[d].d()])
                        order = list(range(18)) if d == 0 else [1, 0] + list(range(17, 1, -1))
                        for c in order:
                            b = it % NB
                            it += 1
                            cs = slice(c * 128, (c + 1) * 128)
                            ps_cb = self.psum()
                            self.mm(ps_cb[:, 0:128], Bfm[:, cs], Cfm[:, cs], True, True, [Bfm.d(), Cfm.d()], [ps_cb.d()])
                            self.tt("dve", cbm[b][:], ps_cb[:, 0:128], tri(d), ALU.mult, [ps_cb.d(), self.cD], [cbm[b].d()])
                            self.tt("dve", rhsla[b][:], bc(tri(d), [128, 2, 128], 1), bc(la_tok[:, c, hd:hd + 2], [128, 2, 128], 2), ALU.mult,
                                    [la_tok.d(), self.cD], [rhsla[b].d()])
                            ps_cum = self.psum()
                            self.mm(ps_cum[:, 0:256], onesf, rhsla[b][:].rearrange("p h l -> p (h l)"), True, True, [rhsla[b].d(), self.cD], [ps_cum.d()])
                            cum3 = ps_cum[:, 0:256].rearrange("p (h l) -> p h l", l=128)
                            ccb = bc(cumcol[:, c, hd:hd + 2], [128, 2, 128], 2)
                            self.tt("dve", mt[b][:], cum3, ccb, ALU.min, [ps_cum.d(), cumcol.d()], [mt[b].d()])
                            self.tt("dve", mt[b][:], mt[b][:], ccb, ALU.subtract, [mt[b].d(), cumcol.d()], [mt[b].d()])
                            self.act(E_[b][:], mt[b][:], AF.Exp, [mt[b].d()], [E_[b].d()])
                            self.tt("dve", Mh[b][:], E_[b][:], bc(cbm[b][:], [128, 2, 128], 1), ALU.mult, [E_[b].d(), cbm[b].d()], [Mh[b].d()])
                            self.act(Eb[b][:], cum3, AF.Exp, [ps_cum.d()], [Eb[b].d()])
                            self.tt("dve", Cex[b][:], Eb[b][:], bc(Cfm[:, cs], [128, 2, 128], 1), ALU.mult, [Eb[b].d(), Cfm.d()], [Cex[b].d()])
                            self.tt("dve", xdt[b][:], xstok[:, c, :].rearrange("p (h q) -> p h q", q=64),
                                    bc(dt_tok[:, c, hd:hd + 2], [128, 2, 64], 2), ALU.mult, [xstok.d(), dt_tok.d()], [xdt[b].d()])
                            ps_y = self.psum()
                            for hh in range(2):
                                self.mm(ps_y[hh * 64:(hh + 1) * 64, 0:128], xdt[b][:, hh, :], Mh[b][:, hh, :], True, False,
                                        [xdt[b].d(), Mh[b].d()], [ps_y.d()])
                                self.mm(ps_y[hh * 64:(hh + 1) * 64, 0:128], Hbf[d][:, hh, :], Cex[b][:, hh, :], False, True,
                                        [Hbf[d].d(), Cex[b].d()], [ps_y.d()])
                            self.tt("dve", mix[:, cx, cs], mix[:, cx, cs], ps_y[:, 0:128], ALU.add, [mix.d(), ps_y.d()], [mix.d()])
                            self.tt("dve", wv[b][:], cum3[:, :, last], cumcol[:, c, hd:hd + 2], ALU.subtract, [ps_cum.d(), cumcol.d()], [wv[b].d()])
                            self.act(wv[b][:], wv[b][:], AF.Exp, [wv[b].d()], [wv[b].d()])
                            self.act(dtot[b][:], cum3[:, :, last], AF.Exp, [ps_cum.d()], [dtot[b].d()])
                            self.tt("dve", xdtw[b][:], xdt[b][:], bc(wv[b][:], [128, 2, 64], 2), ALU.mult, [xdt[b].d(), wv[b].d()], [xdtw[b].d()])
                            ps_h = self.psum()
                            self.mm(ps_h[:, 0:128], Btok[:, c, :], xdtw[b][:].rearrange("p h q -> p (h q)"), True, True, [Btok.d(), xdtw[b].d()], [ps_h.d()])
                            self.tt("dve", Hst[d][:], Hst[d][:], bc(dtot[b][:], [128, 2, 64], 2), ALU.mult, [Hst[d].d(), dtot[b].d()], [Hst[d].d()])
                            self.tt("dve", Hst[d][:], Hst[d][:], ps_h[:, 0:128].rearrange("p (h q) -> p h q", q=64), ALU.add, [Hst[d].d(), ps_h.d()], [Hst[d].d()])
                            self.cp("act", Hbf[d][:], Hst[d][:], [Hst[d].d()], [Hbf[d].d()])
                self.P.barrier()
            self.dump(mix, 0, 8, 8)
            wt = self.sb(st, "swt2", [128, 8, 128], BF16)
            szt = [self.sb(st, "szt%d" % i, [128, 512], BF16) for i in range(2)]
            for ch in range(8):
                self.dma("pool", wt[:], self.evin_d[j, ch], (), [wt.d()])
                for bi, (t0, n) in enumerate(BLKS):
                    ps = self.psum()
                    for k in range(8):
                        self.mm(ps[:, 0:n], wt[:, k, :], self.hx[:, k, t0:t0 + n], k == 0, k == 7, [wt.d(), self.hx.d()], [ps.d()])
                    z_ = szt[bi % 2]
                    self.act(z_[:, 0:n], ps[:, 0:n], AF.Silu, [ps.d()], [z_.d()])
                    self.tt("dve", mix[:, ch, t0:t0 + n], mix[:, ch, t0:t0 + n], z_[:, 0:n], ALU.mult, [mix.d(), z_.d()], [mix.d()])
            self.P.barrier()
        with contextlib.ExitStack() as st:
            go = po["sng%d" % j][0]
            self.rms_mod(st, mix, mix, lambda k, r: self.partile[:, go + k:go + k + 1], None, s)
            self.P.barrier()

    def prologue_lb(self):
        with contextlib.ExitStack() as st:
            lo = self.po["lbl"][0]
            L = self.partile[:, lo:lo + 24].rearrange("p (l h) -> p l h", h=6)
            mx = self.sb(st, "lbmx", [128, 6], F32)
            e = self.sb(st, "lbe", [128, 4, 6], F32)
            sm = self.sb(st, "lbs", [128, 6], F32)
            D = [self.cD]
            self.tt("dve", mx[:], L[:, 0, :], L[:, 1, :], ALU.max, D, D)
            self.tt("dve", mx[:], mx[:], L[:, 2, :], ALU.max, D, D)
            self.tt("dve", mx[:], mx[:], L[:, 3, :], ALU.max, D, D)
            self.tt("dve", e[:], L, mx[:].unsqueeze(1).to_broadcast([128, 4, 6]), ALU.subtract, D, D)
            self.act(e[:], e[:], AF.Exp, D, D)
            self.tt("dve", sm[:], e[:, 0, :], e[:, 1, :], ALU.add, D, D)
            self.tt("dve", sm[:], sm[:], e[:, 2, :], ALU.add, D, D)
            self.tt("dve", sm[:], sm[:], e[:, 3, :], ALU.add, D, D)
            self.P.op("dve", lambda en: en.reciprocal(out=sm[:], in_=sm[:]), D, D)
            self.tt("dve", e[:], e[:], sm[:].unsqueeze(1).to_broadcast([128, 4, 6]), ALU.mult, D, D)
            self.memset("dve", self.lbt[:, 0, :], 0.0, D)
            for l in range(1, 4):
                self.tt("dve", self.lbt[:, l, :], self.lbt[:, l - 1, :], e[:, l, :], ALU.add, D, D)
            self.P.barrier()

    def odd_mixer(self, l, s):
        j = l // 2
        NR = self.NR
        A = lambda k, r: self.amix[:, (l * 8 + k) * NR + r:(l * 8 + k) * NR + r + 1]
        B = lambda k, r: self.mods[:, (l * 48 + k) * NR + r:(l * 48 + k) * NR + r + 1]
        with contextlib.ExitStack() as st0:
            with contextlib.ExitStack() as st:
                self.rms_mod(st, self.x, self.hx, A, B, s)
            self.P.barrier()
            mix = self.sb(st0, "mixo", [128, 8, T], BF16)
            which = getattr(self, "odd_parts", ("hgrn", "s5"))
            if "hgrn" in which:
                self.hgrn(l, s, mix)
            else:
                self.memset("pool", mix[:, 0:6, :], 0.0, [mix.d()])
            if "s5" in which:
                self.s5(l, s, mix)
            else:
                self.memset("pool", mix[:, 6:8, :], 0.0, [mix.d()])
            self.dump(mix, 0, 8, 0)
            self.outproj(lambda oc: self.odout_d[j, oc], 8, mix, l, s, 16)

    def scan(self, out, d0, d1, init, reads, writes):
        return self.P.op("dve", lambda e: e.tensor_tensor_scan(out=out, data0=d0, data1=d1, initial=init, op0=ALU.mult, op1=ALU.add),
                         reads, writes)

    def hgrn(self, l, s, mix):
        j = l // 2
        po = self.po
        hgm = lambda d: self.cst[:, 512 + 128 * d:640 + 128 * d]
        with contextlib.ExitStack() as st:
            lbm = self.sb(st, "lbm", [128, 6, 2], F32)
            rmask_t = self.sb(st, "rmask", [128, T + 32], BF16)
            self.rmask = rmask_t
            self.memset("pool", rmask_t[:], 1.0, [self.cD])
            self.memset("pool", rmask_t[:].rearrange("p (c i) -> p c i", i=32)[:, :, 0:1], 0.0, [self.cD])
            wq = self.sb(st, "hwq", [128, 8, 128], BF16)
            wf = [self.sb(st, "hwf%d" % i, [128, 8, 128], BF16) for i in range(2)]
            wv = self.sb(st, "hwv", [128, 8, 128], BF16)
            vtok = self.sb(st, "vtok", [128, 18, 128], BF16)
            qt = [self.sb(st, "qt%d" % d, [128, T], BF16) for d in range(2)]
            kt = [self.sb(st, "kt%d" % d, [128, T], BF16) for d in range(2)]
            elast = [self.sb(st, "elast%d" % d, [128, 72], F32) for d in range(2)]
            eprev = [self.sb(st, "eprev%d" % d, [128, 72], F32) for d in range(2)]
            R = [self.sb(st, "R%d" % d, [128, 128], F32) for d in range(2)]
            Rb = [self.sb(st, "Rb%d" % d, [128, 128], BF16) for d in range(2)]
            qsl = self.sb(st, "qsl", [128, 512], F32)
            tA = [self.sb(st, "htA", [128, 512], F32)] * 2
            tB = [self.sb(st, "htB", [128, 512], F32)] * 2
            tC = [self.sb(st, "htC", [128, 512], F32)] * 2
            ktok = [self.sb(st, "ktok%d" % i, [128, 128], BF16) for i in range(2)]
            ktok2 = [self.sb(st, "ktokm%d" % i, [128, 128], BF16) for i in range(2)]
            attm = [self.sb(st, "attm%d" % i, [128, 128], BF16) for i in range(2)]
            self.ts("dve", lbm[:, :, 0], self.lbt[:, l, :], -1.0, 1.0, ALU.mult, ALU.add, [self.cD], [lbm.d()])
            self.ts("dve", lbm[:, :, 1], lbm[:, :, 0], -1.0, None, ALU.mult, None, [lbm.d()], [lbm.d()])
            self.memset("pool", mix[:, 0:6, :], 0.0, [mix.d()])
            it = 0
            for hh in range(6):
                oml = lbm[:, hh, 0:1]
                noml = lbm[:, hh, 1:2]
                lb = self.lbt[:, l, hh:hh + 1]
                self.dma("pool", wq[:], self.odin_d[j, hh], (), [wq.d()])
                self.dma("pool", wf[0][:], self.odin_d[j, 6 + hh], (), [wf[0].d()])
                self.dma("pool", wf[1][:], self.odin_d[j, 12 + hh], (), [wf[1].d()])
                self.dma("pool", wv[:], self.odv_d[j, :, :, hh * 128:(hh + 1) * 128], (), [wv.d()])
                c = 0
                while c < 18:
                    nq = min(4, 18 - c)
                    ps = self.psum()
                    for q in range(nq):
                        for k in range(8):
                            self.mm(ps[:, q * 128:(q + 1) * 128], self.hx[:, k, (c + q) * 128:(c + q + 1) * 128], wv[:, k, :], k == 0, k == 7,
                                    [self.hx.d(), wv.d()], [ps.d()])
                    self.cp("act", vtok[:, c:c + nq, :], ps[:, 0:nq * 128].rearrange("p (q f) -> p q f", f=128), [ps.d()], [vtok.d()])
                    c += nq
                for bi, (t0, n) in enumerate(BLKS):
                    ps = self.psum()
                    for k in range(8):
                        self.mm(ps[:, 0:n], wq[:, k, :], self.hx[:, k, t0:t0 + n], k == 0, k == 7, [wq.d(), self.hx.d()], [ps.d()])
                    self.act(qsl[:, 0:n], ps[:, 0:n], AF.Silu, [ps.d()], [qsl.d()])
                    for d in range(2):
                        a_, b_, c_ = tA[d], tB[d], tC[d]
                        ps = self.psum()
                        for k in range(8):
                            self.mm(ps[:, 0:n], wf[d][:, k, :], self.hx[:, k, t0:t0 + n], k == 0, k == 7, [wf[d].d(), self.hx.d()], [ps.d()])
                        self.act(a_[:, 0:n], ps[:, 0:n], AF.Sigmoid, [ps.d()], [a_.d()])
                        self.act(b_[:, 0:n], a_[:, 0:n], AF.Ln, [a_.d(), lbm.d(), self.cD], [b_.d()], scale=oml, bias=lb)
                        self.ts("dve", a_[:, 0:n], a_[:, 0:n], noml, oml, ALU.mult, ALU.add, [a_.d(), lbm.d()], [a_.d()])
                        if d == 0:
                            self.scan(c_[:, 0:n], self.rmask[:, t0:t0 + n], b_[:, 0:n], 0.0, [b_.d(), self.cD], [c_.d()])
                        else:
                            self.scan(rev(c_[:, 0:n]), rev(self.rmask[:, t0 + 1:t0 + n + 1]), rev(b_[:, 0:n]), 0.0, [b_.d(), self.cD], [c_.d()])
                        self.act(b_[:, 0:n], c_[:, 0:n], AF.Exp, [c_.d()], [b_.d()])
                        lastpos = 31 if d == 0 else 0
                        self.cp("dve", elast[d][:, t0 // 32:(t0 + n) // 32], b_[:, 0:n].rearrange("p (c i) -> p c i", i=32)[:, :, lastpos],
                                [b_.d()], [elast[d].d()])
                        self.tt("dve", qt[d][:, t0:t0 + n], qsl[:, 0:n], b_[:, 0:n], ALU.mult, [qsl.d(), b_.d()], [qt[d].d()])
                        self.act(c_[:, 0:n], c_[:, 0:n], AF.Exp, [c_.d()], [c_.d()], scale=-1.0)
                        self.tt("dve", kt[d][:, t0:t0 + n], a_[:, 0:n], c_[:, 0:n], ALU.mult, [a_.d(), c_.d()], [kt[d].d()])
                for d in range(2):
                    self.memset("dve", eprev[d][:], 1.0, [eprev[d].d()])
                    if d == 0:
                        self.cp("dve", eprev[d][:, 1:72], elast[d][:, 0:71], [elast[d].d()], [eprev[d].d()])
                    else:
                        self.cp("dve", eprev[d][:, 0:7], elast[d][:, 1:8], [elast[d].d()], [eprev[d].d()])
                        self.cp("dve", eprev[d][:, 8:71], elast[d][:, 9:72], [elast[d].d()], [eprev[d].d()])
                        self.cp("dve", eprev[d][:, 71:72], elast[d][:, 0:1], [elast[d].d()], [eprev[d].d()])
                    self.memset("pool", R[d][:], 0.0, [R[d].d()])
                    self.memset("pool", Rb[d][:], 0.0, [Rb[d].d()])
                blocks = [list(range(18)), [1, 0] + list(range(17, 1, -1))]
                for step in range(18):
                    for d in range(2):
                        bk = blocks[d][step]
                        b = it % 2
                        it += 1
                        ts_ = slice(bk * 128, (bk + 1) * 128)
                        ps_a = self.psum()
                        self.mm(ps_a[:, 0:128], kt[d][:, ts_], qt[d][:, ts_], True, True, [kt[d].d(), qt[d].d()], [ps_a.d()])
                        self.tt("dve", attm[b][:], ps_a[:, 0:128], hgm(d), ALU.mult, [ps_a.d(), self.cD], [attm[b].d()])
                        ps_t = self.psum()
                        pb = ps_t[:].bitcast(BF16)
                        self.pe_T(pb[:, 0:128], kt[d][:, ts_], [kt[d].d()], [ps_t.d()])
                        self.cp("act", ktok[b][:], pb[:, 0:128], [ps_t.d()], [ktok[b].d()])
                        self.cp("act", ktok2[b][64:128, :], pb[64:128, 0:128], [ps_t.d()], [ktok2[b].d()])
                        self.memset("pool", ktok2[b][64:96, :], 0.0, [ktok2[b].d()])
                        ps_o = self.psum()
                        self.mm(ps_o[:, 0:128], vtok[:, bk, :], attm[b][:], True, False, [vtok.d(), attm[b].d()], [ps_o.d()])
                        corder = [0, 1, 2, 3] if d == 0 else [3, 2, 1, 0]
                        for ci, cc in enumerate(corder):
                            cg = bk * 4 + cc
                            cs_ = slice(cg * 32, (cg + 1) * 32)
                            self.mm(ps_o[:, cc * 32:(cc + 1) * 32], Rb[d][:], qt[d][:, cs_], False, ci == 3, [Rb[d].d(), qt[d].d()], [ps_o.d()])
                            ps_u = self.psum()
                            if cc < 3:
                                self.mm(ps_u[:, 0:128], ktok[b][cc * 32:(cc + 1) * 32, :], vtok[cc * 32:(cc + 1) * 32, bk, :], True, True,
                                        [ktok[b].d(), vtok.d()], [ps_u.d()])
                            else:
                                self.mm(ps_u[:, 0:128], ktok2[b][64:128, :], vtok[64:128, bk, :], True, True,
                                        [ktok2[b].d(), vtok.d()], [ps_u.d()])
                            self.stt("dve", R[d][:], R[d][:], eprev[d][:, cg:cg + 1], ps_u[:, 0:128], ALU.mult, ALU.add,
                                     [R[d].d(), eprev[d].d(), ps_u.d()], [R[d].d()])
                            self.act(Rb[d][:], R[d][:], AF.Copy, [R[d].d(), elast[d].d()], [Rb[d].d()], scale=elast[d][:, cg:cg + 1])
                        self.tt("dve", mix[:, hh, ts_], mix[:, hh, ts_], ps_o[:, 0:128], ALU.add, [mix.d(), ps_o.d()], [mix.d()])
            self.P.barrier()
        with contextlib.ExitStack() as st:
            go = po["hgn%d" % j][0]
            self.rms_mod(st, mix, mix, lambda k, r: self.partile[:, go + k:go + k + 1], None, s, nch=6, pergroup=True)
            self.P.barrier()
        with contextlib.ExitStack() as st:
            wg_ = [self.sb(st, "hwg%d" % i, [128, 8, 128], BF16) for i in range(2)]
            sz = [self.sb(st, "hsz%d" % i, [128, 512], BF16) for i in range(2)]
            for hh in range(6):
                w_ = wg_[hh % 2]
                self.dma("pool", w_[:], self.odin_d[j, 18 + hh], (), [w_.d()])
                for bi, (t0, n) in enumerate(BLKS):
                    ps = self.psum()
                    for k in range(8):
                        self.mm(ps[:, 0:n], w_[:, k, :], self.hx[:, k, t0:t0 + n], k == 0, k == 7, [w_.d(), self.hx.d()], [ps.d()])
                    z_ = sz[bi % 2]
                    self.act(z_[:, 0:n], ps[:, 0:n], AF.Silu, [ps.d()], [z_.d()])
                    self.tt("dve", mix[:, hh, t0:t0 + n], mix[:, hh, t0:t0 + n], z_[:, 0:n], ALU.mult, [mix.d(), z_.d()], [mix.d()])
            self.P.barrier()

    def cexp_small(self, st, name, lr, li, ls, shape, D):
        mk = lambda n: self.sb(st, name + n, shape, F32)
        step, c, sn, mag, t1, t2 = mk("st"), mk("c"), mk("s"), mk("m"), mk("t1"), mk("t2")
        dd = [step.d()]
        self.act(step[:], ls, AF.Exp, D, dd)
        self.tt("dve", mag[:], lr, step[:], ALU.mult, D + dd, dd)
        self.act(mag[:], mag[:], AF.Exp, dd, dd)
        self.tt("dve", t1[:], li, step[:], ALU.mult, D + dd, dd)
        self.act(sn[:], t1[:], AF.Sin, dd, dd, scale=1.0 / 16.0)
        self.ts("dve", t2[:], t1[:], 1.0 / 16.0, 1.5707963267948966, ALU.mult, ALU.add, dd, dd)
        self.act(c[:], t2[:], AF.Sin, dd, dd)
        for _ in range(4):
            self.tt("dve", t1[:], c[:], c[:], ALU.mult, dd, dd)
            self.tt("dve", t2[:], sn[:], sn[:], ALU.mult, dd, dd)
            self.tt("dve", sn[:], sn[:], c[:], ALU.mult, dd, dd)
            self.ts("dve", sn[:], sn[:], 2.0, None, ALU.mult, None, dd, dd)
            self.tt("dve", c[:], t1[:], t2[:], ALU.subtract, dd, dd)
        for t_ in (c, sn, mag, t1, t2):
            t_.deps[None] = step.d()
        return c, sn, mag, step, t1, t2

    def s5(self, l, s, mix):
        j = l // 2
        po = self.po
        L = 256
        NTC = T // L
        D = [self.cD]
        with contextlib.ExitStack() as st:
            ufm = self.sb(st, "ufm", [128, 2, T], BF16)
            wu = self.sb(st, "s5wu", [128, 8, 128], BF16)
            for c in range(2):
                self.dma("pool", wu[:], self.odin_d[j, 24 + c], (), [wu.d()])
                for (t0, n) in BLKS:
                    ps = self.psum()
                    for k in range(8):
                        self.mm(ps[:, 0:n], wu[:, k, :], self.hx[:, k, t0:t0 + n], k == 0, k == 7, [wu.d(), self.hx.d()], [ps.d()])
                    self.cp("act", ufm[:, c, t0:t0 + n], ps[:, 0:n], [ps.d()], [ufm.d()])
            so = po["s5p%d" % j][0]
            pc, psn, pmag, _, _, _ = self.cexp_small(st, "sp", self.partile[:, so:so + 16], self.partile[:, so + 16:so + 32],
                                                    self.partile[:, so + 32:so + 48], [128, 16], D)
            pdep = [pc.d()]
            E = self.sb(st, "s5E", [128, 4, 2], F32)
            for d in range(2):
                for c in range(2):
                    with contextlib.ExitStack() as st2:
                        tabc = self.sb(st2, "tabc", [128, 4, L], F32)
                        tabs = self.sb(st2, "tabs", [128, 4, L], F32)
                        BM = self.sb(st2, "BM", [128, 4, 2, 128], BF16)
                        CM = self.sb(st2, "CM", [128, 4, 2, 128], BF16)
                        tdep = [tabc.d()]
                        tmp1 = self.sb(st2, "s5tm", [128, 128], F32)
                        for q4 in range(4):
                            q = c * 4 + q4
                            col = d * 8 + q
                            self.cp("dve", tabc[:, q4, 0:1], pc[:, col:col + 1], pdep, tdep)
                            self.cp("dve", tabs[:, q4, 0:1], psn[:, col:col + 1], pdep, tdep)
                            span = 1
                            while span < L:
                                cm_ = tabc[:, q4, span - 1:span]
                                sm_ = tabs[:, q4, span - 1:span]
                                lo, hi = slice(0, span), slice(span, 2 * span)
                                self.ts("dve", tmp1[:, 0:span], tabs[:, q4, lo], sm_, None, ALU.mult, None, tdep, [tmp1.d()])
                                self.stt("dve", tabc[:, q4, hi], tabc[:, q4, lo], cm_, tmp1[:, 0:span], ALU.mult, ALU.subtract, tdep + [tmp1.d()], tdep)
                                self.ts("dve", tmp1[:, 0:span], tabs[:, q4, lo], cm_, None, ALU.mult, None, tdep, [tmp1.d()])
                                self.stt("dve", tabs[:, q4, hi], tabc[:, q4, lo], sm_, tmp1[:, 0:span], ALU.mult, ALU.add, tdep + [tmp1.d()], tdep)
                                span *= 2
                            with contextlib.ExitStack() as st3:
                                rows = self.sb(st3, "s5rows", [128, 3, 128], F32)
                                bpad = self.sb(st3, "s5bp", [128, 2, 128], F32)
                                self.dma("pool", rows[:], self.s5row_d[j, d, :, q, :].unsqueeze(0).to_broadcast([128, 3, 128]), (), [rows.d()])
                                self.dma("act", bpad[:], self.s5b_d[j, q].rearrange("r p c -> p r c"), (), [bpad.d()])
                                self.dma("pool", CM[:, q4, :, :], self.s5c_d[j, d, q].rearrange("r p c -> p r c"), (), [CM.d()])
                                rd = [rows.d()]
                                rc, rs_, rmag, rstep, t1, t2 = self.cexp_small(st3, "sr", rows[:, 0, :], rows[:, 1, :], rows[:, 2, :], [128, 128], rd)
                                w = [rc.d()]
                                lr_, li_ = rows[:, 0, :], rows[:, 1, :]
                                ar, ai, den, zr, zi = rc, rs_, rstep, t1, t2
                                self.tt("dve", ar[:], rc[:], rmag[:], ALU.mult, w, w)
                                self.tt("dve", ai[:], rs_[:], rmag[:], ALU.mult, w, w)
                                self.tt("dve", den[:], lr_, lr_, ALU.mult, rd + w, w)
                                self.tt("dve", rmag[:], li_, li_, ALU.mult, rd + w, w)
                                self.tt("dve", den[:], den[:], rmag[:], ALU.add, w, w)
                                self.P.op("dve", (lambda t_: lambda e: e.reciprocal(out=t_[:], in_=t_[:]))(den), w, w)
                                self.ts("dve", ar[:], ar[:], -1.0, None, ALU.add, None, w, w)
                                self.tt("dve", zr[:], ar[:], lr_, ALU.mult, rd + w, w)
                                self.tt("dve", rmag[:], ai[:], li_, ALU.mult, rd + w, w)
                                self.tt("dve", zr[:], zr[:], rmag[:], ALU.add, w, w)
                                self.tt("dve", zr[:], zr[:], den[:], ALU.mult, w, w)
                                self.tt("dve", zi[:], ai[:], lr_, ALU.mult, rd + w, w)
                                self.tt("dve", rmag[:], ar[:], li_, ALU.mult, rd + w, w)
                                self.tt("dve", zi[:], zi[:], rmag[:], ALU.subtract, w, w)
                                self.tt("dve", zi[:], zi[:], den[:], ALU.mult, w, w)
                                bd = [bpad.d()]
                                self.tt("dve", ar[:], zr[:], bpad[:, 0, :], ALU.mult, w + bd, w)
                                self.tt("dve", ai[:], zi[:], bpad[:, 1, :], ALU.mult, w + bd, w)
                                self.tt("dve", BM[:, q4, 0, :], ar[:], ai[:], ALU.subtract, w, [BM.d()])
                                self.tt("dve", ar[:], zr[:], bpad[:, 1, :], ALU.mult, w + bd, w)
                                self.tt("dve", ai[:], zi[:], bpad[:, 0, :], ALU.mult, w + bd, w)
                                self.tt("dve", BM[:, q4, 1, :], ar[:], ai[:], ALU.add, w, [BM.d()])
                                self.ts("dve", CM[:, q4, 1, :], CM[:, q4, 1, :], -1.0, None, ALU.mult, None, [CM.d()], [CM.d()])
                                self.P.barrier()
                        self.memset("dve", E[:], 0.0, [E.d()])
                        h32 = [self.sb(st2, "s5h%d" % i, [128, 2, L], F32) for i in range(2)]
                        ta = [self.sb(st2, "s5ta%d" % i, [128, 4, L], F32) for i in range(2)]
                        hb = [self.sb(st2, "s5hb%d" % i, [128, 4, 2, L], BF16) for i in range(2)]
                        order = list(range(NTC)) if d == 0 else [0] + list(range(NTC - 1, 0, -1))
                        fl = (lambda a: a) if d == 0 else rev
                        lastpos = L - 1 if d == 0 else 0
                        it = 0
                        for ti, tc in enumerate(order):
                            t0 = tc * L
                            hbt = hb[ti % 2]
                            for q4 in range(4):
                                q = c * 4 + q4
                                col = d * 8 + q
                                b = it % 2
                                it += 1
                                ps_x = self.psum()
                                self.mm(ps_x[:, 0:L], BM[:, q4, 0, :], ufm[:, c, t0:t0 + L], True, True, [BM.d(), ufm.d()], [ps_x.d()])
                                self.mm(ps_x[:, L:2 * L], BM[:, q4, 1, :], ufm[:, c, t0:t0 + L], True, True, [BM.d(), ufm.d()], [ps_x.d()])
                                xr, xi = fl(ps_x[:, 0:L]), fl(ps_x[:, L:2 * L])
                                tcq, tsq = tabc[:, q4, :], tabs[:, q4, :]
                                a = ta[b]
                                hh_ = h32[b]
                                self.tt("dve", a[:, 0, :], tcq, xr, ALU.mult, tdep + [ps_x.d()], [a.d()])
                                self.tt("dve", a[:, 1, :], tsq, xi, ALU.mult, tdep + [ps_x.d()], [a.d()])
                                self.tt("dve", a[:, 2, :], tcq, xi, ALU.mult, tdep + [ps_x.d()], [a.d()])
                                self.tt("dve", a[:, 3, :], tsq, xr, ALU.mult, tdep + [ps_x.d()], [a.d()])
                                ad = [a.d()]
                                self.tt("dve", a[:, 0, :], a[:, 0, :], a[:, 1, :], ALU.add, ad, ad)
                                self.tt("dve", a[:, 2, :], a[:, 2, :], a[:, 3, :], ALU.subtract, ad, ad)
                                mg = pmag[:, col:col + 1].to_broadcast([128, L])
                                self.scan(a[:, 0, :], mg, a[:, 0, :], E[:, q4, 0:1], ad + [E.d()] + pdep, ad)
                                self.scan(a[:, 2, :], mg, a[:, 2, :], E[:, q4, 1:2], ad + [E.d()] + pdep, ad)
                                self.tt("dve", a[:, 1, :], tcq, a[:, 0, :], ALU.mult, tdep + ad, ad)
                                self.tt("dve", a[:, 3, :], tsq, a[:, 2, :], ALU.mult, tdep + ad, ad)
                                self.tt("dve", fl(hh_[:, 0, :]), a[:, 1, :], a[:, 3, :], ALU.subtract, ad, [hh_.d()])
                                self.tt("dve", a[:, 1, :], tsq, a[:, 0, :], ALU.mult, tdep + ad, ad)
                                self.tt("dve", a[:, 3, :], tcq, a[:, 2, :], ALU.mult, tdep + ad, ad)
                                self.tt("dve", fl(hh_[:, 1, :]), a[:, 1, :], a[:, 3, :], ALU.add, ad, [hh_.d()])
                                self.cp("dve", E[:, q4, :], hh_[:, :, lastpos], [hh_.d()], [E.d()])
                                self.cp("act", hbt[:, q4, :, :], hh_[:, :, :], [hh_.d()], [hbt.d()])
                            ps_y = self.psum()
                            for q4 in range(4):
                                for ri in range(2):
                                    self.mm(ps_y[:, 0:L], CM[:, q4, ri, :], hbt[:, q4, ri, :], q4 == 0 and ri == 0, q4 == 3 and ri == 1,
                                            [CM.d(), hbt.d()], [ps_y.d()])
                            if d == 0:
                                self.cp("act", mix[:, 6 + c, t0:t0 + L], ps_y[:, 0:L], [ps_y.d()], [mix.d()])
                            else:
                                self.tt("dve", mix[:, 6 + c, t0:t0 + L], mix[:, 6 + c, t0:t0 + L], ps_y[:, 0:L], ALU.add, [mix.d(), ps_y.d()], [mix.d()])
                        self.P.barrier()
            do = po["s5d%d" % j][0]
            gb = po["glub%d" % j][0]
            wgl = self.sb(st, "wglu", [128, 2, 256], BF16)
            yt = [self.sb(st, "s5yt%d" % i, [128, 512], F32) for i in range(2)]
            self.dma("pool", wgl[:], self.gluw_d[j], (), [wgl.d()])
            for c in range(2):
                for bi, (t0, n) in enumerate(BLKS):
                    y_ = yt[bi % 2]
                    self.stt("dve", y_[:, 0:n], ufm[:, c, t0:t0 + n], self.partile[:, do + c:do + c + 1], mix[:, 6 + c, t0:t0 + n], ALU.mult, ALU.add,
                             [ufm.d(), mix.d(), self.cD], [y_.d()])
                    self.act(ufm[:, c, t0:t0 + n], y_[:, 0:n], AF.Gelu, [y_.d()], [ufm.d()])
            for c in range(2):
                for bi, (t0, n) in enumerate(BLKS):
                    ps = self.psum()
                    for k in range(2):
                        self.mm(ps[:, 0:n], wgl[:, k, c * 128:(c + 1) * 128], ufm[:, k, t0:t0 + n], k == 0, k == 1, [wgl.d(), ufm.d()], [ps.d()])
                    y_ = yt[bi % 2]
                    self.act(y_[:, 0:n], ps[:, 0:n], AF.Sigmoid, [ps.d(), self.cD], [y_.d()], bias=self.partile[:, gb + c:gb + c + 1])
                    self.tt("dve", mix[:, 6 + c, t0:t0 + n], ufm[:, c, t0:t0 + n], y_[:, 0:n], ALU.mult, [ufm.d(), y_.d()], [mix.d()])
            self.P.barrier()

    def final_out(self, s):
        NR = self.NR
        if self.final:
            go, _ = self.po["fng"]
            A = lambda k, r: self.partile[:, go + k:go + k + 1]
            with contextlib.ExitStack() as st:
                self.rms_mod(st, self.x, self.x, A, None, s)
                self.out_ops.append(self.dma("sp", self.yout[s, :, 0:4, :], self.x[:, 0:4, CTX:T], [self.x.d()], ()))
                self.out_ops.append(self.dma("act", self.yout[s, :, 4:8, :], self.x[:, 4:8, CTX:T], [self.x.d()], ()))
                self.P.barrier()
        else:
            self.out_ops.append(self.dma("sp", self.yout[s, :, 0:4, :], self.x[:, 0:4, CTX:T], [self.x.d()], ()))
            self.out_ops.append(self.dma("act", self.yout[s, :, 4:8, :], self.x[:, 4:8, CTX:T], [self.x.d()], ()))


def make_consts():
    c = np.zeros((128, 768), np.float32)
    c[:, 0:128] = np.eye(128, dtype=np.float32)
    i = np.arange(128)
    c[:, 128:256] = (i[:, None] <= i[None, :]).astype(np.float32)
    c[:, 256:384] = (i[:, None] >= i[None, :]).astype(np.float32)
    c[:, 384:512] = 1.0
    same = (i[:, None] // 32) == (i[None, :] // 32)
    c[:, 512:640] = (same & (i[:, None] <= i[None, :])).astype(np.float32)
    c[:, 640:768] = (same & (i[:, None] >= i[None, :])).astype(np.float32)
    return c


def kernel(**inputs):
    nseq = 4
    inp = {k: np.asarray(v) for k, v in inputs.items()}
    phases = []
    for l in range(4):
        phases += [("mix", l), ("ffn", l)]
    kb = K(nseq, phases, final=True)
    nc = kb.build()
    sh = host_prep(inp)
    sh["cst"] = make_consts()
    in_maps = []
    for c in range(NCORES):
        m = dict(sh)
        m.update(core_inputs(inp, c, nseq))
        in_maps.append(m)
    res = run_bass_kernel_spmd(nc, in_maps, core_ids=list(range(NCORES)))
    outs = []
    for c in range(NCORES):
        y = res.results[c]["yout"]
        outs.append(y.transpose(0, 3, 2, 1).reshape(nseq, 2048, 1024))
    return np.ascontiguousarray(np.concatenate(outs, axis=0)).astype(np.float32)
```

```python
import contextlib
import numpy as np
import concourse.bass as bass
import concourse.mybir as mybir
from concourse.ap import AP
from concourse.bass_utils import run_bass_kernel_spmd

F32 = mybir.dt.float32
BF16 = mybir.dt.bfloat16
AF = mybir.ActivationFunctionType
ALU = mybir.AluOpType

T = 2304
CTX = 256
BLKS = [(0, 256), (256, 512), (768, 512), (1280, 512), (1792, 512)]
NCORES = 8
EPS = 1e-6


class Dep:
    __slots__ = ("w", "r", "rd", "const")

    def __init__(self, const=False):
        self.w = None
        self.r = {}
        self.rd = []
        self.const = const


class Op:
    __slots__ = ("eng", "fn", "deps", "marked", "ev", "dma", "idx", "bar", "epoch")


class Prog:
    DMA_SEMS = {"sp": 6, "act": 6, "pool": 24}
    ENGS = ("pe", "act", "dve", "pool", "sp")

    def __init__(self, nc):
        self.nc = nc
        self.ops = []
        self.last = {}
        self.pending_dma = []
        self.nbar = 0

    def _new(self, eng, fn, dma):
        o = Op()
        o.eng = eng
        o.fn = fn
        o.dma = dma
        o.marked = dma
        o.ev = None
        o.bar = 0
        o.epoch = -1
        o.idx = len(self.ops)
        self.ops.append(o)
        return o

    def op(self, eng, fn, reads=(), writes=(), dma=False, pe_acc=False):
        deps = set()
        for d in reads:
            if d.w is not None:
                deps.add(d.w)
        for d in writes:
            if d.w is not None:
                if not (pe_acc and d.w.eng == "pe" and not d.w.dma):
                    deps.add(d.w)
            for r in d.r.values():
                deps.add(r)
            for r in d.rd:
                deps.add(r)
        o = self._new(eng, fn, dma)
        o.deps = deps
        for d in reads:
            if not d.const:
                if dma:
                    d.rd.append(o)
                else:
                    d.r[eng] = o
        for d in writes:
            d.w = o
            d.r = {}
            d.rd = []
        if dma:
            self.pending_dma.append(o)
        else:
            self.last[eng] = o
        return o

    def barrier(self):
        deps = set(self.last.values()) | set(self.pending_dma)
        self.pending_dma = []
        self.last = {}
        self.nbar += 1
        for e in self.ENGS:
            o = self._new(e, None, False)
            o.deps = set(deps)
            o.bar = self.nbar

    def emit(self, final_deps):
        nc = self.nc
        engs = {"pe": nc.tensor, "act": nc.scalar, "dve": nc.vector, "pool": nc.gpsimd, "sp": nc.sync}
        fin = self._new("sp", None, False)
        fin.deps = set(final_deps)
        for o in self.ops:
            for d in o.deps:
                d.marked = True
        cnt = {e: 0 for e in engs}
        dma_rr = {e: 0 for e in engs}
        dma_cnt = {}
        seen = {e: {} for e in engs}
        per_eng = {e: [] for e in engs}
        epoch = 0
        maxv = 0
        nb_in_group = 0
        for o in self.ops:
            mw = {}
            o.epoch = epoch
            if o.dma:
                k = dma_rr[o.eng] % self.DMA_SEMS[o.eng]
                dma_rr[o.eng] += 1
                sk_own = ("dma", o.eng, k)
                prev = dma_cnt.get(sk_own, 0)
                if prev > 0 and seen[o.eng].get(sk_own, 0) < prev:
                    mw[sk_own] = prev
                    seen[o.eng][sk_own] = prev
                dma_cnt[sk_own] = prev + 16
                o.ev = (sk_own, prev + 16)
                maxv = max(maxv, prev + 16)
            elif o.marked:
                cnt[o.eng] += 1
                o.ev = (("eng", o.eng), cnt[o.eng])
                maxv = max(maxv, cnt[o.eng])
            for d in o.deps:
                if d.epoch != epoch:
                    continue
                sk, v = d.ev
                if seen[o.eng].get(sk, 0) < v:
                    seen[o.eng][sk] = v
                    mw[sk] = max(mw.get(sk, 0), v)
            per_eng[o.eng].append((o, list(mw.items())))
            if o.bar:
                nb_in_group += 1
                if nb_in_group == len(self.ENGS):
                    nb_in_group = 0
                    epoch += 1
                    cnt = {e: 0 for e in engs}
                    seen = {e: {sk: v for sk, v in seen[e].items() if sk[0] == "dma"} for e in engs}
        assert maxv < 8000, maxv
        self.stats = {e: len(per_eng[e]) for e in engs}
        self.stats["sem_maxv"] = maxv
        self.stats["nbar"] = self.nbar
        with contextlib.ExitStack() as st:
            sems = {}
            for e in engs:
                sems[("eng", e)] = st.enter_context(nc.semaphore("s_" + e))
            for e in ("sp", "act", "pool"):
                for k in range(self.DMA_SEMS[e]):
                    sems[("dma", e, k)] = st.enter_context(nc.semaphore("d_%s%d" % (e, k)))
            bsemA = st.enter_context(nc.semaphore("barA"))
            bsemB = st.enter_context(nc.semaphore("barB"))
            block = st.enter_context(nc.Block())
            NE = len(self.ENGS)

            def mk(ename):
                def body(eng):
                    for o, waits in per_eng[ename]:
                        for sk, v in waits:
                            eng.wait_ge(sems[sk], v)
                        if o.bar:
                            eng.sem_inc(bsemA, 1)
                            if ename == "sp":
                                eng.wait_ge(bsemA, NE * o.bar)
                                for sk_, sm in sems.items():
                                    if sk_[0] == "eng":
                                        eng.sem_clear(sm)
                                eng.sem_inc(bsemB, 1)
                            eng.wait_ge(bsemB, o.bar)
                            continue
                        if o.fn is None:
                            continue
                        ins = o.fn(eng)
                        if o.dma:
                            ins.then_inc(sems[o.ev[0]], 16)
                        elif o.marked:
                            ins.then_inc(sems[("eng", ename)], 1)
                return body

            block.tensor(mk("pe"))
            block.scalar(mk("act"))
            block.vector(mk("dve"))
            block.gpsimd(mk("pool"))
            block.sync(mk("sp"))


class Tile:
    def __init__(self, t):
        self.t = t
        self.deps = {}

    def d(self, key=None):
        if key not in self.deps:
            self.deps[key] = Dep()
        return self.deps[key]

    def __getitem__(self, idx):
        return self.t[idx]


def rev(ap):
    apl = [list(x) for x in ap.ap]
    n = apl[-1][1]
    off = ap.offset + (n - 1) * apl[-1][0]
    apl[-1][0] = -apl[-1][0]
    return AP(ap.tensor, off, apl)


def fm_vec(v):
    v = np.asarray(v, np.float32).reshape(-1, 128)
    return np.ascontiguousarray(v.T)


def w_colchunks(w, nk):
    K, N = w.shape
    return np.ascontiguousarray(w.reshape(nk, 128, N // 128, 128).transpose(2, 1, 0, 3))


def w_rows(w, nk):
    K, N = w.shape
    return np.ascontiguousarray(w.reshape(nk, 128, N).transpose(1, 0, 2))


class ParPack:
    def __init__(self):
        self.cols = []
        self.off = {}
        self.n = 0

    def add(self, name, arr):
        arr = np.asarray(arr, np.float32)
        arr = arr.reshape(arr.shape[0], -1)
        if arr.shape[0] < 128:
            arr = np.concatenate([arr, np.zeros((128 - arr.shape[0], arr.shape[1]), np.float32)], 0)
        self.off[name] = (self.n, arr.shape[1])
        self.cols.append(arr)
        self.n += arr.shape[1]

    def pack(self):
        return np.ascontiguousarray(np.concatenate(self.cols, axis=1))


def pack_params(inp, off_only=False):
    pp = ParPack()
    z = (lambda *s: np.zeros(s, np.float32))
    g = (lambda k: inp[k]) if not off_only else None
    for l in range(4):
        pp.add("nmg%d" % l, fm_vec(g("norm_mix_g")[l]) if g else z(128, 8))
        pp.add("nfg%d" % l, fm_vec(g("norm_ffn_g")[l]) if g else z(128, 8))
        pp.add("bmod%d" % l, fm_vec(g("b_mod")[l]) if g else z(128, 48))
    pp.add("fng", fm_vec(g("final_norm_g")) if g else z(128, 8))
    for j in range(2):
        if g:
            pp.add("scw%d" % j, g("ssd_conv_w")[j].reshape(4, 12, 128).transpose(2, 1, 0))
            pp.add("scb%d" % j, fm_vec(g("ssd_conv_b")[j]))
            pp.add("sng%d" % j, fm_vec(g("ssd_norm_g")[j]))
            pp.add("sd%d" % j, fm_vec(np.repeat(g("ssd_d")[j], 64)))
            pp.add("dtb%d" % j, g("ssd_dt_bias")[j].reshape(32, 1))
            pp.add("alog%d" % j, g("ssd_a_log")[j].reshape(32, 1))
            pp.add("dtbrow%d" % j, np.tile(g("ssd_dt_bias")[j].reshape(1, 32), (128, 1)))
            pp.add("alogrow%d" % j, np.tile(g("ssd_a_log")[j].reshape(1, 32), (128, 1)))
            pp.add("lcw%d" % j, g("lru_conv_w")[j].reshape(4, 8, 128).transpose(2, 1, 0))
            pp.add("lcb%d" % j, fm_vec(g("lru_conv_b")[j]))
            pp.add("lba%d" % j, g("lru_b_a")[j].reshape(2, 8, 128).transpose(2, 0, 1))
            pp.add("lbi%d" % j, g("lru_b_i")[j].reshape(2, 8, 128).transpose(2, 0, 1))
            pp.add("llam%d" % j, g("lru_lam")[j].reshape(2, 8, 128).transpose(2, 0, 1))
        else:
            pp.add("scw%d" % j, z(128, 48)); pp.add("scb%d" % j, z(128, 12)); pp.add("sng%d" % j, z(128, 8))
            pp.add("sd%d" % j, z(128, 8)); pp.add("dtb%d" % j, z(128, 1)); pp.add("alog%d" % j, z(128, 1))
            pp.add("dtbrow%d" % j, z(128, 32)); pp.add("alogrow%d" % j, z(128, 32))
            pp.add("lcw%d" % j, z(128, 32)); pp.add("lcb%d" % j, z(128, 8)); pp.add("lba%d" % j, z(128, 16))
            pp.add("lbi%d" % j, z(128, 16)); pp.add("llam%d" % j, z(128, 16))
    if g:
        pp.add("lbl", g("hg_lb_logits").reshape(4, 6, 128).transpose(2, 0, 1))
    else:
        pp.add("lbl", z(128, 24))
    for j in range(2):
        if g:
            pp.add("hgn%d" % j, g("hg_norm_g")[j].reshape(6, 128).T)
            pp.add("s5d%d" % j, fm_vec(g("s5_d")[j]))
            pp.add("glub%d" % j, fm_vec(g("s5_glu_b")[j]))
            sp_ = np.zeros((128, 3, 2, 8), np.float32)
            for gg in range(16):
                sp_[(gg % 2) * 64:(gg % 2) * 64 + 64, 0, :, gg // 2] = g("s5_lam_re")[j][:, gg].T
                sp_[(gg % 2) * 64:(gg % 2) * 64 + 64, 1, :, gg // 2] = g("s5_lam_im")[j][:, gg].T
                sp_[(gg % 2) * 64:(gg % 2) * 64 + 64, 2, :, gg // 2] = g("s5_log_step")[j][:, gg][None, :]
            pp.add("s5p%d" % j, sp_)
        else:
            pp.add("hgn%d" % j, z(128, 6)); pp.add("s5d%d" % j, z(128, 2)); pp.add("glub%d" % j, z(128, 2))
            pp.add("s5p%d" % j, z(128, 48))
    pp.nres = pp.n
    for l in range(4):
        if g:
            cw = g("ffn_conv_w")[l].reshape(9, 22, 128).transpose(2, 1, 0)
            pp.add("fcw%d" % l, cw)
            pp.add("fcb%d" % l, fm_vec(g("ffn_conv_b")[l]))
        else:
            pp.add("fcw%d" % l, z(128, 22 * 9))
            pp.add("fcb%d" % l, z(128, 22))
    return pp


def host_prep(inp, nseq_total=32):
    sh = {}
    sh["par"] = pack_params(inp).pack()
    sh["wmod"] = np.stack([w_rows(inp["w_mod"][l], 8) for l in range(4)])
    sh["ffg"] = np.stack([w_colchunks(inp["ffn_w_gate"][l], 8) for l in range(4)])
    sh["ffu"] = np.stack([w_colchunks(inp["ffn_w_up"][l], 8) for l in range(4)])
    sh["ffd"] = np.stack([w_colchunks(inp["ffn_w_down"][l], 22) for l in range(4)])
    ev = inp["ev_w_in"]
    evc = np.concatenate([ev[:, :, 0:2560], ev[:, :, 2592:4640]], axis=2)
    sh["evin"] = np.stack([w_colchunks(evc[j], 8) for j in range(2)])
    sh["evdt"] = np.stack([w_rows(ev[j][:, 2560:2592], 8) for j in range(2)])
    sh["evout"] = np.stack([w_colchunks(inp["ev_w_out"][j], 16) for j in range(2)])
    la = np.stack([inp["lru_w_a"], inp["lru_w_i"]], axis=1)
    sh["lruw"] = np.ascontiguousarray(la.transpose(0, 4, 1, 2, 3, 5))
    od = inp["od_w_in"]
    odc = np.concatenate([od[:, :, 0:2304], od[:, :, 3072:4096]], axis=2)
    sh["odin"] = np.stack([w_colchunks(odc[j], 8) for j in range(2)])
    sh["odv"] = np.stack([w_rows(od[j][:, 2304:3072], 8) for j in range(2)])
    sh["odout"] = np.stack([w_colchunks(inp["od_w_out"][j], 8) for j in range(2)])
    sh["gluw"] = np.stack([w_rows(inp["s5_glu_w"][j], 2) for j in range(2)])
    s5b = np.zeros((2, 8, 2, 128, 128), np.float32)
    s5c = np.zeros((2, 2, 8, 2, 128, 128), np.float32)
    s5row = np.zeros((2, 2, 3, 8, 128), np.float32)
    for g in range(16):
        q, gi, go = g // 2, g % 8, g % 2
        for ri, nm in enumerate(("s5_b_re", "s5_b_im")):
            s5b[:, q, ri, gi * 16:(gi + 1) * 16, go * 64:(go + 1) * 64] = inp[nm][:, g].transpose(0, 2, 1)
        for ri, nm in enumerate(("s5_c_re", "s5_c_im")):
            s5c[:, :, q, ri, go * 64:(go + 1) * 64, gi * 16:(gi + 1) * 16] = inp[nm][:, :, g].transpose(0, 1, 3, 2)
        s5row[:, :, 0, q, go * 64:(go + 1) * 64] = inp["s5_lam_re"][:, :, g]
        s5row[:, :, 1, q, go * 64:(go + 1) * 64] = inp["s5_lam_im"][:, :, g]
        s5row[:, :, 2, q, go * 64:(go + 1) * 64] = inp["s5_log_step"][:, :, g][:, :, None]
    sh["s5b"] = s5b
    sh["s5c"] = s5c
    sh["s5row"] = s5row
    return sh


def core_inputs(inp, core, nseq):
    b0 = core * nseq
    xs = []
    for s in range(nseq):
        full = np.concatenate([inp["ctx"][b0 + s], inp["x"][b0 + s]], axis=0)
        xs.append(full.reshape(T, 8, 128).transpose(2, 1, 0))
    cc = np.concatenate([inp["c"][b0:b0 + nseq], inp["c_ctx"][None, :]], axis=0)
    ccf = cc.reshape(nseq + 1, 8, 128).transpose(2, 1, 0)
    return {"xin": np.ascontiguousarray(np.stack(xs)), "cc": np.ascontiguousarray(ccf)}


class K:
    def __init__(self, nseq, phases, final=True):
        self.nseq = nseq
        self.NR = nseq + 1
        self.phases = phases
        self.final = final
        self.nc = bass.Bass("TRN2", target_bir_lowering=False)
        self.P = Prog(self.nc)
        self.po = pack_params(None, off_only=True).off
        self.npar = pack_params(None, off_only=True).n
        self.nres = pack_params(None, off_only=True).nres

    def sb(self, st, name, shape, dt):
        self.uid = getattr(self, "uid", 0) + 1
        return Tile(st.enter_context(self.nc.sbuf_tensor("t%d_%s" % (self.uid, name), shape, dt)))

    def dram_in(self, name, shape):
        return self.nc.dram_tensor(name, list(shape), F32, kind="ExternalInput").ap()

    def psum(self):
        t = self.ps[self.psi % 8]
        self.psi += 1
        return t

    def par(self, name, c0=0, n=1):
        o, w = self.po[name]
        return self.partile[:, o + c0:o + c0 + n]

    def mm(self, out, lhsT, rhs, start, stop, reads, writes):
        return self.P.op("pe", lambda e: e.matmul(out, lhsT, rhs, start=start, stop=stop), reads, writes, pe_acc=True)

    def act(self, out, in_, func, reads, writes, bias=None, scale=None):
        kw = {}
        if bias is not None:
            kw["bias"] = bias
        if scale is not None:
            kw["scale"] = scale
        return self.P.op("act", lambda e: e.activation(out=out, in_=in_, func=func, **kw), reads, writes)

    def tt(self, eng, out, in0, in1, op, reads, writes):
        return self.P.op(eng, lambda e: e.tensor_tensor(out=out, in0=in0, in1=in1, op=op), reads, writes)

    def ts(self, eng, out, in0, s1, s2, op0, op1, reads, writes):
        if s2 is None:
            return self.P.op(eng, lambda e: e.tensor_scalar(out=out, in0=in0, scalar1=s1, scalar2=None, op0=op0), reads, writes)
        return self.P.op(eng, lambda e: e.tensor_scalar(out=out, in0=in0, scalar1=s1, scalar2=s2, op0=op0, op1=op1), reads, writes)

    def stt(self, eng, out, in0, scalar, in1, op0, op1, reads, writes):
        return self.P.op(eng, lambda e: e.scalar_tensor_tensor(out=out, in0=in0, scalar=scalar, in1=in1, op0=op0, op1=op1), reads, writes)

    def cp(self, eng, out, in_, reads, writes):
        if eng == "act":
            return self.P.op("act", lambda e: e.copy(out=out, in_=in_), reads, writes)
        return self.P.op(eng, lambda e: e.tensor_copy(out=out, in_=in_), reads, writes)

    def dma(self, eng, out, in_, reads, writes):
        return self.P.op(eng, lambda e: e.dma_start(out=out, in_=in_), reads, writes, dma=True)

    def memset(self, eng, ap, val, writes):
        return self.P.op(eng, lambda e: e.memset(ap, val), (), writes)

    def build(self):
        nc, P = self.nc, self.P
        NR = self.NR
        self.xin = self.dram_in("xin", [self.nseq, 128, 8, T])
        self.cc = self.dram_in("cc", [128, 8, NR])
        self.par_d = self.dram_in("par", [128, self.npar])
        self.wmod_d = self.dram_in("wmod", [4, 128, 8, 6144])
        self.ffg_d = self.dram_in("ffg", [4, 22, 128, 8, 128])
        self.ffu_d = self.dram_in("ffu", [4, 22, 128, 8, 128])
        self.ffd_d = self.dram_in("ffd", [4, 8, 128, 22, 128])
        self.evin_d = self.dram_in("evin", [2, 36, 128, 8, 128])
        self.evdt_d = self.dram_in("evdt", [2, 128, 8, 32])
        self.evout_d = self.dram_in("evout", [2, 8, 128, 16, 128])
        self.lruw_d = self.dram_in("lruw", [2, 128, 2, 2, 8, 128])
        self.cst_d = self.dram_in("cst", [128, 768])
        self.odin_d = self.dram_in("odin", [2, 26, 128, 8, 128])
        self.odv_d = self.dram_in("odv", [2, 128, 8, 768])
        self.odout_d = self.dram_in("odout", [2, 8, 128, 8, 128])
        self.gluw_d = self.dram_in("gluw", [2, 128, 2, 256])
        self.s5b_d = self.dram_in("s5b", [2, 8, 2, 128, 128])
        self.s5c_d = self.dram_in("s5c", [2, 2, 8, 2, 128, 128])
        self.s5row_d = self.dram_in("s5row", [2, 2, 3, 8, 128])
        self.yout = nc.dram_tensor("yout", [self.nseq, 128, 8, 2048], F32, kind="ExternalOutput").ap()
        if getattr(self, "debug", False):
            self.dbg = nc.dram_tensor("dbg", [128, 16, T], F32, kind="ExternalOutput").ap()
        self.out_ops = []
        with contextlib.ExitStack() as st:
            self.ps = [Tile(st.enter_context(nc.psum_tensor("ps%d" % i, [128, 512], F32))) for i in range(8)]
            self.psi = 0
            self.partile = self.sb(st, "par", [128, self.nres], F32)
            self.cst = self.sb(st, "cst", [128, 768], F32)
            self.identb = self.sb(st, "identb", [128, 128], BF16)
            self.onesb = self.sb(st, "onesb", [128, 128], BF16)
            self.mods = self.sb(st, "mods", [128, 4 * 48 * NR], F32)
            self.amix = self.sb(st, "amix", [128, 4 * 8 * NR], F32)
            self.affn = self.sb(st, "affn", [128, 4 * 8 * NR], F32)
            self.lbt = self.sb(st, "lbt", [128, 4, 6], F32)
            self.x = self.sb(st, "x", [128, 8, T], F32)
            self.hx = self.sb(st, "hx", [128, 8, T], BF16)
            self.cD = Dep(const=True)
            o1 = self.dma("sp", self.partile[:], self.par_d[:, 0:self.nres], (), [self.cD])
            o2 = self.dma("sp", self.cst[:], self.cst_d, (), [self.cD])
            self.cp("dve", self.identb[:], self.cst[:, 0:128], [self.cD], [self.cD])
            self.memset("dve", self.onesb[:], 1.0, [self.cD])
            self.prologue_mods()
            self.prologue_lb()
            P.barrier()
            for s in range(self.nseq):
                self.dma("sp", self.x[:, 0:4, :], self.xin[s, :, 0:4, :], (), [self.x.d()])
                self.dma("act", self.x[:, 4:8, :], self.xin[s, :, 4:8, :], (), [self.x.d()])
                for kind, l in self.phases:
                    if kind == "ffn":
                        self.ffn(l, s)
                    elif kind == "mix":
                        if l % 2 == 0:
                            self.even_mixer(l, s)
                        else:
                            self.odd_mixer(l, s)
                    P.barrier()
                self.final_out(s)
                P.barrier()
            P.emit(self.out_ops)
        return nc

    def mod(self, l, chunk0, r):
        NR = self.NR
        base = (l * 48 + chunk0) * NR + r
        return lambda k: self.mods[:, base + k * NR: base + k * NR + 1]

    def prologue_mods(self):
        NR = self.NR
        with contextlib.ExitStack() as st:
            ccf = self.sb(st, "ccf", [128, 8, NR], F32)
            sfm = self.sb(st, "sfm", [128, 8, NR], BF16)
            wm = [self.sb(st, "wm%d" % i, [128, 8, 1536], BF16) for i in range(2)]
            self.dma("sp", ccf[:], self.cc, (), [ccf.d()])
            self.act(sfm[:], ccf[:], AF.Silu, [ccf.d()], [sfm.d()])
            it = 0
            for l in range(4):
                ps = self.psum()
                for piece in range(4):
                    w = wm[it % 2]
                    it += 1
                    self.dma("pool", w[:], self.wmod_d[l, :, :, piece * 1536:(piece + 1) * 1536], (), [w.d()])
                    for c in range(12):
                        ch = piece * 12 + c
                        for k in range(8):
                            self.mm(ps[:, ch * NR:(ch + 1) * NR], w[:, k, c * 128:(c + 1) * 128], sfm[:, k, :],
                                    k == 0, k == 7, [w.d(), sfm.d()], [ps.d()])
                mo = self.mods[:, l * 48 * NR:(l + 1) * 48 * NR].rearrange("p (c r) -> p c r", r=NR)
                o, _ = self.po["bmod%d" % l]
                self.tt("dve", mo, ps[:, 0:48 * NR].rearrange("p (c r) -> p c r", r=NR),
                        self.partile[:, o:o + 48].unsqueeze(2).to_broadcast([128, 48, NR]), ALU.add,
                        [ps.d(), self.cD], [self.cD])
                for dst, gname, c0 in ((self.amix, "nmg%d" % l, 8), (self.affn, "nfg%d" % l, 32)):
                    dv = dst[:, l * 8 * NR:(l + 1) * 8 * NR].rearrange("p (c r) -> p c r", r=NR)
                    sc = self.mods[:, (l * 48 + c0) * NR:(l * 48 + c0 + 8) * NR].rearrange("p (c r) -> p c r", r=NR)
                    go, _ = self.po[gname]
                    self.ts("dve", dv, sc, 1.0, None, ALU.add, None, [self.cD], [self.cD])
                    self.tt("dve", dv, dv, self.partile[:, go:go + 8].unsqueeze(2).to_broadcast([128, 8, NR]), ALU.mult,
                            [self.cD], [self.cD])

    def rms_mod(self, st, src, dst, A, B, s, nch=8, srcdep=None, dstdep=None, pergroup=False):
        sq = [self.sb(st, "rm_sq%d" % i, [128, nch, 512], BF16) for i in range(2)]
        rs = [self.sb(st, "rm_rs%d" % i, [128, nch if pergroup else 1, 512], F32) for i in range(2)]
        tm = [self.sb(st, "rm_tm%d" % i, [128, 512], F32) for i in range(2)]
        sd = srcdep or src.d()
        dd = dstdep or dst.d()
        ndiv = 128.0 if pergroup else 128.0 * nch
        for bi, (t0, n) in enumerate(BLKS):
            r = self.nseq if t0 == 0 else s
            q = sq[bi % 2]
            rr = rs[bi % 2]
            self.act(q[:, :, 0:n], src[:, 0:nch, t0:t0 + n], AF.Square, [sd], [q.d()])
            groups = [[k] for k in range(nch)] if pergroup else [list(range(nch))]
            for gi, grp in enumerate(groups):
                ps = self.psum()
                for i, k in enumerate(grp):
                    self.mm(ps[:, 0:n], self.onesb[:], q[:, k, 0:n], i == 0, i == len(grp) - 1, [q.d(), self.cD], [ps.d()])
                self.ts("dve", rr[:, gi, 0:n], ps[:, 0:n], 1.0 / ndiv, EPS, ALU.mult, ALU.add, [ps.d()], [rr.d()])
                self.P.op("dve", (lambda o_, i_: lambda e: e.reciprocal(out=o_, in_=i_))(rr[:, gi, 0:n], rr[:, gi, 0:n]), [rr.d()], [rr.d()])
                self.act(rr[:, gi, 0:n], rr[:, gi, 0:n], AF.Sqrt, [rr.d()], [rr.d()])
            for k in range(nch):
                tmp = tm[k % 2]
                gi = k if pergroup else 0
                if A is not None:
                    self.stt("dve", tmp[:, 0:n], src[:, k, t0:t0 + n], A(k, r), rr[:, gi, 0:n], ALU.mult, ALU.mult,
                             [sd, rr.d(), self.cD], [tmp.d()])
                else:
                    self.tt("dve", tmp[:, 0:n], src[:, k, t0:t0 + n], rr[:, gi, 0:n], ALU.mult, [sd, rr.d()], [tmp.d()])
                if B is not None:
                    self.act(dst[:, k, t0:t0 + n], tmp[:, 0:n], AF.Identity, [tmp.d(), self.cD], [dd], bias=B(k, r))
                else:
                    self.cp("act", dst[:, k, t0:t0 + n], tmp[:, 0:n], [tmp.d()], [dd])

    def ffn(self, l, s):
        NR = self.NR
        A = lambda k, r: self.affn[:, (l * 8 + k) * NR + r:(l * 8 + k) * NR + r + 1]
        B = lambda k, r: self.mods[:, (l * 48 + 24 + k) * NR + r:(l * 48 + 24 + k) * NR + r + 1]
        with contextlib.ExitStack() as st0:
            with contextlib.ExitStack() as st:
                self.rms_mod(st, self.x, self.hx, A, B, s)
            self.P.barrier()
            gh = self.sb(st0, "gh", [128, 22, 1280], BF16)
            fpar = self.sb(st0, "fpar", [128, 220], F32)
            fo0 = self.po["fcw%d" % l][0]
            self.dma("sp", fpar[:], self.par_d[:, fo0:fo0 + 220], (), [fpar.d()])
            fo, bo = 0, 198
            it = 0
            for half in range(2):
              with contextlib.ExitStack() as st1:
                wg = [self.sb(st1, "wg%d" % i, [128, 8, 128], BF16) for i in range(2)]
                wu = [self.sb(st1, "wu%d" % i, [128, 8, 128], BF16) for i in range(2)]
                dg = [self.sb(st1, "dg%d" % i, [128, 9, 128], BF16) for i in range(2)]
                apc = [self.sb(st1, "apc%d" % i, [128, 258], BF16) for i in range(2)]
                apl = [None, None]
                apl[half] = [self.sb(st1, "apl%d_%d" % (half, i), [128, 18, 66], BF16) for i in range(2)]
                sg = [self.sb(st1, "sg%d" % i, [128, 512], BF16) for i in range(2)]
                for t_ in apc + apl[half]:
                    self.memset("pool", t_[:], 0.0, [t_.d()])
                if half == 0:
                    pieces = [(256, 8, 1), (256 + 512, 8, 9), (256 + 1024, 1, 17)]
                    oblks = [("c", 0, 256, 0), ("l", 256, 512, 0), ("l", 768, 512, 8)]
                else:
                    pieces = [(256 + 960, 1, 0), (256 + 1024, 8, 1), (256 + 1536, 8, 9)]
                    oblks = [("l", 1280, 512, 0), ("l", 1792, 512, 8)]
                row_off = 1 if half == 0 else 1
                def load(f, it):
                    self.dma("pool", wg[it % 2][:], self.ffg_d[l, f], (), [wg[it % 2].d()])
                    self.dma("pool", wu[it % 2][:], self.ffu_d[l, f], (), [wu[it % 2].d()])
                load(0, it)
                for f in range(22):
                    if f + 1 < 22:
                        load(f + 1, it + 1)
                    g_, u_, d_ = wg[it % 2], wu[it % 2], dg[it % 2]
                    pc, pl = apc[it % 2], apl[half][it % 2]
                    self.tt("dve", d_[:], self.identb[:].unsqueeze(1).to_broadcast([128, 9, 128]),
                            fpar[:, fo + f * 9:fo + f * 9 + 9].unsqueeze(2).to_broadcast([128, 9, 128]), ALU.mult, [self.cD, fpar.d()], [d_.d()])
                    if half == 0:
                        ps = self.psum()
                        for k in range(8):
                            self.mm(ps[:, 0:256], g_[:, k, :], self.hx[:, k, 0:256], k == 0, k == 7, [g_.d(), self.hx.d()], [ps.d()])
                        self.cp("act", pc[:, 1:257], ps[:, 0:256], [ps.d()], [pc.d()])
                    for (tk0, nr, pr0) in pieces:
                        ps = self.psum()
                        n = nr * 64
                        for k in range(8):
                            self.mm(ps[:, 0:n], g_[:, k, :], self.hx[:, k, tk0:tk0 + n], k == 0, k == 7, [g_.d(), self.hx.d()], [ps.d()])
                        self.cp("act", pl[:, pr0:pr0 + nr, 1:65], ps[:, 0:n].rearrange("p (r c) -> p r c", c=64), [ps.d()], [pl.d()])
                    gcol = 0
                    for (kind, tk0, n, lr0) in oblks:
                        psc = self.psum()
                        if kind == "c":
                            for i, dx in enumerate((-1, 0, 1)):
                                self.mm(psc[:, 0:256], d_[:, 3 + (dx + 1), :], pc[:, 1 + dx:257 + dx], i == 0, i == 2, [d_.d(), pc.d()], [psc.d()])
                        else:
                            i = 0
                            for dy in (-1, 0, 1):
                                for dx in (-1, 0, 1):
                                    r0 = row_off + lr0 + dy
                                    self.mm(psc[:, 0:512].rearrange("p (r c) -> p r c", c=64), d_[:, (dy + 1) * 3 + (dx + 1), :],
                                            pl[:, r0:r0 + 8, 1 + dx:65 + dx], i == 0, i == 8, [d_.d(), pl.d()], [psc.d()])
                                    i += 1
                        sgt = sg[gcol % 2]
                        self.act(sgt[:, 0:n], psc[:, 0:n], AF.Silu, [psc.d(), fpar.d()], [sgt.d()], bias=fpar[:, bo + f:bo + f + 1])
                        psu = self.psum()
                        for k in range(8):
                            self.mm(psu[:, 0:n], u_[:, k, :], self.hx[:, k, tk0:tk0 + n], k == 0, k == 7, [u_.d(), self.hx.d()], [psu.d()])
                        hoff = tk0 if half == 0 else tk0 - 1280
                        self.tt("dve", gh[:, f, hoff:hoff + n], sgt[:, 0:n], psu[:, 0:n], ALU.mult, [sgt.d(), psu.d()], [gh.d(f)])
                        gcol += 1
                    it += 1
              self.P.barrier()
              with contextlib.ExitStack() as st1:
                wd = [self.sb(st1, "wd%d" % i, [128, 22, 128], BF16) for i in range(2)]
                ghd = [gh.d(f) for f in range(22)]
                self.dma("pool", wd[0][:], self.ffd_d[l, 0], (), [wd[0].d()])
                for oc in range(8):
                    if oc + 1 < 8:
                        self.dma("pool", wd[(oc + 1) % 2][:], self.ffd_d[l, oc + 1], (), [wd[(oc + 1) % 2].d()])
                    w_ = wd[oc % 2]
                    for (kind, tk0, n, lr0) in oblks:
                        r = self.nseq if kind == "c" else s
                        hoff = tk0 if half == 0 else tk0 - 1280
                        ps = self.psum()
                        for f in range(22):
                            self.mm(ps[:, 0:n], w_[:, f, :], gh[:, f, hoff:hoff + n], f == 0, f == 21, [w_.d()] + ghd, [ps.d()])
                        m5 = self.mods[:, (l * 48 + 40 + oc) * NR + r:(l * 48 + 40 + oc) * NR + r + 1]
                        self.stt("dve", self.x[:, oc, tk0:tk0 + n], ps[:, 0:n], m5, self.x[:, oc, tk0:tk0 + n], ALU.mult, ALU.add,
                                 [ps.d(), self.cD, self.x.d()], [self.x.d()])
                self.P.barrier()


    def dump(self, tile, ch0, nch, slot0, dep=None):
        if not getattr(self, "debug", False):
            return
        self.P.barrier()
        with contextlib.ExitStack() as st:
            stg = self.sb(st, "dbgstg", [128, T], F32)
            for i in range(nch):
                src = tile[:, ch0 + i, :] if len(tile.t.shape) == 3 else tile[:, :]
                n = src.shape[-1]
                self.cp("dve", stg[:, 0:n], src, [dep or tile.d()], [stg.d()])
                self.out_ops.append(self.dma("sp", self.dbg[:, slot0 + i, 0:n], stg[:, 0:n], [stg.d()], ()))
            self.P.barrier()

    def dump_ap(self, ap, dep, slot):
        if not getattr(self, "debug", False):
            return
        self.P.barrier()
        with contextlib.ExitStack() as st:
            n = ap.shape[-1]
            stg = self.sb(st, "dbgstg2", [128, n], F32)
            self.cp("dve", stg[:, 0:n], ap, [dep], [stg.d()])
            self.out_ops.append(self.dma("sp", self.dbg[:, slot, 0:n], stg[:, 0:n], [stg.d()], ()))
            self.P.barrier()

    def outproj(self, wd_ap_fn, nk, mix, l, s, gate_chunk0):
        NR = self.NR
        with contextlib.ExitStack() as st:
            wo = [self.sb(st, "wo%d" % i, [128, nk, 128], BF16) for i in range(2)]
            self.dma("pool", wo[0][:], wd_ap_fn(0), (), [wo[0].d()])
            for oc in range(8):
                if oc + 1 < 8:
                    self.dma("pool", wo[(oc + 1) % 2][:], wd_ap_fn(oc + 1), (), [wo[(oc + 1) % 2].d()])
                w_ = wo[oc % 2]
                for (t0, n) in BLKS:
                    r = self.nseq if t0 == 0 else s
                    ps = self.psum()
                    for k in range(nk):
                        self.mm(ps[:, 0:n], w_[:, k, :], mix[:, k, t0:t0 + n], k == 0, k == nk - 1, [w_.d(), mix.d()], [ps.d()])
                    m2 = self.mods[:, (l * 48 + gate_chunk0 + oc) * NR + r:(l * 48 + gate_chunk0 + oc) * NR + r + 1]
                    self.stt("dve", self.x[:, oc, t0:t0 + n], ps[:, 0:n], m2, self.x[:, oc, t0:t0 + n], ALU.mult, ALU.add,
                             [ps.d(), self.cD, self.x.d()], [self.x.d()])
            self.P.barrier()

    def proj_conv(self, w_dram, cw_off, cb_off, wt, pad, dgt, dst, func, ntap=4, dst_fn=None, dst_dep=None):
        self.dma("pool", wt[:], w_dram, (), [wt.d()])
        self.tt("dve", dgt[:, 0:ntap, :], self.identb[:].unsqueeze(1).to_broadcast([128, ntap, 128]),
                self.partile[:, cw_off:cw_off + ntap].unsqueeze(2).to_broadcast([128, ntap, 128]), ALU.mult, [self.cD], [dgt.d()])
        for (t0, n) in BLKS:
            ps = self.psum()
            for k in range(8):
                self.mm(ps[:, 0:n], wt[:, k, :], self.hx[:, k, t0:t0 + n], k == 0, k == 7, [wt.d(), self.hx.d()], [ps.d()])
            po = 1 + t0 if t0 == 0 else 260 + (t0 - CTX)
            self.cp("act", pad[:, po:po + n], ps[:, 0:n], [ps.d()], [pad.d()])
        for (t0, n) in BLKS:
            ps = self.psum()
            po = t0 if t0 == 0 else 259 + (t0 - CTX)
            for k in range(ntap):
                self.mm(ps[:, 0:n], dgt[:, k, :], pad[:, po + k:po + k + n], k == 0, k == ntap - 1, [dgt.d(), pad.d()], [ps.d()])
            o_ap = dst_fn(t0, n) if dst_fn is not None else dst[:, t0:t0 + n]
            self.act(o_ap, ps[:, 0:n], func, [ps.d(), self.cD], [dst_dep or dst.d()], bias=self.partile[:, cb_off:cb_off + 1])

    def even_mixer(self, l, s):
        j = l // 2
        NR = self.NR
        A = lambda k, r: self.amix[:, (l * 8 + k) * NR + r:(l * 8 + k) * NR + r + 1]
        B = lambda k, r: self.mods[:, (l * 48 + k) * NR + r:(l * 48 + k) * NR + r + 1]
        with contextlib.ExitStack() as st0:
            with contextlib.ExitStack() as st:
                self.rms_mod(st, self.x, self.hx, A, B, s)
            self.P.barrier()
            mix = self.sb(st0, "mix", [128, 8, T], BF16)
            which = getattr(self, "even_parts", ("ssd", "lru"))
            if "ssd" in which:
                self.ssd(l, s, mix)
                self.dump(mix, 0, 8, 0)
                self.outproj(lambda oc: self.evout_d[j, oc, :, 0:8, :], 8, mix, l, s, 16)
            if "lru" in which:
                self.lru(l, s, mix)
                self.dump(mix, 0, 8, 8)
                self.outproj(lambda oc: self.evout_d[j, oc, :, 8:16, :], 8, mix, l, s, 16)

    def lru(self, l, s, mix):
        j = l // 2
        po = self.po
        with contextlib.ExitStack() as st:
            cA = self.sb(st, "cA", [128, 16], F32)
            wu = self.sb(st, "lwu", [128, 8, 128], BF16)
            wgy = [self.sb(st, "lwgy%d" % i, [128, 8, 128], BF16) for i in range(2)]
            wai = [self.sb(st, "lwai%d" % i, [128, 2, 2, 128], BF16) for i in range(2)]
            pad = self.sb(st, "lpad", [128, 2312], BF16)
            dgt = self.sb(st, "ldg", [128, 4, 128], BF16)
            ucb = self.sb(st, "ucb", [128, T], BF16)
            a_t = self.sb(st, "lru_a", [128, T], F32)
            bx = [self.sb(st, "lru_bx%d" % i, [128, T], BF16) for i in range(2)]
            tmp = [self.sb(st, "ltmp%d" % i, [128, 512], F32 if i < 3 else BF16) for i in range(6)]
            self.memset("pool", pad[:], 0.0, [pad.d()])
            lo, _ = po["llam%d" % j]
            self.act(cA[:], self.partile[:, lo:lo + 16], AF.Exp, [self.cD], [cA.d()], scale=-1.0)
            self.act(cA[:], cA[:], AF.Ln, [cA.d()], [cA.d()], bias=1.0)
            self.ts("dve", cA[:], cA[:], -8.0, None, ALU.mult, None, [cA.d()], [cA.d()])
            for jj in range(8):
                self.dma("pool", wgy[jj % 2][:], self.evin_d[j, 20 + jj], (), [wgy[jj % 2].d()])
                self.dma("pool", wai[jj % 2][:], self.lruw_d[j, :, :, :, jj, :], (), [wai[jj % 2].d()])
                self.proj_conv(self.evin_d[j, 28 + jj], po["lcw%d" % j][0] + jj * 4, po["lcb%d" % j][0] + jj, wu, pad, dgt, ucb, AF.Identity)
                w2 = wai[jj % 2]
                for d in range(2):
                    ba = self.partile[:, po["lba%d" % j][0] + d * 8 + jj:po["lba%d" % j][0] + d * 8 + jj + 1]
                    bi = self.partile[:, po["lbi%d" % j][0] + d * 8 + jj:po["lbi%d" % j][0] + d * 8 + jj + 1]
                    for (t0, n) in BLKS:
                        psa = self.psum()
                        self.mm(psa[:, 0:n], w2[:, 0, d, :], ucb[:, t0:t0 + n], True, True, [w2.d(), ucb.d()], [psa.d()])
                        psi_ = self.psum()
                        self.mm(psi_[:, 0:n], w2[:, 1, d, :], ucb[:, t0:t0 + n], True, True, [w2.d(), ucb.d()], [psi_.d()])
                        rt, it_, sq, t3 = tmp[0], tmp[1], tmp[2], tmp[3]
                        self.act(rt[:, 0:n], psa[:, 0:n], AF.Sigmoid, [psa.d(), self.cD], [rt.d()], bias=ba)
                        self.act(a_t[:, t0:t0 + n], rt[:, 0:n], AF.Exp, [rt.d(), cA.d()], [a_t.d()], scale=cA[:, d * 8 + jj:d * 8 + jj + 1])
                        self.act(it_[:, 0:n], psi_[:, 0:n], AF.Sigmoid, [psi_.d(), self.cD], [it_.d()], bias=bi)
                        self.act(sq[:, 0:n], a_t[:, t0:t0 + n], AF.Square, [a_t.d()], [sq.d()])
                        self.act(sq[:, 0:n], sq[:, 0:n], AF.Sqrt, [sq.d()], [sq.d()], scale=-1.0, bias=1.0)
                        self.tt("dve", t3[:, 0:n], sq[:, 0:n], it_[:, 0:n], ALU.mult, [sq.d(), it_.d()], [t3.d()])
                        self.tt("dve", bx[d][:, t0:t0 + n], t3[:, 0:n], ucb[:, t0:t0 + n], ALU.mult, [t3.d(), ucb.d()], [bx[d].d()])
                    b_ = bx[d]
                    if d == 0:
                        self.P.op("dve", (lambda o_, a_, b2: lambda e: e.tensor_tensor_scan(out=o_, data0=a_, data1=b2, initial=0.0, op0=ALU.mult, op1=ALU.add))(
                            b_[:, 0:T], a_t[:, 0:T], b_[:, 0:T]), [a_t.d(), b_.d()], [b_.d()])
                    else:
                        self.P.op("dve", (lambda o_, a_, b2: lambda e: e.tensor_tensor_scan(out=o_, data0=a_, data1=b2, initial=0.0, op0=ALU.mult, op1=ALU.add))(
                            rev(b_[:, 0:CTX]), rev(a_t[:, 0:CTX]), rev(b_[:, 0:CTX])), [a_t.d(), b_.d()], [b_.d()])
                        self.P.op("dve", (lambda o_, a_, b2, i_: lambda e: e.tensor_tensor_scan(out=o_, data0=a_, data1=b2, initial=i_, op0=ALU.mult, op1=ALU.add))(
                            rev(b_[:, CTX:T]), rev(a_t[:, CTX:T]), rev(b_[:, CTX:T]), b_[:, 0:1]), [a_t.d(), b_.d()], [b_.d()])
                wg_ = wgy[jj % 2]
                for (t0, n) in BLKS:
                    ps = self.psum()
                    for k in range(8):
                        self.mm(ps[:, 0:n], wg_[:, k, :], self.hx[:, k, t0:t0 + n], k == 0, k == 7, [wg_.d(), self.hx.d()], [ps.d()])
                    ge, sm = tmp[4], tmp[5]
                    self.act(ge[:, 0:n], ps[:, 0:n], AF.Gelu, [ps.d()], [ge.d()])
                    self.tt("dve", sm[:, 0:n], bx[0][:, t0:t0 + n], bx[1][:, t0:t0 + n], ALU.add, [bx[0].d(), bx[1].d()], [sm.d()])
                    self.tt("dve", mix[:, jj, t0:t0 + n], sm[:, 0:n], ge[:, 0:n], ALU.mult, [sm.d(), ge.d()], [mix.d()])
            self.P.barrier()

    def pe_T(self, out, in_, reads, writes):
        return self.P.op("pe", lambda e: e.transpose(out, in_, self.identb[:]), list(reads) + [self.cD], writes, pe_acc=True)

    def to_tokmajor_ap(self, src_fn, src_dep, dst):
        c = 0
        while c < 18:
            nq = min(4, 18 - c)
            ps = self.psum()
            pb = ps[:].bitcast(BF16)
            for q in range(nq):
                self.pe_T(pb[:, q * 128:(q + 1) * 128], src_fn(c + q), [src_dep], [ps.d()])
            self.cp("act", dst[:, c:c + nq, :], pb[:, 0:nq * 128].rearrange("p (q f) -> p q f", f=128), [ps.d()], [dst.d()])
            c += nq

    def ssd(self, l, s, mix):
        j = l // 2
        po = self.po
        tri = lambda d: self.cst[:, 128 + 128 * d:256 + 128 * d]
        onesf = self.cst[:, 384:512]
        bc = lambda ap, shape, ax: ap.unsqueeze(ax).to_broadcast(shape)
        with contextlib.ExitStack() as st:
            dt_tok = self.sb(st, "dt_tok", [128, 18, 32], F32)
            la_tok = self.sb(st, "la_tok", [128, 18, 32], F32)
            cumcol = self.sb(st, "cumcol", [128, 18, 32], F32)
            Bfm = self.sb(st, "Bfm", [128, T], BF16)
            Cfm = self.sb(st, "Cfm", [128, T], BF16)
            Hst = [self.sb(st, "Hst%d" % i, [128, 2, 64], F32) for i in range(2)]
            Hbf = [self.sb(st, "Hbf%d" % i, [128, 2, 64], BF16) for i in range(2)]
            NB = 2
            stA = contextlib.ExitStack()
            wdt = self.sb(stA, "swdt", [128, 8, 32], BF16)
            arow = self.sb(stA, "arow", [128, 32], F32)
            t9 = self.sb(stA, "t9", [128, 9, 32], F32)
            self.dma("pool", wdt[:], self.evdt_d[j], (), [wdt.d()])
            ao = po["alogrow%d" % j][0]
            bo = po["dtbrow%d" % j][0]
            self.act(arow[:], self.partile[:, ao:ao + 32], AF.Exp, [self.cD], [arow.d()])
            self.ts("dve", arow[:], arow[:], -1.0, None, ALU.mult, None, [arow.d()], [arow.d()])
            for half in range(2):
                ps = self.psum()
                for ci in range(9):
                    c = half * 9 + ci
                    for k in range(8):
                        self.mm(ps[:, ci * 32:(ci + 1) * 32], self.hx[:, k, c * 128:(c + 1) * 128], wdt[:, k, :], k == 0, k == 7,
                                [self.hx.d(), wdt.d()], [ps.d()])
                self.tt("dve", t9[:], ps[:, 0:288].rearrange("p (c h) -> p c h", h=32),
                        bc(self.partile[:, bo:bo + 32], [128, 9, 32], 1), ALU.add, [ps.d(), self.cD], [t9.d()])
                self.act(t9[:], t9[:], AF.Exp, [t9.d()], [t9.d()])
                self.act(dt_tok[:, half * 9:(half + 1) * 9, :], t9[:], AF.Ln, [t9.d()], [dt_tok.d()], bias=1.0)
                self.tt("dve", la_tok[:, half * 9:(half + 1) * 9, :], dt_tok[:, half * 9:(half + 1) * 9, :],
                        bc(arow[:], [128, 9, 32], 1), ALU.mult, [dt_tok.d(), arow.d()], [la_tok.d()])
            for half in range(2):
                ps = self.psum()
                for ci in range(9):
                    c = half * 9 + ci
                    for d in range(2):
                        self.mm(ps[:, ci * 32 + d * 16:ci * 32 + (d + 1) * 16], tri(d), la_tok[:, c, d * 16:(d + 1) * 16], True, True,
                                [la_tok.d(), self.cD], [ps.d()])
                self.cp("dve", cumcol[:, half * 9:(half + 1) * 9, :], ps[:, 0:288].rearrange("p (c h) -> p c h", h=32), [ps.d()], [cumcol.d()])
            self.P.barrier()
            stA.close()
            it = 0
            for g in range(2):
              with contextlib.ExitStack() as stB:
                wt = self.sb(stB, "swt", [128, 8, 128], BF16)
                pad = self.sb(stB, "spad", [128, 2312], BF16)
                dgt = self.sb(stB, "sdg", [128, 4, 128], BF16)
                self.memset("pool", pad[:], 0.0, [pad.d()])
                self.proj_conv(self.evin_d[j, 8 + 8 + g], po["scw%d" % j][0] + (8 + g) * 4, po["scb%d" % j][0] + 8 + g, wt, pad, dgt, Bfm, AF.Silu)
                self.proj_conv(self.evin_d[j, 8 + 10 + g], po["scw%d" % j][0] + (10 + g) * 4, po["scb%d" % j][0] + 10 + g, wt, pad, dgt, Cfm, AF.Silu)
                for i in range(4):
                    cx = 4 * g + i
                    self.proj_conv(self.evin_d[j, 8 + cx], po["scw%d" % j][0] + cx * 4, po["scb%d" % j][0] + cx, wt, pad, dgt, None, AF.Silu,
                                   dst_fn=(lambda cx_: lambda t0, n: mix[:, cx_, t0:t0 + n])(cx), dst_dep=mix.d())
                self.P.barrier()
              with contextlib.ExitStack() as stC:
                Btok = self.sb(stC, "Btok", [128, 18, 128], BF16)
                xstok = self.sb(stC, "xstok", [128, 18, 128], BF16)
                W = 6
                rhsla = [self.sb(stC, "rhsla%d" % i, [128, 2, 128], F32) for i in range(W)]
                E_ = [self.sb(stC, "E%d" % i, [128, 2, 128], BF16) for i in range(W)]
                Eb = [self.sb(stC, "Eb%d" % i, [128, 2, 128], BF16) for i in range(W)]
                cbm = [self.sb(stC, "cbm%d" % i, [128, 128], BF16) for i in range(W)]
                xdt = [self.sb(stC, "xdt%d" % i, [128, 2, 64], BF16) for i in range(W)]
                xdtw = [self.sb(stC, "xdtw%d" % i, [128, 2, 64], BF16) for i in range(W)]
                wv = [self.sb(stC, "wv%d" % i, [128, 2], F32) for i in range(W)]
                dtot = [self.sb(stC, "dtot%d" % i, [128, 2], F32) for i in range(W)]
                self.to_tokmajor_ap(lambda c: Bfm[:, c * 128:(c + 1) * 128], Bfm.d(), Btok)
                orders = [list(range(18)), [1, 0] + list(range(17, 1, -1))]
                for i in range(4):
                    cx = 4 * g + i
                    self.to_tokmajor_ap((lambda cx_: lambda c: mix[:, cx_, c * 128:(c + 1) * 128])(cx), mix.d(), xstok)
                    sdo = po["sd%d" % j][0] + cx
                    self.ts("dve", mix[:, cx, :], mix[:, cx, :], self.partile[:, sdo:sdo + 1], None, ALU.mult, None, [mix.d(), self.cD], [mix.d()])
                    for d in range(2):
                        self.memset("dve", Hst[d][:], 0.0, [Hst[d].d()])
                        self.memset("dve", Hbf[d][:], 0.0, [Hbf[d].d()])
                    iters = [(d, orders[d][step]) for step in range(18) for d in range(2)]
                    for w0 in range(0, len(iters), W):
                        win = list(enumerate(iters[w0:w0 + W]))
                        pst = {}
                        hdf = lambda d: d * 16 + 8 * g + 2 * i
                        for b, (d, c) in win:
                            hd = hdf(d)
                            self.tt("dve", rhsla[b][:], bc(tri(d), [128, 2, 128], 1), bc(la_tok[:, c, hd:hd + 2], [128, 2, 128], 2), ALU.mult,
                                    [la_tok.d(), self.cD], [rhsla[b].d()])
                        for b, (d, c) in win:
                            cs = slice(c * 128, (c + 1) * 128)
                            ps = self.psum()
                            pst[b] = ps
                            self.mm(ps[:, 0:128], Bfm[:, cs], Cfm[:, cs], True, True, [Bfm.d(), Cfm.d()], [ps.d()])
                            self.mm(ps[:, 128:384], onesf, rhsla[b][:].rearrange("p h l -> p (h l)"), True, True, [rhsla[b].d(), self.cD], [ps.d()])
                        cum3f = lambda b: pst[b][:, 128:384].rearrange("p (h l) -> p h l", l=128)
                        for b, (d, c) in win:
                            hd = hdf(d)
                            ps = pst[b]
                            ccb = bc(cumcol[:, c, hd:hd + 2], [128, 2, 128], 2)
                            self.tt("dve", cbm[b][:], ps[:, 0:128], tri(d), ALU.mult, [ps.d(), self.cD], [cbm[b].d()])
                            self.tt("dve", rhsla[b][:], cum3f(b), ccb, ALU.min, [ps.d(), cumcol.d()], [rhsla[b].d()])
                            self.tt("dve", rhsla[b][:], rhsla[b][:], ccb, ALU.subtract, [rhsla[b].d(), cumcol.d()], [rhsla[b].d()])
                        for b, (d, c) in win:
                            self.act(E_[b][:], rhsla[b][:], AF.Exp, [rhsla[b].d()], [E_[b].d()])
                            self.act(Eb[b][:], cum3f(b), AF.Exp, [pst[b].d()], [Eb[b].d()])
                        for b, (d, c) in win:
                            hd = hdf(d)
                            last = 127 if d == 0 else 0
                            cs = slice(c * 128, (c + 1) * 128)
                            self.tt("dve", E_[b][:], E_[b][:], bc(cbm[b][:], [128, 2, 128], 1), ALU.mult, [E_[b].d(), cbm[b].d()], [E_[b].d()])
                            self.tt("dve", Eb[b][:], Eb[b][:], bc(Cfm[:, cs], [128, 2, 128], 1), ALU.mult, [Eb[b].d(), Cfm.d()], [Eb[b].d()])
                            self.tt("dve", xdt[b][:], xstok[:, c, :].rearrange("p (h q) -> p h q", q=64),
                                    bc(dt_tok[:, c, hd:hd + 2], [128, 2, 64], 2), ALU.mult, [xstok.d(), dt_tok.d()], [xdt[b].d()])
                            self.tt("dve", wv[b][:], cum3f(b)[:, :, last], cumcol[:, c, hd:hd + 2], ALU.subtract, [pst[b].d(), cumcol.d()], [wv[b].d()])
                        for b, (d, c) in win:
                            last = 127 if d == 0 else 0
                            self.act(wv[b][:], wv[b][:], AF.Exp, [wv[b].d()], [wv[b].d()])
                            self.act(dtot[b][:], cum3f(b)[:, :, last], AF.Exp, [pst[b].d()], [dtot[b].d()])
                        for b, (d, c) in win:
                            self.tt("dve", xdtw[b][:], xdt[b][:], bc(wv[b][:], [128, 2, 64], 2), ALU.mult, [xdt[b].d(), wv[b].d()], [xdtw[b].d()])
                        for b, (d, c) in win:
                            cs = slice(c * 128, (c + 1) * 128)
                            ps2 = self.psum()
                            for hh in range(2):
                                self.mm(ps2[hh * 64:(hh + 1) * 64, 0:128], xdt[b][:, hh, :], E_[b][:, hh, :], True, False,
                                        [xdt[b].d(), E_[b].d()], [ps2.d()])
                                self.mm(ps2[hh * 64:(hh + 1) * 64, 0:128], Hbf[d][:, hh, :], Eb[b][:, hh, :], False, True,
                                        [Hbf[d].d(), Eb[b].d()], [ps2.d()])
                            self.mm(ps2[:, 128:256], Btok[:, c, :], xdtw[b][:].rearrange("p h q -> p (h q)"), True, True, [Btok.d(), xdtw[b].d()], [ps2.d()])
                            self.tt("dve", mix[:, cx, cs], mix[:, cx, cs], ps2[:, 0:128], ALU.add, [mix.d(), ps2.d()], [mix.d()])
                            self.tt("dve", Hst[d][:], Hst[d][:], bc(dtot[b][:], [128, 2, 64], 2), ALU.mult, [Hst[d].d(), dtot[b].d()], [Hst[d].d()])
                            self.tt("dve", Hst[d][:], Hst[d][:], ps2[:, 128:256].rearrange("p (h q) -> p h q", q=64), ALU.add, [Hst[d].d(), ps2.d()], [Hst[d].d()])
                            self.cp("act", Hbf[d][:], Hst[d][:], [Hst[d].d()], [Hbf[d].d()])
                self.P.barrier()
            self.dump(mix, 0, 8, 8)
            wt = self.sb(st, "swt2", [128, 8, 128], BF16)
            szt = [self.sb(st, "szt%d" % i, [128, 512], BF16) for i in range(2)]
            for ch in range(8):
                self.dma("pool", wt[:], self.evin_d[j, ch], (), [wt.d()])
                for bi, (t0, n) in enumerate(BLKS):
                    ps = self.psum()
                    for k in range(8):
                        self.mm(ps[:, 0:n], wt[:, k, :], self.hx[:, k, t0:t0 + n], k == 0, k == 7, [wt.d(), self.hx.d()], [ps.d()])
                    z_ = szt[bi % 2]
                    self.act(z_[:, 0:n], ps[:, 0:n], AF.Silu, [ps.d()], [z_.d()])
                    self.tt("dve", mix[:, ch, t0:t0 + n], mix[:, ch, t0:t0 + n], z_[:, 0:n], ALU.mult, [mix.d(), z_.d()], [mix.d()])
            self.P.barrier()
        with contextlib.ExitStack() as st:
            go = po["sng%d" % j][0]
            self.rms_mod(st, mix, mix, lambda k, r: self.partile[:, go + k:go + k + 1], None, s)
            self.P.barrier()

    def prologue_lb(self):
        with contextlib.ExitStack() as st:
            lo = self.po["lbl"][0]
            L = self.partile[:, lo:lo + 24].rearrange("p (l h) -> p l h", h=6)
            mx = self.sb(st, "lbmx", [128, 6], F32)
            e = self.sb(st, "lbe", [128, 4, 6], F32)
            sm = self.sb(st, "lbs", [128, 6], F32)
            D = [self.cD]
            self.tt("dve", mx[:], L[:, 0, :], L[:, 1, :], ALU.max, D, D)
            self.tt("dve", mx[:], mx[:], L[:, 2, :], ALU.max, D, D)
            self.tt("dve", mx[:], mx[:], L[:, 3, :], ALU.max, D, D)
            self.tt("dve", e[:], L, mx[:].unsqueeze(1).to_broadcast([128, 4, 6]), ALU.subtract, D, D)
            self.act(e[:], e[:], AF.Exp, D, D)
            self.tt("dve", sm[:], e[:, 0, :], e[:, 1, :], ALU.add, D, D)
            self.tt("dve", sm[:], sm[:], e[:, 2, :], ALU.add, D, D)
            self.tt("dve", sm[:], sm[:], e[:, 3, :], ALU.add, D, D)
            self.P.op("dve", lambda en: en.reciprocal(out=sm[:], in_=sm[:]), D, D)
            self.tt("dve", e[:], e[:], sm[:].unsqueeze(1).to_broadcast([128, 4, 6]), ALU.mult, D, D)
            self.memset("dve", self.lbt[:, 0, :], 0.0, D)
            for l in range(1, 4):
                self.tt("dve", self.lbt[:, l, :], self.lbt[:, l - 1, :], e[:, l, :], ALU.add, D, D)
            self.P.barrier()

    def odd_mixer(self, l, s):
        j = l // 2
        NR = self.NR
        A = lambda k, r: self.amix[:, (l * 8 + k) * NR + r:(l * 8 + k) * NR + r + 1]
        B = lambda k, r: self.mods[:, (l * 48 + k) * NR + r:(l * 48 + k) * NR + r + 1]
        with contextlib.ExitStack() as st0:
            with contextlib.ExitStack() as st:
                self.rms_mod(st, self.x, self.hx, A, B, s)
            self.P.barrier()
            mix = self.sb(st0, "mixo", [128, 8, T], BF16)
            which = getattr(self, "odd_parts", ("hgrn", "s5"))
            if "hgrn" in which:
                self.hgrn(l, s, mix)
            else:
                self.memset("pool", mix[:, 0:6, :], 0.0, [mix.d()])
            if "s5" in which:
                self.s5(l, s, mix)
            else:
                self.memset("pool", mix[:, 6:8, :], 0.0, [mix.d()])
            self.dump(mix, 0, 8, 0)
            self.outproj(lambda oc: self.odout_d[j, oc], 8, mix, l, s, 16)

    def scan(self, out, d0, d1, init, reads, writes):
        return self.P.op("dve", lambda e: e.tensor_tensor_scan(out=out, data0=d0, data1=d1, initial=init, op0=ALU.mult, op1=ALU.add),
                         reads, writes)

    def hgrn(self, l, s, mix):
        j = l // 2
        po = self.po
        hgm = lambda d: self.cst[:, 512 + 128 * d:640 + 128 * d]
        with contextlib.ExitStack() as st:
            lbm = self.sb(st, "lbm", [128, 6, 2], F32)
            rmask_t = self.sb(st, "rmask", [128, T + 32], BF16)
            self.rmask = rmask_t
            self.memset("pool", rmask_t[:], 1.0, [self.cD])
            self.memset("pool", rmask_t[:].rearrange("p (c i) -> p c i", i=32)[:, :, 0:1], 0.0, [self.cD])
            wq = self.sb(st, "hwq", [128, 8, 128], BF16)
            wf = [self.sb(st, "hwf%d" % i, [128, 8, 128], BF16) for i in range(2)]
            wv = self.sb(st, "hwv", [128, 8, 128], BF16)
            vtok = self.sb(st, "vtok", [128, 18, 128], BF16)
            qt = [self.sb(st, "qt%d" % d, [128, T], BF16) for d in range(2)]
            kt = [self.sb(st, "kt%d" % d, [128, T], BF16) for d in range(2)]
            elast = [self.sb(st, "elast%d" % d, [128, 72], F32) for d in range(2)]
            eprev = [self.sb(st, "eprev%d" % d, [128, 72], F32) for d in range(2)]
            R = [self.sb(st, "R%d" % d, [128, 128], F32) for d in range(2)]
            Rb = [self.sb(st, "Rb%d" % d, [128, 128], BF16) for d in range(2)]
            qsl = self.sb(st, "qsl", [128, 512], F32)
            tA = [self.sb(st, "htA", [128, 512], F32)] * 2
            tB = [self.sb(st, "htB", [128, 512], F32)] * 2
            tC = [self.sb(st, "htC", [128, 512], F32)] * 2
            ktok = [self.sb(st, "ktok%d" % i, [128, 128], BF16) for i in range(2)]
            ktok2 = [self.sb(st, "ktokm%d" % i, [128, 128], BF16) for i in range(2)]
            attm = [self.sb(st, "attm%d" % i, [128, 128], BF16) for i in range(2)]
            self.ts("dve", lbm[:, :, 0], self.lbt[:, l, :], -1.0, 1.0, ALU.mult, ALU.add, [self.cD], [lbm.d()])
            self.ts("dve", lbm[:, :, 1], lbm[:, :, 0], -1.0, None, ALU.mult, None, [lbm.d()], [lbm.d()])
            self.memset("pool", mix[:, 0:6, :], 0.0, [mix.d()])
            it = 0
            for hh in range(6):
                oml = lbm[:, hh, 0:1]
                noml = lbm[:, hh, 1:2]
                lb = self.lbt[:, l, hh:hh + 1]
                self.dma("pool", wq[:], self.odin_d[j, hh], (), [wq.d()])
                self.dma("pool", wf[0][:], self.odin_d[j, 6 + hh], (), [wf[0].d()])
                self.dma("pool", wf[1][:], self.odin_d[j, 12 + hh], (), [wf[1].d()])
                self.dma("pool", wv[:], self.odv_d[j, :, :, hh * 128:(hh + 1) * 128], (), [wv.d()])
                c = 0
                while c < 18:
                    nq = min(4, 18 - c)
                    ps = self.psum()
                    for q in range(nq):
                        for k in range(8):
                            self.mm(ps[:, q * 128:(q + 1) * 128], self.hx[:, k, (c + q) * 128:(c + q + 1) * 128], wv[:, k, :], k == 0, k == 7,
                                    [self.hx.d(), wv.d()], [ps.d()])
                    self.cp("act", vtok[:, c:c + nq, :], ps[:, 0:nq * 128].rearrange("p (q f) -> p q f", f=128), [ps.d()], [vtok.d()])
                    c += nq
                for bi, (t0, n) in enumerate(BLKS):
                    ps = self.psum()
                    for k in range(8):
                        self.mm(ps[:, 0:n], wq[:, k, :], self.hx[:, k, t0:t0 + n], k == 0, k == 7, [wq.d(), self.hx.d()], [ps.d()])
                    self.act(qsl[:, 0:n], ps[:, 0:n], AF.Silu, [ps.d()], [qsl.d()])
                    for d in range(2):
                        a_, b_, c_ = tA[d], tB[d], tC[d]
                        ps = self.psum()
                        for k in range(8):
                            self.mm(ps[:, 0:n], wf[d][:, k, :], self.hx[:, k, t0:t0 + n], k == 0, k == 7, [wf[d].d(), self.hx.d()], [ps.d()])
                        self.act(a_[:, 0:n], ps[:, 0:n], AF.Sigmoid, [ps.d()], [a_.d()])
                        self.act(b_[:, 0:n], a_[:, 0:n], AF.Ln, [a_.d(), lbm.d(), self.cD], [b_.d()], scale=oml, bias=lb)
                        self.ts("dve", a_[:, 0:n], a_[:, 0:n], noml, oml, ALU.mult, ALU.add, [a_.d(), lbm.d()], [a_.d()])
                        if d == 0:
                            self.scan(c_[:, 0:n], self.rmask[:, t0:t0 + n], b_[:, 0:n], 0.0, [b_.d(), self.cD], [c_.d()])
                        else:
                            self.scan(rev(c_[:, 0:n]), rev(self.rmask[:, t0 + 1:t0 + n + 1]), rev(b_[:, 0:n]), 0.0, [b_.d(), self.cD], [c_.d()])
                        self.act(b_[:, 0:n], c_[:, 0:n], AF.Exp, [c_.d()], [b_.d()])
                        lastpos = 31 if d == 0 else 0
                        self.cp("dve", elast[d][:, t0 // 32:(t0 + n) // 32], b_[:, 0:n].rearrange("p (c i) -> p c i", i=32)[:, :, lastpos],
                                [b_.d()], [elast[d].d()])
                        self.tt("dve", qt[d][:, t0:t0 + n], qsl[:, 0:n], b_[:, 0:n], ALU.mult, [qsl.d(), b_.d()], [qt[d].d()])
                        self.act(c_[:, 0:n], c_[:, 0:n], AF.Exp, [c_.d()], [c_.d()], scale=-1.0)
                        self.tt("dve", kt[d][:, t0:t0 + n], a_[:, 0:n], c_[:, 0:n], ALU.mult, [a_.d(), c_.d()], [kt[d].d()])
                for d in range(2):
                    self.memset("dve", eprev[d][:], 1.0, [eprev[d].d()])
                    if d == 0:
                        self.cp("dve", eprev[d][:, 1:72], elast[d][:, 0:71], [elast[d].d()], [eprev[d].d()])
                    else:
                        self.cp("dve", eprev[d][:, 0:7], elast[d][:, 1:8], [elast[d].d()], [eprev[d].d()])
                        self.cp("dve", eprev[d][:, 8:71], elast[d][:, 9:72], [elast[d].d()], [eprev[d].d()])
                        self.cp("dve", eprev[d][:, 71:72], elast[d][:, 0:1], [elast[d].d()], [eprev[d].d()])
                    self.memset("pool", R[d][:], 0.0, [R[d].d()])
                    self.memset("pool", Rb[d][:], 0.0, [Rb[d].d()])
                blocks = [list(range(18)), [1, 0] + list(range(17, 1, -1))]
                for step in range(18):
                    for d in range(2):
                        bk = blocks[d][step]
                        b = it % 2
                        it += 1
                        ts_ = slice(bk * 128, (bk + 1) * 128)
                        ps_a = self.psum()
                        self.mm(ps_a[:, 0:128], kt[d][:, ts_], qt[d][:, ts_], True, True, [kt[d].d(), qt[d].d()], [ps_a.d()])
                        self.tt("dve", attm[b][:], ps_a[:, 0:128], hgm(d), ALU.mult, [ps_a.d(), self.cD], [attm[b].d()])
                        ps_t = self.psum()
                        pb = ps_t[:].bitcast(BF16)
                        self.pe_T(pb[:, 0:128], kt[d][:, ts_], [kt[d].d()], [ps_t.d()])
                        self.cp("act", ktok[b][:], pb[:, 0:128], [ps_t.d()], [ktok[b].d()])
                        self.cp("act", ktok2[b][64:128, :], pb[64:128, 0:128], [ps_t.d()], [ktok2[b].d()])
                        self.memset("pool", ktok2[b][64:96, :], 0.0, [ktok2[b].d()])
                        ps_o = self.psum()
                        self.mm(ps_o[:, 0:128], vtok[:, bk, :], attm[b][:], True, False, [vtok.d(), attm[b].d()], [ps_o.d()])
                        corder = [0, 1, 2, 3] if d == 0 else [3, 2, 1, 0]
                        for ci, cc in enumerate(corder):
                            cg = bk * 4 + cc
                            cs_ = slice(cg * 32, (cg + 1) * 32)
                            self.mm(ps_o[:, cc * 32:(cc + 1) * 32], Rb[d][:], qt[d][:, cs_], False, ci == 3, [Rb[d].d(), qt[d].d()], [ps_o.d()])
                            ps_u = self.psum()
                            if cc < 3:
                                self.mm(ps_u[:, 0:128], ktok[b][cc * 32:(cc + 1) * 32, :], vtok[cc * 32:(cc + 1) * 32, bk, :], True, True,
                                        [ktok[b].d(), vtok.d()], [ps_u.d()])
                            else:
                                self.mm(ps_u[:, 0:128], ktok2[b][64:128, :], vtok[64:128, bk, :], True, True,
                                        [ktok2[b].d(), vtok.d()], [ps_u.d()])
                            self.stt("dve", R[d][:], R[d][:], eprev[d][:, cg:cg + 1], ps_u[:, 0:128], ALU.mult, ALU.add,
                                     [R[d].d(), eprev[d].d(), ps_u.d()], [R[d].d()])
                            self.act(Rb[d][:], R[d][:], AF.Copy, [R[d].d(), elast[d].d()], [Rb[d].d()], scale=elast[d][:, cg:cg + 1])
                        self.tt("dve", mix[:, hh, ts_], mix[:, hh, ts_], ps_o[:, 0:128], ALU.add, [mix.d(), ps_o.d()], [mix.d()])
            self.P.barrier()
        with contextlib.ExitStack() as st:
            go = po["hgn%d" % j][0]
            self.rms_mod(st, mix, mix, lambda k, r: self.partile[:, go + k:go + k + 1], None, s, nch=6, pergroup=True)
            self.P.barrier()
        with contextlib.ExitStack() as st:
            wg_ = [self.sb(st, "hwg%d" % i, [128, 8, 128], BF16) for i in range(2)]
            sz = [self.sb(st, "hsz%d" % i, [128, 512], BF16) for i in range(2)]
            for hh in range(6):
                w_ = wg_[hh % 2]
                self.dma("pool", w_[:], self.odin_d[j, 18 + hh], (), [w_.d()])
                for bi, (t0, n) in enumerate(BLKS):
                    ps = self.psum()
                    for k in range(8):
                        self.mm(ps[:, 0:n], w_[:, k, :], self.hx[:, k, t0:t0 + n], k == 0, k == 7, [w_.d(), self.hx.d()], [ps.d()])
                    z_ = sz[bi % 2]
                    self.act(z_[:, 0:n], ps[:, 0:n], AF.Silu, [ps.d()], [z_.d()])
                    self.tt("dve", mix[:, hh, t0:t0 + n], mix[:, hh, t0:t0 + n], z_[:, 0:n], ALU.mult, [mix.d(), z_.d()], [mix.d()])
            self.P.barrier()

    def cexp_small(self, st, name, lr, li, ls, shape, D):
        mk = lambda n: self.sb(st, name + n, shape, F32)
        step, c, sn, mag, t1, t2 = mk("st"), mk("c"), mk("s"), mk("m"), mk("t1"), mk("t2")
        dd = [step.d()]
        self.act(step[:], ls, AF.Exp, D, dd)
        self.tt("dve", mag[:], lr, step[:], ALU.mult, D + dd, dd)
        self.act(mag[:], mag[:], AF.Exp, dd, dd)
        self.tt("dve", t1[:], li, step[:], ALU.mult, D + dd, dd)
        self.act(sn[:], t1[:], AF.Sin, dd, dd, scale=1.0 / 16.0)
        self.ts("dve", t2[:], t1[:], 1.0 / 16.0, 1.5707963267948966, ALU.mult, ALU.add, dd, dd)
        self.act(c[:], t2[:], AF.Sin, dd, dd)
        for _ in range(4):
            self.tt("dve", t1[:], c[:], c[:], ALU.mult, dd, dd)
            self.tt("dve", t2[:], sn[:], sn[:], ALU.mult, dd, dd)
            self.tt("dve", sn[:], sn[:], c[:], ALU.mult, dd, dd)
            self.ts("dve", sn[:], sn[:], 2.0, None, ALU.mult, None, dd, dd)
            self.tt("dve", c[:], t1[:], t2[:], ALU.subtract, dd, dd)
        for t_ in (c, sn, mag, t1, t2):
            t_.deps[None] = step.d()
        return c, sn, mag, step, t1, t2

    def s5(self, l, s, mix):
        j = l // 2
        po = self.po
        L = 256
        NTC = T // L
        D = [self.cD]
        with contextlib.ExitStack() as st:
            ufm = self.sb(st, "ufm", [128, 2, T], BF16)
            wu = self.sb(st, "s5wu", [128, 8, 128], BF16)
            for c in range(2):
                self.dma("pool", wu[:], self.odin_d[j, 24 + c], (), [wu.d()])
                for (t0, n) in BLKS:
                    ps = self.psum()
                    for k in range(8):
                        self.mm(ps[:, 0:n], wu[:, k, :], self.hx[:, k, t0:t0 + n], k == 0, k == 7, [wu.d(), self.hx.d()], [ps.d()])
                    self.cp("act", ufm[:, c, t0:t0 + n], ps[:, 0:n], [ps.d()], [ufm.d()])
            so = po["s5p%d" % j][0]
            pc, psn, pmag, _, _, _ = self.cexp_small(st, "sp", self.partile[:, so:so + 16], self.partile[:, so + 16:so + 32],
                                                    self.partile[:, so + 32:so + 48], [128, 16], D)
            pdep = [pc.d()]
            E = self.sb(st, "s5E", [128, 4, 2], F32)
            for d in range(2):
                for c in range(2):
                    with contextlib.ExitStack() as st2:
                        tabc = self.sb(st2, "tabc", [128, 4, L], F32)
                        tabs = self.sb(st2, "tabs", [128, 4, L], F32)
                        BM = self.sb(st2, "BM", [128, 4, 2, 128], BF16)
                        CM = self.sb(st2, "CM", [128, 4, 2, 128], BF16)
                        tdep = [tabc.d()]
                        tmp1 = self.sb(st2, "s5tm", [128, 128], F32)
                        for q4 in range(4):
                            q = c * 4 + q4
                            col = d * 8 + q
                            self.cp("dve", tabc[:, q4, 0:1], pc[:, col:col + 1], pdep, tdep)
                            self.cp("dve", tabs[:, q4, 0:1], psn[:, col:col + 1], pdep, tdep)
                            span = 1
                            while span < L:
                                cm_ = tabc[:, q4, span - 1:span]
                                sm_ = tabs[:, q4, span - 1:span]
                                lo, hi = slice(0, span), slice(span, 2 * span)
                                self.ts("dve", tmp1[:, 0:span], tabs[:, q4, lo], sm_, None, ALU.mult, None, tdep, [tmp1.d()])
                                self.stt("dve", tabc[:, q4, hi], tabc[:, q4, lo], cm_, tmp1[:, 0:span], ALU.mult, ALU.subtract, tdep + [tmp1.d()], tdep)
                                self.ts("dve", tmp1[:, 0:span], tabs[:, q4, lo], cm_, None, ALU.mult, None, tdep, [tmp1.d()])
                                self.stt("dve", tabs[:, q4, hi], tabc[:, q4, lo], sm_, tmp1[:, 0:span], ALU.mult, ALU.add, tdep + [tmp1.d()], tdep)
                                span *= 2
                            with contextlib.ExitStack() as st3:
                                rows = self.sb(st3, "s5rows", [128, 3, 128], F32)
                                bpad = self.sb(st3, "s5bp", [128, 2, 128], F32)
                                self.dma("pool", rows[:], self.s5row_d[j, d, :, q, :].unsqueeze(0).to_broadcast([128, 3, 128]), (), [rows.d()])
                                self.dma("act", bpad[:], self.s5b_d[j, q].rearrange("r p c -> p r c"), (), [bpad.d()])
                                self.dma("pool", CM[:, q4, :, :], self.s5c_d[j, d, q].rearrange("r p c -> p r c"), (), [CM.d()])
                                rd = [rows.d()]
                                rc, rs_, rmag, rstep, t1, t2 = self.cexp_small(st3, "sr", rows[:, 0, :], rows[:, 1, :], rows[:, 2, :], [128, 128], rd)
                                w = [rc.d()]
                                lr_, li_ = rows[:, 0, :], rows[:, 1, :]
                                ar, ai, den, zr, zi = rc, rs_, rstep, t1, t2
                                self.tt("dve", ar[:], rc[:], rmag[:], ALU.mult, w, w)
                                self.tt("dve", ai[:], rs_[:], rmag[:], ALU.mult, w, w)
                                self.tt("dve", den[:], lr_, lr_, ALU.mult, rd + w, w)
                                self.tt("dve", rmag[:], li_, li_, ALU.mult, rd + w, w)
                                self.tt("dve", den[:], den[:], rmag[:], ALU.add, w, w)
                                self.P.op("dve", (lambda t_: lambda e: e.reciprocal(out=t_[:], in_=t_[:]))(den), w, w)
                                self.ts("dve", ar[:], ar[:], -1.0, None, ALU.add, None, w, w)
                                self.tt("dve", zr[:], ar[:], lr_, ALU.mult, rd + w, w)
                                self.tt("dve", rmag[:], ai[:], li_, ALU.mult, rd + w, w)
                                self.tt("dve", zr[:], zr[:], rmag[:], ALU.add, w, w)
                                self.tt("dve", zr[:], zr[:], den[:], ALU.mult, w, w)
                                self.tt("dve", zi[:], ai[:], lr_, ALU.mult, rd + w, w)
                                self.tt("dve", rmag[:], ar[:], li_, ALU.mult, rd + w, w)
                                self.tt("dve", zi[:], zi[:], rmag[:], ALU.subtract, w, w)
                                self.tt("dve", zi[:], zi[:], den[:], ALU.mult, w, w)
                                bd = [bpad.d()]
                                self.tt("dve", ar[:], zr[:], bpad[:, 0, :], ALU.mult, w + bd, w)
                                self.tt("dve", ai[:], zi[:], bpad[:, 1, :], ALU.mult, w + bd, w)
                                self.tt("dve", BM[:, q4, 0, :], ar[:], ai[:], ALU.subtract, w, [BM.d()])
                                self.tt("dve", ar[:], zr[:], bpad[:, 1, :], ALU.mult, w + bd, w)
                                self.tt("dve", ai[:], zi[:], bpad[:, 0, :], ALU.mult, w + bd, w)
                                self.tt("dve", BM[:, q4, 1, :], ar[:], ai[:], ALU.add, w, [BM.d()])
                                self.ts("dve", CM[:, q4, 1, :], CM[:, q4, 1, :], -1.0, None, ALU.mult, None, [CM.d()], [CM.d()])
                                self.P.barrier()
                        self.memset("dve", E[:], 0.0, [E.d()])
                        h32 = [self.sb(st2, "s5h%d" % i, [128, 2, L], F32) for i in range(2)]
                        ta = [self.sb(st2, "s5ta%d" % i, [128, 4, L], F32) for i in range(2)]
                        hb = [self.sb(st2, "s5hb%d" % i, [128, 4, 2, L], BF16) for i in range(2)]
                        order = list(range(NTC)) if d == 0 else [0] + list(range(NTC - 1, 0, -1))
                        fl = (lambda a: a) if d == 0 else rev
                        lastpos = L - 1 if d == 0 else 0
                        it = 0
                        for ti, tc in enumerate(order):
                            t0 = tc * L
                            hbt = hb[ti % 2]
                            for q4 in range(4):
                                q = c * 4 + q4
                                col = d * 8 + q
                                b = it % 2
                                it += 1
                                ps_x = self.psum()
                                self.mm(ps_x[:, 0:L], BM[:, q4, 0, :], ufm[:, c, t0:t0 + L], True, True, [BM.d(), ufm.d()], [ps_x.d()])
                                self.mm(ps_x[:, L:2 * L], BM[:, q4, 1, :], ufm[:, c, t0:t0 + L], True, True, [BM.d(), ufm.d()], [ps_x.d()])
                                xr, xi = fl(ps_x[:, 0:L]), fl(ps_x[:, L:2 * L])
                                tcq, tsq = tabc[:, q4, :], tabs[:, q4, :]
                                a = ta[b]
                                hh_ = h32[b]
                                self.tt("dve", a[:, 0, :], tcq, xr, ALU.mult, tdep + [ps_x.d()], [a.d()])
                                self.tt("dve", a[:, 1, :], tsq, xi, ALU.mult, tdep + [ps_x.d()], [a.d()])
                                self.tt("dve", a[:, 2, :], tcq, xi, ALU.mult, tdep + [ps_x.d()], [a.d()])
                                self.tt("dve", a[:, 3, :], tsq, xr, ALU.mult, tdep + [ps_x.d()], [a.d()])
                                ad = [a.d()]
                                self.tt("dve", a[:, 0, :], a[:, 0, :], a[:, 1, :], ALU.add, ad, ad)
                                self.tt("dve", a[:, 2, :], a[:, 2, :], a[:, 3, :], ALU.subtract, ad, ad)
                                mg = pmag[:, col:col + 1].to_broadcast([128, L])
                                self.scan(a[:, 0, :], mg, a[:, 0, :], E[:, q4, 0:1], ad + [E.d()] + pdep, ad)
                                self.scan(a[:, 2, :], mg, a[:, 2, :], E[:, q4, 1:2], ad + [E.d()] + pdep, ad)
                                self.tt("dve", a[:, 1, :], tcq, a[:, 0, :], ALU.mult, tdep + ad, ad)
                                self.tt("dve", a[:, 3, :], tsq, a[:, 2, :], ALU.mult, tdep + ad, ad)
                                self.tt("dve", fl(hh_[:, 0, :]), a[:, 1, :], a[:, 3, :], ALU.subtract, ad, [hh_.d()])
                                self.tt("dve", a[:, 1, :], tsq, a[:, 0, :], ALU.mult, tdep + ad, ad)
                                self.tt("dve", a[:, 3, :], tcq, a[:, 2, :], ALU.mult, tdep + ad, ad)
                                self.tt("dve", fl(hh_[:, 1, :]), a[:, 1, :], a[:, 3, :], ALU.add, ad, [hh_.d()])
                                self.cp("dve", E[:, q4, :], hh_[:, :, lastpos], [hh_.d()], [E.d()])
                                self.cp("act", hbt[:, q4, :, :], hh_[:, :, :], [hh_.d()], [hbt.d()])
                            ps_y = self.psum()
                            for q4 in range(4):
                                for ri in range(2):
                                    self.mm(ps_y[:, 0:L], CM[:, q4, ri, :], hbt[:, q4, ri, :], q4 == 0 and ri == 0, q4 == 3 and ri == 1,
                                            [CM.d(), hbt.d()], [ps_y.d()])
                            if d == 0:
                                self.cp("act", mix[:, 6 + c, t0:t0 + L], ps_y[:, 0:L], [ps_y.d()], [mix.d()])
                            else:
                                self.tt("dve", mix[:, 6 + c, t0:t0 + L], mix[:, 6 + c, t0:t0 + L], ps_y[:, 0:L], ALU.add, [mix.d(), ps_y.d()], [mix.d()])
                        self.P.barrier()
            do = po["s5d%d" % j][0]
            gb = po["glub%d" % j][0]
            wgl = self.sb(st, "wglu", [128, 2, 256], BF16)
            yt = [self.sb(st, "s5yt%d" % i, [128, 512], F32) for i in range(2)]
            self.dma("pool", wgl[:], self.gluw_d[j], (), [wgl.d()])
            for c in range(2):
                for bi, (t0, n) in enumerate(BLKS):
                    y_ = yt[bi % 2]
                    self.stt("dve", y_[:, 0:n], ufm[:, c, t0:t0 + n], self.partile[:, do + c:do + c + 1], mix[:, 6 + c, t0:t0 + n], ALU.mult, ALU.add,
                             [ufm.d(), mix.d(), self.cD], [y_.d()])
                    self.act(ufm[:, c, t0:t0 + n], y_[:, 0:n], AF.Gelu, [y_.d()], [ufm.d()])
            for c in range(2):
                for bi, (t0, n) in enumerate(BLKS):
                    ps = self.psum()
                    for k in range(2):
                        self.mm(ps[:, 0:n], wgl[:, k, c * 128:(c + 1) * 128], ufm[:, k, t0:t0 + n], k == 0, k == 1, [wgl.d(), ufm.d()], [ps.d()])
                    y_ = yt[bi % 2]
                    self.act(y_[:, 0:n], ps[:, 0:n], AF.Sigmoid, [ps.d(), self.cD], [y_.d()], bias=self.partile[:, gb + c:gb + c + 1])
                    self.tt("dve", mix[:, 6 + c, t0:t0 + n], ufm[:, c, t0:t0 + n], y_[:, 0:n], ALU.mult, [ufm.d(), y_.d()], [mix.d()])
            self.P.barrier()

    def final_out(self, s):
        NR = self.NR
        if self.final:
            go, _ = self.po["fng"]
            A = lambda k, r: self.partile[:, go + k:go + k + 1]
            with contextlib.ExitStack() as st:
                self.rms_mod(st, self.x, self.x, A, None, s)
                self.out_ops.append(self.dma("sp", self.yout[s, :, 0:4, :], self.x[:, 0:4, CTX:T], [self.x.d()], ()))
                self.out_ops.append(self.dma("act", self.yout[s, :, 4:8, :], self.x[:, 4:8, CTX:T], [self.x.d()], ()))
                self.P.barrier()
        else:
            self.out_ops.append(self.dma("sp", self.yout[s, :, 0:4, :], self.x[:, 0:4, CTX:T], [self.x.d()], ()))
            self.out_ops.append(self.dma("act", self.yout[s, :, 4:8, :], self.x[:, 4:8, CTX:T], [self.x.d()], ()))


def make_consts():
    c = np.zeros((128, 768), np.float32)
    c[:, 0:128] = np.eye(128, dtype=np.float32)
    i = np.arange(128)
    c[:, 128:256] = (i[:, None] <= i[None, :]).astype(np.float32)
    c[:, 256:384] = (i[:, None] >= i[None, :]).astype(np.float32)
    c[:, 384:512] = 1.0
    same = (i[:, None] // 32) == (i[None, :] // 32)
    c[:, 512:640] = (same & (i[:, None] <= i[None, :])).astype(np.float32)
    c[:, 640:768] = (same & (i[:, None] >= i[None, :])).astype(np.float32)
    return c


def kernel(**inputs):
    nseq = 4
    inp = {k: np.asarray(v) for k, v in inputs.items()}
    phases = []
    for l in range(4):
        phases += [("mix", l), ("ffn", l)]
    kb = K(nseq, phases, final=True)
    nc = kb.build()
    sh = host_prep(inp)
    sh["cst"] = make_consts()
    in_maps = []
    for c in range(NCORES):
        m = dict(sh)
        m.update(core_inputs(inp, c, nseq))
        in_maps.append(m)
    res = run_bass_kernel_spmd(nc, in_maps, core_ids=list(range(NCORES)))
    outs = []
    for c in range(NCORES):
        y = res.results[c]["yout"]
        outs.append(y.transpose(0, 3, 2, 1).reshape(nseq, 2048, 1024))
    return np.ascontiguousarray(np.concatenate(outs, axis=0)).astype(np.float32)
```

```python
import contextlib
import numpy as np
import concourse.bass as bass
import concourse.mybir as mybir
from concourse.ap import AP
from concourse.bass_utils import run_bass_kernel_spmd

F32 = mybir.dt.float32
BF16 = mybir.dt.bfloat16
AF = mybir.ActivationFunctionType
ALU = mybir.AluOpType

T = 2304
CTX = 256
BLKS = [(0, 256), (256, 512), (768, 512), (1280, 512), (1792, 512)]
NCORES = 8
EPS = 1e-6


class Dep:
    __slots__ = ("w", "r", "rd", "const")

    def __init__(self, const=False):
        self.w = None
        self.r = {}
        self.rd = []
        self.const = const


class Op:
    __slots__ = ("eng", "fn", "deps", "marked", "ev", "dma", "idx", "bar", "epoch")


class Prog:
    DMA_SEMS = {"sp": 6, "act": 6, "pool": 24}
    ENGS = ("pe", "act", "dve", "pool", "sp")

    def __init__(self, nc):
        self.nc = nc
        self.ops = []
        self.last = {}
        self.pending_dma = []
        self.nbar = 0

    def _new(self, eng, fn, dma):
        o = Op()
        o.eng = eng
        o.fn = fn
        o.dma = dma
        o.marked = dma
        o.ev = None
        o.bar = 0
        o.epoch = -1
        o.idx = len(self.ops)
        self.ops.append(o)
        return o

    def op(self, eng, fn, reads=(), writes=(), dma=False, pe_acc=False):
        deps = set()
        for d in reads:
            if d.w is not None:
                deps.add(d.w)
        for d in writes:
            if d.w is not None:
                if not (pe_acc and d.w.eng == "pe" and not d.w.dma):
                    deps.add(d.w)
            for r in d.r.values():
                deps.add(r)
            for r in d.rd:
                deps.add(r)
        o = self._new(eng, fn, dma)
        o.deps = deps
        for d in reads:
            if not d.const:
                if dma:
                    d.rd.append(o)
                else:
                    d.r[eng] = o
        for d in writes:
            d.w = o
            d.r = {}
            d.rd = []
        if dma:
            self.pending_dma.append(o)
        else:
            self.last[eng] = o
        return o

    def barrier(self):
        deps = set(self.last.values()) | set(self.pending_dma)
        self.pending_dma = []
        self.last = {}
        self.nbar += 1
        for e in self.ENGS:
            o = self._new(e, None, False)
            o.deps = set(deps)
            o.bar = self.nbar

    def emit(self, final_deps):
        nc = self.nc
        engs = {"pe": nc.tensor, "act": nc.scalar, "dve": nc.vector, "pool": nc.gpsimd, "sp": nc.sync}
        fin = self._new("sp", None, False)
        fin.deps = set(final_deps)
        for o in self.ops:
            for d in o.deps:
                d.marked = True
        cnt = {e: 0 for e in engs}
        dma_rr = {e: 0 for e in engs}
        dma_cnt = {}
        seen = {e: {} for e in engs}
        per_eng = {e: [] for e in engs}
        epoch = 0
        maxv = 0
        nb_in_group = 0
        for o in self.ops:
            mw = {}
            o.epoch = epoch
            if o.dma:
                k = dma_rr[o.eng] % self.DMA_SEMS[o.eng]
                dma_rr[o.eng] += 1
                sk_own = ("dma", o.eng, k)
                prev = dma_cnt.get(sk_own, 0)
                if prev > 0 and seen[o.eng].get(sk_own, 0) < prev:
                    mw[sk_own] = prev
                    seen[o.eng][sk_own] = prev
                dma_cnt[sk_own] = prev + 16
                o.ev = (sk_own, prev + 16)
                maxv = max(maxv, prev + 16)
            elif o.marked:
                cnt[o.eng] += 1
                o.ev = (("eng", o.eng), cnt[o.eng])
                maxv = max(maxv, cnt[o.eng])
            for d in o.deps:
                if d.epoch != epoch:
                    continue
                sk, v = d.ev
                if seen[o.eng].get(sk, 0) < v:
                    seen[o.eng][sk] = v
                    mw[sk] = max(mw.get(sk, 0), v)
            per_eng[o.eng].append((o, list(mw.items())))
            if o.bar:
                nb_in_group += 1
                if nb_in_group == len(self.ENGS):
                    nb_in_group = 0
                    epoch += 1
                    cnt = {e: 0 for e in engs}
                    seen = {e: {sk: v for sk, v in seen[e].items() if sk[0] == "dma"} for e in engs}
        assert maxv < 8000, maxv
        self.stats = {e: len(per_eng[e]) for e in engs}
        self.stats["sem_maxv"] = maxv
        self.stats["nbar"] = self.nbar
        with contextlib.ExitStack() as st:
            sems = {}
            for e in engs:
                sems[("eng", e)] = st.enter_context(nc.semaphore("s_" + e))
            for e in ("sp", "act", "pool"):
                for k in range(self.DMA_SEMS[e]):
                    sems[("dma", e, k)] = st.enter_context(nc.semaphore("d_%s%d" % (e, k)))
            bsemA = st.enter_context(nc.semaphore("barA"))
            bsemB = st.enter_context(nc.semaphore("barB"))
            block = st.enter_context(nc.Block())
            NE = len(self.ENGS)

            def mk(ename):
                def body(eng):
                    for o, waits in per_eng[ename]:
                        for sk, v in waits:
                            eng.wait_ge(sems[sk], v)
                        if o.bar:
                            eng.sem_inc(bsemA, 1)
                            if ename == "sp":
                                eng.wait_ge(bsemA, NE * o.bar)
                                for sk_, sm in sems.items():
                                    if sk_[0] == "eng":
                                        eng.sem_clear(sm)
                                eng.sem_inc(bsemB, 1)
                            eng.wait_ge(bsemB, o.bar)
                            continue
                        if o.fn is None:
                            continue
                        ins = o.fn(eng)
                        if o.dma:
                            ins.then_inc(sems[o.ev[0]], 16)
                        elif o.marked:
                            ins.then_inc(sems[("eng", ename)], 1)
                return body

            block.tensor(mk("pe"))
            block.scalar(mk("act"))
            block.vector(mk("dve"))
            block.gpsimd(mk("pool"))
            block.sync(mk("sp"))


class Tile:
    def __init__(self, t):
        self.t = t
        self.deps = {}

    def d(self, key=None):
        if key not in self.deps:
            self.deps[key] = Dep()
        return self.deps[key]

    def __getitem__(self, idx):
        return self.t[idx]


def rev(ap):
    apl = [list(x) for x in ap.ap]
    n = apl[-1][1]
    off = ap.offset + (n - 1) * apl[-1][0]
    apl[-1][0] = -apl[-1][0]
    return AP(ap.tensor, off, apl)


def fm_vec(v):
    v = np.asarray(v, np.float32).reshape(-1, 128)
    return np.ascontiguousarray(v.T)


def w_colchunks(w, nk):
    K, N = w.shape
    return np.ascontiguousarray(w.reshape(nk, 128, N // 128, 128).transpose(2, 1, 0, 3))


def w_rows(w, nk):
    K, N = w.shape
    return np.ascontiguousarray(w.reshape(nk, 128, N).transpose(1, 0, 2))


class ParPack:
    def __init__(self):
        self.cols = []
        self.off = {}
        self.n = 0

    def add(self, name, arr):
        arr = np.asarray(arr, np.float32)
        arr = arr.reshape(arr.shape[0], -1)
        if arr.shape[0] < 128:
            arr = np.concatenate([arr, np.zeros((128 - arr.shape[0], arr.shape[1]), np.float32)], 0)
        self.off[name] = (self.n, arr.shape[1])
        self.cols.append(arr)
        self.n += arr.shape[1]

    def pack(self):
        return np.ascontiguousarray(np.concatenate(self.cols, axis=1))


def pack_params(inp, off_only=False):
    pp = ParPack()
    z = (lambda *s: np.zeros(s, np.float32))
    g = (lambda k: inp[k]) if not off_only else None
    for l in range(4):
        pp.add("nmg%d" % l, fm_vec(g("norm_mix_g")[l]) if g else z(128, 8))
        pp.add("nfg%d" % l, fm_vec(g("norm_ffn_g")[l]) if g else z(128, 8))
        pp.add("bmod%d" % l, fm_vec(g("b_mod")[l]) if g else z(128, 48))
    pp.add("fng", fm_vec(g("final_norm_g")) if g else z(128, 8))
    for j in range(2):
        if g:
            pp.add("scw%d" % j, g("ssd_conv_w")[j].reshape(4, 12, 128).transpose(2, 1, 0))
            pp.add("scb%d" % j, fm_vec(g("ssd_conv_b")[j]))
            pp.add("sng%d" % j, fm_vec(g("ssd_norm_g")[j]))
            pp.add("sd%d" % j, fm_vec(np.repeat(g("ssd_d")[j], 64)))
            pp.add("dtb%d" % j, g("ssd_dt_bias")[j].reshape(32, 1))
            pp.add("alog%d" % j, g("ssd_a_log")[j].reshape(32, 1))
            pp.add("dtbrow%d" % j, np.tile(g("ssd_dt_bias")[j].reshape(1, 32), (128, 1)))
            pp.add("alogrow%d" % j, np.tile(g("ssd_a_log")[j].reshape(1, 32), (128, 1)))
            pp.add("lcw%d" % j, g("lru_conv_w")[j].reshape(4, 8, 128).transpose(2, 1, 0))
            pp.add("lcb%d" % j, fm_vec(g("lru_conv_b")[j]))
            pp.add("lba%d" % j, g("lru_b_a")[j].reshape(2, 8, 128).transpose(2, 0, 1))
            pp.add("lbi%d" % j, g("lru_b_i")[j].reshape(2, 8, 128).transpose(2, 0, 1))
            pp.add("llam%d" % j, g("lru_lam")[j].reshape(2, 8, 128).transpose(2, 0, 1))
        else:
            pp.add("scw%d" % j, z(128, 48)); pp.add("scb%d" % j, z(128, 12)); pp.add("sng%d" % j, z(128, 8))
            pp.add("sd%d" % j, z(128, 8)); pp.add("dtb%d" % j, z(128, 1)); pp.add("alog%d" % j, z(128, 1))
            pp.add("dtbrow%d" % j, z(128, 32)); pp.add("alogrow%d" % j, z(128, 32))
            pp.add("lcw%d" % j, z(128, 32)); pp.add("lcb%d" % j, z(128, 8)); pp.add("lba%d" % j, z(128, 16))
            pp.add("lbi%d" % j, z(128, 16)); pp.add("llam%d" % j, z(128, 16))
    if g:
        pp.add("lbl", g("hg_lb_logits").reshape(4, 6, 128).transpose(2, 0, 1))
    else:
        pp.add("lbl", z(128, 24))
    for j in range(2):
        if g:
            pp.add("hgn%d" % j, g("hg_norm_g")[j].reshape(6, 128).T)
            pp.add("s5d%d" % j, fm_vec(g("s5_d")[j]))
            pp.add("glub%d" % j, fm_vec(g("s5_glu_b")[j]))
            sp_ = np.zeros((128, 3, 2, 8), np.float32)
            for gg in range(16):
                sp_[(gg % 2) * 64:(gg % 2) * 64 + 64, 0, :, gg // 2] = g("s5_lam_re")[j][:, gg].T
                sp_[(gg % 2) * 64:(gg % 2) * 64 + 64, 1, :, gg // 2] = g("s5_lam_im")[j][:, gg].T
                sp_[(gg % 2) * 64:(gg % 2) * 64 + 64, 2, :, gg // 2] = g("s5_log_step")[j][:, gg][None, :]
            pp.add("s5p%d" % j, sp_)
        else:
            pp.add("hgn%d" % j, z(128, 6)); pp.add("s5d%d" % j, z(128, 2)); pp.add("glub%d" % j, z(128, 2))
            pp.add("s5p%d" % j, z(128, 48))
    pp.nres = pp.n
    for l in range(4):
        if g:
            cw = g("ffn_conv_w")[l].reshape(9, 22, 128).transpose(2, 1, 0)
            pp.add("fcw%d" % l, cw)
            pp.add("fcb%d" % l, fm_vec(g("ffn_conv_b")[l]))
        else:
            pp.add("fcw%d" % l, z(128, 22 * 9))
            pp.add("fcb%d" % l, z(128, 22))
    return pp


def host_prep(inp, nseq_total=32):
    sh = {}
    sh["par"] = pack_params(inp).pack()
    sh["wmod"] = np.stack([w_rows(inp["w_mod"][l], 8) for l in range(4)])
    sh["ffg"] = np.stack([w_colchunks(inp["ffn_w_gate"][l], 8) for l in range(4)])
    sh["ffu"] = np.stack([w_colchunks(inp["ffn_w_up"][l], 8) for l in range(4)])
    sh["ffd"] = np.stack([w_colchunks(inp["ffn_w_down"][l], 22) for l in range(4)])
    ev = inp["ev_w_in"]
    evc = np.concatenate([ev[:, :, 0:2560], ev[:, :, 2592:4640]], axis=2)
    sh["evin"] = np.stack([w_colchunks(evc[j], 8) for j in range(2)])
    sh["evdt"] = np.stack([w_rows(ev[j][:, 2560:2592], 8) for j in range(2)])
    sh["evout"] = np.stack([w_colchunks(inp["ev_w_out"][j], 16) for j in range(2)])
    la = np.stack([inp["lru_w_a"], inp["lru_w_i"]], axis=1)
    sh["lruw"] = np.ascontiguousarray(la.transpose(0, 4, 1, 2, 3, 5))
    od = inp["od_w_in"]
    odc = np.concatenate([od[:, :, 0:2304], od[:, :, 3072:4096]], axis=2)
    sh["odin"] = np.stack([w_colchunks(odc[j], 8) for j in range(2)])
    sh["odv"] = np.stack([w_rows(od[j][:, 2304:3072], 8) for j in range(2)])
    sh["odout"] = np.stack([w_colchunks(inp["od_w_out"][j], 8) for j in range(2)])
    sh["gluw"] = np.stack([w_rows(inp["s5_glu_w"][j], 2) for j in range(2)])
    s5b = np.zeros((2, 8, 2, 128, 128), np.float32)
    s5c = np.zeros((2, 2, 8, 2, 128, 128), np.float32)
    s5row = np.zeros((2, 2, 3, 8, 128), np.float32)
    for g in range(16):
        q, gi, go = g // 2, g % 8, g % 2
        for ri, nm in enumerate(("s5_b_re", "s5_b_im")):
            s5b[:, q, ri, gi * 16:(gi + 1) * 16, go * 64:(go + 1) * 64] = inp[nm][:, g].transpose(0, 2, 1)
        for ri, nm in enumerate(("s5_c_re", "s5_c_im")):
            s5c[:, :, q, ri, go * 64:(go + 1) * 64, gi * 16:(gi + 1) * 16] = inp[nm][:, :, g].transpose(0, 1, 3, 2)
        s5row[:, :, 0, q, go * 64:(go + 1) * 64] = inp["s5_lam_re"][:, :, g]
        s5row[:, :, 1, q, go * 64:(go + 1) * 64] = inp["s5_lam_im"][:, :, g]
        s5row[:, :, 2, q, go * 64:(go + 1) * 64] = inp["s5_log_step"][:, :, g][:, :, None]
    sh["s5b"] = s5b
    sh["s5c"] = s5c
    sh["s5row"] = s5row
    return sh


def core_inputs(inp, core, nseq):
    b0 = core * nseq
    xs = []
    for s in range(nseq):
        full = np.concatenate([inp["ctx"][b0 + s], inp["x"][b0 + s]], axis=0)
        xs.append(full.reshape(T, 8, 128).transpose(2, 1, 0))
    cc = np.concatenate([inp["c"][b0:b0 + nseq], inp["c_ctx"][None, :]], axis=0)
    ccf = cc.reshape(nseq + 1, 8, 128).transpose(2, 1, 0)
    return {"xin": np.ascontiguousarray(np.stack(xs)), "cc": np.ascontiguousarray(ccf)}


class K:
    def __init__(self, nseq, phases, final=True):
        self.nseq = nseq
        self.NR = nseq + 1
        self.phases = phases
        self.final = final
        self.nc = bass.Bass("TRN2", target_bir_lowering=False)
        self.P = Prog(self.nc)
        self.po = pack_params(None, off_only=True).off
        self.npar = pack_params(None, off_only=True).n
        self.nres = pack_params(None, off_only=True).nres

    def sb(self, st, name, shape, dt):
        self.uid = getattr(self, "uid", 0) + 1
        return Tile(st.enter_context(self.nc.sbuf_tensor("t%d_%s" % (self.uid, name), shape, dt)))

    def dram_in(self, name, shape):
        return self.nc.dram_tensor(name, list(shape), F32, kind="ExternalInput").ap()

    def psum(self):
        t = self.ps[self.psi % 8]
        self.psi += 1
        return t

    def par(self, name, c0=0, n=1):
        o, w = self.po[name]
        return self.partile[:, o + c0:o + c0 + n]

    def mm(self, out, lhsT, rhs, start, stop, reads, writes):
        return self.P.op("pe", lambda e: e.matmul(out, lhsT, rhs, start=start, stop=stop), reads, writes, pe_acc=True)

    def act(self, out, in_, func, reads, writes, bias=None, scale=None):
        kw = {}
        if bias is not None:
            kw["bias"] = bias
        if scale is not None:
            kw["scale"] = scale
        return self.P.op("act", lambda e: e.activation(out=out, in_=in_, func=func, **kw), reads, writes)

    def tt(self, eng, out, in0, in1, op, reads, writes):
        return self.P.op(eng, lambda e: e.tensor_tensor(out=out, in0=in0, in1=in1, op=op), reads, writes)

    def ts(self, eng, out, in0, s1, s2, op0, op1, reads, writes):
        if s2 is None:
            return self.P.op(eng, lambda e: e.tensor_scalar(out=out, in0=in0, scalar1=s1, scalar2=None, op0=op0), reads, writes)
        return self.P.op(eng, lambda e: e.tensor_scalar(out=out, in0=in0, scalar1=s1, scalar2=s2, op0=op0, op1=op1), reads, writes)

    def stt(self, eng, out, in0, scalar, in1, op0, op1, reads, writes):
        return self.P.op(eng, lambda e: e.scalar_tensor_tensor(out=out, in0=in0, scalar=scalar, in1=in1, op0=op0, op1=op1), reads, writes)

    def cp(self, eng, out, in_, reads, writes):
        if eng == "act":
            return self.P.op("act", lambda e: e.copy(out=out, in_=in_), reads, writes)
        return self.P.op(eng, lambda e: e.tensor_copy(out=out, in_=in_), reads, writes)

    def dma(self, eng, out, in_, reads, writes):
        return self.P.op(eng, lambda e: e.dma_start(out=out, in_=in_), reads, writes, dma=True)

    def memset(self, eng, ap, val, writes):
        return self.P.op(eng, lambda e: e.memset(ap, val), (), writes)

    def build(self):
        nc, P = self.nc, self.P
        NR = self.NR
        self.xin = self.dram_in("xin", [self.nseq, 128, 8, T])
        self.cc = self.dram_in("cc", [128, 8, NR])
        self.par_d = self.dram_in("par", [128, self.npar])
        self.wmod_d = self.dram_in("wmod", [4, 128, 8, 6144])
        self.ffg_d = self.dram_in("ffg", [4, 22, 128, 8, 128])
        self.ffu_d = self.dram_in("ffu", [4, 22, 128, 8, 128])
        self.ffd_d = self.dram_in("ffd", [4, 8, 128, 22, 128])
        self.evin_d = self.dram_in("evin", [2, 36, 128, 8, 128])
        self.evdt_d = self.dram_in("evdt", [2, 128, 8, 32])
        self.evout_d = self.dram_in("evout", [2, 8, 128, 16, 128])
        self.lruw_d = self.dram_in("lruw", [2, 128, 2, 2, 8, 128])
        self.cst_d = self.dram_in("cst", [128, 768])
        self.odin_d = self.dram_in("odin", [2, 26, 128, 8, 128])
        self.odv_d = self.dram_in("odv", [2, 128, 8, 768])
        self.odout_d = self.dram_in("odout", [2, 8, 128, 8, 128])
        self.gluw_d = self.dram_in("gluw", [2, 128, 2, 256])
        self.s5b_d = self.dram_in("s5b", [2, 8, 2, 128, 128])
        self.s5c_d = self.dram_in("s5c", [2, 2, 8, 2, 128, 128])
        self.s5row_d = self.dram_in("s5row", [2, 2, 3, 8, 128])
        self.yout = nc.dram_tensor("yout", [self.nseq, 128, 8, 2048], F32, kind="ExternalOutput").ap()
        if getattr(self, "debug", False):
            self.dbg = nc.dram_tensor("dbg", [128, 16, T], F32, kind="ExternalOutput").ap()
        self.out_ops = []
        with contextlib.ExitStack() as st:
            self.ps = [Tile(st.enter_context(nc.psum_tensor("ps%d" % i, [128, 512], F32))) for i in range(8)]
            self.psi = 0
            self.partile = self.sb(st, "par", [128, self.nres], F32)
            self.cst = self.sb(st, "cst", [128, 768], F32)
            self.identb = self.sb(st, "identb", [128, 128], BF16)
            self.onesb = self.sb(st, "onesb", [128, 128], BF16)
            self.mods = self.sb(st, "mods", [128, 4 * 48 * NR], F32)
            self.amix = self.sb(st, "amix", [128, 4 * 8 * NR], F32)
            self.affn = self.sb(st, "affn", [128, 4 * 8 * NR], F32)
            self.lbt = self.sb(st, "lbt", [128, 4, 6], F32)
            self.x = self.sb(st, "x", [128, 8, T], F32)
            self.hx = self.sb(st, "hx", [128, 8, T], BF16)
            self.cD = Dep(const=True)
            o1 = self.dma("sp", self.partile[:], self.par_d[:, 0:self.nres], (), [self.cD])
            o2 = self.dma("sp", self.cst[:], self.cst_d, (), [self.cD])
            self.cp("dve", self.identb[:], self.cst[:, 0:128], [self.cD], [self.cD])
            self.memset("dve", self.onesb[:], 1.0, [self.cD])
            self.prologue_mods()
            self.prologue_lb()
            P.barrier()
            for s in range(self.nseq):
                self.dma("sp", self.x[:, 0:4, :], self.xin[s, :, 0:4, :], (), [self.x.d()])
                self.dma("act", self.x[:, 4:8, :], self.xin[s, :, 4:8, :], (), [self.x.d()])
                for kind, l in self.phases:
                    if kind == "ffn":
                        self.ffn(l, s)
                    elif kind == "mix":
                        if l % 2 == 0:
                            self.even_mixer(l, s)
                        else:
                            self.odd_mixer(l, s)
                    P.barrier()
                self.final_out(s)
                P.barrier()
            P.emit(self.out_ops)
        return nc

    def mod(self, l, chunk0, r):
        NR = self.NR
        base = (l * 48 + chunk0) * NR + r
        return lambda k: self.mods[:, base + k * NR: base + k * NR + 1]

    def prologue_mods(self):
        NR = self.NR
        with contextlib.ExitStack() as st:
            ccf = self.sb(st, "ccf", [128, 8, NR], F32)
            sfm = self.sb(st, "sfm", [128, 8, NR], BF16)
            wm = [self.sb(st, "wm%d" % i, [128, 8, 1536], BF16) for i in range(2)]
            self.dma("sp", ccf[:], self.cc, (), [ccf.d()])
            self.act(sfm[:], ccf[:], AF.Silu, [ccf.d()], [sfm.d()])
            it = 0
            for l in range(4):
                ps = self.psum()
                for piece in range(4):
                    w = wm[it % 2]
                    it += 1
                    self.dma("pool", w[:], self.wmod_d[l, :, :, piece * 1536:(piece + 1) * 1536], (), [w.d()])
                    for c in range(12):
                        ch = piece * 12 + c
                        for k in range(8):
                            self.mm(ps[:, ch * NR:(ch + 1) * NR], w[:, k, c * 128:(c + 1) * 128], sfm[:, k, :],
                                    k == 0, k == 7, [w.d(), sfm.d()], [ps.d()])
                mo = self.mods[:, l * 48 * NR:(l + 1) * 48 * NR].rearrange("p (c r) -> p c r", r=NR)
                o, _ = self.po["bmod%d" % l]
                self.tt("dve", mo, ps[:, 0:48 * NR].rearrange("p (c r) -> p c r", r=NR),
                        self.partile[:, o:o + 48].unsqueeze(2).to_broadcast([128, 48, NR]), ALU.add,
                        [ps.d(), self.cD], [self.cD])
                for dst, gname, c0 in ((self.amix, "nmg%d" % l, 8), (self.affn, "nfg%d" % l, 32)):
                    dv = dst[:, l * 8 * NR:(l + 1) * 8 * NR].rearrange("p (c r) -> p c r", r=NR)
                    sc = self.mods[:, (l * 48 + c0) * NR:(l * 48 + c0 + 8) * NR].rearrange("p (c r) -> p c r", r=NR)
                    go, _ = self.po[gname]
                    self.ts("dve", dv, sc, 1.0, None, ALU.add, None, [self.cD], [self.cD])
                    self.tt("dve", dv, dv, self.partile[:, go:go + 8].unsqueeze(2).to_broadcast([128, 8, NR]), ALU.mult,
                            [self.cD], [self.cD])

    def rms_mod(self, st, src, dst, A, B, s, nch=8, srcdep=None, dstdep=None, pergroup=False):
        sq = [self.sb(st, "rm_sq%d" % i, [128, nch, 512], BF16) for i in range(2)]
        rs = [self.sb(st, "rm_rs%d" % i, [128, nch if pergroup else 1, 512], F32) for i in range(2)]
        tm = [self.sb(st, "rm_tm%d" % i, [128, 512], F32) for i in range(2)]
        sd = srcdep or src.d()
        dd = dstdep or dst.d()
        ndiv = 128.0 if pergroup else 128.0 * nch
        for bi, (t0, n) in enumerate(BLKS):
            r = self.nseq if t0 == 0 else s
            q = sq[bi % 2]
            rr = rs[bi % 2]
            self.act(q[:, :, 0:n], src[:, 0:nch, t0:t0 + n], AF.Square, [sd], [q.d()])
            groups = [[k] for k in range(nch)] if pergroup else [list(range(nch))]
            for gi, grp in enumerate(groups):
                ps = self.psum()
                for i, k in enumerate(grp):
                    self.mm(ps[:, 0:n], self.onesb[:], q[:, k, 0:n], i == 0, i == len(grp) - 1, [q.d(), self.cD], [ps.d()])
                self.ts("dve", rr[:, gi, 0:n], ps[:, 0:n], 1.0 / ndiv, EPS, ALU.mult, ALU.add, [ps.d()], [rr.d()])
                self.P.op("dve", (lambda o_, i_: lambda e: e.reciprocal(out=o_, in_=i_))(rr[:, gi, 0:n], rr[:, gi, 0:n]), [rr.d()], [rr.d()])
                self.act(rr[:, gi, 0:n], rr[:, gi, 0:n], AF.Sqrt, [rr.d()], [rr.d()])
            for k in range(nch):
                tmp = tm[k % 2]
                gi = k if pergroup else 0
                if A is not None:
                    self.stt("dve", tmp[:, 0:n], src[:, k, t0:t0 + n], A(k, r), rr[:, gi, 0:n], ALU.mult, ALU.mult,
                             [sd, rr.d(), self.cD], [tmp.d()])
                else:
                    self.tt("dve", tmp[:, 0:n], src[:, k, t0:t0 + n], rr[:, gi, 0:n], ALU.mult, [sd, rr.d()], [tmp.d()])
                if B is not None:
                    self.act(dst[:, k, t0:t0 + n], tmp[:, 0:n], AF.Identity, [tmp.d(), self.cD], [dd], bias=B(k, r))
                else:
                    self.cp("act", dst[:, k, t0:t0 + n], tmp[:, 0:n], [tmp.d()], [dd])

    def ffn(self, l, s):
        NR = self.NR
        A = lambda k, r: self.affn[:, (l * 8 + k) * NR + r:(l * 8 + k) * NR + r + 1]
        B = lambda k, r: self.mods[:, (l * 48 + 24 + k) * NR + r:(l * 48 + 24 + k) * NR + r + 1]
        with contextlib.ExitStack() as st0:
            with contextlib.ExitStack() as st:
                self.rms_mod(st, self.x, self.hx, A, B, s)
            self.P.barrier()
            gh = self.sb(st0, "gh", [128, 22, 1280], BF16)
            fpar = self.sb(st0, "fpar", [128, 220], F32)
            fo0 = self.po["fcw%d" % l][0]
            self.dma("sp", fpar[:], self.par_d[:, fo0:fo0 + 220], (), [fpar.d()])
            fo, bo = 0, 198
            it = 0
            for half in range(2):
              with contextlib.ExitStack() as st1:
                wg = [self.sb(st1, "wg%d" % i, [128, 8, 128], BF16) for i in range(2)]
                wu = [self.sb(st1, "wu%d" % i, [128, 8, 128], BF16) for i in range(2)]
                dg = [self.sb(st1, "dg%d" % i, [128, 9, 128], BF16) for i in range(2)]
                apc = [self.sb(st1, "apc%d" % i, [128, 258], BF16) for i in range(2)]
                apl = [None, None]
                apl[half] = [self.sb(st1, "apl%d_%d" % (half, i), [128, 18, 66], BF16) for i in range(2)]
                sg = [self.sb(st1, "sg%d" % i, [128, 512], BF16) for i in range(2)]
                for t_ in apc + apl[half]:
                    self.memset("pool", t_[:], 0.0, [t_.d()])
                if half == 0:
                    pieces = [(256, 8, 1), (256 + 512, 8, 9), (256 + 1024, 1, 17)]
                    oblks = [("c", 0, 256, 0), ("l", 256, 512, 0), ("l", 768, 512, 8)]
                else:
                    pieces = [(256 + 960, 1, 0), (256 + 1024, 8, 1), (256 + 1536, 8, 9)]
                    oblks = [("l", 1280, 512, 0), ("l", 1792, 512, 8)]
                row_off = 1 if half == 0 else 1
                def load(f, it):
                    self.dma("pool", wg[it % 2][:], self.ffg_d[l, f], (), [wg[it % 2].d()])
                    self.dma("pool", wu[it % 2][:], self.ffu_d[l, f], (), [wu[it % 2].d()])
                load(0, it)
                for f in range(22):
                    if f + 1 < 22:
                        load(f + 1, it + 1)
                    g_, u_, d_ = wg[it % 2], wu[it % 2], dg[it % 2]
                    pc, pl = apc[it % 2], apl[half][it % 2]
                    self.tt("dve", d_[:], self.identb[:].unsqueeze(1).to_broadcast([128, 9, 128]),
                            fpar[:, fo + f * 9:fo + f * 9 + 9].unsqueeze(2).to_broadcast([128, 9, 128]), ALU.mult, [self.cD, fpar.d()], [d_.d()])
                    if half == 0:
                        ps = self.psum()
                        for k in range(8):
                            self.mm(ps[:, 0:256], g_[:, k, :], self.hx[:, k, 0:256], k == 0, k == 7, [g_.d(), self.hx.d()], [ps.d()])
                        self.cp("act", pc[:, 1:257], ps[:, 0:256], [ps.d()], [pc.d()])
                    for (tk0, nr, pr0) in pieces:
                        ps = self.psum()
                        n = nr * 64
                        for k in range(8):
                            self.mm(ps[:, 0:n], g_[:, k, :], self.hx[:, k, tk0:tk0 + n], k == 0, k == 7, [g_.d(), self.hx.d()], [ps.d()])
                        self.cp("act", pl[:, pr0:pr0 + nr, 1:65], ps[:, 0:n].rearrange("p (r c) -> p r c", c=64), [ps.d()], [pl.d()])
                    gcol = 0
                    for (kind, tk0, n, lr0) in oblks:
                        psc = self.psum()
                        if kind == "c":
                            for i, dx in enumerate((-1, 0, 1)):
                                self.mm(psc[:, 0:256], d_[:, 3 + (dx + 1), :], pc[:, 1 + dx:257 + dx], i == 0, i == 2, [d_.d(), pc.d()], [psc.d()])
                        else:
                            i = 0
                            for dy in (-1, 0, 1):
                                for dx in (-1, 0, 1):
                                    r0 = row_off + lr0 + dy
                                    self.mm(psc[:, 0:512].rearrange("p (r c) -> p r c", c=64), d_[:, (dy + 1) * 3 + (dx + 1), :],
                                            pl[:, r0:r0 + 8, 1 + dx:65 + dx], i == 0, i == 8, [d_.d(), pl.d()], [psc.d()])
                                    i += 1
                        sgt = sg[gcol % 2]
                        self.act(sgt[:, 0:n], psc[:, 0:n], AF.Silu, [psc.d(), fpar.d()], [sgt.d()], bias=fpar[:, bo + f:bo + f + 1])
                        psu = self.psum()
                        for k in range(8):
                            self.mm(psu[:, 0:n], u_[:, k, :], self.hx[:, k, tk0:tk0 + n], k == 0, k == 7, [u_.d(), self.hx.d()], [psu.d()])
                        hoff = tk0 if half == 0 else tk0 - 1280
                        self.tt("dve", gh[:, f, hoff:hoff + n], sgt[:, 0:n], psu[:, 0:n], ALU.mult, [sgt.d(), psu.d()], [gh.d(f)])
                        gcol += 1
                    it += 1
              self.P.barrier()
              with contextlib.ExitStack() as st1:
                wd = [self.sb(st1, "wd%d" % i, [128, 22, 128], BF16) for i in range(2)]
                ghd = [gh.d(f) for f in range(22)]
                self.dma("pool", wd[0][:], self.ffd_d[l, 0], (), [wd[0].d()])
                for oc in range(8):
                    if oc + 1 < 8:
                        self.dma("pool", wd[(oc + 1) % 2][:], self.ffd_d[l, oc + 1], (), [wd[(oc + 1) % 2].d()])
                    w_ = wd[oc % 2]
                    for (kind, tk0, n, lr0) in oblks:
                        r = self.nseq if kind == "c" else s
                        hoff = tk0 if half == 0 else tk0 - 1280
                        ps = self.psum()
                        for f in range(22):
                            self.mm(ps[:, 0:n], w_[:, f, :], gh[:, f, hoff:hoff + n], f == 0, f == 21, [w_.d()] + ghd, [ps.d()])
                        m5 = self.mods[:, (l * 48 + 40 + oc) * NR + r:(l * 48 + 40 + oc) * NR + r + 1]
                        self.stt("dve", self.x[:, oc, tk0:tk0 + n], ps[:, 0:n], m5, self.x[:, oc, tk0:tk0 + n], ALU.mult, ALU.add,
                                 [ps.d(), self.cD, self.x.d()], [self.x.d()])
                self.P.barrier()


    def dump(self, tile, ch0, nch, slot0, dep=None):
        if not getattr(self, "debug", False):
            return
        self.P.barrier()
        with contextlib.ExitStack() as st:
            stg = self.sb(st, "dbgstg", [128, T], F32)
            for i in range(nch):
                src = tile[:, ch0 + i, :] if len(tile.t.shape) == 3 else tile[:, :]
                n = src.shape[-1]
                self.cp("dve", stg[:, 0:n], src, [dep or tile.d()], [stg.d()])
                self.out_ops.append(self.dma("sp", self.dbg[:, slot0 + i, 0:n], stg[:, 0:n], [stg.d()], ()))
            self.P.barrier()

    def dump_ap(self, ap, dep, slot):
        if not getattr(self, "debug", False):
            return
        self.P.barrier()
        with contextlib.ExitStack() as st:
            n = ap.shape[-1]
            stg = self.sb(st, "dbgstg2", [128, n], F32)
            self.cp("dve", stg[:, 0:n], ap, [dep], [stg.d()])
            self.out_ops.append(self.dma("sp", self.dbg[:, slot, 0:n], stg[:, 0:n], [stg.d()], ()))
            self.P.barrier()

    def outproj(self, wd_ap_fn, nk, mix, l, s, gate_chunk0):
        NR = self.NR
        with contextlib.ExitStack() as st:
            wo = [self.sb(st, "wo%d" % i, [128, nk, 128], BF16) for i in range(2)]
            self.dma("pool", wo[0][:], wd_ap_fn(0), (), [wo[0].d()])
            for oc in range(8):
                if oc + 1 < 8:
                    self.dma("pool", wo[(oc + 1) % 2][:], wd_ap_fn(oc + 1), (), [wo[(oc + 1) % 2].d()])
                w_ = wo[oc % 2]
                for (t0, n) in BLKS:
                    r = self.nseq if t0 == 0 else s
                    ps = self.psum()
                    for k in range(nk):
                        self.mm(ps[:, 0:n], w_[:, k, :], mix[:, k, t0:t0 + n], k == 0, k == nk - 1, [w_.d(), mix.d()], [ps.d()])
                    m2 = self.mods[:, (l * 48 + gate_chunk0 + oc) * NR + r:(l * 48 + gate_chunk0 + oc) * NR + r + 1]
                    self.stt("dve", self.x[:, oc, t0:t0 + n], ps[:, 0:n], m2, self.x[:, oc, t0:t0 + n], ALU.mult, ALU.add,
                             [ps.d(), self.cD, self.x.d()], [self.x.d()])
            self.P.barrier()

    def proj_conv(self, w_dram, cw_off, cb_off, wt, pad, dgt, dst, func, ntap=4, dst_fn=None, dst_dep=None):
        self.dma("pool", wt[:], w_dram, (), [wt.d()])
        self.tt("dve", dgt[:, 0:ntap, :], self.identb[:].unsqueeze(1).to_broadcast([128, ntap, 128]),
                self.partile[:, cw_off:cw_off + ntap].unsqueeze(2).to_broadcast([128, ntap, 128]), ALU.mult, [self.cD], [dgt.d()])
        for (t0, n) in BLKS:
            ps = self.psum()
            for k in range(8):
                self.mm(ps[:, 0:n], wt[:, k, :], self.hx[:, k, t0:t0 + n], k == 0, k == 7, [wt.d(), self.hx.d()], [ps.d()])
            po = 1 + t0 if t0 == 0 else 260 + (t0 - CTX)
            self.cp("act", pad[:, po:po + n], ps[:, 0:n], [ps.d()], [pad.d()])
        for (t0, n) in BLKS:
            ps = self.psum()
            po = t0 if t0 == 0 else 259 + (t0 - CTX)
            for k in range(ntap):
                self.mm(ps[:, 0:n], dgt[:, k, :], pad[:, po + k:po + k + n], k == 0, k == ntap - 1, [dgt.d(), pad.d()], [ps.d()])
            o_ap = dst_fn(t0, n) if dst_fn is not None else dst[:, t0:t0 + n]
            self.act(o_ap, ps[:, 0:n], func, [ps.d(), self.cD], [dst_dep or dst.d()], bias=self.partile[:, cb_off:cb_off + 1])

    def even_mixer(self, l, s):
        j = l // 2
        NR = self.NR
        A = lambda k, r: self.amix[:, (l * 8 + k) * NR + r:(l * 8 + k) * NR + r + 1]
        B = lambda k, r: self.mods[:, (l * 48 + k) * NR + r:(l * 48 + k) * NR + r + 1]
        with contextlib.ExitStack() as st0:
            with contextlib.ExitStack() as st:
                self.rms_mod(st, self.x, self.hx, A, B, s)
            self.P.barrier()
            mix = self.sb(st0, "mix", [128, 8, T], BF16)
            which = getattr(self, "even_parts", ("ssd", "lru"))
            if "ssd" in which:
                self.ssd(l, s, mix)
                self.dump(mix, 0, 8, 0)
                self.outproj(lambda oc: self.evout_d[j, oc, :, 0:8, :], 8, mix, l, s, 16)
            if "lru" in which:
                self.lru(l, s, mix)
                self.dump(mix, 0, 8, 8)
                self.outproj(lambda oc: self.evout_d[j, oc, :, 8:16, :], 8, mix, l, s, 16)

    def lru(self, l, s, mix):
        j = l // 2
        po = self.po
        with contextlib.ExitStack() as st:
            cA = self.sb(st, "cA", [128, 16], F32)
            wu = self.sb(st, "lwu", [128, 8, 128], BF16)
            wgy = [self.sb(st, "lwgy%d" % i, [128, 8, 128], BF16) for i in range(2)]
            wai = [self.sb(st, "lwai%d" % i, [128, 2, 2, 128], BF16) for i in range(2)]
            pad = self.sb(st, "lpad", [128, 2312], BF16)
            dgt = self.sb(st, "ldg", [128, 4, 128], BF16)
            ucb = self.sb(st, "ucb", [128, T], BF16)
            a_t = self.sb(st, "lru_a", [128, T], F32)
            sqT = self.sb(st, "lru_sq", [128, T], F32)
            bx = [self.sb(st, "lru_bx%d" % i, [128, T], BF16) for i in range(2)]
            self.memset("pool", pad[:], 0.0, [pad.d()])
            lo, _ = po["llam%d" % j]
            self.act(cA[:], self.partile[:, lo:lo + 16], AF.Exp, [self.cD], [cA.d()], scale=-1.0)
            self.act(cA[:], cA[:], AF.Ln, [cA.d()], [cA.d()], bias=1.0)
            self.ts("dve", cA[:], cA[:], -8.0, None, ALU.mult, None, [cA.d()], [cA.d()])
            sc = lambda o_, a_, b2, i_: (lambda e: e.tensor_tensor_scan(out=o_, data0=a_, data1=b2, initial=i_, op0=ALU.mult, op1=ALU.add))
            for jj in range(8):
                self.dma("pool", wgy[jj % 2][:], self.evin_d[j, 20 + jj], (), [wgy[jj % 2].d()])
                self.dma("pool", wai[jj % 2][:], self.lruw_d[j, :, :, :, jj, :], (), [wai[jj % 2].d()])
                self.proj_conv(self.evin_d[j, 28 + jj], po["lcw%d" % j][0] + jj * 4, po["lcb%d" % j][0] + jj, wu, pad, dgt, ucb, AF.Identity)
                w2 = wai[jj % 2]
                for d in range(2):
                    ba = self.partile[:, po["lba%d" % j][0] + d * 8 + jj:po["lba%d" % j][0] + d * 8 + jj + 1]
                    bi = self.partile[:, po["lbi%d" % j][0] + d * 8 + jj:po["lbi%d" % j][0] + d * 8 + jj + 1]
                    b_ = bx[d]
                    for (t0, n) in BLKS:
                        psa = self.psum()
                        self.mm(psa[:, 0:n], w2[:, 0, d, :], ucb[:, t0:t0 + n], True, True, [w2.d(), ucb.d()], [psa.d()])
                        psi_ = self.psum()
                        self.mm(psi_[:, 0:n], w2[:, 1, d, :], ucb[:, t0:t0 + n], True, True, [w2.d(), ucb.d()], [psi_.d()])
                        self.act(a_t[:, t0:t0 + n], psa[:, 0:n], AF.Sigmoid, [psa.d(), self.cD], [a_t.d()], bias=ba)
                        self.act(b_[:, t0:t0 + n], psi_[:, 0:n], AF.Sigmoid, [psi_.d(), self.cD], [b_.d()], bias=bi)
                    self.act(a_t[:, :], a_t[:, :], AF.Exp, [a_t.d(), cA.d()], [a_t.d()], scale=cA[:, d * 8 + jj:d * 8 + jj + 1])
                    self.act(sqT[:, :], a_t[:, :], AF.Square, [a_t.d()], [sqT.d()])
                    self.act(sqT[:, :], sqT[:, :], AF.Sqrt, [sqT.d()], [sqT.d()], scale=-1.0, bias=1.0)
                    self.tt("dve", b_[:, :], b_[:, :], sqT[:, :], ALU.mult, [b_.d(), sqT.d()], [b_.d()])
                    self.tt("dve", b_[:, :], b_[:, :], ucb[:, :], ALU.mult, [b_.d(), ucb.d()], [b_.d()])
                    if d == 0:
                        self.P.op("dve", sc(b_[:, 0:T], a_t[:, 0:T], b_[:, 0:T], 0.0), [a_t.d(), b_.d()], [b_.d()])
                    else:
                        self.P.op("dve", sc(rev(b_[:, 0:CTX]), rev(a_t[:, 0:CTX]), rev(b_[:, 0:CTX]), 0.0), [a_t.d(), b_.d()], [b_.d()])
                        self.P.op("dve", sc(rev(b_[:, CTX:T]), rev(a_t[:, CTX:T]), rev(b_[:, CTX:T]), b_[:, 0:1]), [a_t.d(), b_.d()], [b_.d()])
                self.tt("dve", bx[0][:, :], bx[0][:, :], bx[1][:, :], ALU.add, [bx[0].d(), bx[1].d()], [bx[0].d()])
                wg_ = wgy[jj % 2]
                for (t0, n) in BLKS:
                    ps = self.psum()
                    for k in range(8):
                        self.mm(ps[:, 0:n], wg_[:, k, :], self.hx[:, k, t0:t0 + n], k == 0, k == 7, [wg_.d(), self.hx.d()], [ps.d()])
                    self.act(bx[1][:, t0:t0 + n], ps[:, 0:n], AF.Gelu, [ps.d()], [bx[1].d()])
                self.tt("dve", mix[:, jj, :], bx[0][:, :], bx[1][:, :], ALU.mult, [bx[0].d(), bx[1].d()], [mix.d()])
            self.P.barrier()

    def pe_T(self, out, in_, reads, writes):
        return self.P.op("pe", lambda e: e.transpose(out, in_, self.identb[:]), list(reads) + [self.cD], writes, pe_acc=True)

    def to_tokmajor_ap(self, src_fn, src_dep, dst):
        c = 0
        while c < 18:
            nq = min(4, 18 - c)
            ps = self.psum()
            pb = ps[:].bitcast(BF16)
            for q in range(nq):
                self.pe_T(pb[:, q * 128:(q + 1) * 128], src_fn(c + q), [src_dep], [ps.d()])
            self.cp("act", dst[:, c:c + nq, :], pb[:, 0:nq * 128].rearrange("p (q f) -> p q f", f=128), [ps.d()], [dst.d()])
            c += nq

    def ssd(self, l, s, mix):
        j = l // 2
        po = self.po
        tri = lambda d: self.cst[:, 128 + 128 * d:256 + 128 * d]
        onesf = self.cst[:, 384:512]
        bc = lambda ap, shape, ax: ap.unsqueeze(ax).to_broadcast(shape)
        with contextlib.ExitStack() as st:
            dt_tok = self.sb(st, "dt_tok", [128, 18, 32], F32)
            la_tok = self.sb(st, "la_tok", [128, 18, 32], F32)
            cumcol = self.sb(st, "cumcol", [128, 18, 32], F32)
            Bfm = self.sb(st, "Bfm", [128, T], BF16)
            Cfm = self.sb(st, "Cfm", [128, T], BF16)
            Hst = [self.sb(st, "Hst%d" % i, [128, 2, 64], F32) for i in range(2)]
            Hbf = [self.sb(st, "Hbf%d" % i, [128, 2, 64], BF16) for i in range(2)]
            NB = 2
            stA = contextlib.ExitStack()
            wdt = self.sb(stA, "swdt", [128, 8, 32], BF16)
            arow = self.sb(stA, "arow", [128, 32], F32)
            t9 = self.sb(stA, "t9", [128, 9, 32], F32)
            self.dma("pool", wdt[:], self.evdt_d[j], (), [wdt.d()])
            ao = po["alogrow%d" % j][0]
            bo = po["dtbrow%d" % j][0]
            self.act(arow[:], self.partile[:, ao:ao + 32], AF.Exp, [self.cD], [arow.d()])
            self.ts("dve", arow[:], arow[:], -1.0, None, ALU.mult, None, [arow.d()], [arow.d()])
            for half in range(2):
                ps = self.psum()
                for ci in range(9):
                    c = half * 9 + ci
                    for k in range(8):
                        self.mm(ps[:, ci * 32:(ci + 1) * 32], self.hx[:, k, c * 128:(c + 1) * 128], wdt[:, k, :], k == 0, k == 7,
                                [self.hx.d(), wdt.d()], [ps.d()])
                self.tt("dve", t9[:], ps[:, 0:288].rearrange("p (c h) -> p c h", h=32),
                        bc(self.partile[:, bo:bo + 32], [128, 9, 32], 1), ALU.add, [ps.d(), self.cD], [t9.d()])
                self.act(t9[:], t9[:], AF.Exp, [t9.d()], [t9.d()])
                self.act(dt_tok[:, half * 9:(half + 1) * 9, :], t9[:], AF.Ln, [t9.d()], [dt_tok.d()], bias=1.0)
                self.tt("dve", la_tok[:, half * 9:(half + 1) * 9, :], dt_tok[:, half * 9:(half + 1) * 9, :],
                        bc(arow[:], [128, 9, 32], 1), ALU.mult, [dt_tok.d(), arow.d()], [la_tok.d()])
            for half in range(2):
                ps = self.psum()
                for ci in range(9):
                    c = half * 9 + ci
                    for d in range(2):
                        self.mm(ps[:, ci * 32 + d * 16:ci * 32 + (d + 1) * 16], tri(d), la_tok[:, c, d * 16:(d + 1) * 16], True, True,
                                [la_tok.d(), self.cD], [ps.d()])
                self.cp("dve", cumcol[:, half * 9:(half + 1) * 9, :], ps[:, 0:288].rearrange("p (c h) -> p c h", h=32), [ps.d()], [cumcol.d()])
            self.P.barrier()
            stA.close()
            it = 0
            for g in range(2):
              with contextlib.ExitStack() as stB:
                wt = self.sb(stB, "swt", [128, 8, 128], BF16)
                pad = self.sb(stB, "spad", [128, 2312], BF16)
                dgt = self.sb(stB, "sdg", [128, 4, 128], BF16)
                self.memset("pool", pad[:], 0.0, [pad.d()])
                self.proj_conv(self.evin_d[j, 8 + 8 + g], po["scw%d" % j][0] + (8 + g) * 4, po["scb%d" % j][0] + 8 + g, wt, pad, dgt, Bfm, AF.Silu)
                self.proj_conv(self.evin_d[j, 8 + 10 + g], po["scw%d" % j][0] + (10 + g) * 4, po["scb%d" % j][0] + 10 + g, wt, pad, dgt, Cfm, AF.Silu)
                for i in range(4):
                    cx = 4 * g + i
                    self.proj_conv(self.evin_d[j, 8 + cx], po["scw%d" % j][0] + cx * 4, po["scb%d" % j][0] + cx, wt, pad, dgt, None, AF.Silu,
                                   dst_fn=(lambda cx_: lambda t0, n: mix[:, cx_, t0:t0 + n])(cx), dst_dep=mix.d())
                self.P.barrier()
              with contextlib.ExitStack() as stC:
                Btok = self.sb(stC, "Btok", [128, 18, 128], BF16)
                xstok = self.sb(stC, "xstok", [128, 18, 128], BF16)
                W = 6
                rhsla = [self.sb(stC, "rhsla%d" % i, [128, 2, 128], F32) for i in range(W)]
                E_ = [self.sb(stC, "E%d" % i, [128, 2, 128], BF16) for i in range(W)]
                Eb = [self.sb(stC, "Eb%d" % i, [128, 2, 128], BF16) for i in range(W)]
                cbm = [self.sb(stC, "cbm%d" % i, [128, 128], BF16) for i in range(W)]
                xdt = [self.sb(stC, "xdt%d" % i, [128, 2, 64], BF16) for i in range(W)]
                xdtw = [self.sb(stC, "xdtw%d" % i, [128, 2, 64], BF16) for i in range(W)]
                wv = [self.sb(stC, "wv%d" % i, [128, 2], F32) for i in range(W)]
                dtot = [self.sb(stC, "dtot%d" % i, [128, 2], F32) for i in range(W)]
                self.to_tokmajor_ap(lambda c: Bfm[:, c * 128:(c + 1) * 128], Bfm.d(), Btok)
                orders = [list(range(18)), [1, 0] + list(range(17, 1, -1))]
                for i in range(4):
                    cx = 4 * g + i
                    self.to_tokmajor_ap((lambda cx_: lambda c: mix[:, cx_, c * 128:(c + 1) * 128])(cx), mix.d(), xstok)
                    sdo = po["sd%d" % j][0] + cx
                    self.ts("dve", mix[:, cx, :], mix[:, cx, :], self.partile[:, sdo:sdo + 1], None, ALU.mult, None, [mix.d(), self.cD], [mix.d()])
                    for d in range(2):
                        self.memset("dve", Hst[d][:], 0.0, [Hst[d].d()])
                        self.memset("dve", Hbf[d][:], 0.0, [Hbf[d].d()])
                    iters = [(d, orders[d][step]) for step in range(18) for d in range(2)]
                    for w0 in range(0, len(iters), W):
                        win = list(enumerate(iters[w0:w0 + W]))
                        pst = {}
                        hdf = lambda d: d * 16 + 8 * g + 2 * i
                        for b, (d, c) in win:
                            hd = hdf(d)
                            self.tt("dve", rhsla[b][:], bc(tri(d), [128, 2, 128], 1), bc(la_tok[:, c, hd:hd + 2], [128, 2, 128], 2), ALU.mult,
                                    [la_tok.d(), self.cD], [rhsla[b].d()])
                        for b, (d, c) in win:
                            cs = slice(c * 128, (c + 1) * 128)
                            ps = self.psum()
                            pst[b] = ps
                            self.mm(ps[:, 0:128], Bfm[:, cs], Cfm[:, cs], True, True, [Bfm.d(), Cfm.d()], [ps.d()])
                            self.mm(ps[:, 128:384], onesf, rhsla[b][:].rearrange("p h l -> p (h l)"), True, True, [rhsla[b].d(), self.cD], [ps.d()])
                        cum3f = lambda b: pst[b][:, 128:384].rearrange("p (h l) -> p h l", l=128)
                        for b, (d, c) in win:
                            hd = hdf(d)
                            ps = pst[b]
                            ccb = bc(cumcol[:, c, hd:hd + 2], [128, 2, 128], 2)
                            self.tt("dve", cbm[b][:], ps[:, 0:128], tri(d), ALU.mult, [ps.d(), self.cD], [cbm[b].d()])
                            self.tt("dve", rhsla[b][:], cum3f(b), ccb, ALU.min, [ps.d(), cumcol.d()], [rhsla[b].d()])
                            self.tt("dve", rhsla[b][:], rhsla[b][:], ccb, ALU.subtract, [rhsla[b].d(), cumcol.d()], [rhsla[b].d()])
                        for b, (d, c) in win:
                            self.act(E_[b][:], rhsla[b][:], AF.Exp, [rhsla[b].d()], [E_[b].d()])
                            self.act(Eb[b][:], cum3f(b), AF.Exp, [pst[b].d()], [Eb[b].d()])
                        for b, (d, c) in win:
                            hd = hdf(d)
                            last = 127 if d == 0 else 0
                            cs = slice(c * 128, (c + 1) * 128)
                            self.tt("dve", E_[b][:], E_[b][:], bc(cbm[b][:], [128, 2, 128], 1), ALU.mult, [E_[b].d(), cbm[b].d()], [E_[b].d()])
                            self.tt("dve", Eb[b][:], Eb[b][:], bc(Cfm[:, cs], [128, 2, 128], 1), ALU.mult, [Eb[b].d(), Cfm.d()], [Eb[b].d()])
                            self.tt("dve", xdt[b][:], xstok[:, c, :].rearrange("p (h q) -> p h q", q=64),
                                    bc(dt_tok[:, c, hd:hd + 2], [128, 2, 64], 2), ALU.mult, [xstok.d(), dt_tok.d()], [xdt[b].d()])
                            self.tt("dve", wv[b][:], cum3f(b)[:, :, last], cumcol[:, c, hd:hd + 2], ALU.subtract, [pst[b].d(), cumcol.d()], [wv[b].d()])
                        for b, (d, c) in win:
                            last = 127 if d == 0 else 0
                            self.act(wv[b][:], wv[b][:], AF.Exp, [wv[b].d()], [wv[b].d()])
                            self.act(dtot[b][:], cum3f(b)[:, :, last], AF.Exp, [pst[b].d()], [dtot[b].d()])
                        for b, (d, c) in win:
                            self.tt("dve", xdtw[b][:], xdt[b][:], bc(wv[b][:], [128, 2, 64], 2), ALU.mult, [xdt[b].d(), wv[b].d()], [xdtw[b].d()])
                        for b, (d, c) in win:
                            cs = slice(c * 128, (c + 1) * 128)
                            ps2 = self.psum()
                            for hh in range(2):
                                self.mm(ps2[hh * 64:(hh + 1) * 64, 0:128], xdt[b][:, hh, :], E_[b][:, hh, :], True, False,
                                        [xdt[b].d(), E_[b].d()], [ps2.d()])
                                self.mm(ps2[hh * 64:(hh + 1) * 64, 0:128], Hbf[d][:, hh, :], Eb[b][:, hh, :], False, True,
                                        [Hbf[d].d(), Eb[b].d()], [ps2.d()])
                            self.mm(ps2[:, 128:256], Btok[:, c, :], xdtw[b][:].rearrange("p h q -> p (h q)"), True, True, [Btok.d(), xdtw[b].d()], [ps2.d()])
                            self.tt("dve", mix[:, cx, cs], mix[:, cx, cs], ps2[:, 0:128], ALU.add, [mix.d(), ps2.d()], [mix.d()])
                            self.tt("dve", Hst[d][:], Hst[d][:], bc(dtot[b][:], [128, 2, 64], 2), ALU.mult, [Hst[d].d(), dtot[b].d()], [Hst[d].d()])
                            self.tt("dve", Hst[d][:], Hst[d][:], ps2[:, 128:256].rearrange("p (h q) -> p h q", q=64), ALU.add, [Hst[d].d(), ps2.d()], [Hst[d].d()])
                            self.cp("act", Hbf[d][:], Hst[d][:], [Hst[d].d()], [Hbf[d].d()])
                self.P.barrier()
            self.dump(mix, 0, 8, 8)
            wt = self.sb(st, "swt2", [128, 8, 128], BF16)
            szt = [self.sb(st, "szt%d" % i, [128, 512], BF16) for i in range(2)]
            for ch in range(8):
                self.dma("pool", wt[:], self.evin_d[j, ch], (), [wt.d()])
                for bi, (t0, n) in enumerate(BLKS):
                    ps = self.psum()
                    for k in range(8):
                        self.mm(ps[:, 0:n], wt[:, k, :], self.hx[:, k, t0:t0 + n], k == 0, k == 7, [wt.d(), self.hx.d()], [ps.d()])
                    z_ = szt[bi % 2]
                    self.act(z_[:, 0:n], ps[:, 0:n], AF.Silu, [ps.d()], [z_.d()])
                    self.tt("dve", mix[:, ch, t0:t0 + n], mix[:, ch, t0:t0 + n], z_[:, 0:n], ALU.mult, [mix.d(), z_.d()], [mix.d()])
            self.P.barrier()
        with contextlib.ExitStack() as st:
            go = po["sng%d" % j][0]
            self.rms_mod(st, mix, mix, lambda k, r: self.partile[:, go + k:go + k + 1], None, s)
            self.P.barrier()

    def prologue_lb(self):
        with contextlib.ExitStack() as st:
            lo = self.po["lbl"][0]
            L = self.partile[:, lo:lo + 24].rearrange("p (l h) -> p l h", h=6)
            mx = self.sb(st, "lbmx", [128, 6], F32)
            e = self.sb(st, "lbe", [128, 4, 6], F32)
            sm = self.sb(st, "lbs", [128, 6], F32)
            D = [self.cD]
            self.tt("dve", mx[:], L[:, 0, :], L[:, 1, :], ALU.max, D, D)
            self.tt("dve", mx[:], mx[:], L[:, 2, :], ALU.max, D, D)
            self.tt("dve", mx[:], mx[:], L[:, 3, :], ALU.max, D, D)
            self.tt("dve", e[:], L, mx[:].unsqueeze(1).to_broadcast([128, 4, 6]), ALU.subtract, D, D)
            self.act(e[:], e[:], AF.Exp, D, D)
            self.tt("dve", sm[:], e[:, 0, :], e[:, 1, :], ALU.add, D, D)
            self.tt("dve", sm[:], sm[:], e[:, 2, :], ALU.add, D, D)
            self.tt("dve", sm[:], sm[:], e[:, 3, :], ALU.add, D, D)
            self.P.op("dve", lambda en: en.reciprocal(out=sm[:], in_=sm[:]), D, D)
            self.tt("dve", e[:], e[:], sm[:].unsqueeze(1).to_broadcast([128, 4, 6]), ALU.mult, D, D)
            self.memset("dve", self.lbt[:, 0, :], 0.0, D)
            for l in range(1, 4):
                self.tt("dve", self.lbt[:, l, :], self.lbt[:, l - 1, :], e[:, l, :], ALU.add, D, D)
            self.P.barrier()

    def odd_mixer(self, l, s):
        j = l // 2
        NR = self.NR
        A = lambda k, r: self.amix[:, (l * 8 + k) * NR + r:(l * 8 + k) * NR + r + 1]
        B = lambda k, r: self.mods[:, (l * 48 + k) * NR + r:(l * 48 + k) * NR + r + 1]
        with contextlib.ExitStack() as st0:
            with contextlib.ExitStack() as st:
                self.rms_mod(st, self.x, self.hx, A, B, s)
            self.P.barrier()
            mix = self.sb(st0, "mixo", [128, 8, T], BF16)
            which = getattr(self, "odd_parts", ("hgrn", "s5"))
            if "hgrn" in which:
                self.hgrn(l, s, mix)
            else:
                self.memset("pool", mix[:, 0:6, :], 0.0, [mix.d()])
            if "s5" in which:
                self.s5(l, s, mix)
            else:
                self.memset("pool", mix[:, 6:8, :], 0.0, [mix.d()])
            self.dump(mix, 0, 8, 0)
            self.outproj(lambda oc: self.odout_d[j, oc], 8, mix, l, s, 16)

    def scan(self, out, d0, d1, init, reads, writes):
        return self.P.op("dve", lambda e: e.tensor_tensor_scan(out=out, data0=d0, data1=d1, initial=init, op0=ALU.mult, op1=ALU.add),
                         reads, writes)

    def hgrn(self, l, s, mix):
        j = l // 2
        po = self.po
        hgm = lambda d: self.cst[:, 512 + 128 * d:640 + 128 * d]
        with contextlib.ExitStack() as st:
            lbm = self.sb(st, "lbm", [128, 6, 2], F32)
            rmask_t = self.sb(st, "rmask", [128, T + 32], BF16)
            self.rmask = rmask_t
            self.memset("pool", rmask_t[:], 1.0, [self.cD])
            self.memset("pool", rmask_t[:].rearrange("p (c i) -> p c i", i=32)[:, :, 0:1], 0.0, [self.cD])
            wq = self.sb(st, "hwq", [128, 8, 128], BF16)
            wf = [self.sb(st, "hwf%d" % i, [128, 8, 128], BF16) for i in range(2)]
            wv = self.sb(st, "hwv", [128, 8, 128], BF16)
            vtok = self.sb(st, "vtok", [128, 18, 128], BF16)
            qt = [self.sb(st, "qt%d" % d, [128, T], BF16) for d in range(2)]
            kt = [self.sb(st, "kt%d" % d, [128, T], BF16) for d in range(2)]
            elast = [self.sb(st, "elast%d" % d, [128, 72], F32) for d in range(2)]
            eprev = [self.sb(st, "eprev%d" % d, [128, 72], F32) for d in range(2)]
            R = [self.sb(st, "R%d" % d, [128, 128], F32) for d in range(2)]
            Rb = [self.sb(st, "Rb%d" % d, [128, 128], BF16) for d in range(2)]
            qsl = self.sb(st, "qsl", [128, 512], F32)
            tA = [self.sb(st, "htA", [128, 512], F32)] * 2
            tB = [self.sb(st, "htB", [128, 512], F32)] * 2
            tC = [self.sb(st, "htC", [128, 512], F32)] * 2
            ktok = [self.sb(st, "ktok%d" % i, [128, 128], BF16) for i in range(2)]
            ktok2 = [self.sb(st, "ktokm%d" % i, [128, 128], BF16) for i in range(2)]
            attm = [self.sb(st, "attm%d" % i, [128, 128], BF16) for i in range(2)]
            self.ts("dve", lbm[:, :, 0], self.lbt[:, l, :], -1.0, 1.0, ALU.mult, ALU.add, [self.cD], [lbm.d()])
            self.ts("dve", lbm[:, :, 1], lbm[:, :, 0], -1.0, None, ALU.mult, None, [lbm.d()], [lbm.d()])
            self.memset("pool", mix[:, 0:6, :], 0.0, [mix.d()])
            it = 0
            for hh in range(6):
                oml = lbm[:, hh, 0:1]
                noml = lbm[:, hh, 1:2]
                lb = self.lbt[:, l, hh:hh + 1]
                self.dma("pool", wq[:], self.odin_d[j, hh], (), [wq.d()])
                self.dma("pool", wf[0][:], self.odin_d[j, 6 + hh], (), [wf[0].d()])
                self.dma("pool", wf[1][:], self.odin_d[j, 12 + hh], (), [wf[1].d()])
                self.dma("pool", wv[:], self.odv_d[j, :, :, hh * 128:(hh + 1) * 128], (), [wv.d()])
                c = 0
                while c < 18:
                    nq = min(4, 18 - c)
                    ps = self.psum()
                    for q in range(nq):
                        for k in range(8):
                            self.mm(ps[:, q * 128:(q + 1) * 128], self.hx[:, k, (c + q) * 128:(c + q + 1) * 128], wv[:, k, :], k == 0, k == 7,
                                    [self.hx.d(), wv.d()], [ps.d()])
                    self.cp("act", vtok[:, c:c + nq, :], ps[:, 0:nq * 128].rearrange("p (q f) -> p q f", f=128), [ps.d()], [vtok.d()])
                    c += nq
                for bi, (t0, n) in enumerate(BLKS):
                    ps = self.psum()
                    for k in range(8):
                        self.mm(ps[:, 0:n], wq[:, k, :], self.hx[:, k, t0:t0 + n], k == 0, k == 7, [wq.d(), self.hx.d()], [ps.d()])
                    self.act(qsl[:, 0:n], ps[:, 0:n], AF.Silu, [ps.d()], [qsl.d()])
                    for d in range(2):
                        a_, b_, c_ = tA[d], tB[d], tC[d]
                        ps = self.psum()
                        for k in range(8):
                            self.mm(ps[:, 0:n], wf[d][:, k, :], self.hx[:, k, t0:t0 + n], k == 0, k == 7, [wf[d].d(), self.hx.d()], [ps.d()])
                        self.act(a_[:, 0:n], ps[:, 0:n], AF.Sigmoid, [ps.d()], [a_.d()])
                        self.act(b_[:, 0:n], a_[:, 0:n], AF.Ln, [a_.d(), lbm.d(), self.cD], [b_.d()], scale=oml, bias=lb)
                        self.ts("dve", a_[:, 0:n], a_[:, 0:n], noml, oml, ALU.mult, ALU.add, [a_.d(), lbm.d()], [a_.d()])
                        if d == 0:
                            self.scan(c_[:, 0:n], self.rmask[:, t0:t0 + n], b_[:, 0:n], 0.0, [b_.d(), self.cD], [c_.d()])
                        else:
                            self.scan(rev(c_[:, 0:n]), rev(self.rmask[:, t0 + 1:t0 + n + 1]), rev(b_[:, 0:n]), 0.0, [b_.d(), self.cD], [c_.d()])
                        self.act(b_[:, 0:n], c_[:, 0:n], AF.Exp, [c_.d()], [b_.d()])
                        lastpos = 31 if d == 0 else 0
                        self.cp("dve", elast[d][:, t0 // 32:(t0 + n) // 32], b_[:, 0:n].rearrange("p (c i) -> p c i", i=32)[:, :, lastpos],
                                [b_.d()], [elast[d].d()])
                        self.tt("dve", qt[d][:, t0:t0 + n], qsl[:, 0:n], b_[:, 0:n], ALU.mult, [qsl.d(), b_.d()], [qt[d].d()])
                        self.act(c_[:, 0:n], c_[:, 0:n], AF.Exp, [c_.d()], [c_.d()], scale=-1.0)
                        self.tt("dve", kt[d][:, t0:t0 + n], a_[:, 0:n], c_[:, 0:n], ALU.mult, [a_.d(), c_.d()], [kt[d].d()])
                for d in range(2):
                    self.memset("dve", eprev[d][:], 1.0, [eprev[d].d()])
                    if d == 0:
                        self.cp("dve", eprev[d][:, 1:72], elast[d][:, 0:71], [elast[d].d()], [eprev[d].d()])
                    else:
                        self.cp("dve", eprev[d][:, 0:7], elast[d][:, 1:8], [elast[d].d()], [eprev[d].d()])
                        self.cp("dve", eprev[d][:, 8:71], elast[d][:, 9:72], [elast[d].d()], [eprev[d].d()])
                        self.cp("dve", eprev[d][:, 71:72], elast[d][:, 0:1], [elast[d].d()], [eprev[d].d()])
                    self.memset("pool", R[d][:], 0.0, [R[d].d()])
                    self.memset("pool", Rb[d][:], 0.0, [Rb[d].d()])
                blocks = [list(range(18)), [1, 0] + list(range(17, 1, -1))]
                for step in range(18):
                    for d in range(2):
                        bk = blocks[d][step]
                        b = it % 2
                        it += 1
                        ts_ = slice(bk * 128, (bk + 1) * 128)
                        ps_a = self.psum()
                        self.mm(ps_a[:, 0:128], kt[d][:, ts_], qt[d][:, ts_], True, True, [kt[d].d(), qt[d].d()], [ps_a.d()])
                        self.tt("dve", attm[b][:], ps_a[:, 0:128], hgm(d), ALU.mult, [ps_a.d(), self.cD], [attm[b].d()])
                        ps_t = self.psum()
                        pb = ps_t[:].bitcast(BF16)
                        self.pe_T(pb[:, 0:128], kt[d][:, ts_], [kt[d].d()], [ps_t.d()])
                        self.cp("act", ktok[b][:], pb[:, 0:128], [ps_t.d()], [ktok[b].d()])
                        self.cp("act", ktok2[b][64:128, :], pb[64:128, 0:128], [ps_t.d()], [ktok2[b].d()])
                        self.memset("pool", ktok2[b][64:96, :], 0.0, [ktok2[b].d()])
                        ps_o = self.psum()
                        self.mm(ps_o[:, 0:128], vtok[:, bk, :], attm[b][:], True, False, [vtok.d(), attm[b].d()], [ps_o.d()])
                        corder = [0, 1, 2, 3] if d == 0 else [3, 2, 1, 0]
                        for ci, cc in enumerate(corder):
                            cg = bk * 4 + cc
                            cs_ = slice(cg * 32, (cg + 1) * 32)
                            self.mm(ps_o[:, cc * 32:(cc + 1) * 32], Rb[d][:], qt[d][:, cs_], False, ci == 3, [Rb[d].d(), qt[d].d()], [ps_o.d()])
                            ps_u = self.psum()
                            if cc < 3:
                                self.mm(ps_u[:, 0:128], ktok[b][cc * 32:(cc + 1) * 32, :], vtok[cc * 32:(cc + 1) * 32, bk, :], True, True,
                                        [ktok[b].d(), vtok.d()], [ps_u.d()])
                            else:
                                self.mm(ps_u[:, 0:128], ktok2[b][64:128, :], vtok[64:128, bk, :], True, True,
                                        [ktok2[b].d(), vtok.d()], [ps_u.d()])
                            self.stt("dve", R[d][:], R[d][:], eprev[d][:, cg:cg + 1], ps_u[:, 0:128], ALU.mult, ALU.add,
                                     [R[d].d(), eprev[d].d(), ps_u.d()], [R[d].d()])
                            self.act(Rb[d][:], R[d][:], AF.Copy, [R[d].d(), elast[d].d()], [Rb[d].d()], scale=elast[d][:, cg:cg + 1])
                        self.tt("dve", mix[:, hh, ts_], mix[:, hh, ts_], ps_o[:, 0:128], ALU.add, [mix.d(), ps_o.d()], [mix.d()])
            self.P.barrier()
        with contextlib.ExitStack() as st:
            go = po["hgn%d" % j][0]
            self.rms_mod(st, mix, mix, lambda k, r: self.partile[:, go + k:go + k + 1], None, s, nch=6, pergroup=True)
            self.P.barrier()
        with contextlib.ExitStack() as st:
            wg_ = [self.sb(st, "hwg%d" % i, [128, 8, 128], BF16) for i in range(2)]
            sz = [self.sb(st, "hsz%d" % i, [128, 512], BF16) for i in range(2)]
            for hh in range(6):
                w_ = wg_[hh % 2]
                self.dma("pool", w_[:], self.odin_d[j, 18 + hh], (), [w_.d()])
                for bi, (t0, n) in enumerate(BLKS):
                    ps = self.psum()
                    for k in range(8):
                        self.mm(ps[:, 0:n], w_[:, k, :], self.hx[:, k, t0:t0 + n], k == 0, k == 7, [w_.d(), self.hx.d()], [ps.d()])
                    z_ = sz[bi % 2]
                    self.act(z_[:, 0:n], ps[:, 0:n], AF.Silu, [ps.d()], [z_.d()])
                    self.tt("dve", mix[:, hh, t0:t0 + n], mix[:, hh, t0:t0 + n], z_[:, 0:n], ALU.mult, [mix.d(), z_.d()], [mix.d()])
            self.P.barrier()

    def cexp_small(self, st, name, lr, li, ls, shape, D):
        mk = lambda n: self.sb(st, name + n, shape, F32)
        step, c, sn, mag, t1, t2 = mk("st"), mk("c"), mk("s"), mk("m"), mk("t1"), mk("t2")
        dd = [step.d()]
        self.act(step[:], ls, AF.Exp, D, dd)
        self.tt("dve", mag[:], lr, step[:], ALU.mult, D + dd, dd)
        self.act(mag[:], mag[:], AF.Exp, dd, dd)
        self.tt("dve", t1[:], li, step[:], ALU.mult, D + dd, dd)
        self.act(sn[:], t1[:], AF.Sin, dd, dd, scale=1.0 / 16.0)
        self.ts("dve", t2[:], t1[:], 1.0 / 16.0, 1.5707963267948966, ALU.mult, ALU.add, dd, dd)
        self.act(c[:], t2[:], AF.Sin, dd, dd)
        for _ in range(4):
            self.tt("dve", t1[:], c[:], c[:], ALU.mult, dd, dd)
            self.tt("dve", t2[:], sn[:], sn[:], ALU.mult, dd, dd)
            self.tt("dve", sn[:], sn[:], c[:], ALU.mult, dd, dd)
            self.ts("dve", sn[:], sn[:], 2.0, None, ALU.mult, None, dd, dd)
            self.tt("dve", c[:], t1[:], t2[:], ALU.subtract, dd, dd)
        for t_ in (c, sn, mag, t1, t2):
            t_.deps[None] = step.d()
        return c, sn, mag, step, t1, t2

    def s5(self, l, s, mix):
        j = l // 2
        po = self.po
        L = 256
        NTC = T // L
        D = [self.cD]
        with contextlib.ExitStack() as st:
            ufm = self.sb(st, "ufm", [128, 2, T], BF16)
            wu = self.sb(st, "s5wu", [128, 8, 128], BF16)
            for c in range(2):
                self.dma("pool", wu[:], self.odin_d[j, 24 + c], (), [wu.d()])
                for (t0, n) in BLKS:
                    ps = self.psum()
                    for k in range(8):
                        self.mm(ps[:, 0:n], wu[:, k, :], self.hx[:, k, t0:t0 + n], k == 0, k == 7, [wu.d(), self.hx.d()], [ps.d()])
                    self.cp("act", ufm[:, c, t0:t0 + n], ps[:, 0:n], [ps.d()], [ufm.d()])
            so = po["s5p%d" % j][0]
            pc, psn, pmag, _, _, _ = self.cexp_small(st, "sp", self.partile[:, so:so + 16], self.partile[:, so + 16:so + 32],
                                                    self.partile[:, so + 32:so + 48], [128, 16], D)
            pdep = [pc.d()]
            E = self.sb(st, "s5E", [128, 4, 2], F32)
            for d in range(2):
                for c in range(2):
                    with contextlib.ExitStack() as st2:
                        tabc = self.sb(st2, "tabc", [128, 4, L], F32)
                        tabs = self.sb(st2, "tabs", [128, 4, L], F32)
                        BM = self.sb(st2, "BM", [128, 4, 2, 128], BF16)
                        CM = self.sb(st2, "CM", [128, 4, 2, 128], BF16)
                        tdep = [tabc.d()]
                        tmp1 = self.sb(st2, "s5tm", [128, 128], F32)
                        for q4 in range(4):
                            q = c * 4 + q4
                            col = d * 8 + q
                            self.cp("dve", tabc[:, q4, 0:1], pc[:, col:col + 1], pdep, tdep)
                            self.cp("dve", tabs[:, q4, 0:1], psn[:, col:col + 1], pdep, tdep)
                            span = 1
                            while span < L:
                                cm_ = tabc[:, q4, span - 1:span]
                                sm_ = tabs[:, q4, span - 1:span]
                                lo, hi = slice(0, span), slice(span, 2 * span)
                                self.ts("dve", tmp1[:, 0:span], tabs[:, q4, lo], sm_, None, ALU.mult, None, tdep, [tmp1.d()])
                                self.stt("dve", tabc[:, q4, hi], tabc[:, q4, lo], cm_, tmp1[:, 0:span], ALU.mult, ALU.subtract, tdep + [tmp1.d()], tdep)
                                self.ts("dve", tmp1[:, 0:span], tabs[:, q4, lo], cm_, None, ALU.mult, None, tdep, [tmp1.d()])
                                self.stt("dve", tabs[:, q4, hi], tabc[:, q4, lo], sm_, tmp1[:, 0:span], ALU.mult, ALU.add, tdep + [tmp1.d()], tdep)
                                span *= 2
                            with contextlib.ExitStack() as st3:
                                rows = self.sb(st3, "s5rows", [128, 3, 128], F32)
                                bpad = self.sb(st3, "s5bp", [128, 2, 128], F32)
                                self.dma("pool", rows[:], self.s5row_d[j, d, :, q, :].unsqueeze(0).to_broadcast([128, 3, 128]), (), [rows.d()])
                                self.dma("act", bpad[:], self.s5b_d[j, q].rearrange("r p c -> p r c"), (), [bpad.d()])
                                self.dma("pool", CM[:, q4, :, :], self.s5c_d[j, d, q].rearrange("r p c -> p r c"), (), [CM.d()])
                                rd = [rows.d()]
                                rc, rs_, rmag, rstep, t1, t2 = self.cexp_small(st3, "sr", rows[:, 0, :], rows[:, 1, :], rows[:, 2, :], [128, 128], rd)
                                w = [rc.d()]
                                lr_, li_ = rows[:, 0, :], rows[:, 1, :]
                                ar, ai, den, zr, zi = rc, rs_, rstep, t1, t2
                                self.tt("dve", ar[:], rc[:], rmag[:], ALU.mult, w, w)
                                self.tt("dve", ai[:], rs_[:], rmag[:], ALU.mult, w, w)
                                self.tt("dve", den[:], lr_, lr_, ALU.mult, rd + w, w)
                                self.tt("dve", rmag[:], li_, li_, ALU.mult, rd + w, w)
                                self.tt("dve", den[:], den[:], rmag[:], ALU.add, w, w)
                                self.P.op("dve", (lambda t_: lambda e: e.reciprocal(out=t_[:], in_=t_[:]))(den), w, w)
                                self.ts("dve", ar[:], ar[:], -1.0, None, ALU.add, None, w, w)
                                self.tt("dve", zr[:], ar[:], lr_, ALU.mult, rd + w, w)
                                self.tt("dve", rmag[:], ai[:], li_, ALU.mult, rd + w, w)
                                self.tt("dve", zr[:], zr[:], rmag[:], ALU.add, w, w)
                                self.tt("dve", zr[:], zr[:], den[:], ALU.mult, w, w)
                                self.tt("dve", zi[:], ai[:], lr_, ALU.mult, rd + w, w)
                                self.tt("dve", rmag[:], ar[:], li_, ALU.mult, rd + w, w)
                                self.tt("dve", zi[:], zi[:], rmag[:], ALU.subtract, w, w)
                                self.tt("dve", zi[:], zi[:], den[:], ALU.mult, w, w)
                                bd = [bpad.d()]
                                self.tt("dve", ar[:], zr[:], bpad[:, 0, :], ALU.mult, w + bd, w)
                                self.tt("dve", ai[:], zi[:], bpad[:, 1, :], ALU.mult, w + bd, w)
                                self.tt("dve", BM[:, q4, 0, :], ar[:], ai[:], ALU.subtract, w, [BM.d()])
                                self.tt("dve", ar[:], zr[:], bpad[:, 1, :], ALU.mult, w + bd, w)
                                self.tt("dve", ai[:], zi[:], bpad[:, 0, :], ALU.mult, w + bd, w)
                                self.tt("dve", BM[:, q4, 1, :], ar[:], ai[:], ALU.add, w, [BM.d()])
                                self.ts("dve", CM[:, q4, 1, :], CM[:, q4, 1, :], -1.0, None, ALU.mult, None, [CM.d()], [CM.d()])
                                self.P.barrier()
                        self.memset("dve", E[:], 0.0, [E.d()])
                        zk = [self.sb(st2, "s5zk%d" % i, [128, 2, L], F32) for i in range(4)]
                        h32 = [self.sb(st2, "s5h%d" % i, [128, 2, L], F32) for i in range(4)]
                        tt_ = self.sb(st2, "s5t", [128, L], F32)
                        hb = [self.sb(st2, "s5hb%d" % i, [128, 4, 2, L], BF16) for i in range(2)]
                        order = list(range(NTC)) if d == 0 else [0] + list(range(NTC - 1, 0, -1))
                        fl = (lambda a: a) if d == 0 else rev
                        lastpos = L - 1 if d == 0 else 0
                        for ti, tc in enumerate(order):
                            t0 = tc * L
                            hbt = hb[ti % 2]
                            px = {}
                            for q4 in range(4):
                                ps_x = self.psum()
                                px[q4] = ps_x
                                self.mm(ps_x[:, 0:L], BM[:, q4, 0, :], ufm[:, c, t0:t0 + L], True, True, [BM.d(), ufm.d()], [ps_x.d()])
                                self.mm(ps_x[:, L:2 * L], BM[:, q4, 1, :], ufm[:, c, t0:t0 + L], True, True, [BM.d(), ufm.d()], [ps_x.d()])
                            td = [tt_.d()]
                            for q4 in range(4):
                                xr, xi = fl(px[q4][:, 0:L]), fl(px[q4][:, L:2 * L])
                                tcq, tsq = tabc[:, q4, :], tabs[:, q4, :]
                                z = zk[q4]
                                zd = [z.d()]
                                pd_ = [px[q4].d()]
                                self.tt("dve", z[:, 0, :], tcq, xr, ALU.mult, tdep + pd_, zd)
                                self.tt("dve", tt_[:], tsq, xi, ALU.mult, tdep + pd_, td)
                                self.tt("dve", z[:, 0, :], z[:, 0, :], tt_[:], ALU.add, zd + td, zd)
                                self.tt("dve", z[:, 1, :], tcq, xi, ALU.mult, tdep + pd_, zd)
                                self.tt("dve", tt_[:], tsq, xr, ALU.mult, tdep + pd_, td)
                                self.tt("dve", z[:, 1, :], z[:, 1, :], tt_[:], ALU.subtract, zd + td, zd)
                            for q4 in range(4):
                                col = d * 8 + c * 4 + q4
                                z = zk[q4]
                                zd = [z.d()]
                                mg = pmag[:, col:col + 1].to_broadcast([128, L])
                                self.scan(z[:, 0, :], mg, z[:, 0, :], E[:, q4, 0:1], zd + [E.d()] + pdep, zd)
                                self.scan(z[:, 1, :], mg, z[:, 1, :], E[:, q4, 1:2], zd + [E.d()] + pdep, zd)
                            for q4 in range(4):
                                tcq, tsq = tabc[:, q4, :], tabs[:, q4, :]
                                z, hh_ = zk[q4], h32[q4]
                                zd, hd_ = [z.d()], [hh_.d()]
                                self.tt("dve", fl(hh_[:, 0, :]), tcq, z[:, 0, :], ALU.mult, tdep + zd, hd_)
                                self.tt("dve", tt_[:], tsq, z[:, 1, :], ALU.mult, tdep + zd, td)
                                self.tt("dve", fl(hh_[:, 0, :]), fl(hh_[:, 0, :]), tt_[:], ALU.subtract, hd_ + td, hd_)
                                self.tt("dve", fl(hh_[:, 1, :]), tsq, z[:, 0, :], ALU.mult, tdep + zd, hd_)
                                self.tt("dve", tt_[:], tcq, z[:, 1, :], ALU.mult, tdep + zd, td)
                                self.tt("dve", fl(hh_[:, 1, :]), fl(hh_[:, 1, :]), tt_[:], ALU.add, hd_ + td, hd_)
                                self.cp("dve", E[:, q4, :], hh_[:, :, lastpos], hd_, [E.d()])
                            for q4 in range(4):
                                self.cp("act", hbt[:, q4, :, :], h32[q4][:, :, :], [h32[q4].d()], [hbt.d()])
                            ps_y = self.psum()
                            for q4 in range(4):
                                for ri in range(2):
                                    self.mm(ps_y[:, 0:L], CM[:, q4, ri, :], hbt[:, q4, ri, :], q4 == 0 and ri == 0, q4 == 3 and ri == 1,
                                            [CM.d(), hbt.d()], [ps_y.d()])
                            if d == 0:
                                self.cp("act", mix[:, 6 + c, t0:t0 + L], ps_y[:, 0:L], [ps_y.d()], [mix.d()])
                            else:
                                self.tt("dve", mix[:, 6 + c, t0:t0 + L], mix[:, 6 + c, t0:t0 + L], ps_y[:, 0:L], ALU.add, [mix.d(), ps_y.d()], [mix.d()])
                        self.P.barrier()
            do = po["s5d%d" % j][0]
            gb = po["glub%d" % j][0]
            wgl = self.sb(st, "wglu", [128, 2, 256], BF16)
            yt = [self.sb(st, "s5yt%d" % i, [128, 512], F32) for i in range(2)]
            self.dma("pool", wgl[:], self.gluw_d[j], (), [wgl.d()])
            for c in range(2):
                for bi, (t0, n) in enumerate(BLKS):
                    y_ = yt[bi % 2]
                    self.stt("dve", y_[:, 0:n], ufm[:, c, t0:t0 + n], self.partile[:, do + c:do + c + 1], mix[:, 6 + c, t0:t0 + n], ALU.mult, ALU.add,
                             [ufm.d(), mix.d(), self.cD], [y_.d()])
                    self.act(ufm[:, c, t0:t0 + n], y_[:, 0:n], AF.Gelu, [y_.d()], [ufm.d()])
            for c in range(2):
                for bi, (t0, n) in enumerate(BLKS):
                    ps = self.psum()
                    for k in range(2):
                        self.mm(ps[:, 0:n], wgl[:, k, c * 128:(c + 1) * 128], ufm[:, k, t0:t0 + n], k == 0, k == 1, [wgl.d(), ufm.d()], [ps.d()])
                    y_ = yt[bi % 2]
                    self.act(y_[:, 0:n], ps[:, 0:n], AF.Sigmoid, [ps.d(), self.cD], [y_.d()], bias=self.partile[:, gb + c:gb + c + 1])
                    self.tt("dve", mix[:, 6 + c, t0:t0 + n], ufm[:, c, t0:t0 + n], y_[:, 0:n], ALU.mult, [ufm.d(), y_.d()], [mix.d()])
            self.P.barrier()

    def final_out(self, s):
        NR = self.NR
        if self.final:
            go, _ = self.po["fng"]
            A = lambda k, r: self.partile[:, go + k:go + k + 1]
            with contextlib.ExitStack() as st:
                self.rms_mod(st, self.x, self.x, A, None, s)
                self.out_ops.append(self.dma("sp", self.yout[s, :, 0:4, :], self.x[:, 0:4, CTX:T], [self.x.d()], ()))
                self.out_ops.append(self.dma("act", self.yout[s, :, 4:8, :], self.x[:, 4:8, CTX:T], [self.x.d()], ()))
                self.P.barrier()
        else:
            self.out_ops.append(self.dma("sp", self.yout[s, :, 0:4, :], self.x[:, 0:4, CTX:T], [self.x.d()], ()))
            self.out_ops.append(self.dma("act", self.yout[s, :, 4:8, :], self.x[:, 4:8, CTX:T], [self.x.d()], ()))


def make_consts():
    c = np.zeros((128, 768), np.float32)
    c[:, 0:128] = np.eye(128, dtype=np.float32)
    i = np.arange(128)
    c[:, 128:256] = (i[:, None] <= i[None, :]).astype(np.float32)
    c[:, 256:384] = (i[:, None] >= i[None, :]).astype(np.float32)
    c[:, 384:512] = 1.0
    same = (i[:, None] // 32) == (i[None, :] // 32)
    c[:, 512:640] = (same & (i[:, None] <= i[None, :])).astype(np.float32)
    c[:, 640:768] = (same & (i[:, None] >= i[None, :])).astype(np.float32)
    return c


def kernel(**inputs):
    nseq = 4
    inp = {k: np.asarray(v) for k, v in inputs.items()}
    phases = []
    for l in range(4):
        phases += [("mix", l), ("ffn", l)]
    kb = K(nseq, phases, final=True)
    nc = kb.build()
    sh = host_prep(inp)
    sh["cst"] = make_consts()
    in_maps = []
    for c in range(NCORES):
        m = dict(sh)
        m.update(core_inputs(inp, c, nseq))
        in_maps.append(m)
    res = run_bass_kernel_spmd(nc, in_maps, core_ids=list(range(NCORES)))
    outs = []
    for c in range(NCORES):
        y = res.results[c]["yout"]
        outs.append(y.transpose(0, 3, 2, 1).reshape(nseq, 2048, 1024))
    return np.ascontiguousarray(np.concatenate(outs, axis=0)).astype(np.float32)
```

```python
import contextlib
import numpy as np
import concourse.bass as bass
import concourse.mybir as mybir
from concourse.ap import AP
from concourse.bass_utils import run_bass_kernel_spmd

F32 = mybir.dt.float32
BF16 = mybir.dt.bfloat16
AF = mybir.ActivationFunctionType
ALU = mybir.AluOpType

T = 2304
CTX = 256
BLKS = [(0, 256), (256, 512), (768, 512), (1280, 512), (1792, 512)]
NCORES = 8
EPS = 1e-6


class Dep:
    __slots__ = ("w", "r", "rd", "const")

    def __init__(self, const=False):
        self.w = None
        self.r = {}
        self.rd = []
        self.const = const


class Op:
    __slots__ = ("eng", "fn", "deps", "marked", "ev", "dma", "idx", "bar", "epoch")


class Prog:
    DMA_SEMS = {"sp": 6, "act": 6, "pool": 24}
    ENGS = ("pe", "act", "dve", "pool", "sp")

    def __init__(self, nc):
        self.nc = nc
        self.ops = []
        self.last = {}
        self.pending_dma = []
        self.nbar = 0

    def _new(self, eng, fn, dma):
        o = Op()
        o.eng = eng
        o.fn = fn
        o.dma = dma
        o.marked = dma
        o.ev = None
        o.bar = 0
        o.epoch = -1
        o.idx = len(self.ops)
        self.ops.append(o)
        return o

    def op(self, eng, fn, reads=(), writes=(), dma=False, pe_acc=False):
        deps = set()
        for d in reads:
            if d.w is not None:
                deps.add(d.w)
        for d in writes:
            if d.w is not None:
                if not (pe_acc and d.w.eng == "pe" and not d.w.dma):
                    deps.add(d.w)
            for r in d.r.values():
                deps.add(r)
            for r in d.rd:
                deps.add(r)
        o = self._new(eng, fn, dma)
        o.deps = deps
        for d in reads:
            if not d.const:
                if dma:
                    d.rd.append(o)
                else:
                    d.r[eng] = o
        for d in writes:
            d.w = o
            d.r = {}
            d.rd = []
        if dma:
            self.pending_dma.append(o)
        else:
            self.last[eng] = o
        return o

    def barrier(self):
        deps = set(self.last.values()) | set(self.pending_dma)
        self.pending_dma = []
        self.last = {}
        self.nbar += 1
        for e in self.ENGS:
            o = self._new(e, None, False)
            o.deps = set(deps)
            o.bar = self.nbar

    def emit(self, final_deps):
        nc = self.nc
        engs = {"pe": nc.tensor, "act": nc.scalar, "dve": nc.vector, "pool": nc.gpsimd, "sp": nc.sync}
        fin = self._new("sp", None, False)
        fin.deps = set(final_deps)
        for o in self.ops:
            for d in o.deps:
                d.marked = True
        cnt = {e: 0 for e in engs}
        dma_rr = {e: 0 for e in engs}
        dma_cnt = {}
        seen = {e: {} for e in engs}
        per_eng = {e: [] for e in engs}
        epoch = 0
        maxv = 0
        nb_in_group = 0
        for o in self.ops:
            mw = {}
            o.epoch = epoch
            if o.dma:
                k = dma_rr[o.eng] % self.DMA_SEMS[o.eng]
                dma_rr[o.eng] += 1
                sk_own = ("dma", o.eng, k)
                prev = dma_cnt.get(sk_own, 0)
                if prev > 0 and seen[o.eng].get(sk_own, 0) < prev:
                    mw[sk_own] = prev
                    seen[o.eng][sk_own] = prev
                dma_cnt[sk_own] = prev + 16
                o.ev = (sk_own, prev + 16)
                maxv = max(maxv, prev + 16)
            elif o.marked:
                cnt[o.eng] += 1
                o.ev = (("eng", o.eng), cnt[o.eng])
                maxv = max(maxv, cnt[o.eng])
            for d in o.deps:
                if d.epoch != epoch:
                    continue
                sk, v = d.ev
                if seen[o.eng].get(sk, 0) < v:
                    seen[o.eng][sk] = v
                    mw[sk] = max(mw.get(sk, 0), v)
            per_eng[o.eng].append((o, list(mw.items())))
            if o.bar:
                nb_in_group += 1
                if nb_in_group == len(self.ENGS):
                    nb_in_group = 0
                    epoch += 1
                    cnt = {e: 0 for e in engs}
                    seen = {e: {sk: v for sk, v in seen[e].items() if sk[0] == "dma"} for e in engs}
        assert maxv < 8000, maxv
        self.stats = {e: len(per_eng[e]) for e in engs}
        self.stats["sem_maxv"] = maxv
        self.stats["nbar"] = self.nbar
        with contextlib.ExitStack() as st:
            sems = {}
            for e in engs:
                sems[("eng", e)] = st.enter_context(nc.semaphore("s_" + e))
            for e in ("sp", "act", "pool"):
                for k in range(self.DMA_SEMS[e]):
                    sems[("dma", e, k)] = st.enter_context(nc.semaphore("d_%s%d" % (e, k)))
            bsemA = st.enter_context(nc.semaphore("barA"))
            bsemB = st.enter_context(nc.semaphore("barB"))
            block = st.enter_context(nc.Block())
            NE = len(self.ENGS)

            def mk(ename):
                def body(eng):
                    for o, waits in per_eng[ename]:
                        for sk, v in waits:
                            eng.wait_ge(sems[sk], v)
                        if o.bar:
                            eng.sem_inc(bsemA, 1)
                            if ename == "sp":
                                eng.wait_ge(bsemA, NE * o.bar)
                                for sk_, sm in sems.items():
                                    if sk_[0] == "eng":
                                        eng.sem_clear(sm)
                                eng.sem_inc(bsemB, 1)
                            eng.wait_ge(bsemB, o.bar)
                            continue
                        if o.fn is None:
                            continue
                        ins = o.fn(eng)
                        if o.dma:
                            ins.then_inc(sems[o.ev[0]], 16)
                        elif o.marked:
                            ins.then_inc(sems[("eng", ename)], 1)
                return body

            block.tensor(mk("pe"))
            block.scalar(mk("act"))
            block.vector(mk("dve"))
            block.gpsimd(mk("pool"))
            block.sync(mk("sp"))


class Tile:
    def __init__(self, t):
        self.t = t
        self.deps = {}

    def d(self, key=None):
        if key not in self.deps:
            self.deps[key] = Dep()
        return self.deps[key]

    def __getitem__(self, idx):
        return self.t[idx]


def rev(ap):
    apl = [list(x) for x in ap.ap]
    n = apl[-1][1]
    off = ap.offset + (n - 1) * apl[-1][0]
    apl[-1][0] = -apl[-1][0]
    return AP(ap.tensor, off, apl)


def fm_vec(v):
    v = np.asarray(v, np.float32).reshape(-1, 128)
    return np.ascontiguousarray(v.T)


def w_colchunks(w, nk):
    K, N = w.shape
    return np.ascontiguousarray(w.reshape(nk, 128, N // 128, 128).transpose(2, 1, 0, 3))


def w_rows(w, nk):
    K, N = w.shape
    return np.ascontiguousarray(w.reshape(nk, 128, N).transpose(1, 0, 2))


class ParPack:
    def __init__(self):
        self.cols = []
        self.off = {}
        self.n = 0

    def add(self, name, arr):
        arr = np.asarray(arr, np.float32)
        arr = arr.reshape(arr.shape[0], -1)
        if arr.shape[0] < 128:
            arr = np.concatenate([arr, np.zeros((128 - arr.shape[0], arr.shape[1]), np.float32)], 0)
        self.off[name] = (self.n, arr.shape[1])
        self.cols.append(arr)
        self.n += arr.shape[1]

    def pack(self):
        return np.ascontiguousarray(np.concatenate(self.cols, axis=1))


def pack_params(inp, off_only=False):
    pp = ParPack()
    z = (lambda *s: np.zeros(s, np.float32))
    g = (lambda k: inp[k]) if not off_only else None
    for l in range(4):
        pp.add("nmg%d" % l, fm_vec(g("norm_mix_g")[l]) if g else z(128, 8))
        pp.add("nfg%d" % l, fm_vec(g("norm_ffn_g")[l]) if g else z(128, 8))
        pp.add("bmod%d" % l, fm_vec(g("b_mod")[l]) if g else z(128, 48))
    pp.add("fng", fm_vec(g("final_norm_g")) if g else z(128, 8))
    for j in range(2):
        if g:
            pp.add("scw%d" % j, g("ssd_conv_w")[j].reshape(4, 12, 128).transpose(2, 1, 0))
            pp.add("scb%d" % j, fm_vec(g("ssd_conv_b")[j]))
            pp.add("sng%d" % j, fm_vec(g("ssd_norm_g")[j]))
            pp.add("sd%d" % j, fm_vec(np.repeat(g("ssd_d")[j], 64)))
            pp.add("dtb%d" % j, g("ssd_dt_bias")[j].reshape(32, 1))
            pp.add("alog%d" % j, g("ssd_a_log")[j].reshape(32, 1))
            pp.add("dtbrow%d" % j, np.tile(g("ssd_dt_bias")[j].reshape(1, 32), (128, 1)))
            pp.add("alogrow%d" % j, np.tile(g("ssd_a_log")[j].reshape(1, 32), (128, 1)))
            pp.add("lcw%d" % j, g("lru_conv_w")[j].reshape(4, 8, 128).transpose(2, 1, 0))
            pp.add("lcb%d" % j, fm_vec(g("lru_conv_b")[j]))
            pp.add("lba%d" % j, g("lru_b_a")[j].reshape(2, 8, 128).transpose(2, 0, 1))
            pp.add("lbi%d" % j, g("lru_b_i")[j].reshape(2, 8, 128).transpose(2, 0, 1))
            pp.add("llam%d" % j, g("lru_lam")[j].reshape(2, 8, 128).transpose(2, 0, 1))
        else:
            pp.add("scw%d" % j, z(128, 48)); pp.add("scb%d" % j, z(128, 12)); pp.add("sng%d" % j, z(128, 8))
            pp.add("sd%d" % j, z(128, 8)); pp.add("dtb%d" % j, z(128, 1)); pp.add("alog%d" % j, z(128, 1))
            pp.add("dtbrow%d" % j, z(128, 32)); pp.add("alogrow%d" % j, z(128, 32))
            pp.add("lcw%d" % j, z(128, 32)); pp.add("lcb%d" % j, z(128, 8)); pp.add("lba%d" % j, z(128, 16))
            pp.add("lbi%d" % j, z(128, 16)); pp.add("llam%d" % j, z(128, 16))
    if g:
        pp.add("lbl", g("hg_lb_logits").reshape(4, 6, 128).transpose(2, 0, 1))
    else:
        pp.add("lbl", z(128, 24))
    for j in range(2):
        if g:
            pp.add("hgn%d" % j, g("hg_norm_g")[j].reshape(6, 128).T)
            pp.add("s5d%d" % j, fm_vec(g("s5_d")[j]))
            pp.add("glub%d" % j, fm_vec(g("s5_glu_b")[j]))
            sp_ = np.zeros((128, 3, 2, 8), np.float32)
            for gg in range(16):
                sp_[(gg % 2) * 64:(gg % 2) * 64 + 64, 0, :, gg // 2] = g("s5_lam_re")[j][:, gg].T
                sp_[(gg % 2) * 64:(gg % 2) * 64 + 64, 1, :, gg // 2] = g("s5_lam_im")[j][:, gg].T
                sp_[(gg % 2) * 64:(gg % 2) * 64 + 64, 2, :, gg // 2] = g("s5_log_step")[j][:, gg][None, :]
            pp.add("s5p%d" % j, sp_)
        else:
            pp.add("hgn%d" % j, z(128, 6)); pp.add("s5d%d" % j, z(128, 2)); pp.add("glub%d" % j, z(128, 2))
            pp.add("s5p%d" % j, z(128, 48))
    pp.nres = pp.n
    for l in range(4):
        if g:
            cw = g("ffn_conv_w")[l].reshape(9, 22, 128).transpose(2, 1, 0)
            pp.add("fcw%d" % l, cw)
            pp.add("fcb%d" % l, fm_vec(g("ffn_conv_b")[l]))
        else:
            pp.add("fcw%d" % l, z(128, 22 * 9))
            pp.add("fcb%d" % l, z(128, 22))
    return pp


def host_prep(inp, nseq_total=32):
    sh = {}
    sh["par"] = pack_params(inp).pack()
    sh["wmod"] = np.stack([w_rows(inp["w_mod"][l], 8) for l in range(4)])
    sh["ffg"] = np.stack([w_colchunks(inp["ffn_w_gate"][l], 8) for l in range(4)])
    sh["ffu"] = np.stack([w_colchunks(inp["ffn_w_up"][l], 8) for l in range(4)])
    sh["ffd"] = np.stack([w_colchunks(inp["ffn_w_down"][l], 22) for l in range(4)])
    ev = inp["ev_w_in"]
    evc = np.concatenate([ev[:, :, 0:2560], ev[:, :, 2592:4640]], axis=2)
    sh["evin"] = np.stack([w_colchunks(evc[j], 8) for j in range(2)])
    sh["evdt"] = np.stack([w_rows(ev[j][:, 2560:2592], 8) for j in range(2)])
    sh["evout"] = np.stack([w_colchunks(inp["ev_w_out"][j], 16) for j in range(2)])
    la = np.stack([inp["lru_w_a"], inp["lru_w_i"]], axis=1)
    sh["lruw"] = np.ascontiguousarray(la.transpose(0, 4, 1, 2, 3, 5))
    od = inp["od_w_in"]
    odc = np.concatenate([od[:, :, 0:2304], od[:, :, 3072:4096]], axis=2)
    sh["odin"] = np.stack([w_colchunks(odc[j], 8) for j in range(2)])
    sh["odv"] = np.stack([w_rows(od[j][:, 2304:3072], 8) for j in range(2)])
    sh["odout"] = np.stack([w_colchunks(inp["od_w_out"][j], 8) for j in range(2)])
    sh["gluw"] = np.stack([w_rows(inp["s5_glu_w"][j], 2) for j in range(2)])
    s5b = np.zeros((2, 8, 2, 128, 128), np.float32)
    s5c = np.zeros((2, 2, 8, 2, 128, 128), np.float32)
    s5row = np.zeros((2, 2, 3, 8, 128), np.float32)
    for g in range(16):
        q, gi, go = g // 2, g % 8, g % 2
        for ri, nm in enumerate(("s5_b_re", "s5_b_im")):
            s5b[:, q, ri, gi * 16:(gi + 1) * 16, go * 64:(go + 1) * 64] = inp[nm][:, g].transpose(0, 2, 1)
        for ri, nm in enumerate(("s5_c_re", "s5_c_im")):
            s5c[:, :, q, ri, go * 64:(go + 1) * 64, gi * 16:(gi + 1) * 16] = inp[nm][:, :, g].transpose(0, 1, 3, 2)
        s5row[:, :, 0, q, go * 64:(go + 1) * 64] = inp["s5_lam_re"][:, :, g]
        s5row[:, :, 1, q, go * 64:(go + 1) * 64] = inp["s5_lam_im"][:, :, g]
        s5row[:, :, 2, q, go * 64:(go + 1) * 64] = inp["s5_log_step"][:, :, g][:, :, None]
    sh["s5b"] = s5b
    sh["s5c"] = s5c
    sh["s5row"] = s5row
    return sh


def core_inputs(inp, core, nseq):
    b0 = core * nseq
    xs = []
    for s in range(nseq):
        full = np.concatenate([inp["ctx"][b0 + s], inp["x"][b0 + s]], axis=0)
        xs.append(full.reshape(T, 8, 128).transpose(2, 1, 0))
    cc = np.concatenate([inp["c"][b0:b0 + nseq], inp["c_ctx"][None, :]], axis=0)
    ccf = cc.reshape(nseq + 1, 8, 128).transpose(2, 1, 0)
    return {"xin": np.ascontiguousarray(np.stack(xs)), "cc": np.ascontiguousarray(ccf)}


class K:
    def __init__(self, nseq, phases, final=True):
        self.nseq = nseq
        self.NR = nseq + 1
        self.phases = phases
        self.final = final
        self.nc = bass.Bass("TRN2", target_bir_lowering=False)
        self.P = Prog(self.nc)
        self.po = pack_params(None, off_only=True).off
        self.npar = pack_params(None, off_only=True).n
        self.nres = pack_params(None, off_only=True).nres

    def sb(self, st, name, shape, dt):
        self.uid = getattr(self, "uid", 0) + 1
        return Tile(st.enter_context(self.nc.sbuf_tensor("t%d_%s" % (self.uid, name), shape, dt)))

    def dram_in(self, name, shape):
        return self.nc.dram_tensor(name, list(shape), F32, kind="ExternalInput").ap()

    def psum(self):
        t = self.ps[self.psi % 8]
        self.psi += 1
        return t

    def par(self, name, c0=0, n=1):
        o, w = self.po[name]
        return self.partile[:, o + c0:o + c0 + n]

    def mm(self, out, lhsT, rhs, start, stop, reads, writes):
        return self.P.op("pe", lambda e: e.matmul(out, lhsT, rhs, start=start, stop=stop), reads, writes, pe_acc=True)

    def act(self, out, in_, func, reads, writes, bias=None, scale=None):
        kw = {}
        if bias is not None:
            kw["bias"] = bias
        if scale is not None:
            kw["scale"] = scale
        return self.P.op("act", lambda e: e.activation(out=out, in_=in_, func=func, **kw), reads, writes)

    def tt(self, eng, out, in0, in1, op, reads, writes):
        return self.P.op(eng, lambda e: e.tensor_tensor(out=out, in0=in0, in1=in1, op=op), reads, writes)

    def ts(self, eng, out, in0, s1, s2, op0, op1, reads, writes):
        if s2 is None:
            return self.P.op(eng, lambda e: e.tensor_scalar(out=out, in0=in0, scalar1=s1, scalar2=None, op0=op0), reads, writes)
        return self.P.op(eng, lambda e: e.tensor_scalar(out=out, in0=in0, scalar1=s1, scalar2=s2, op0=op0, op1=op1), reads, writes)

    def stt(self, eng, out, in0, scalar, in1, op0, op1, reads, writes):
        return self.P.op(eng, lambda e: e.scalar_tensor_tensor(out=out, in0=in0, scalar=scalar, in1=in1, op0=op0, op1=op1), reads, writes)

    def cp(self, eng, out, in_, reads, writes):
        if eng == "act":
            return self.P.op("act", lambda e: e.copy(out=out, in_=in_), reads, writes)
        return self.P.op(eng, lambda e: e.tensor_copy(out=out, in_=in_), reads, writes)

    def dma(self, eng, out, in_, reads, writes):
        return self.P.op(eng, lambda e: e.dma_start(out=out, in_=in_), reads, writes, dma=True)

    def memset(self, eng, ap, val, writes):
        return self.P.op(eng, lambda e: e.memset(ap, val), (), writes)

    def build(self):
        nc, P = self.nc, self.P
        NR = self.NR
        self.xin = self.dram_in("xin", [self.nseq, 128, 8, T])
        self.cc = self.dram_in("cc", [128, 8, NR])
        self.par_d = self.dram_in("par", [128, self.npar])
        self.wmod_d = self.dram_in("wmod", [4, 128, 8, 6144])
        self.ffg_d = self.dram_in("ffg", [4, 22, 128, 8, 128])
        self.ffu_d = self.dram_in("ffu", [4, 22, 128, 8, 128])
        self.ffd_d = self.dram_in("ffd", [4, 8, 128, 22, 128])
        self.evin_d = self.dram_in("evin", [2, 36, 128, 8, 128])
        self.evdt_d = self.dram_in("evdt", [2, 128, 8, 32])
        self.evout_d = self.dram_in("evout", [2, 8, 128, 16, 128])
        self.lruw_d = self.dram_in("lruw", [2, 128, 2, 2, 8, 128])
        self.cst_d = self.dram_in("cst", [128, 768])
        self.odin_d = self.dram_in("odin", [2, 26, 128, 8, 128])
        self.odv_d = self.dram_in("odv", [2, 128, 8, 768])
        self.odout_d = self.dram_in("odout", [2, 8, 128, 8, 128])
        self.gluw_d = self.dram_in("gluw", [2, 128, 2, 256])
        self.s5b_d = self.dram_in("s5b", [2, 8, 2, 128, 128])
        self.s5c_d = self.dram_in("s5c", [2, 2, 8, 2, 128, 128])
        self.s5row_d = self.dram_in("s5row", [2, 2, 3, 8, 128])
        self.yout = nc.dram_tensor("yout", [self.nseq, 128, 8, 2048], F32, kind="ExternalOutput").ap()
        if getattr(self, "debug", False):
            self.dbg = nc.dram_tensor("dbg", [128, 16, T], F32, kind="ExternalOutput").ap()
        self.out_ops = []
        with contextlib.ExitStack() as st:
            self.ps = [Tile(st.enter_context(nc.psum_tensor("ps%d" % i, [128, 512], F32))) for i in range(8)]
            self.psi = 0
            self.partile = self.sb(st, "par", [128, self.nres], F32)
            self.cst = self.sb(st, "cst", [128, 768], F32)
            self.identb = self.sb(st, "identb", [128, 128], BF16)
            self.onesb = self.sb(st, "onesb", [128, 128], BF16)
            self.mods = self.sb(st, "mods", [128, 4 * 48 * NR], F32)
            self.amix = self.sb(st, "amix", [128, 4 * 8 * NR], F32)
            self.affn = self.sb(st, "affn", [128, 4 * 8 * NR], F32)
            self.lbt = self.sb(st, "lbt", [128, 4, 6], F32)
            self.x = self.sb(st, "x", [128, 8, T], F32)
            self.hx = self.sb(st, "hx", [128, 8, T], BF16)
            self.cD = Dep(const=True)
            o1 = self.dma("sp", self.partile[:], self.par_d[:, 0:self.nres], (), [self.cD])
            o2 = self.dma("sp", self.cst[:], self.cst_d, (), [self.cD])
            self.cp("dve", self.identb[:], self.cst[:, 0:128], [self.cD], [self.cD])
            self.memset("dve", self.onesb[:], 1.0, [self.cD])
            self.prologue_mods()
            self.prologue_lb()
            P.barrier()
            for s in range(self.nseq):
                self.dma("sp", self.x[:, 0:4, :], self.xin[s, :, 0:4, :], (), [self.x.d()])
                self.dma("act", self.x[:, 4:8, :], self.xin[s, :, 4:8, :], (), [self.x.d()])
                for kind, l in self.phases:
                    if kind == "ffn":
                        self.ffn(l, s)
                    elif kind == "mix":
                        if l % 2 == 0:
                            self.even_mixer(l, s)
                        else:
                            self.odd_mixer(l, s)
                    P.barrier()
                self.final_out(s)
                P.barrier()
            P.emit(self.out_ops)
        return nc

    def mod(self, l, chunk0, r):
        NR = self.NR
        base = (l * 48 + chunk0) * NR + r
        return lambda k: self.mods[:, base + k * NR: base + k * NR + 1]

    def prologue_mods(self):
        NR = self.NR
        with contextlib.ExitStack() as st:
            ccf = self.sb(st, "ccf", [128, 8, NR], F32)
            sfm = self.sb(st, "sfm", [128, 8, NR], BF16)
            wm = [self.sb(st, "wm%d" % i, [128, 8, 1536], BF16) for i in range(2)]
            self.dma("sp", ccf[:], self.cc, (), [ccf.d()])
            self.act(sfm[:], ccf[:], AF.Silu, [ccf.d()], [sfm.d()])
            it = 0
            for l in range(4):
                ps = self.psum()
                for piece in range(4):
                    w = wm[it % 2]
                    it += 1
                    self.dma("pool", w[:], self.wmod_d[l, :, :, piece * 1536:(piece + 1) * 1536], (), [w.d()])
                    for c in range(12):
                        ch = piece * 12 + c
                        for k in range(8):
                            self.mm(ps[:, ch * NR:(ch + 1) * NR], w[:, k, c * 128:(c + 1) * 128], sfm[:, k, :],
                                    k == 0, k == 7, [w.d(), sfm.d()], [ps.d()])
                mo = self.mods[:, l * 48 * NR:(l + 1) * 48 * NR].rearrange("p (c r) -> p c r", r=NR)
                o, _ = self.po["bmod%d" % l]
                self.tt("dve", mo, ps[:, 0:48 * NR].rearrange("p (c r) -> p c r", r=NR),
                        self.partile[:, o:o + 48].unsqueeze(2).to_broadcast([128, 48, NR]), ALU.add,
                        [ps.d(), self.cD], [self.cD])
                for dst, gname, c0 in ((self.amix, "nmg%d" % l, 8), (self.affn, "nfg%d" % l, 32)):
                    dv = dst[:, l * 8 * NR:(l + 1) * 8 * NR].rearrange("p (c r) -> p c r", r=NR)
                    sc = self.mods[:, (l * 48 + c0) * NR:(l * 48 + c0 + 8) * NR].rearrange("p (c r) -> p c r", r=NR)
                    go, _ = self.po[gname]
                    self.ts("dve", dv, sc, 1.0, None, ALU.add, None, [self.cD], [self.cD])
                    self.tt("dve", dv, dv, self.partile[:, go:go + 8].unsqueeze(2).to_broadcast([128, 8, NR]), ALU.mult,
                            [self.cD], [self.cD])

    def rms_mod(self, st, src, dst, A, B, s, nch=8, srcdep=None, dstdep=None, pergroup=False):
        sq = [self.sb(st, "rm_sq%d" % i, [128, nch, 512], BF16) for i in range(2)]
        rs = [self.sb(st, "rm_rs%d" % i, [128, nch if pergroup else 1, 512], F32) for i in range(2)]
        tm = [self.sb(st, "rm_tm%d" % i, [128, 512], F32) for i in range(2)]
        sd = srcdep or src.d()
        dd = dstdep or dst.d()
        ndiv = 128.0 if pergroup else 128.0 * nch
        for bi, (t0, n) in enumerate(BLKS):
            r = self.nseq if t0 == 0 else s
            q = sq[bi % 2]
            rr = rs[bi % 2]
            self.act(q[:, :, 0:n], src[:, 0:nch, t0:t0 + n], AF.Square, [sd], [q.d()])
            groups = [[k] for k in range(nch)] if pergroup else [list(range(nch))]
            for gi, grp in enumerate(groups):
                ps = self.psum()
                for i, k in enumerate(grp):
                    self.mm(ps[:, 0:n], self.onesb[:], q[:, k, 0:n], i == 0, i == len(grp) - 1, [q.d(), self.cD], [ps.d()])
                self.ts("dve", rr[:, gi, 0:n], ps[:, 0:n], 1.0 / ndiv, EPS, ALU.mult, ALU.add, [ps.d()], [rr.d()])
                self.P.op("dve", (lambda o_, i_: lambda e: e.reciprocal(out=o_, in_=i_))(rr[:, gi, 0:n], rr[:, gi, 0:n]), [rr.d()], [rr.d()])
                self.act(rr[:, gi, 0:n], rr[:, gi, 0:n], AF.Sqrt, [rr.d()], [rr.d()])
            for k in range(nch):
                tmp = tm[k % 2]
                gi = k if pergroup else 0
                if A is not None:
                    self.stt("dve", tmp[:, 0:n], src[:, k, t0:t0 + n], A(k, r), rr[:, gi, 0:n], ALU.mult, ALU.mult,
                             [sd, rr.d(), self.cD], [tmp.d()])
                else:
                    self.tt("dve", tmp[:, 0:n], src[:, k, t0:t0 + n], rr[:, gi, 0:n], ALU.mult, [sd, rr.d()], [tmp.d()])
                if B is not None:
                    self.act(dst[:, k, t0:t0 + n], tmp[:, 0:n], AF.Identity, [tmp.d(), self.cD], [dd], bias=B(k, r))
                else:
                    self.cp("act", dst[:, k, t0:t0 + n], tmp[:, 0:n], [tmp.d()], [dd])

    def ffn(self, l, s):
        NR = self.NR
        A = lambda k, r: self.affn[:, (l * 8 + k) * NR + r:(l * 8 + k) * NR + r + 1]
        B = lambda k, r: self.mods[:, (l * 48 + 24 + k) * NR + r:(l * 48 + 24 + k) * NR + r + 1]
        with contextlib.ExitStack() as st0:
            with contextlib.ExitStack() as st:
                self.rms_mod(st, self.x, self.hx, A, B, s)
            self.P.barrier()
            gh = self.sb(st0, "gh", [128, 22, 1280], BF16)
            fpar = self.sb(st0, "fpar", [128, 220], F32)
            fo0 = self.po["fcw%d" % l][0]
            self.dma("sp", fpar[:], self.par_d[:, fo0:fo0 + 220], (), [fpar.d()])
            fo, bo = 0, 198
            it = 0
            for half in range(2):
              with contextlib.ExitStack() as st1:
                wg = [self.sb(st1, "wg%d" % i, [128, 8, 128], BF16) for i in range(2)]
                wu = [self.sb(st1, "wu%d" % i, [128, 8, 128], BF16) for i in range(2)]
                dg = [self.sb(st1, "dg%d" % i, [128, 9, 128], BF16) for i in range(2)]
                apc = [self.sb(st1, "apc%d" % i, [128, 258], BF16) for i in range(2)]
                apl = [None, None]
                apl[half] = [self.sb(st1, "apl%d_%d" % (half, i), [128, 18, 66], BF16) for i in range(2)]
                sg = [self.sb(st1, "sg%d" % i, [128, 512], BF16) for i in range(2)]
                for t_ in apc + apl[half]:
                    self.memset("pool", t_[:], 0.0, [t_.d()])
                if half == 0:
                    pieces = [(256, 8, 1), (256 + 512, 8, 9), (256 + 1024, 1, 17)]
                    oblks = [("c", 0, 256, 0), ("l", 256, 512, 0), ("l", 768, 512, 8)]
                else:
                    pieces = [(256 + 960, 1, 0), (256 + 1024, 8, 1), (256 + 1536, 8, 9)]
                    oblks = [("l", 1280, 512, 0), ("l", 1792, 512, 8)]
                row_off = 1 if half == 0 else 1
                def load(f, it):
                    self.dma("pool", wg[it % 2][:], self.ffg_d[l, f], (), [wg[it % 2].d()])
                    self.dma("pool", wu[it % 2][:], self.ffu_d[l, f], (), [wu[it % 2].d()])
                load(0, it)
                for f in range(22):
                    if f + 1 < 22:
                        load(f + 1, it + 1)
                    g_, u_, d_ = wg[it % 2], wu[it % 2], dg[it % 2]
                    pc, pl = apc[it % 2], apl[half][it % 2]
                    self.tt("dve", d_[:], self.identb[:].unsqueeze(1).to_broadcast([128, 9, 128]),
                            fpar[:, fo + f * 9:fo + f * 9 + 9].unsqueeze(2).to_broadcast([128, 9, 128]), ALU.mult, [self.cD, fpar.d()], [d_.d()])
                    if half == 0:
                        ps = self.psum()
                        for k in range(8):
                            self.mm(ps[:, 0:256], g_[:, k, :], self.hx[:, k, 0:256], k == 0, k == 7, [g_.d(), self.hx.d()], [ps.d()])
                        self.cp("act", pc[:, 1:257], ps[:, 0:256], [ps.d()], [pc.d()])
                    for (tk0, nr, pr0) in pieces:
                        ps = self.psum()
                        n = nr * 64
                        for k in range(8):
                            self.mm(ps[:, 0:n], g_[:, k, :], self.hx[:, k, tk0:tk0 + n], k == 0, k == 7, [g_.d(), self.hx.d()], [ps.d()])
                        self.cp("act", pl[:, pr0:pr0 + nr, 1:65], ps[:, 0:n].rearrange("p (r c) -> p r c", c=64), [ps.d()], [pl.d()])
                    gcol = 0
                    for (kind, tk0, n, lr0) in oblks:
                        psc = self.psum()
                        if kind == "c":
                            for i, dx in enumerate((-1, 0, 1)):
                                self.mm(psc[:, 0:256], d_[:, 3 + (dx + 1), :], pc[:, 1 + dx:257 + dx], i == 0, i == 2, [d_.d(), pc.d()], [psc.d()])
                        else:
                            i = 0
                            for dy in (-1, 0, 1):
                                for dx in (-1, 0, 1):
                                    r0 = row_off + lr0 + dy
                                    self.mm(psc[:, 0:512].rearrange("p (r c) -> p r c", c=64), d_[:, (dy + 1) * 3 + (dx + 1), :],
                                            pl[:, r0:r0 + 8, 1 + dx:65 + dx], i == 0, i == 8, [d_.d(), pl.d()], [psc.d()])
                                    i += 1
                        sgt = sg[gcol % 2]
                        self.act(sgt[:, 0:n], psc[:, 0:n], AF.Silu, [psc.d(), fpar.d()], [sgt.d()], bias=fpar[:, bo + f:bo + f + 1])
                        psu = self.psum()
                        for k in range(8):
                            self.mm(psu[:, 0:n], u_[:, k, :], self.hx[:, k, tk0:tk0 + n], k == 0, k == 7, [u_.d(), self.hx.d()], [psu.d()])
                        hoff = tk0 if half == 0 else tk0 - 1280
                        self.tt("dve", gh[:, f, hoff:hoff + n], sgt[:, 0:n], psu[:, 0:n], ALU.mult, [sgt.d(), psu.d()], [gh.d(f)])
                        gcol += 1
                    it += 1
              self.P.barrier()
              with contextlib.ExitStack() as st1:
                wd = [self.sb(st1, "wd%d" % i, [128, 22, 128], BF16) for i in range(2)]
                ghd = [gh.d(f) for f in range(22)]
                self.dma("pool", wd[0][:], self.ffd_d[l, 0], (), [wd[0].d()])
                for oc in range(8):
                    if oc + 1 < 8:
                        self.dma("pool", wd[(oc + 1) % 2][:], self.ffd_d[l, oc + 1], (), [wd[(oc + 1) % 2].d()])
                    w_ = wd[oc % 2]
                    for (kind, tk0, n, lr0) in oblks:
                        r = self.nseq if kind == "c" else s
                        hoff = tk0 if half == 0 else tk0 - 1280
                        ps = self.psum()
                        for f in range(22):
                            self.mm(ps[:, 0:n], w_[:, f, :], gh[:, f, hoff:hoff + n], f == 0, f == 21, [w_.d()] + ghd, [ps.d()])
                        m5 = self.mods[:, (l * 48 + 40 + oc) * NR + r:(l * 48 + 40 + oc) * NR + r + 1]
                        self.stt("dve", self.x[:, oc, tk0:tk0 + n], ps[:, 0:n], m5, self.x[:, oc, tk0:tk0 + n], ALU.mult, ALU.add,
                                 [ps.d(), self.cD, self.x.d()], [self.x.d()])
                self.P.barrier()


    def dump(self, tile, ch0, nch, slot0, dep=None):
        if not getattr(self, "debug", False):
            return
        self.P.barrier()
        with contextlib.ExitStack() as st:
            stg = self.sb(st, "dbgstg", [128, T], F32)
            for i in range(nch):
                src = tile[:, ch0 + i, :] if len(tile.t.shape) == 3 else tile[:, :]
                n = src.shape[-1]
                self.cp("dve", stg[:, 0:n], src, [dep or tile.d()], [stg.d()])
                self.out_ops.append(self.dma("sp", self.dbg[:, slot0 + i, 0:n], stg[:, 0:n], [stg.d()], ()))
            self.P.barrier()

    def dump_ap(self, ap, dep, slot):
        if not getattr(self, "debug", False):
            return
        self.P.barrier()
        with contextlib.ExitStack() as st:
            n = ap.shape[-1]
            stg = self.sb(st, "dbgstg2", [128, n], F32)
            self.cp("dve", stg[:, 0:n], ap, [dep], [stg.d()])
            self.out_ops.append(self.dma("sp", self.dbg[:, slot, 0:n], stg[:, 0:n], [stg.d()], ()))
            self.P.barrier()

    def outproj(self, wd_ap_fn, nk, mix, l, s, gate_chunk0):
        NR = self.NR
        with contextlib.ExitStack() as st:
            wo = [self.sb(st, "wo%d" % i, [128, nk, 128], BF16) for i in range(2)]
            self.dma("pool", wo[0][:], wd_ap_fn(0), (), [wo[0].d()])
            for oc in range(8):
                if oc + 1 < 8:
                    self.dma("pool", wo[(oc + 1) % 2][:], wd_ap_fn(oc + 1), (), [wo[(oc + 1) % 2].d()])
                w_ = wo[oc % 2]
                for (t0, n) in BLKS:
                    r = self.nseq if t0 == 0 else s
                    ps = self.psum()
                    for k in range(nk):
                        self.mm(ps[:, 0:n], w_[:, k, :], mix[:, k, t0:t0 + n], k == 0, k == nk - 1, [w_.d(), mix.d()], [ps.d()])
                    m2 = self.mods[:, (l * 48 + gate_chunk0 + oc) * NR + r:(l * 48 + gate_chunk0 + oc) * NR + r + 1]
                    self.stt("dve", self.x[:, oc, t0:t0 + n], ps[:, 0:n], m2, self.x[:, oc, t0:t0 + n], ALU.mult, ALU.add,
                             [ps.d(), self.cD, self.x.d()], [self.x.d()])
            self.P.barrier()

    def proj_conv(self, w_dram, cw_off, cb_off, wt, pad, dgt, dst, func, ntap=4, dst_fn=None, dst_dep=None):
        self.dma("pool", wt[:], w_dram, (), [wt.d()])
        self.tt("dve", dgt[:, 0:ntap, :], self.identb[:].unsqueeze(1).to_broadcast([128, ntap, 128]),
                self.partile[:, cw_off:cw_off + ntap].unsqueeze(2).to_broadcast([128, ntap, 128]), ALU.mult, [self.cD], [dgt.d()])
        for (t0, n) in BLKS:
            ps = self.psum()
            for k in range(8):
                self.mm(ps[:, 0:n], wt[:, k, :], self.hx[:, k, t0:t0 + n], k == 0, k == 7, [wt.d(), self.hx.d()], [ps.d()])
            po = 1 + t0 if t0 == 0 else 260 + (t0 - CTX)
            self.cp("act", pad[:, po:po + n], ps[:, 0:n], [ps.d()], [pad.d()])
        for (t0, n) in BLKS:
            ps = self.psum()
            po = t0 if t0 == 0 else 259 + (t0 - CTX)
            for k in range(ntap):
                self.mm(ps[:, 0:n], dgt[:, k, :], pad[:, po + k:po + k + n], k == 0, k == ntap - 1, [dgt.d(), pad.d()], [ps.d()])
            o_ap = dst_fn(t0, n) if dst_fn is not None else dst[:, t0:t0 + n]
            self.act(o_ap, ps[:, 0:n], func, [ps.d(), self.cD], [dst_dep or dst.d()], bias=self.partile[:, cb_off:cb_off + 1])

    def even_mixer(self, l, s):
        j = l // 2
        NR = self.NR
        A = lambda k, r: self.amix[:, (l * 8 + k) * NR + r:(l * 8 + k) * NR + r + 1]
        B = lambda k, r: self.mods[:, (l * 48 + k) * NR + r:(l * 48 + k) * NR + r + 1]
        with contextlib.ExitStack() as st0:
            with contextlib.ExitStack() as st:
                self.rms_mod(st, self.x, self.hx, A, B, s)
            self.P.barrier()
            mix = self.sb(st0, "mix", [128, 8, T], BF16)
            which = getattr(self, "even_parts", ("ssd", "lru"))
            if "ssd" in which:
                self.ssd(l, s, mix)
                self.dump(mix, 0, 8, 0)
                self.outproj(lambda oc: self.evout_d[j, oc, :, 0:8, :], 8, mix, l, s, 16)
            if "lru" in which:
                self.lru(l, s, mix)
                self.dump(mix, 0, 8, 8)
                self.outproj(lambda oc: self.evout_d[j, oc, :, 8:16, :], 8, mix, l, s, 16)

    def lru(self, l, s, mix):
        j = l // 2
        po = self.po
        with contextlib.ExitStack() as st:
            cA = self.sb(st, "cA", [128, 16], F32)
            wu = self.sb(st, "lwu", [128, 8, 128], BF16)
            wgy = [self.sb(st, "lwgy%d" % i, [128, 8, 128], BF16) for i in range(2)]
            wai = [self.sb(st, "lwai%d" % i, [128, 2, 2, 128], BF16) for i in range(2)]
            pad = self.sb(st, "lpad", [128, 2312], BF16)
            dgt = self.sb(st, "ldg", [128, 4, 128], BF16)
            ucb = self.sb(st, "ucb", [128, T], BF16)
            a_t = self.sb(st, "lru_a", [128, T], F32)
            sqT = self.sb(st, "lru_sq", [128, T], F32)
            bx = [self.sb(st, "lru_bx%d" % i, [128, T], BF16) for i in range(2)]
            self.memset("pool", pad[:], 0.0, [pad.d()])
            lo, _ = po["llam%d" % j]
            self.act(cA[:], self.partile[:, lo:lo + 16], AF.Exp, [self.cD], [cA.d()], scale=-1.0)
            self.act(cA[:], cA[:], AF.Ln, [cA.d()], [cA.d()], bias=1.0)
            self.ts("dve", cA[:], cA[:], -8.0, None, ALU.mult, None, [cA.d()], [cA.d()])
            sc = lambda o_, a_, b2, i_: (lambda e: e.tensor_tensor_scan(out=o_, data0=a_, data1=b2, initial=i_, op0=ALU.mult, op1=ALU.add))
            for jj in range(8):
                self.dma("pool", wgy[jj % 2][:], self.evin_d[j, 20 + jj], (), [wgy[jj % 2].d()])
                self.dma("pool", wai[jj % 2][:], self.lruw_d[j, :, :, :, jj, :], (), [wai[jj % 2].d()])
                self.proj_conv(self.evin_d[j, 28 + jj], po["lcw%d" % j][0] + jj * 4, po["lcb%d" % j][0] + jj, wu, pad, dgt, ucb, AF.Identity)
                w2 = wai[jj % 2]
                for d in range(2):
                    ba = self.partile[:, po["lba%d" % j][0] + d * 8 + jj:po["lba%d" % j][0] + d * 8 + jj + 1]
                    bi = self.partile[:, po["lbi%d" % j][0] + d * 8 + jj:po["lbi%d" % j][0] + d * 8 + jj + 1]
                    b_ = bx[d]
                    for (t0, n) in BLKS:
                        psa = self.psum()
                        self.mm(psa[:, 0:n], w2[:, 0, d, :], ucb[:, t0:t0 + n], True, True, [w2.d(), ucb.d()], [psa.d()])
                        psi_ = self.psum()
                        self.mm(psi_[:, 0:n], w2[:, 1, d, :], ucb[:, t0:t0 + n], True, True, [w2.d(), ucb.d()], [psi_.d()])
                        self.act(a_t[:, t0:t0 + n], psa[:, 0:n], AF.Sigmoid, [psa.d(), self.cD], [a_t.d()], bias=ba)
                        self.act(b_[:, t0:t0 + n], psi_[:, 0:n], AF.Sigmoid, [psi_.d(), self.cD], [b_.d()], bias=bi)
                    self.act(a_t[:, :], a_t[:, :], AF.Exp, [a_t.d(), cA.d()], [a_t.d()], scale=cA[:, d * 8 + jj:d * 8 + jj + 1])
                    self.act(sqT[:, :], a_t[:, :], AF.Square, [a_t.d()], [sqT.d()])
                    self.act(sqT[:, :], sqT[:, :], AF.Sqrt, [sqT.d()], [sqT.d()], scale=-1.0, bias=1.0)
                    self.tt("dve", b_[:, :], b_[:, :], sqT[:, :], ALU.mult, [b_.d(), sqT.d()], [b_.d()])
                    self.tt("dve", b_[:, :], b_[:, :], ucb[:, :], ALU.mult, [b_.d(), ucb.d()], [b_.d()])
                    if d == 0:
                        self.P.op("dve", sc(b_[:, 0:T], a_t[:, 0:T], b_[:, 0:T], 0.0), [a_t.d(), b_.d()], [b_.d()])
                    else:
                        self.P.op("dve", sc(rev(b_[:, 0:CTX]), rev(a_t[:, 0:CTX]), rev(b_[:, 0:CTX]), 0.0), [a_t.d(), b_.d()], [b_.d()])
                        self.P.op("dve", sc(rev(b_[:, CTX:T]), rev(a_t[:, CTX:T]), rev(b_[:, CTX:T]), b_[:, 0:1]), [a_t.d(), b_.d()], [b_.d()])
                self.tt("dve", bx[0][:, :], bx[0][:, :], bx[1][:, :], ALU.add, [bx[0].d(), bx[1].d()], [bx[0].d()])
                wg_ = wgy[jj % 2]
                for (t0, n) in BLKS:
                    ps = self.psum()
                    for k in range(8):
                        self.mm(ps[:, 0:n], wg_[:, k, :], self.hx[:, k, t0:t0 + n], k == 0, k == 7, [wg_.d(), self.hx.d()], [ps.d()])
                    self.act(bx[1][:, t0:t0 + n], ps[:, 0:n], AF.Gelu, [ps.d()], [bx[1].d()])
                self.tt("dve", mix[:, jj, :], bx[0][:, :], bx[1][:, :], ALU.mult, [bx[0].d(), bx[1].d()], [mix.d()])
            self.P.barrier()

    def pe_T(self, out, in_, reads, writes):
        return self.P.op("pe", lambda e: e.transpose(out, in_, self.identb[:]), list(reads) + [self.cD], writes, pe_acc=True)

    def to_tokmajor_ap(self, src_fn, src_dep, dst):
        c = 0
        while c < 18:
            nq = min(4, 18 - c)
            ps = self.psum()
            pb = ps[:].bitcast(BF16)
            for q in range(nq):
                self.pe_T(pb[:, q * 128:(q + 1) * 128], src_fn(c + q), [src_dep], [ps.d()])
            self.cp("act", dst[:, c:c + nq, :], pb[:, 0:nq * 128].rearrange("p (q f) -> p q f", f=128), [ps.d()], [dst.d()])
            c += nq

    def ssd(self, l, s, mix):
        j = l // 2
        po = self.po
        tri = lambda d: self.cst[:, 128 + 128 * d:256 + 128 * d]
        onesf = self.cst[:, 384:512]
        bc = lambda ap, shape, ax: ap.unsqueeze(ax).to_broadcast(shape)
        with contextlib.ExitStack() as st:
            dt_tok = self.sb(st, "dt_tok", [128, 18, 32], F32)
            la_tok = self.sb(st, "la_tok", [128, 18, 32], F32)
            cumcol = self.sb(st, "cumcol", [128, 18, 32], F32)
            Bfm = self.sb(st, "Bfm", [128, T], BF16)
            Cfm = self.sb(st, "Cfm", [128, T], BF16)
            Hst = [self.sb(st, "Hst%d" % i, [128, 4, 64], F32) for i in range(2)]
            Hbf = [self.sb(st, "Hbf%d" % i, [128, 4, 64], BF16) for i in range(2)]
            NB = 2
            stA = contextlib.ExitStack()
            wdt = self.sb(stA, "swdt", [128, 8, 32], BF16)
            arow = self.sb(stA, "arow", [128, 32], F32)
            t9 = self.sb(stA, "t9", [128, 9, 32], F32)
            self.dma("pool", wdt[:], self.evdt_d[j], (), [wdt.d()])
            ao = po["alogrow%d" % j][0]
            bo = po["dtbrow%d" % j][0]
            self.act(arow[:], self.partile[:, ao:ao + 32], AF.Exp, [self.cD], [arow.d()])
            self.ts("dve", arow[:], arow[:], -1.0, None, ALU.mult, None, [arow.d()], [arow.d()])
            for half in range(2):
                ps = self.psum()
                for ci in range(9):
                    c = half * 9 + ci
                    for k in range(8):
                        self.mm(ps[:, ci * 32:(ci + 1) * 32], self.hx[:, k, c * 128:(c + 1) * 128], wdt[:, k, :], k == 0, k == 7,
                                [self.hx.d(), wdt.d()], [ps.d()])
                self.tt("dve", t9[:], ps[:, 0:288].rearrange("p (c h) -> p c h", h=32),
                        bc(self.partile[:, bo:bo + 32], [128, 9, 32], 1), ALU.add, [ps.d(), self.cD], [t9.d()])
                self.act(t9[:], t9[:], AF.Exp, [t9.d()], [t9.d()])
                self.act(dt_tok[:, half * 9:(half + 1) * 9, :], t9[:], AF.Ln, [t9.d()], [dt_tok.d()], bias=1.0)
                self.tt("dve", la_tok[:, half * 9:(half + 1) * 9, :], dt_tok[:, half * 9:(half + 1) * 9, :],
                        bc(arow[:], [128, 9, 32], 1), ALU.mult, [dt_tok.d(), arow.d()], [la_tok.d()])
            for half in range(2):
                ps = self.psum()
                for ci in range(9):
                    c = half * 9 + ci
                    for d in range(2):
                        self.mm(ps[:, ci * 32 + d * 16:ci * 32 + (d + 1) * 16], tri(d), la_tok[:, c, d * 16:(d + 1) * 16], True, True,
                                [la_tok.d(), self.cD], [ps.d()])
                self.cp("dve", cumcol[:, half * 9:(half + 1) * 9, :], ps[:, 0:288].rearrange("p (c h) -> p c h", h=32), [ps.d()], [cumcol.d()])
            self.P.barrier()
            stA.close()
            it = 0
            for g in range(2):
              with contextlib.ExitStack() as stB:
                wt = self.sb(stB, "swt", [128, 8, 128], BF16)
                pad = self.sb(stB, "spad", [128, 2312], BF16)
                dgt = self.sb(stB, "sdg", [128, 4, 128], BF16)
                self.memset("pool", pad[:], 0.0, [pad.d()])
                self.proj_conv(self.evin_d[j, 8 + 8 + g], po["scw%d" % j][0] + (8 + g) * 4, po["scb%d" % j][0] + 8 + g, wt, pad, dgt, Bfm, AF.Silu)
                self.proj_conv(self.evin_d[j, 8 + 10 + g], po["scw%d" % j][0] + (10 + g) * 4, po["scb%d" % j][0] + 10 + g, wt, pad, dgt, Cfm, AF.Silu)
                for i in range(4):
                    cx = 4 * g + i
                    self.proj_conv(self.evin_d[j, 8 + cx], po["scw%d" % j][0] + cx * 4, po["scb%d" % j][0] + cx, wt, pad, dgt, None, AF.Silu,
                                   dst_fn=(lambda cx_: lambda t0, n: mix[:, cx_, t0:t0 + n])(cx), dst_dep=mix.d())
                self.P.barrier()
              with contextlib.ExitStack() as stC:
                Btok = self.sb(stC, "Btok", [128, 18, 128], BF16)
                xstok = self.sb(stC, "xstok", [128, 18, 256], BF16)
                W = 3
                NH = 4
                rhsla = [self.sb(stC, "rhsla%d" % i, [128, NH, 128], F32) for i in range(W)]
                E_ = [self.sb(stC, "E%d" % i, [128, NH, 128], BF16) for i in range(W)]
                Eb = [self.sb(stC, "Eb%d" % i, [128, NH, 128], BF16) for i in range(W)]
                cbm = [self.sb(stC, "cbm%d" % i, [128, 128], BF16) for i in range(W)]
                xdt = [self.sb(stC, "xdt%d" % i, [128, NH, 64], BF16) for i in range(W)]
                xdtw = [self.sb(stC, "xdtw%d" % i, [128, NH, 64], BF16) for i in range(W)]
                wv = [self.sb(stC, "wv%d" % i, [128, NH], F32) for i in range(W)]
                dtot = [self.sb(stC, "dtot%d" % i, [128, NH], F32) for i in range(W)]
                self.to_tokmajor_ap(lambda c: Bfm[:, c * 128:(c + 1) * 128], Bfm.d(), Btok)
                orders = [list(range(18)), [1, 0] + list(range(17, 1, -1))]
                for ip in range(2):
                    cx0 = 4 * g + 2 * ip
                    for e_ in range(2):
                        cx = cx0 + e_
                        c = 0
                        while c < 18:
                            nq = min(4, 18 - c)
                            ps = self.psum()
                            pb = ps[:].bitcast(BF16)
                            for q in range(nq):
                                self.pe_T(pb[:, q * 128:(q + 1) * 128], mix[:, cx, (c + q) * 128:(c + q + 1) * 128], [mix.d()], [ps.d()])
                            self.cp("act", xstok[:, c:c + nq, e_ * 128:(e_ + 1) * 128], pb[:, 0:nq * 128].rearrange("p (q f) -> p q f", f=128),
                                    [ps.d()], [xstok.d()])
                            c += nq
                        sdo = po["sd%d" % j][0] + cx
                        self.ts("dve", mix[:, cx, :], mix[:, cx, :], self.partile[:, sdo:sdo + 1], None, ALU.mult, None, [mix.d(), self.cD], [mix.d()])
                    for d in range(2):
                        self.memset("dve", Hst[d][:], 0.0, [Hst[d].d()])
                        self.memset("dve", Hbf[d][:], 0.0, [Hbf[d].d()])
                    iters = [(d, orders[d][step]) for step in range(18) for d in range(2)]
                    hdf = lambda d: d * 16 + 8 * g + 4 * ip
                    for w0 in range(0, len(iters), W):
                        win = list(enumerate(iters[w0:w0 + W]))
                        pcum, pcb = {}, {}
                        for b, (d, c) in win:
                            hd = hdf(d)
                            self.tt("dve", rhsla[b][:], bc(tri(d), [128, NH, 128], 1), bc(la_tok[:, c, hd:hd + NH], [128, NH, 128], 2), ALU.mult,
                                    [la_tok.d(), self.cD], [rhsla[b].d()])
                        for b, (d, c) in win:
                            cs = slice(c * 128, (c + 1) * 128)
                            ps = self.psum()
                            pcb[b] = ps
                            self.mm(ps[:, 0:128], Bfm[:, cs], Cfm[:, cs], True, True, [Bfm.d(), Cfm.d()], [ps.d()])
                            ps = self.psum()
                            pcum[b] = ps
                            self.mm(ps[:, 0:512], onesf, rhsla[b][:].rearrange("p h l -> p (h l)"), True, True, [rhsla[b].d(), self.cD], [ps.d()])
                        cum3f = lambda b: pcum[b][:, 0:512].rearrange("p (h l) -> p h l", l=128)
                        for b, (d, c) in win:
                            hd = hdf(d)
                            ccb = bc(cumcol[:, c, hd:hd + NH], [128, NH, 128], 2)
                            self.tt("dve", cbm[b][:], pcb[b][:, 0:128], tri(d), ALU.mult, [pcb[b].d(), self.cD], [cbm[b].d()])
                            self.tt("dve", rhsla[b][:], cum3f(b), ccb, ALU.min, [pcum[b].d(), cumcol.d()], [rhsla[b].d()])
                            self.tt("dve", rhsla[b][:], rhsla[b][:], ccb, ALU.subtract, [rhsla[b].d(), cumcol.d()], [rhsla[b].d()])
                        for b, (d, c) in win:
                            self.act(E_[b][:], rhsla[b][:], AF.Exp, [rhsla[b].d()], [E_[b].d()])
                            self.act(Eb[b][:], cum3f(b), AF.Exp, [pcum[b].d()], [Eb[b].d()])
                        for b, (d, c) in win:
                            hd = hdf(d)
                            last = 127 if d == 0 else 0
                            cs = slice(c * 128, (c + 1) * 128)
                            self.tt("dve", E_[b][:], E_[b][:], bc(cbm[b][:], [128, NH, 128], 1), ALU.mult, [E_[b].d(), cbm[b].d()], [E_[b].d()])
                            self.tt("dve", Eb[b][:], Eb[b][:], bc(Cfm[:, cs], [128, NH, 128], 1), ALU.mult, [Eb[b].d(), Cfm.d()], [Eb[b].d()])
                            self.tt("dve", xdt[b][:], xstok[:, c, :].rearrange("p (h q) -> p h q", q=64),
                                    bc(dt_tok[:, c, hd:hd + NH], [128, NH, 64], 2), ALU.mult, [xstok.d(), dt_tok.d()], [xdt[b].d()])
                            self.tt("dve", wv[b][:], cum3f(b)[:, :, last], cumcol[:, c, hd:hd + NH], ALU.subtract, [pcum[b].d(), cumcol.d()], [wv[b].d()])
                        for b, (d, c) in win:
                            last = 127 if d == 0 else 0
                            self.act(wv[b][:], wv[b][:], AF.Exp, [wv[b].d()], [wv[b].d()])
                            self.act(dtot[b][:], cum3f(b)[:, :, last], AF.Exp, [pcum[b].d()], [dtot[b].d()])
                        for b, (d, c) in win:
                            self.tt("dve", xdtw[b][:], xdt[b][:], bc(wv[b][:], [128, NH, 64], 2), ALU.mult, [xdt[b].d(), wv[b].d()], [xdtw[b].d()])
                        for b, (d, c) in win:
                            cs = slice(c * 128, (c + 1) * 128)
                            ps2 = self.psum()
                            for hh in range(NH):
                                o_ap = ps2[(hh % 2) * 64:(hh % 2 + 1) * 64, (hh // 2) * 128:(hh // 2 + 1) * 128]
                                self.mm(o_ap, xdt[b][:, hh, :], E_[b][:, hh, :], True, False, [xdt[b].d(), E_[b].d()], [ps2.d()])
                                self.mm(o_ap, Hbf[d][:, hh, :], Eb[b][:, hh, :], False, True, [Hbf[d].d(), Eb[b].d()], [ps2.d()])
                            self.mm(ps2[:, 256:512], Btok[:, c, :], xdtw[b][:].rearrange("p h q -> p (h q)"), True, True, [Btok.d(), xdtw[b].d()], [ps2.d()])
                            self.tt("dve", mix[:, cx0:cx0 + 2, cs], mix[:, cx0:cx0 + 2, cs], ps2[:, 0:256].rearrange("p (e l) -> p e l", l=128), ALU.add,
                                    [mix.d(), ps2.d()], [mix.d()])
                            self.tt("dve", Hst[d][:], Hst[d][:], bc(dtot[b][:], [128, NH, 64], 2), ALU.mult, [Hst[d].d(), dtot[b].d()], [Hst[d].d()])
                            self.tt("dve", Hst[d][:], Hst[d][:], ps2[:, 256:512].rearrange("p (h q) -> p h q", q=64), ALU.add, [Hst[d].d(), ps2.d()], [Hst[d].d()])
                            self.cp("act", Hbf[d][:], Hst[d][:], [Hst[d].d()], [Hbf[d].d()])
                self.P.barrier()
            self.dump(mix, 0, 8, 8)
            wt = self.sb(st, "swt2", [128, 8, 128], BF16)
            szt = [self.sb(st, "szt%d" % i, [128, 512], BF16) for i in range(2)]
            for ch in range(8):
                self.dma("pool", wt[:], self.evin_d[j, ch], (), [wt.d()])
                for bi, (t0, n) in enumerate(BLKS):
                    ps = self.psum()
                    for k in range(8):
                        self.mm(ps[:, 0:n], wt[:, k, :], self.hx[:, k, t0:t0 + n], k == 0, k == 7, [wt.d(), self.hx.d()], [ps.d()])
                    z_ = szt[bi % 2]
                    self.act(z_[:, 0:n], ps[:, 0:n], AF.Silu, [ps.d()], [z_.d()])
                    self.tt("dve", mix[:, ch, t0:t0 + n], mix[:, ch, t0:t0 + n], z_[:, 0:n], ALU.mult, [mix.d(), z_.d()], [mix.d()])
            self.P.barrier()
        with contextlib.ExitStack() as st:
            go = po["sng%d" % j][0]
            self.rms_mod(st, mix, mix, lambda k, r: self.partile[:, go + k:go + k + 1], None, s)
            self.P.barrier()

    def prologue_lb(self):
        with contextlib.ExitStack() as st:
            lo = self.po["lbl"][0]
            L = self.partile[:, lo:lo + 24].rearrange("p (l h) -> p l h", h=6)
            mx = self.sb(st, "lbmx", [128, 6], F32)
            e = self.sb(st, "lbe", [128, 4, 6], F32)
            sm = self.sb(st, "lbs", [128, 6], F32)
            D = [self.cD]
            self.tt("dve", mx[:], L[:, 0, :], L[:, 1, :], ALU.max, D, D)
            self.tt("dve", mx[:], mx[:], L[:, 2, :], ALU.max, D, D)
            self.tt("dve", mx[:], mx[:], L[:, 3, :], ALU.max, D, D)
            self.tt("dve", e[:], L, mx[:].unsqueeze(1).to_broadcast([128, 4, 6]), ALU.subtract, D, D)
            self.act(e[:], e[:], AF.Exp, D, D)
            self.tt("dve", sm[:], e[:, 0, :], e[:, 1, :], ALU.add, D, D)
            self.tt("dve", sm[:], sm[:], e[:, 2, :], ALU.add, D, D)
            self.tt("dve", sm[:], sm[:], e[:, 3, :], ALU.add, D, D)
            self.P.op("dve", lambda en: en.reciprocal(out=sm[:], in_=sm[:]), D, D)
            self.tt("dve", e[:], e[:], sm[:].unsqueeze(1).to_broadcast([128, 4, 6]), ALU.mult, D, D)
            self.memset("dve", self.lbt[:, 0, :], 0.0, D)
            for l in range(1, 4):
                self.tt("dve", self.lbt[:, l, :], self.lbt[:, l - 1, :], e[:, l, :], ALU.add, D, D)
            self.P.barrier()

    def odd_mixer(self, l, s):
        j = l // 2
        NR = self.NR
        A = lambda k, r: self.amix[:, (l * 8 + k) * NR + r:(l * 8 + k) * NR + r + 1]
        B = lambda k, r: self.mods[:, (l * 48 + k) * NR + r:(l * 48 + k) * NR + r + 1]
        with contextlib.ExitStack() as st0:
            with contextlib.ExitStack() as st:
                self.rms_mod(st, self.x, self.hx, A, B, s)
            self.P.barrier()
            mix = self.sb(st0, "mixo", [128, 8, T], BF16)
            which = getattr(self, "odd_parts", ("hgrn", "s5"))
            if "hgrn" in which:
                self.hgrn(l, s, mix)
            else:
                self.memset("pool", mix[:, 0:6, :], 0.0, [mix.d()])
            if "s5" in which:
                self.s5(l, s, mix)
            else:
                self.memset("pool", mix[:, 6:8, :], 0.0, [mix.d()])
            self.dump(mix, 0, 8, 0)
            self.outproj(lambda oc: self.odout_d[j, oc], 8, mix, l, s, 16)

    def scan(self, out, d0, d1, init, reads, writes):
        return self.P.op("dve", lambda e: e.tensor_tensor_scan(out=out, data0=d0, data1=d1, initial=init, op0=ALU.mult, op1=ALU.add),
                         reads, writes)

    def hgrn(self, l, s, mix):
        j = l // 2
        po = self.po
        hgm = lambda d: self.cst[:, 512 + 128 * d:640 + 128 * d]
        with contextlib.ExitStack() as st:
            lbm = self.sb(st, "lbm", [128, 6, 2], F32)
            rmask_t = self.sb(st, "rmask", [128, T + 32], BF16)
            self.rmask = rmask_t
            self.memset("pool", rmask_t[:], 1.0, [self.cD])
            self.memset("pool", rmask_t[:].rearrange("p (c i) -> p c i", i=32)[:, :, 0:1], 0.0, [self.cD])
            wq = self.sb(st, "hwq", [128, 8, 128], BF16)
            wf = [self.sb(st, "hwf%d" % i, [128, 8, 128], BF16) for i in range(2)]
            wv = self.sb(st, "hwv", [128, 8, 128], BF16)
            vtok = self.sb(st, "vtok", [128, 18, 128], BF16)
            qt = [self.sb(st, "qt%d" % d, [128, T], BF16) for d in range(2)]
            kt = [self.sb(st, "kt%d" % d, [128, T], BF16) for d in range(2)]
            elast = [self.sb(st, "elast%d" % d, [128, 72], F32) for d in range(2)]
            eprev = [self.sb(st, "eprev%d" % d, [128, 72], F32) for d in range(2)]
            R = [self.sb(st, "R%d" % d, [128, 128], F32) for d in range(2)]
            Rb = [self.sb(st, "Rb%d" % d, [128, 128], BF16) for d in range(2)]
            qsl = self.sb(st, "qsl", [128, 512], F32)
            tA = [self.sb(st, "htA", [128, 512], F32)] * 2
            tB = [self.sb(st, "htB", [128, 512], F32)] * 2
            tC = [self.sb(st, "htC", [128, 512], F32)] * 2
            ktok = [self.sb(st, "ktok%d" % i, [128, 128], BF16) for i in range(2)]
            ktok2 = [self.sb(st, "ktokm%d" % i, [128, 128], BF16) for i in range(2)]
            attm = [self.sb(st, "attm%d" % i, [128, 128], BF16) for i in range(2)]
            self.ts("dve", lbm[:, :, 0], self.lbt[:, l, :], -1.0, 1.0, ALU.mult, ALU.add, [self.cD], [lbm.d()])
            self.ts("dve", lbm[:, :, 1], lbm[:, :, 0], -1.0, None, ALU.mult, None, [lbm.d()], [lbm.d()])
            self.memset("pool", mix[:, 0:6, :], 0.0, [mix.d()])
            it = 0
            for hh in range(6):
                oml = lbm[:, hh, 0:1]
                noml = lbm[:, hh, 1:2]
                lb = self.lbt[:, l, hh:hh + 1]
                self.dma("pool", wq[:], self.odin_d[j, hh], (), [wq.d()])
                self.dma("pool", wf[0][:], self.odin_d[j, 6 + hh], (), [wf[0].d()])
                self.dma("pool", wf[1][:], self.odin_d[j, 12 + hh], (), [wf[1].d()])
                self.dma("pool", wv[:], self.odv_d[j, :, :, hh * 128:(hh + 1) * 128], (), [wv.d()])
                c = 0
                while c < 18:
                    nq = min(4, 18 - c)
                    ps = self.psum()
                    for q in range(nq):
                        for k in range(8):
                            self.mm(ps[:, q * 128:(q + 1) * 128], self.hx[:, k, (c + q) * 128:(c + q + 1) * 128], wv[:, k, :], k == 0, k == 7,
                                    [self.hx.d(), wv.d()], [ps.d()])
                    self.cp("act", vtok[:, c:c + nq, :], ps[:, 0:nq * 128].rearrange("p (q f) -> p q f", f=128), [ps.d()], [vtok.d()])
                    c += nq
                for bi, (t0, n) in enumerate(BLKS):
                    ps = self.psum()
                    for k in range(8):
                        self.mm(ps[:, 0:n], wq[:, k, :], self.hx[:, k, t0:t0 + n], k == 0, k == 7, [wq.d(), self.hx.d()], [ps.d()])
                    self.act(qsl[:, 0:n], ps[:, 0:n], AF.Silu, [ps.d()], [qsl.d()])
                    for d in range(2):
                        a_, b_, c_ = tA[d], tB[d], tC[d]
                        ps = self.psum()
                        for k in range(8):
                            self.mm(ps[:, 0:n], wf[d][:, k, :], self.hx[:, k, t0:t0 + n], k == 0, k == 7, [wf[d].d(), self.hx.d()], [ps.d()])
                        self.act(a_[:, 0:n], ps[:, 0:n], AF.Sigmoid, [ps.d()], [a_.d()])
                        self.act(b_[:, 0:n], a_[:, 0:n], AF.Ln, [a_.d(), lbm.d(), self.cD], [b_.d()], scale=oml, bias=lb)
                        self.ts("dve", a_[:, 0:n], a_[:, 0:n], noml, oml, ALU.mult, ALU.add, [a_.d(), lbm.d()], [a_.d()])
                        if d == 0:
                            self.scan(c_[:, 0:n], self.rmask[:, t0:t0 + n], b_[:, 0:n], 0.0, [b_.d(), self.cD], [c_.d()])
                        else:
                            self.scan(rev(c_[:, 0:n]), rev(self.rmask[:, t0 + 1:t0 + n + 1]), rev(b_[:, 0:n]), 0.0, [b_.d(), self.cD], [c_.d()])
                        self.act(b_[:, 0:n], c_[:, 0:n], AF.Exp, [c_.d()], [b_.d()])
                        lastpos = 31 if d == 0 else 0
                        self.cp("dve", elast[d][:, t0 // 32:(t0 + n) // 32], b_[:, 0:n].rearrange("p (c i) -> p c i", i=32)[:, :, lastpos],
                                [b_.d()], [elast[d].d()])
                        self.tt("dve", qt[d][:, t0:t0 + n], qsl[:, 0:n], b_[:, 0:n], ALU.mult, [qsl.d(), b_.d()], [qt[d].d()])
                        self.act(c_[:, 0:n], c_[:, 0:n], AF.Exp, [c_.d()], [c_.d()], scale=-1.0)
                        self.tt("dve", kt[d][:, t0:t0 + n], a_[:, 0:n], c_[:, 0:n], ALU.mult, [a_.d(), c_.d()], [kt[d].d()])
                for d in range(2):
                    self.memset("dve", eprev[d][:], 1.0, [eprev[d].d()])
                    if d == 0:
                        self.cp("dve", eprev[d][:, 1:72], elast[d][:, 0:71], [elast[d].d()], [eprev[d].d()])
                    else:
                        self.cp("dve", eprev[d][:, 0:7], elast[d][:, 1:8], [elast[d].d()], [eprev[d].d()])
                        self.cp("dve", eprev[d][:, 8:71], elast[d][:, 9:72], [elast[d].d()], [eprev[d].d()])
                        self.cp("dve", eprev[d][:, 71:72], elast[d][:, 0:1], [elast[d].d()], [eprev[d].d()])
                    self.memset("pool", R[d][:], 0.0, [R[d].d()])
                    self.memset("pool", Rb[d][:], 0.0, [Rb[d].d()])
                blocks = [list(range(18)), [1, 0] + list(range(17, 1, -1))]
                for step in range(18):
                    for d in range(2):
                        bk = blocks[d][step]
                        b = it % 2
                        it += 1
                        ts_ = slice(bk * 128, (bk + 1) * 128)
                        ps_a = self.psum()
                        self.mm(ps_a[:, 0:128], kt[d][:, ts_], qt[d][:, ts_], True, True, [kt[d].d(), qt[d].d()], [ps_a.d()])
                        self.tt("dve", attm[b][:], ps_a[:, 0:128], hgm(d), ALU.mult, [ps_a.d(), self.cD], [attm[b].d()])
                        ps_t = self.psum()
                        pb = ps_t[:].bitcast(BF16)
                        self.pe_T(pb[:, 0:128], kt[d][:, ts_], [kt[d].d()], [ps_t.d()])
                        self.cp("act", ktok[b][:], pb[:, 0:128], [ps_t.d()], [ktok[b].d()])
                        self.cp("act", ktok2[b][64:128, :], pb[64:128, 0:128], [ps_t.d()], [ktok2[b].d()])
                        self.memset("pool", ktok2[b][64:96, :], 0.0, [ktok2[b].d()])
                        ps_o = self.psum()
                        self.mm(ps_o[:, 0:128], vtok[:, bk, :], attm[b][:], True, False, [vtok.d(), attm[b].d()], [ps_o.d()])
                        corder = [0, 1, 2, 3] if d == 0 else [3, 2, 1, 0]
                        for ci, cc in enumerate(corder):
                            cg = bk * 4 + cc
                            cs_ = slice(cg * 32, (cg + 1) * 32)
                            self.mm(ps_o[:, cc * 32:(cc + 1) * 32], Rb[d][:], qt[d][:, cs_], False, ci == 3, [Rb[d].d(), qt[d].d()], [ps_o.d()])
                            ps_u = self.psum()
                            if cc < 3:
                                self.mm(ps_u[:, 0:128], ktok[b][cc * 32:(cc + 1) * 32, :], vtok[cc * 32:(cc + 1) * 32, bk, :], True, True,
                                        [ktok[b].d(), vtok.d()], [ps_u.d()])
                            else:
                                self.mm(ps_u[:, 0:128], ktok2[b][64:128, :], vtok[64:128, bk, :], True, True,
                                        [ktok2[b].d(), vtok.d()], [ps_u.d()])
                            self.stt("dve", R[d][:], R[d][:], eprev[d][:, cg:cg + 1], ps_u[:, 0:128], ALU.mult, ALU.add,
                                     [R[d].d(), eprev[d].d(), ps_u.d()], [R[d].d()])
                            self.act(Rb[d][:], R[d][:], AF.Copy, [R[d].d(), elast[d].d()], [Rb[d].d()], scale=elast[d][:, cg:cg + 1])
                        self.tt("dve", mix[:, hh, ts_], mix[:, hh, ts_], ps_o[:, 0:128], ALU.add, [mix.d(), ps_o.d()], [mix.d()])
            self.P.barrier()
        with contextlib.ExitStack() as st:
            go = po["hgn%d" % j][0]
            self.rms_mod(st, mix, mix, lambda k, r: self.partile[:, go + k:go + k + 1], None, s, nch=6, pergroup=True)
            self.P.barrier()
        with contextlib.ExitStack() as st:
            wg_ = [self.sb(st, "hwg%d" % i, [128, 8, 128], BF16) for i in range(2)]
            sz = [self.sb(st, "hsz%d" % i, [128, 512], BF16) for i in range(2)]
            for hh in range(6):
                w_ = wg_[hh % 2]
                self.dma("pool", w_[:], self.odin_d[j, 18 + hh], (), [w_.d()])
                for bi, (t0, n) in enumerate(BLKS):
                    ps = self.psum()
                    for k in range(8):
                        self.mm(ps[:, 0:n], w_[:, k, :], self.hx[:, k, t0:t0 + n], k == 0, k == 7, [w_.d(), self.hx.d()], [ps.d()])
                    z_ = sz[bi % 2]
                    self.act(z_[:, 0:n], ps[:, 0:n], AF.Silu, [ps.d()], [z_.d()])
                    self.tt("dve", mix[:, hh, t0:t0 + n], mix[:, hh, t0:t0 + n], z_[:, 0:n], ALU.mult, [mix.d(), z_.d()], [mix.d()])
            self.P.barrier()

    def cexp_small(self, st, name, lr, li, ls, shape, D):
        mk = lambda n: self.sb(st, name + n, shape, F32)
        step, c, sn, mag, t1, t2 = mk("st"), mk("c"), mk("s"), mk("m"), mk("t1"), mk("t2")
        dd = [step.d()]
        self.act(step[:], ls, AF.Exp, D, dd)
        self.tt("dve", mag[:], lr, step[:], ALU.mult, D + dd, dd)
        self.act(mag[:], mag[:], AF.Exp, dd, dd)
        self.tt("dve", t1[:], li, step[:], ALU.mult, D + dd, dd)
        self.act(sn[:], t1[:], AF.Sin, dd, dd, scale=1.0 / 16.0)
        self.ts("dve", t2[:], t1[:], 1.0 / 16.0, 1.5707963267948966, ALU.mult, ALU.add, dd, dd)
        self.act(c[:], t2[:], AF.Sin, dd, dd)
        for _ in range(4):
            self.tt("dve", t1[:], c[:], c[:], ALU.mult, dd, dd)
            self.tt("dve", t2[:], sn[:], sn[:], ALU.mult, dd, dd)
            self.tt("dve", sn[:], sn[:], c[:], ALU.mult, dd, dd)
            self.ts("dve", sn[:], sn[:], 2.0, None, ALU.mult, None, dd, dd)
            self.tt("dve", c[:], t1[:], t2[:], ALU.subtract, dd, dd)
        for t_ in (c, sn, mag, t1, t2):
            t_.deps[None] = step.d()
        return c, sn, mag, step, t1, t2

    def s5(self, l, s, mix):
        j = l // 2
        po = self.po
        L = 256
        NTC = T // L
        D = [self.cD]
        with contextlib.ExitStack() as st:
            ufm = self.sb(st, "ufm", [128, 2, T], BF16)
            wu = self.sb(st, "s5wu", [128, 8, 128], BF16)
            for c in range(2):
                self.dma("pool", wu[:], self.odin_d[j, 24 + c], (), [wu.d()])
                for (t0, n) in BLKS:
                    ps = self.psum()
                    for k in range(8):
                        self.mm(ps[:, 0:n], wu[:, k, :], self.hx[:, k, t0:t0 + n], k == 0, k == 7, [wu.d(), self.hx.d()], [ps.d()])
                    self.cp("act", ufm[:, c, t0:t0 + n], ps[:, 0:n], [ps.d()], [ufm.d()])
            so = po["s5p%d" % j][0]
            pc, psn, pmag, _, _, _ = self.cexp_small(st, "sp", self.partile[:, so:so + 16], self.partile[:, so + 16:so + 32],
                                                    self.partile[:, so + 32:so + 48], [128, 16], D)
            pdep = [pc.d()]
            E = self.sb(st, "s5E", [128, 4, 2], F32)
            for d in range(2):
                for c in range(2):
                    with contextlib.ExitStack() as st2:
                        tabc = self.sb(st2, "tabc", [128, 4, L], F32)
                        tabs = self.sb(st2, "tabs", [128, 4, L], F32)
                        BM = self.sb(st2, "BM", [128, 4, 2, 128], BF16)
                        CM = self.sb(st2, "CM", [128, 4, 2, 128], BF16)
                        tdep = [tabc.d()]
                        tmp1 = self.sb(st2, "s5tm", [128, 128], F32)
                        for q4 in range(4):
                            q = c * 4 + q4
                            col = d * 8 + q
                            self.cp("dve", tabc[:, q4, 0:1], pc[:, col:col + 1], pdep, tdep)
                            self.cp("dve", tabs[:, q4, 0:1], psn[:, col:col + 1], pdep, tdep)
                            span = 1
                            while span < L:
                                cm_ = tabc[:, q4, span - 1:span]
                                sm_ = tabs[:, q4, span - 1:span]
                                lo, hi = slice(0, span), slice(span, 2 * span)
                                self.ts("dve", tmp1[:, 0:span], tabs[:, q4, lo], sm_, None, ALU.mult, None, tdep, [tmp1.d()])
                                self.stt("dve", tabc[:, q4, hi], tabc[:, q4, lo], cm_, tmp1[:, 0:span], ALU.mult, ALU.subtract, tdep + [tmp1.d()], tdep)
                                self.ts("dve", tmp1[:, 0:span], tabs[:, q4, lo], cm_, None, ALU.mult, None, tdep, [tmp1.d()])
                                self.stt("dve", tabs[:, q4, hi], tabc[:, q4, lo], sm_, tmp1[:, 0:span], ALU.mult, ALU.add, tdep + [tmp1.d()], tdep)
                                span *= 2
                            with contextlib.ExitStack() as st3:
                                rows = self.sb(st3, "s5rows", [128, 3, 128], F32)
                                bpad = self.sb(st3, "s5bp", [128, 2, 128], F32)
                                self.dma("pool", rows[:], self.s5row_d[j, d, :, q, :].unsqueeze(0).to_broadcast([128, 3, 128]), (), [rows.d()])
                                self.dma("act", bpad[:], self.s5b_d[j, q].rearrange("r p c -> p r c"), (), [bpad.d()])
                                self.dma("pool", CM[:, q4, :, :], self.s5c_d[j, d, q].rearrange("r p c -> p r c"), (), [CM.d()])
                                rd = [rows.d()]
                                rc, rs_, rmag, rstep, t1, t2 = self.cexp_small(st3, "sr", rows[:, 0, :], rows[:, 1, :], rows[:, 2, :], [128, 128], rd)
                                w = [rc.d()]
                                lr_, li_ = rows[:, 0, :], rows[:, 1, :]
                                ar, ai, den, zr, zi = rc, rs_, rstep, t1, t2
                                self.tt("dve", ar[:], rc[:], rmag[:], ALU.mult, w, w)
                                self.tt("dve", ai[:], rs_[:], rmag[:], ALU.mult, w, w)
                                self.tt("dve", den[:], lr_, lr_, ALU.mult, rd + w, w)
                                self.tt("dve", rmag[:], li_, li_, ALU.mult, rd + w, w)
                                self.tt("dve", den[:], den[:], rmag[:], ALU.add, w, w)
                                self.P.op("dve", (lambda t_: lambda e: e.reciprocal(out=t_[:], in_=t_[:]))(den), w, w)
                                self.ts("dve", ar[:], ar[:], -1.0, None, ALU.add, None, w, w)
                                self.tt("dve", zr[:], ar[:], lr_, ALU.mult, rd + w, w)
                                self.tt("dve", rmag[:], ai[:], li_, ALU.mult, rd + w, w)
                                self.tt("dve", zr[:], zr[:], rmag[:], ALU.add, w, w)
                                self.tt("dve", zr[:], zr[:], den[:], ALU.mult, w, w)
                                self.tt("dve", zi[:], ai[:], lr_, ALU.mult, rd + w, w)
                                self.tt("dve", rmag[:], ar[:], li_, ALU.mult, rd + w, w)
                                self.tt("dve", zi[:], zi[:], rmag[:], ALU.subtract, w, w)
                                self.tt("dve", zi[:], zi[:], den[:], ALU.mult, w, w)
                                bd = [bpad.d()]
                                self.tt("dve", ar[:], zr[:], bpad[:, 0, :], ALU.mult, w + bd, w)
                                self.tt("dve", ai[:], zi[:], bpad[:, 1, :], ALU.mult, w + bd, w)
                                self.tt("dve", BM[:, q4, 0, :], ar[:], ai[:], ALU.subtract, w, [BM.d()])
                                self.tt("dve", ar[:], zr[:], bpad[:, 1, :], ALU.mult, w + bd, w)
                                self.tt("dve", ai[:], zi[:], bpad[:, 0, :], ALU.mult, w + bd, w)
                                self.tt("dve", BM[:, q4, 1, :], ar[:], ai[:], ALU.add, w, [BM.d()])
                                self.ts("dve", CM[:, q4, 1, :], CM[:, q4, 1, :], -1.0, None, ALU.mult, None, [CM.d()], [CM.d()])
                                self.P.barrier()
                        self.memset("dve", E[:], 0.0, [E.d()])
                        zk = [self.sb(st2, "s5zk%d" % i, [128, 2, L], F32) for i in range(4)]
                        h32 = [self.sb(st2, "s5h%d" % i, [128, 2, L], F32) for i in range(4)]
                        tt_ = self.sb(st2, "s5t", [128, L], F32)
                        hb = [self.sb(st2, "s5hb%d" % i, [128, 4, 2, L], BF16) for i in range(2)]
                        order = list(range(NTC)) if d == 0 else [0] + list(range(NTC - 1, 0, -1))
                        fl = (lambda a: a) if d == 0 else rev
                        lastpos = L - 1 if d == 0 else 0
                        for ti, tc in enumerate(order):
                            t0 = tc * L
                            hbt = hb[ti % 2]
                            px = {}
                            for q4 in range(4):
                                ps_x = self.psum()
                                px[q4] = ps_x
                                self.mm(ps_x[:, 0:L], BM[:, q4, 0, :], ufm[:, c, t0:t0 + L], True, True, [BM.d(), ufm.d()], [ps_x.d()])
                                self.mm(ps_x[:, L:2 * L], BM[:, q4, 1, :], ufm[:, c, t0:t0 + L], True, True, [BM.d(), ufm.d()], [ps_x.d()])
                            td = [tt_.d()]
                            for q4 in range(4):
                                xr, xi = fl(px[q4][:, 0:L]), fl(px[q4][:, L:2 * L])
                                tcq, tsq = tabc[:, q4, :], tabs[:, q4, :]
                                z = zk[q4]
                                zd = [z.d()]
                                pd_ = [px[q4].d()]
                                self.tt("dve", z[:, 0, :], tcq, xr, ALU.mult, tdep + pd_, zd)
                                self.tt("dve", tt_[:], tsq, xi, ALU.mult, tdep + pd_, td)
                                self.tt("dve", z[:, 0, :], z[:, 0, :], tt_[:], ALU.add, zd + td, zd)
                                self.tt("dve", z[:, 1, :], tcq, xi, ALU.mult, tdep + pd_, zd)
                                self.tt("dve", tt_[:], tsq, xr, ALU.mult, tdep + pd_, td)
                                self.tt("dve", z[:, 1, :], z[:, 1, :], tt_[:], ALU.subtract, zd + td, zd)
                            for q4 in range(4):
                                col = d * 8 + c * 4 + q4
                                z = zk[q4]
                                zd = [z.d()]
                                mg = pmag[:, col:col + 1].to_broadcast([128, L])
                                self.scan(z[:, 0, :], mg, z[:, 0, :], E[:, q4, 0:1], zd + [E.d()] + pdep, zd)
                                self.scan(z[:, 1, :], mg, z[:, 1, :], E[:, q4, 1:2], zd + [E.d()] + pdep, zd)
                            for q4 in range(4):
                                tcq, tsq = tabc[:, q4, :], tabs[:, q4, :]
                                z, hh_ = zk[q4], h32[q4]
                                zd, hd_ = [z.d()], [hh_.d()]
                                self.tt("dve", fl(hh_[:, 0, :]), tcq, z[:, 0, :], ALU.mult, tdep + zd, hd_)
                                self.tt("dve", tt_[:], tsq, z[:, 1, :], ALU.mult, tdep + zd, td)
                                self.tt("dve", fl(hh_[:, 0, :]), fl(hh_[:, 0, :]), tt_[:], ALU.subtract, hd_ + td, hd_)
                                self.tt("dve", fl(hh_[:, 1, :]), tsq, z[:, 0, :], ALU.mult, tdep + zd, hd_)
                                self.tt("dve", tt_[:], tcq, z[:, 1, :], ALU.mult, tdep + zd, td)
                                self.tt("dve", fl(hh_[:, 1, :]), fl(hh_[:, 1, :]), tt_[:], ALU.add, hd_ + td, hd_)
                                self.cp("dve", E[:, q4, :], hh_[:, :, lastpos], hd_, [E.d()])
                            for q4 in range(4):
                                self.cp("act", hbt[:, q4, :, :], h32[q4][:, :, :], [h32[q4].d()], [hbt.d()])
                            ps_y = self.psum()
                            for q4 in range(4):
                                for ri in range(2):
                                    self.mm(ps_y[:, 0:L], CM[:, q4, ri, :], hbt[:, q4, ri, :], q4 == 0 and ri == 0, q4 == 3 and ri == 1,
                                            [CM.d(), hbt.d()], [ps_y.d()])
                            if d == 0:
                                self.cp("act", mix[:, 6 + c, t0:t0 + L], ps_y[:, 0:L], [ps_y.d()], [mix.d()])
                            else:
                                self.tt("dve", mix[:, 6 + c, t0:t0 + L], mix[:, 6 + c, t0:t0 + L], ps_y[:, 0:L], ALU.add, [mix.d(), ps_y.d()], [mix.d()])
                        self.P.barrier()
            do = po["s5d%d" % j][0]
            gb = po["glub%d" % j][0]
            wgl = self.sb(st, "wglu", [128, 2, 256], BF16)
            yt = [self.sb(st, "s5yt%d" % i, [128, 512], F32) for i in range(2)]
            self.dma("pool", wgl[:], self.gluw_d[j], (), [wgl.d()])
            for c in range(2):
                for bi, (t0, n) in enumerate(BLKS):
                    y_ = yt[bi % 2]
                    self.stt("dve", y_[:, 0:n], ufm[:, c, t0:t0 + n], self.partile[:, do + c:do + c + 1], mix[:, 6 + c, t0:t0 + n], ALU.mult, ALU.add,
                             [ufm.d(), mix.d(), self.cD], [y_.d()])
                    self.act(ufm[:, c, t0:t0 + n], y_[:, 0:n], AF.Gelu, [y_.d()], [ufm.d()])
            for c in range(2):
                for bi, (t0, n) in enumerate(BLKS):
                    ps = self.psum()
                    for k in range(2):
                        self.mm(ps[:, 0:n], wgl[:, k, c * 128:(c + 1) * 128], ufm[:, k, t0:t0 + n], k == 0, k == 1, [wgl.d(), ufm.d()], [ps.d()])
                    y_ = yt[bi % 2]
                    self.act(y_[:, 0:n], ps[:, 0:n], AF.Sigmoid, [ps.d(), self.cD], [y_.d()], bias=self.partile[:, gb + c:gb + c + 1])
                    self.tt("dve", mix[:, 6 + c, t0:t0 + n], ufm[:, c, t0:t0 + n], y_[:, 0:n], ALU.mult, [ufm.d(), y_.d()], [mix.d()])
            self.P.barrier()

    def final_out(self, s):
        NR = self.NR
        if self.final:
            go, _ = self.po["fng"]
            A = lambda k, r: self.partile[:, go + k:go + k + 1]
            with contextlib.ExitStack() as st:
                self.rms_mod(st, self.x, self.x, A, None, s)
                self.out_ops.append(self.dma("sp", self.yout[s, :, 0:4, :], self.x[:, 0:4, CTX:T], [self.x.d()], ()))
                self.out_ops.append(self.dma("act", self.yout[s, :, 4:8, :], self.x[:, 4:8, CTX:T], [self.x.d()], ()))
                self.P.barrier()
        else:
            self.out_ops.append(self.dma("sp", self.yout[s, :, 0:4, :], self.x[:, 0:4, CTX:T], [self.x.d()], ()))
            self.out_ops.append(self.dma("act", self.yout[s, :, 4:8, :], self.x[:, 4:8, CTX:T], [self.x.d()], ()))


def make_consts():
    c = np.zeros((128, 768), np.float32)
    c[:, 0:128] = np.eye(128, dtype=np.float32)
    i = np.arange(128)
    c[:, 128:256] = (i[:, None] <= i[None, :]).astype(np.float32)
    c[:, 256:384] = (i[:, None] >= i[None, :]).astype(np.float32)
    c[:, 384:512] = 1.0
    same = (i[:, None] // 32) == (i[None, :] // 32)
    c[:, 512:640] = (same & (i[:, None] <= i[None, :])).astype(np.float32)
    c[:, 640:768] = (same & (i[:, None] >= i[None, :])).astype(np.float32)
    return c


def kernel(**inputs):
    nseq = 4
    inp = {k: np.asarray(v) for k, v in inputs.items()}
    phases = []
    for l in range(4):
        phases += [("mix", l), ("ffn", l)]
    kb = K(nseq, phases, final=True)
    nc = kb.build()
    sh = host_prep(inp)
    sh["cst"] = make_consts()
    in_maps = []
    for c in range(NCORES):
        m = dict(sh)
        m.update(core_inputs(inp, c, nseq))
        in_maps.append(m)
    res = run_bass_kernel_spmd(nc, in_maps, core_ids=list(range(NCORES)))
    outs = []
    for c in range(NCORES):
        y = res.results[c]["yout"]
        outs.append(y.transpose(0, 3, 2, 1).reshape(nseq, 2048, 1024))
    return np.ascontiguousarray(np.concatenate(outs, axis=0)).astype(np.float32)
```

```python
import contextlib
import numpy as np
import concourse.bass as bass
import concourse.mybir as mybir
from concourse.ap import AP
from concourse.bass_utils import run_bass_kernel_spmd

F32 = mybir.dt.float32
BF16 = mybir.dt.bfloat16
AF = mybir.ActivationFunctionType
ALU = mybir.AluOpType

T = 2304
CTX = 256
BLKS = [(0, 256), (256, 512), (768, 512), (1280, 512), (1792, 512)]
NCORES = 8
EPS = 1e-6


class Dep:
    __slots__ = ("w", "r", "rd", "const")

    def __init__(self, const=False):
        self.w = None
        self.r = {}
        self.rd = []
        self.const = const


class Op:
    __slots__ = ("eng", "fn", "deps", "marked", "ev", "dma", "idx", "bar", "epoch")


class Prog:
    DMA_SEMS = {"sp": 6, "act": 6, "pool": 24}
    ENGS = ("pe", "act", "dve", "pool", "sp")

    def __init__(self, nc):
        self.nc = nc
        self.ops = []
        self.last = {}
        self.pending_dma = []
        self.nbar = 0

    def _new(self, eng, fn, dma):
        o = Op()
        o.eng = eng
        o.fn = fn
        o.dma = dma
        o.marked = dma
        o.ev = None
        o.bar = 0
        o.epoch = -1
        o.idx = len(self.ops)
        self.ops.append(o)
        return o

    def op(self, eng, fn, reads=(), writes=(), dma=False, pe_acc=False):
        deps = set()
        for d in reads:
            if d.w is not None:
                deps.add(d.w)
        for d in writes:
            if d.w is not None:
                if not (pe_acc and d.w.eng == "pe" and not d.w.dma):
                    deps.add(d.w)
            for r in d.r.values():
                deps.add(r)
            for r in d.rd:
                deps.add(r)
        o = self._new(eng, fn, dma)
        o.deps = deps
        for d in reads:
            if not d.const:
                if dma:
                    d.rd.append(o)
                else:
                    d.r[eng] = o
        for d in writes:
            d.w = o
            d.r = {}
            d.rd = []
        if dma:
            self.pending_dma.append(o)
        else:
            self.last[eng] = o
        return o

    def barrier(self):
        deps = set(self.last.values()) | set(self.pending_dma)
        self.pending_dma = []
        self.last = {}
        self.nbar += 1
        for e in self.ENGS:
            o = self._new(e, None, False)
            o.deps = set(deps)
            o.bar = self.nbar

    def emit(self, final_deps):
        nc = self.nc
        engs = {"pe": nc.tensor, "act": nc.scalar, "dve": nc.vector, "pool": nc.gpsimd, "sp": nc.sync}
        fin = self._new("sp", None, False)
        fin.deps = set(final_deps)
        for o in self.ops:
            for d in o.deps:
                d.marked = True
        cnt = {e: 0 for e in engs}
        dma_rr = {e: 0 for e in engs}
        dma_cnt = {}
        seen = {e: {} for e in engs}
        per_eng = {e: [] for e in engs}
        epoch = 0
        maxv = 0
        nb_in_group = 0
        for o in self.ops:
            mw = {}
            o.epoch = epoch
            if o.dma:
                k = dma_rr[o.eng] % self.DMA_SEMS[o.eng]
                dma_rr[o.eng] += 1
                sk_own = ("dma", o.eng, k)
                prev = dma_cnt.get(sk_own, 0)
                if prev > 0 and seen[o.eng].get(sk_own, 0) < prev:
                    mw[sk_own] = prev
                    seen[o.eng][sk_own] = prev
                dma_cnt[sk_own] = prev + 16
                o.ev = (sk_own, prev + 16)
                maxv = max(maxv, prev + 16)
            elif o.marked:
                cnt[o.eng] += 1
                o.ev = (("eng", o.eng), cnt[o.eng])
                maxv = max(maxv, cnt[o.eng])
            for d in o.deps:
                if d.epoch != epoch:
                    continue
                sk, v = d.ev
                if seen[o.eng].get(sk, 0) < v:
                    seen[o.eng][sk] = v
                    mw[sk] = max(mw.get(sk, 0), v)
            per_eng[o.eng].append((o, list(mw.items())))
            if o.bar:
                nb_in_group += 1
                if nb_in_group == len(self.ENGS):
                    nb_in_group = 0
                    epoch += 1
                    cnt = {e: 0 for e in engs}
                    seen = {e: {sk: v for sk, v in seen[e].items() if sk[0] == "dma"} for e in engs}
        assert maxv < 8000, maxv
        self.stats = {e: len(per_eng[e]) for e in engs}
        self.stats["sem_maxv"] = maxv
        self.stats["nbar"] = self.nbar
        with contextlib.ExitStack() as st:
            sems = {}
            for e in engs:
                sems[("eng", e)] = st.enter_context(nc.semaphore("s_" + e))
            for e in ("sp", "act", "pool"):
                for k in range(self.DMA_SEMS[e]):
                    sems[("dma", e, k)] = st.enter_context(nc.semaphore("d_%s%d" % (e, k)))
            bsemA = st.enter_context(nc.semaphore("barA"))
            bsemB = st.enter_context(nc.semaphore("barB"))
            block = st.enter_context(nc.Block())
            NE = len(self.ENGS)

            def mk(ename):
                def body(eng):
                    for o, waits in per_eng[ename]:
                        for sk, v in waits:
                            eng.wait_ge(sems[sk], v)
                        if o.bar:
                            eng.sem_inc(bsemA, 1)
                            if ename == "sp":
                                eng.wait_ge(bsemA, NE * o.bar)
                                for sk_, sm in sems.items():
                                    if sk_[0] == "eng":
                                        eng.sem_clear(sm)
                                eng.sem_inc(bsemB, 1)
                            eng.wait_ge(bsemB, o.bar)
                            continue
                        if o.fn is None:
                            continue
                        ins = o.fn(eng)
                        if o.dma:
                            ins.then_inc(sems[o.ev[0]], 16)
                        elif o.marked:
                            ins.then_inc(sems[("eng", ename)], 1)
                return body

            block.tensor(mk("pe"))
            block.scalar(mk("act"))
            block.vector(mk("dve"))
            block.gpsimd(mk("pool"))
            block.sync(mk("sp"))


class Tile:
    def __init__(self, t):
        self.t = t
        self.deps = {}

    def d(self, key=None):
        if key not in self.deps:
            self.deps[key] = Dep()
        return self.deps[key]

    def __getitem__(self, idx):
        return self.t[idx]


def rev(ap):
    apl = [list(x) for x in ap.ap]
    n = apl[-1][1]
    off = ap.offset + (n - 1) * apl[-1][0]
    apl[-1][0] = -apl[-1][0]
    return AP(ap.tensor, off, apl)


def fm_vec(v):
    v = np.asarray(v, np.float32).reshape(-1, 128)
    return np.ascontiguousarray(v.T)


def w_colchunks(w, nk):
    K, N = w.shape
    return np.ascontiguousarray(w.reshape(nk, 128, N // 128, 128).transpose(2, 1, 0, 3))


def w_rows(w, nk):
    K, N = w.shape
    return np.ascontiguousarray(w.reshape(nk, 128, N).transpose(1, 0, 2))


class ParPack:
    def __init__(self):
        self.cols = []
        self.off = {}
        self.n = 0

    def add(self, name, arr):
        arr = np.asarray(arr, np.float32)
        arr = arr.reshape(arr.shape[0], -1)
        if arr.shape[0] < 128:
            arr = np.concatenate([arr, np.zeros((128 - arr.shape[0], arr.shape[1]), np.float32)], 0)
        self.off[name] = (self.n, arr.shape[1])
        self.cols.append(arr)
        self.n += arr.shape[1]

    def pack(self):
        return np.ascontiguousarray(np.concatenate(self.cols, axis=1))


def pack_params(inp, off_only=False):
    pp = ParPack()
    z = (lambda *s: np.zeros(s, np.float32))
    g = (lambda k: inp[k]) if not off_only else None
    for l in range(4):
        pp.add("nmg%d" % l, fm_vec(g("norm_mix_g")[l]) if g else z(128, 8))
        pp.add("nfg%d" % l, fm_vec(g("norm_ffn_g")[l]) if g else z(128, 8))
        pp.add("bmod%d" % l, fm_vec(g("b_mod")[l]) if g else z(128, 48))
    pp.add("fng", fm_vec(g("final_norm_g")) if g else z(128, 8))
    for j in range(2):
        if g:
            pp.add("scw%d" % j, g("ssd_conv_w")[j].reshape(4, 12, 128).transpose(2, 1, 0))
            pp.add("scb%d" % j, fm_vec(g("ssd_conv_b")[j]))
            pp.add("sng%d" % j, fm_vec(g("ssd_norm_g")[j]))
            pp.add("sd%d" % j, fm_vec(np.repeat(g("ssd_d")[j], 64)))
            pp.add("dtb%d" % j, g("ssd_dt_bias")[j].reshape(32, 1))
            pp.add("alog%d" % j, g("ssd_a_log")[j].reshape(32, 1))
            pp.add("dtbrow%d" % j, np.tile(g("ssd_dt_bias")[j].reshape(1, 32), (128, 1)))
            pp.add("alogrow%d" % j, np.tile(g("ssd_a_log")[j].reshape(1, 32), (128, 1)))
            pp.add("lcw%d" % j, g("lru_conv_w")[j].reshape(4, 8, 128).transpose(2, 1, 0))
            pp.add("lcb%d" % j, fm_vec(g("lru_conv_b")[j]))
            pp.add("lba%d" % j, g("lru_b_a")[j].reshape(2, 8, 128).transpose(2, 0, 1))
            pp.add("lbi%d" % j, g("lru_b_i")[j].reshape(2, 8, 128).transpose(2, 0, 1))
            pp.add("llam%d" % j, g("lru_lam")[j].reshape(2, 8, 128).transpose(2, 0, 1))
        else:
            pp.add("scw%d" % j, z(128, 48)); pp.add("scb%d" % j, z(128, 12)); pp.add("sng%d" % j, z(128, 8))
            pp.add("sd%d" % j, z(128, 8)); pp.add("dtb%d" % j, z(128, 1)); pp.add("alog%d" % j, z(128, 1))
            pp.add("dtbrow%d" % j, z(128, 32)); pp.add("alogrow%d" % j, z(128, 32))
            pp.add("lcw%d" % j, z(128, 32)); pp.add("lcb%d" % j, z(128, 8)); pp.add("lba%d" % j, z(128, 16))
            pp.add("lbi%d" % j, z(128, 16)); pp.add("llam%d" % j, z(128, 16))
    if g:
        pp.add("lbl", g("hg_lb_logits").reshape(4, 6, 128).transpose(2, 0, 1))
    else:
        pp.add("lbl", z(128, 24))
    for j in range(2):
        if g:
            pp.add("hgn%d" % j, g("hg_norm_g")[j].reshape(6, 128).T)
            pp.add("s5d%d" % j, fm_vec(g("s5_d")[j]))
            pp.add("glub%d" % j, fm_vec(g("s5_glu_b")[j]))
            sp_ = np.zeros((128, 3, 2, 8), np.float32)
            for gg in range(16):
                sp_[(gg % 2) * 64:(gg % 2) * 64 + 64, 0, :, gg // 2] = g("s5_lam_re")[j][:, gg].T
                sp_[(gg % 2) * 64:(gg % 2) * 64 + 64, 1, :, gg // 2] = g("s5_lam_im")[j][:, gg].T
                sp_[(gg % 2) * 64:(gg % 2) * 64 + 64, 2, :, gg // 2] = g("s5_log_step")[j][:, gg][None, :]
            pp.add("s5p%d" % j, sp_)
        else:
            pp.add("hgn%d" % j, z(128, 6)); pp.add("s5d%d" % j, z(128, 2)); pp.add("glub%d" % j, z(128, 2))
            pp.add("s5p%d" % j, z(128, 48))
    pp.nres = pp.n
    for l in range(4):
        if g:
            cw = g("ffn_conv_w")[l].reshape(9, 22, 128).transpose(2, 1, 0)
            pp.add("fcw%d" % l, cw)
            pp.add("fcb%d" % l, fm_vec(g("ffn_conv_b")[l]))
        else:
            pp.add("fcw%d" % l, z(128, 22 * 9))
            pp.add("fcb%d" % l, z(128, 22))
    return pp


def host_prep(inp, nseq_total=32):
    sh = {}
    sh["par"] = pack_params(inp).pack()
    sh["wmod"] = np.stack([w_rows(inp["w_mod"][l], 8) for l in range(4)])
    sh["ffg"] = np.stack([w_colchunks(inp["ffn_w_gate"][l], 8) for l in range(4)])
    sh["ffu"] = np.stack([w_colchunks(inp["ffn_w_up"][l], 8) for l in range(4)])
    sh["ffd"] = np.stack([w_colchunks(inp["ffn_w_down"][l], 22) for l in range(4)])
    ev = inp["ev_w_in"]
    evc = np.concatenate([ev[:, :, 0:2560], ev[:, :, 2592:4640]], axis=2)
    sh["evin"] = np.stack([w_colchunks(evc[j], 8) for j in range(2)])
    sh["evdt"] = np.stack([w_rows(ev[j][:, 2560:2592], 8) for j in range(2)])
    sh["evout"] = np.stack([w_colchunks(inp["ev_w_out"][j], 16) for j in range(2)])
    la = np.stack([inp["lru_w_a"], inp["lru_w_i"]], axis=1)
    sh["lruw"] = np.ascontiguousarray(la.transpose(0, 4, 1, 2, 3, 5))
    od = inp["od_w_in"]
    odc = np.concatenate([od[:, :, 0:2304], od[:, :, 3072:4096]], axis=2)
    sh["odin"] = np.stack([w_colchunks(odc[j], 8) for j in range(2)])
    sh["odv"] = np.stack([w_rows(od[j][:, 2304:3072], 8) for j in range(2)])
    sh["odout"] = np.stack([w_colchunks(inp["od_w_out"][j], 8) for j in range(2)])
    sh["gluw"] = np.stack([w_rows(inp["s5_glu_w"][j], 2) for j in range(2)])
    s5b = np.zeros((2, 8, 2, 128, 128), np.float32)
    s5c = np.zeros((2, 2, 8, 2, 128, 128), np.float32)
    s5row = np.zeros((2, 2, 3, 8, 128), np.float32)
    for g in range(16):
        q, gi, go = g // 2, g % 8, g % 2
        for ri, nm in enumerate(("s5_b_re", "s5_b_im")):
            s5b[:, q, ri, gi * 16:(gi + 1) * 16, go * 64:(go + 1) * 64] = inp[nm][:, g].transpose(0, 2, 1)
        for ri, nm in enumerate(("s5_c_re", "s5_c_im")):
            s5c[:, :, q, ri, go * 64:(go + 1) * 64, gi * 16:(gi + 1) * 16] = inp[nm][:, :, g].transpose(0, 1, 3, 2)
        s5row[:, :, 0, q, go * 64:(go + 1) * 64] = inp["s5_lam_re"][:, :, g]
        s5row[:, :, 1, q, go * 64:(go + 1) * 64] = inp["s5_lam_im"][:, :, g]
        s5row[:, :, 2, q, go * 64:(go + 1) * 64] = inp["s5_log_step"][:, :, g][:, :, None]
    sh["s5b"] = s5b
    sh["s5c"] = s5c
    sh["s5row"] = s5row
    return sh


def core_inputs(inp, core, nseq):
    b0 = core * nseq
    xs = []
    for s in range(nseq):
        full = np.concatenate([inp["ctx"][b0 + s], inp["x"][b0 + s]], axis=0)
        xs.append(full.reshape(T, 8, 128).transpose(2, 1, 0))
    cc = np.concatenate([inp["c"][b0:b0 + nseq], inp["c_ctx"][None, :]], axis=0)
    ccf = cc.reshape(nseq + 1, 8, 128).transpose(2, 1, 0)
    return {"xin": np.ascontiguousarray(np.stack(xs)), "cc": np.ascontiguousarray(ccf)}


class K:
    def __init__(self, nseq, phases, final=True):
        self.nseq = nseq
        self.NR = nseq + 1
        self.phases = phases
        self.final = final
        self.nc = bass.Bass("TRN2", target_bir_lowering=False)
        self.P = Prog(self.nc)
        self.po = pack_params(None, off_only=True).off
        self.npar = pack_params(None, off_only=True).n
        self.nres = pack_params(None, off_only=True).nres

    def sb(self, st, name, shape, dt):
        self.uid = getattr(self, "uid", 0) + 1
        return Tile(st.enter_context(self.nc.sbuf_tensor("t%d_%s" % (self.uid, name), shape, dt)))

    def dram_in(self, name, shape):
        return self.nc.dram_tensor(name, list(shape), F32, kind="ExternalInput").ap()

    def psum(self):
        t = self.ps[self.psi % 8]
        self.psi += 1
        return t

    def par(self, name, c0=0, n=1):
        o, w = self.po[name]
        return self.partile[:, o + c0:o + c0 + n]

    def mm(self, out, lhsT, rhs, start, stop, reads, writes):
        return self.P.op("pe", lambda e: e.matmul(out, lhsT, rhs, start=start, stop=stop), reads, writes, pe_acc=True)

    def act(self, out, in_, func, reads, writes, bias=None, scale=None):
        kw = {}
        if bias is not None:
            kw["bias"] = bias
        if scale is not None:
            kw["scale"] = scale
        return self.P.op("act", lambda e: e.activation(out=out, in_=in_, func=func, **kw), reads, writes)

    def tt(self, eng, out, in0, in1, op, reads, writes):
        return self.P.op(eng, lambda e: e.tensor_tensor(out=out, in0=in0, in1=in1, op=op), reads, writes)

    def ts(self, eng, out, in0, s1, s2, op0, op1, reads, writes):
        if s2 is None:
            return self.P.op(eng, lambda e: e.tensor_scalar(out=out, in0=in0, scalar1=s1, scalar2=None, op0=op0), reads, writes)
        return self.P.op(eng, lambda e: e.tensor_scalar(out=out, in0=in0, scalar1=s1, scalar2=s2, op0=op0, op1=op1), reads, writes)

    def stt(self, eng, out, in0, scalar, in1, op0, op1, reads, writes):
        return self.P.op(eng, lambda e: e.scalar_tensor_tensor(out=out, in0=in0, scalar=scalar, in1=in1, op0=op0, op1=op1), reads, writes)

    def cp(self, eng, out, in_, reads, writes):
        if eng == "act":
            return self.P.op("act", lambda e: e.copy(out=out, in_=in_), reads, writes)
        return self.P.op(eng, lambda e: e.tensor_copy(out=out, in_=in_), reads, writes)

    def dma(self, eng, out, in_, reads, writes):
        return self.P.op(eng, lambda e: e.dma_start(out=out, in_=in_), reads, writes, dma=True)

    def memset(self, eng, ap, val, writes):
        return self.P.op(eng, lambda e: e.memset(ap, val), (), writes)

    def build(self):
        nc, P = self.nc, self.P
        NR = self.NR
        self.xin = self.dram_in("xin", [self.nseq, 128, 8, T])
        self.cc = self.dram_in("cc", [128, 8, NR])
        self.par_d = self.dram_in("par", [128, self.npar])
        self.wmod_d = self.dram_in("wmod", [4, 128, 8, 6144])
        self.ffg_d = self.dram_in("ffg", [4, 22, 128, 8, 128])
        self.ffu_d = self.dram_in("ffu", [4, 22, 128, 8, 128])
        self.ffd_d = self.dram_in("ffd", [4, 8, 128, 22, 128])
        self.evin_d = self.dram_in("evin", [2, 36, 128, 8, 128])
        self.evdt_d = self.dram_in("evdt", [2, 128, 8, 32])
        self.evout_d = self.dram_in("evout", [2, 8, 128, 16, 128])
        self.lruw_d = self.dram_in("lruw", [2, 128, 2, 2, 8, 128])
        self.cst_d = self.dram_in("cst", [128, 768])
        self.odin_d = self.dram_in("odin", [2, 26, 128, 8, 128])
        self.odv_d = self.dram_in("odv", [2, 128, 8, 768])
        self.odout_d = self.dram_in("odout", [2, 8, 128, 8, 128])
        self.gluw_d = self.dram_in("gluw", [2, 128, 2, 256])
        self.s5b_d = self.dram_in("s5b", [2, 8, 2, 128, 128])
        self.s5c_d = self.dram_in("s5c", [2, 2, 8, 2, 128, 128])
        self.s5row_d = self.dram_in("s5row", [2, 2, 3, 8, 128])
        self.yout = nc.dram_tensor("yout", [self.nseq, 128, 8, 2048], F32, kind="ExternalOutput").ap()
        if getattr(self, "debug", False):
            self.dbg = nc.dram_tensor("dbg", [128, 16, T], F32, kind="ExternalOutput").ap()
        self.out_ops = []
        with contextlib.ExitStack() as st:
            self.ps = [Tile(st.enter_context(nc.psum_tensor("ps%d" % i, [128, 512], F32))) for i in range(8)]
            self.psi = 0
            self.partile = self.sb(st, "par", [128, self.nres], F32)
            self.cst = self.sb(st, "cst", [128, 768], F32)
            self.identb = self.sb(st, "identb", [128, 128], BF16)
            self.onesb = self.sb(st, "onesb", [128, 128], BF16)
            self.mods = self.sb(st, "mods", [128, 4 * 48 * NR], F32)
            self.amix = self.sb(st, "amix", [128, 4 * 8 * NR], F32)
            self.affn = self.sb(st, "affn", [128, 4 * 8 * NR], F32)
            self.lbt = self.sb(st, "lbt", [128, 4, 6], F32)
            self.x = self.sb(st, "x", [128, 8, T], F32)
            self.hx = self.sb(st, "hx", [128, 8, T], BF16)
            self.cD = Dep(const=True)
            o1 = self.dma("sp", self.partile[:], self.par_d[:, 0:self.nres], (), [self.cD])
            o2 = self.dma("sp", self.cst[:], self.cst_d, (), [self.cD])
            self.cp("dve", self.identb[:], self.cst[:, 0:128], [self.cD], [self.cD])
            self.memset("dve", self.onesb[:], 1.0, [self.cD])
            self.prologue_mods()
            self.prologue_lb()
            P.barrier()
            for s in range(self.nseq):
                self.dma("sp", self.x[:, 0:4, :], self.xin[s, :, 0:4, :], (), [self.x.d()])
                self.dma("act", self.x[:, 4:8, :], self.xin[s, :, 4:8, :], (), [self.x.d()])
                for kind, l in self.phases:
                    if kind == "ffn":
                        self.ffn(l, s)
                    elif kind == "mix":
                        if l % 2 == 0:
                            self.even_mixer(l, s)
                        else:
                            self.odd_mixer(l, s)
                    P.barrier()
                self.final_out(s)
                P.barrier()
            P.emit(self.out_ops)
        return nc

    def mod(self, l, chunk0, r):
        NR = self.NR
        base = (l * 48 + chunk0) * NR + r
        return lambda k: self.mods[:, base + k * NR: base + k * NR + 1]

    def prologue_mods(self):
        NR = self.NR
        with contextlib.ExitStack() as st:
            ccf = self.sb(st, "ccf", [128, 8, NR], F32)
            sfm = self.sb(st, "sfm", [128, 8, NR], BF16)
            wm = [self.sb(st, "wm%d" % i, [128, 8, 1536], BF16) for i in range(2)]
            self.dma("sp", ccf[:], self.cc, (), [ccf.d()])
            self.act(sfm[:], ccf[:], AF.Silu, [ccf.d()], [sfm.d()])
            it = 0
            for l in range(4):
                ps = self.psum()
                for piece in range(4):
                    w = wm[it % 2]
                    it += 1
                    self.dma("pool", w[:], self.wmod_d[l, :, :, piece * 1536:(piece + 1) * 1536], (), [w.d()])
                    for c in range(12):
                        ch = piece * 12 + c
                        for k in range(8):
                            self.mm(ps[:, ch * NR:(ch + 1) * NR], w[:, k, c * 128:(c + 1) * 128], sfm[:, k, :],
                                    k == 0, k == 7, [w.d(), sfm.d()], [ps.d()])
                mo = self.mods[:, l * 48 * NR:(l + 1) * 48 * NR].rearrange("p (c r) -> p c r", r=NR)
                o, _ = self.po["bmod%d" % l]
                self.tt("dve", mo, ps[:, 0:48 * NR].rearrange("p (c r) -> p c r", r=NR),
                        self.partile[:, o:o + 48].unsqueeze(2).to_broadcast([128, 48, NR]), ALU.add,
                        [ps.d(), self.cD], [self.cD])
                for dst, gname, c0 in ((self.amix, "nmg%d" % l, 8), (self.affn, "nfg%d" % l, 32)):
                    dv = dst[:, l * 8 * NR:(l + 1) * 8 * NR].rearrange("p (c r) -> p c r", r=NR)
                    sc = self.mods[:, (l * 48 + c0) * NR:(l * 48 + c0 + 8) * NR].rearrange("p (c r) -> p c r", r=NR)
                    go, _ = self.po[gname]
                    self.ts("dve", dv, sc, 1.0, None, ALU.add, None, [self.cD], [self.cD])
                    self.tt("dve", dv, dv, self.partile[:, go:go + 8].unsqueeze(2).to_broadcast([128, 8, NR]), ALU.mult,
                            [self.cD], [self.cD])

    def rms_mod(self, st, src, dst, A, B, s, nch=8, srcdep=None, dstdep=None, pergroup=False):
        sq = [self.sb(st, "rm_sq%d" % i, [128, nch, 512], BF16) for i in range(2)]
        rs = [self.sb(st, "rm_rs%d" % i, [128, nch if pergroup else 1, 512], F32) for i in range(2)]
        tm = [self.sb(st, "rm_tm%d" % i, [128, 512], F32) for i in range(2)]
        sd = srcdep or src.d()
        dd = dstdep or dst.d()
        ndiv = 128.0 if pergroup else 128.0 * nch
        for bi, (t0, n) in enumerate(BLKS):
            r = self.nseq if t0 == 0 else s
            q = sq[bi % 2]
            rr = rs[bi % 2]
            self.act(q[:, :, 0:n], src[:, 0:nch, t0:t0 + n], AF.Square, [sd], [q.d()])
            groups = [[k] for k in range(nch)] if pergroup else [list(range(nch))]
            for gi, grp in enumerate(groups):
                ps = self.psum()
                for i, k in enumerate(grp):
                    self.mm(ps[:, 0:n], self.onesb[:], q[:, k, 0:n], i == 0, i == len(grp) - 1, [q.d(), self.cD], [ps.d()])
                self.ts("dve", rr[:, gi, 0:n], ps[:, 0:n], 1.0 / ndiv, EPS, ALU.mult, ALU.add, [ps.d()], [rr.d()])
                self.P.op("dve", (lambda o_, i_: lambda e: e.reciprocal(out=o_, in_=i_))(rr[:, gi, 0:n], rr[:, gi, 0:n]), [rr.d()], [rr.d()])
                self.act(rr[:, gi, 0:n], rr[:, gi, 0:n], AF.Sqrt, [rr.d()], [rr.d()])
            for k in range(nch):
                tmp = tm[k % 2]
                gi = k if pergroup else 0
                if A is not None:
                    self.stt("dve", tmp[:, 0:n], src[:, k, t0:t0 + n], A(k, r), rr[:, gi, 0:n], ALU.mult, ALU.mult,
                             [sd, rr.d(), self.cD], [tmp.d()])
                else:
                    self.tt("dve", tmp[:, 0:n], src[:, k, t0:t0 + n], rr[:, gi, 0:n], ALU.mult, [sd, rr.d()], [tmp.d()])
                if B is not None:
                    self.act(dst[:, k, t0:t0 + n], tmp[:, 0:n], AF.Identity, [tmp.d(), self.cD], [dd], bias=B(k, r))
                else:
                    self.cp("act", dst[:, k, t0:t0 + n], tmp[:, 0:n], [tmp.d()], [dd])

    def ffn(self, l, s):
        NR = self.NR
        A = lambda k, r: self.affn[:, (l * 8 + k) * NR + r:(l * 8 + k) * NR + r + 1]
        B = lambda k, r: self.mods[:, (l * 48 + 24 + k) * NR + r:(l * 48 + 24 + k) * NR + r + 1]
        with contextlib.ExitStack() as st0:
            with contextlib.ExitStack() as st:
                self.rms_mod(st, self.x, self.hx, A, B, s)
            self.P.barrier()
            gh = self.sb(st0, "gh", [128, 22, 1280], BF16)
            fpar = self.sb(st0, "fpar", [128, 220], F32)
            fo0 = self.po["fcw%d" % l][0]
            self.dma("sp", fpar[:], self.par_d[:, fo0:fo0 + 220], (), [fpar.d()])
            fo, bo = 0, 198
            it = 0
            for half in range(2):
              with contextlib.ExitStack() as st1:
                wg = [self.sb(st1, "wg%d" % i, [128, 8, 128], BF16) for i in range(2)]
                wu = [self.sb(st1, "wu%d" % i, [128, 8, 128], BF16) for i in range(2)]
                dg = [self.sb(st1, "dg%d" % i, [128, 9, 128], BF16) for i in range(2)]
                apc = [self.sb(st1, "apc%d" % i, [128, 258], BF16) for i in range(2)]
                apl = [None, None]
                apl[half] = [self.sb(st1, "apl%d_%d" % (half, i), [128, 18, 66], BF16) for i in range(2)]
                sg = [self.sb(st1, "sg%d" % i, [128, 512], BF16) for i in range(2)]
                for t_ in apc + apl[half]:
                    self.memset("pool", t_[:], 0.0, [t_.d()])
                if half == 0:
                    pieces = [(256, 8, 1), (256 + 512, 8, 9), (256 + 1024, 1, 17)]
                    oblks = [("c", 0, 256, 0), ("l", 256, 512, 0), ("l", 768, 512, 8)]
                else:
                    pieces = [(256 + 960, 1, 0), (256 + 1024, 8, 1), (256 + 1536, 8, 9)]
                    oblks = [("l", 1280, 512, 0), ("l", 1792, 512, 8)]
                row_off = 1 if half == 0 else 1
                def load(f, it):
                    self.dma("pool", wg[it % 2][:], self.ffg_d[l, f], (), [wg[it % 2].d()])
                    self.dma("pool", wu[it % 2][:], self.ffu_d[l, f], (), [wu[it % 2].d()])
                load(0, it)
                for f in range(22):
                    if f + 1 < 22:
                        load(f + 1, it + 1)
                    g_, u_, d_ = wg[it % 2], wu[it % 2], dg[it % 2]
                    pc, pl = apc[it % 2], apl[half][it % 2]
                    self.tt("dve", d_[:], self.identb[:].unsqueeze(1).to_broadcast([128, 9, 128]),
                            fpar[:, fo + f * 9:fo + f * 9 + 9].unsqueeze(2).to_broadcast([128, 9, 128]), ALU.mult, [self.cD, fpar.d()], [d_.d()])
                    if half == 0:
                        ps = self.psum()
                        for k in range(8):
                            self.mm(ps[:, 0:256], g_[:, k, :], self.hx[:, k, 0:256], k == 0, k == 7, [g_.d(), self.hx.d()], [ps.d()])
                        self.cp("act", pc[:, 1:257], ps[:, 0:256], [ps.d()], [pc.d()])
                    for (tk0, nr, pr0) in pieces:
                        ps = self.psum()
                        n = nr * 64
                        for k in range(8):
                            self.mm(ps[:, 0:n], g_[:, k, :], self.hx[:, k, tk0:tk0 + n], k == 0, k == 7, [g_.d(), self.hx.d()], [ps.d()])
                        self.cp("act", pl[:, pr0:pr0 + nr, 1:65], ps[:, 0:n].rearrange("p (r c) -> p r c", c=64), [ps.d()], [pl.d()])
                    gcol = 0
                    for (kind, tk0, n, lr0) in oblks:
                        psc = self.psum()
                        if kind == "c":
                            for i, dx in enumerate((-1, 0, 1)):
                                self.mm(psc[:, 0:256], d_[:, 3 + (dx + 1), :], pc[:, 1 + dx:257 + dx], i == 0, i == 2, [d_.d(), pc.d()], [psc.d()])
                        else:
                            i = 0
                            for dy in (-1, 0, 1):
                                for dx in (-1, 0, 1):
                                    r0 = row_off + lr0 + dy
                                    self.mm(psc[:, 0:512].rearrange("p (r c) -> p r c", c=64), d_[:, (dy + 1) * 3 + (dx + 1), :],
                                            pl[:, r0:r0 + 8, 1 + dx:65 + dx], i == 0, i == 8, [d_.d(), pl.d()], [psc.d()])
                                    i += 1
                        sgt = sg[gcol % 2]
                        self.act(sgt[:, 0:n], psc[:, 0:n], AF.Silu, [psc.d(), fpar.d()], [sgt.d()], bias=fpar[:, bo + f:bo + f + 1])
                        psu = self.psum()
                        for k in range(8):
                            self.mm(psu[:, 0:n], u_[:, k, :], self.hx[:, k, tk0:tk0 + n], k == 0, k == 7, [u_.d(), self.hx.d()], [psu.d()])
                        hoff = tk0 if half == 0 else tk0 - 1280
                        self.tt("dve", gh[:, f, hoff:hoff + n], sgt[:, 0:n], psu[:, 0:n], ALU.mult, [sgt.d(), psu.d()], [gh.d(f)])
                        gcol += 1
                    it += 1
              self.P.barrier()
              with contextlib.ExitStack() as st1:
                wd = [self.sb(st1, "wd%d" % i, [128, 22, 128], BF16) for i in range(2)]
                ghd = [gh.d(f) for f in range(22)]
                self.dma("pool", wd[0][:], self.ffd_d[l, 0], (), [wd[0].d()])
                for oc in range(8):
                    if oc + 1 < 8:
                        self.dma("pool", wd[(oc + 1) % 2][:], self.ffd_d[l, oc + 1], (), [wd[(oc + 1) % 2].d()])
                    w_ = wd[oc % 2]
                    for (kind, tk0, n, lr0) in oblks:
                        r = self.nseq if kind == "c" else s
                        hoff = tk0 if half == 0 else tk0 - 1280
                        ps = self.psum()
                        for f in range(22):
                            self.mm(ps[:, 0:n], w_[:, f, :], gh[:, f, hoff:hoff + n], f == 0, f == 21, [w_.d()] + ghd, [ps.d()])
                        m5 = self.mods[:, (l * 48 + 40 + oc) * NR + r:(l * 48 + 40 + oc) * NR + r + 1]
                        self.stt("dve", self.x[:, oc, tk0:tk0 + n], ps[:, 0:n], m5, self.x[:, oc, tk0:tk0 + n], ALU.mult, ALU.add,
                                 [ps.d(), self.cD, self.x.d()], [self.x.d()])
                self.P.barrier()


    def dump(self, tile, ch0, nch, slot0, dep=None):
        if not getattr(self, "debug", False):
            return
        self.P.barrier()
        with contextlib.ExitStack() as st:
            stg = self.sb(st, "dbgstg", [128, T], F32)
            for i in range(nch):
                src = tile[:, ch0 + i, :] if len(tile.t.shape) == 3 else tile[:, :]
                n = src.shape[-1]
                self.cp("dve", stg[:, 0:n], src, [dep or tile.d()], [stg.d()])
                self.out_ops.append(self.dma("sp", self.dbg[:, slot0 + i, 0:n], stg[:, 0:n], [stg.d()], ()))
            self.P.barrier()

    def dump_ap(self, ap, dep, slot):
        if not getattr(self, "debug", False):
            return
        self.P.barrier()
        with contextlib.ExitStack() as st:
            n = ap.shape[-1]
            stg = self.sb(st, "dbgstg2", [128, n], F32)
            self.cp("dve", stg[:, 0:n], ap, [dep], [stg.d()])
            self.out_ops.append(self.dma("sp", self.dbg[:, slot, 0:n], stg[:, 0:n], [stg.d()], ()))
            self.P.barrier()

    def outproj(self, wd_ap_fn, nk, mix, l, s, gate_chunk0):
        NR = self.NR
        with contextlib.ExitStack() as st:
            wo = [self.sb(st, "wo%d" % i, [128, nk, 128], BF16) for i in range(2)]
            self.dma("pool", wo[0][:], wd_ap_fn(0), (), [wo[0].d()])
            for oc in range(8):
                if oc + 1 < 8:
                    self.dma("pool", wo[(oc + 1) % 2][:], wd_ap_fn(oc + 1), (), [wo[(oc + 1) % 2].d()])
                w_ = wo[oc % 2]
                for (t0, n) in BLKS:
                    r = self.nseq if t0 == 0 else s
                    ps = self.psum()
                    for k in range(nk):
                        self.mm(ps[:, 0:n], w_[:, k, :], mix[:, k, t0:t0 + n], k == 0, k == nk - 1, [w_.d(), mix.d()], [ps.d()])
                    m2 = self.mods[:, (l * 48 + gate_chunk0 + oc) * NR + r:(l * 48 + gate_chunk0 + oc) * NR + r + 1]
                    self.stt("dve", self.x[:, oc, t0:t0 + n], ps[:, 0:n], m2, self.x[:, oc, t0:t0 + n], ALU.mult, ALU.add,
                             [ps.d(), self.cD, self.x.d()], [self.x.d()])
            self.P.barrier()

    def proj_conv(self, w_dram, cw_off, cb_off, wt, pad, dgt, dst, func, ntap=4, dst_fn=None, dst_dep=None):
        self.dma("pool", wt[:], w_dram, (), [wt.d()])
        self.tt("dve", dgt[:, 0:ntap, :], self.identb[:].unsqueeze(1).to_broadcast([128, ntap, 128]),
                self.partile[:, cw_off:cw_off + ntap].unsqueeze(2).to_broadcast([128, ntap, 128]), ALU.mult, [self.cD], [dgt.d()])
        for (t0, n) in BLKS:
            ps = self.psum()
            for k in range(8):
                self.mm(ps[:, 0:n], wt[:, k, :], self.hx[:, k, t0:t0 + n], k == 0, k == 7, [wt.d(), self.hx.d()], [ps.d()])
            po = 1 + t0 if t0 == 0 else 260 + (t0 - CTX)
            self.cp("act", pad[:, po:po + n], ps[:, 0:n], [ps.d()], [pad.d()])
        for (t0, n) in BLKS:
            ps = self.psum()
            po = t0 if t0 == 0 else 259 + (t0 - CTX)
            for k in range(ntap):
                self.mm(ps[:, 0:n], dgt[:, k, :], pad[:, po + k:po + k + n], k == 0, k == ntap - 1, [dgt.d(), pad.d()], [ps.d()])
            o_ap = dst_fn(t0, n) if dst_fn is not None else dst[:, t0:t0 + n]
            self.act(o_ap, ps[:, 0:n], func, [ps.d(), self.cD], [dst_dep or dst.d()], bias=self.partile[:, cb_off:cb_off + 1])

    def even_mixer(self, l, s):
        j = l // 2
        NR = self.NR
        A = lambda k, r: self.amix[:, (l * 8 + k) * NR + r:(l * 8 + k) * NR + r + 1]
        B = lambda k, r: self.mods[:, (l * 48 + k) * NR + r:(l * 48 + k) * NR + r + 1]
        with contextlib.ExitStack() as st0:
            with contextlib.ExitStack() as st:
                self.rms_mod(st, self.x, self.hx, A, B, s)
            self.P.barrier()
            mix = self.sb(st0, "mix", [128, 8, T], BF16)
            which = getattr(self, "even_parts", ("ssd", "lru"))
            if "ssd" in which:
                self.ssd(l, s, mix)
                self.dump(mix, 0, 8, 0)
                self.outproj(lambda oc: self.evout_d[j, oc, :, 0:8, :], 8, mix, l, s, 16)
            if "lru" in which:
                self.lru(l, s, mix)
                self.dump(mix, 0, 8, 8)
                self.outproj(lambda oc: self.evout_d[j, oc, :, 8:16, :], 8, mix, l, s, 16)

    def lru(self, l, s, mix):
        j = l // 2
        po = self.po
        with contextlib.ExitStack() as st:
            cA = self.sb(st, "cA", [128, 16], F32)
            wu = self.sb(st, "lwu", [128, 8, 128], BF16)
            wgy = [self.sb(st, "lwgy%d" % i, [128, 8, 128], BF16) for i in range(2)]
            wai = [self.sb(st, "lwai%d" % i, [128, 2, 2, 128], BF16) for i in range(2)]
            pad = self.sb(st, "lpad", [128, 2312], BF16)
            dgt = self.sb(st, "ldg", [128, 4, 128], BF16)
            ucb = self.sb(st, "ucb", [128, T], BF16)
            a_t = self.sb(st, "lru_a", [128, T], F32)
            sqT = self.sb(st, "lru_sq", [128, T], F32)
            bx = [self.sb(st, "lru_bx%d" % i, [128, T], BF16) for i in range(2)]
            self.memset("pool", pad[:], 0.0, [pad.d()])
            lo, _ = po["llam%d" % j]
            self.act(cA[:], self.partile[:, lo:lo + 16], AF.Exp, [self.cD], [cA.d()], scale=-1.0)
            self.act(cA[:], cA[:], AF.Ln, [cA.d()], [cA.d()], bias=1.0)
            self.ts("dve", cA[:], cA[:], -8.0, None, ALU.mult, None, [cA.d()], [cA.d()])
            sc = lambda o_, a_, b2, i_: (lambda e: e.tensor_tensor_scan(out=o_, data0=a_, data1=b2, initial=i_, op0=ALU.mult, op1=ALU.add))
            for jj in range(8):
                self.dma("pool", wgy[jj % 2][:], self.evin_d[j, 20 + jj], (), [wgy[jj % 2].d()])
                self.dma("pool", wai[jj % 2][:], self.lruw_d[j, :, :, :, jj, :], (), [wai[jj % 2].d()])
                self.proj_conv(self.evin_d[j, 28 + jj], po["lcw%d" % j][0] + jj * 4, po["lcb%d" % j][0] + jj, wu, pad, dgt, ucb, AF.Identity)
                w2 = wai[jj % 2]
                for d in range(2):
                    ba = self.partile[:, po["lba%d" % j][0] + d * 8 + jj:po["lba%d" % j][0] + d * 8 + jj + 1]
                    bi = self.partile[:, po["lbi%d" % j][0] + d * 8 + jj:po["lbi%d" % j][0] + d * 8 + jj + 1]
                    b_ = bx[d]
                    for (t0, n) in BLKS:
                        psa = self.psum()
                        self.mm(psa[:, 0:n], w2[:, 0, d, :], ucb[:, t0:t0 + n], True, True, [w2.d(), ucb.d()], [psa.d()])
                        psi_ = self.psum()
                        self.mm(psi_[:, 0:n], w2[:, 1, d, :], ucb[:, t0:t0 + n], True, True, [w2.d(), ucb.d()], [psi_.d()])
                        self.act(a_t[:, t0:t0 + n], psa[:, 0:n], AF.Sigmoid, [psa.d(), self.cD], [a_t.d()], bias=ba)
                        self.act(b_[:, t0:t0 + n], psi_[:, 0:n], AF.Sigmoid, [psi_.d(), self.cD], [b_.d()], bias=bi)
                    self.act(a_t[:, :], a_t[:, :], AF.Exp, [a_t.d(), cA.d()], [a_t.d()], scale=cA[:, d * 8 + jj:d * 8 + jj + 1])
                    self.act(sqT[:, :], a_t[:, :], AF.Square, [a_t.d()], [sqT.d()])
                    self.act(sqT[:, :], sqT[:, :], AF.Sqrt, [sqT.d()], [sqT.d()], scale=-1.0, bias=1.0)
                    self.tt("dve", b_[:, :], b_[:, :], sqT[:, :], ALU.mult, [b_.d(), sqT.d()], [b_.d()])
                    self.tt("dve", b_[:, :], b_[:, :], ucb[:, :], ALU.mult, [b_.d(), ucb.d()], [b_.d()])
                    if d == 0:
                        self.P.op("dve", sc(b_[:, 0:T], a_t[:, 0:T], b_[:, 0:T], 0.0), [a_t.d(), b_.d()], [b_.d()])
                    else:
                        self.P.op("dve", sc(rev(b_[:, 0:CTX]), rev(a_t[:, 0:CTX]), rev(b_[:, 0:CTX]), 0.0), [a_t.d(), b_.d()], [b_.d()])
                        self.P.op("dve", sc(rev(b_[:, CTX:T]), rev(a_t[:, CTX:T]), rev(b_[:, CTX:T]), b_[:, 0:1]), [a_t.d(), b_.d()], [b_.d()])
                self.tt("dve", bx[0][:, :], bx[0][:, :], bx[1][:, :], ALU.add, [bx[0].d(), bx[1].d()], [bx[0].d()])
                wg_ = wgy[jj % 2]
                for (t0, n) in BLKS:
                    ps = self.psum()
                    for k in range(8):
                        self.mm(ps[:, 0:n], wg_[:, k, :], self.hx[:, k, t0:t0 + n], k == 0, k == 7, [wg_.d(), self.hx.d()], [ps.d()])
                    self.act(bx[1][:, t0:t0 + n], ps[:, 0:n], AF.Gelu, [ps.d()], [bx[1].d()])
                self.tt("dve", mix[:, jj, :], bx[0][:, :], bx[1][:, :], ALU.mult, [bx[0].d(), bx[1].d()], [mix.d()])
            self.P.barrier()

    def pe_T(self, out, in_, reads, writes):
        return self.P.op("pe", lambda e: e.transpose(out, in_, self.identb[:]), list(reads) + [self.cD], writes, pe_acc=True)

    def to_tokmajor_ap(self, src_fn, src_dep, dst):
        c = 0
        while c < 18:
            nq = min(4, 18 - c)
            ps = self.psum()
            pb = ps[:].bitcast(BF16)
            for q in range(nq):
                self.pe_T(pb[:, q * 128:(q + 1) * 128], src_fn(c + q), [src_dep], [ps.d()])
            self.cp("act", dst[:, c:c + nq, :], pb[:, 0:nq * 128].rearrange("p (q f) -> p q f", f=128), [ps.d()], [dst.d()])
            c += nq

    def ssd(self, l, s, mix):
        j = l // 2
        po = self.po
        tri = lambda d: self.cst[:, 128 + 128 * d:256 + 128 * d]
        onesf = self.cst[:, 384:512]
        bc = lambda ap, shape, ax: ap.unsqueeze(ax).to_broadcast(shape)
        with contextlib.ExitStack() as st:
            dt_tok = self.sb(st, "dt_tok", [128, 18, 32], F32)
            la_tok = self.sb(st, "la_tok", [128, 18, 32], F32)
            cumcol = self.sb(st, "cumcol", [128, 18, 32], F32)
            Bfm = self.sb(st, "Bfm", [128, T], BF16)
            Cfm = self.sb(st, "Cfm", [128, T], BF16)
            Hst = [self.sb(st, "Hst%d" % i, [128, 4, 64], F32) for i in range(2)]
            Hbf = [self.sb(st, "Hbf%d" % i, [128, 4, 64], BF16) for i in range(2)]
            NB = 2
            stA = contextlib.ExitStack()
            wdt = self.sb(stA, "swdt", [128, 8, 32], BF16)
            arow = self.sb(stA, "arow", [128, 32], F32)
            t9 = self.sb(stA, "t9", [128, 9, 32], F32)
            self.dma("pool", wdt[:], self.evdt_d[j], (), [wdt.d()])
            ao = po["alogrow%d" % j][0]
            bo = po["dtbrow%d" % j][0]
            self.act(arow[:], self.partile[:, ao:ao + 32], AF.Exp, [self.cD], [arow.d()])
            self.ts("dve", arow[:], arow[:], -1.0, None, ALU.mult, None, [arow.d()], [arow.d()])
            for half in range(2):
                ps = self.psum()
                for ci in range(9):
                    c = half * 9 + ci
                    for k in range(8):
                        self.mm(ps[:, ci * 32:(ci + 1) * 32], self.hx[:, k, c * 128:(c + 1) * 128], wdt[:, k, :], k == 0, k == 7,
                                [self.hx.d(), wdt.d()], [ps.d()])
                self.tt("dve", t9[:], ps[:, 0:288].rearrange("p (c h) -> p c h", h=32),
                        bc(self.partile[:, bo:bo + 32], [128, 9, 32], 1), ALU.add, [ps.d(), self.cD], [t9.d()])
                self.act(t9[:], t9[:], AF.Exp, [t9.d()], [t9.d()])
                self.act(dt_tok[:, half * 9:(half + 1) * 9, :], t9[:], AF.Ln, [t9.d()], [dt_tok.d()], bias=1.0)
                self.tt("dve", la_tok[:, half * 9:(half + 1) * 9, :], dt_tok[:, half * 9:(half + 1) * 9, :],
                        bc(arow[:], [128, 9, 32], 1), ALU.mult, [dt_tok.d(), arow.d()], [la_tok.d()])
            for half in range(2):
                ps = self.psum()
                for ci in range(9):
                    c = half * 9 + ci
                    for d in range(2):
                        self.mm(ps[:, ci * 32 + d * 16:ci * 32 + (d + 1) * 16], tri(d), la_tok[:, c, d * 16:(d + 1) * 16], True, True,
                                [la_tok.d(), self.cD], [ps.d()])
                self.cp("dve", cumcol[:, half * 9:(half + 1) * 9, :], ps[:, 0:288].rearrange("p (c h) -> p c h", h=32), [ps.d()], [cumcol.d()])
            self.P.barrier()
            stA.close()
            it = 0
            for g in range(2):
              with contextlib.ExitStack() as stB:
                wt = self.sb(stB, "swt", [128, 8, 128], BF16)
                pad = self.sb(stB, "spad", [128, 2312], BF16)
                dgt = self.sb(stB, "sdg", [128, 4, 128], BF16)
                self.memset("pool", pad[:], 0.0, [pad.d()])
                self.proj_conv(self.evin_d[j, 8 + 8 + g], po["scw%d" % j][0] + (8 + g) * 4, po["scb%d" % j][0] + 8 + g, wt, pad, dgt, Bfm, AF.Silu)
                self.proj_conv(self.evin_d[j, 8 + 10 + g], po["scw%d" % j][0] + (10 + g) * 4, po["scb%d" % j][0] + 10 + g, wt, pad, dgt, Cfm, AF.Silu)
                for i in range(4):
                    cx = 4 * g + i
                    self.proj_conv(self.evin_d[j, 8 + cx], po["scw%d" % j][0] + cx * 4, po["scb%d" % j][0] + cx, wt, pad, dgt, None, AF.Silu,
                                   dst_fn=(lambda cx_: lambda t0, n: mix[:, cx_, t0:t0 + n])(cx), dst_dep=mix.d())
                self.P.barrier()
              with contextlib.ExitStack() as stC:
                Btok = self.sb(stC, "Btok", [128, 18, 128], BF16)
                xstok = self.sb(stC, "xstok", [128, 18, 256], BF16)
                W = 3
                NH = 4
                rhsla = [self.sb(stC, "rhsla%d" % i, [128, NH, 128], F32) for i in range(W)]
                E_ = [self.sb(stC, "E%d" % i, [128, NH, 128], BF16) for i in range(W)]
                Eb = [self.sb(stC, "Eb%d" % i, [128, NH, 128], BF16) for i in range(W)]
                cbm = [self.sb(stC, "cbm%d" % i, [128, 128], BF16) for i in range(W)]
                xdt = [self.sb(stC, "xdt%d" % i, [128, NH, 64], BF16) for i in range(W)]
                xdtw = [self.sb(stC, "xdtw%d" % i, [128, NH, 64], BF16) for i in range(W)]
                wv = [self.sb(stC, "wv%d" % i, [128, NH], F32) for i in range(W)]
                dtot = [self.sb(stC, "dtot%d" % i, [128, NH], F32) for i in range(W)]
                self.to_tokmajor_ap(lambda c: Bfm[:, c * 128:(c + 1) * 128], Bfm.d(), Btok)
                orders = [list(range(18)), [1, 0] + list(range(17, 1, -1))]
                for ip in range(2):
                    cx0 = 4 * g + 2 * ip
                    for e_ in range(2):
                        cx = cx0 + e_
                        c = 0
                        while c < 18:
                            nq = min(4, 18 - c)
                            ps = self.psum()
                            pb = ps[:].bitcast(BF16)
                            for q in range(nq):
                                self.pe_T(pb[:, q * 128:(q + 1) * 128], mix[:, cx, (c + q) * 128:(c + q + 1) * 128], [mix.d()], [ps.d()])
                            self.cp("act", xstok[:, c:c + nq, e_ * 128:(e_ + 1) * 128], pb[:, 0:nq * 128].rearrange("p (q f) -> p q f", f=128),
                                    [ps.d()], [xstok.d()])
                            c += nq
                        sdo = po["sd%d" % j][0] + cx
                        self.ts("dve", mix[:, cx, :], mix[:, cx, :], self.partile[:, sdo:sdo + 1], None, ALU.mult, None, [mix.d(), self.cD], [mix.d()])
                    for d in range(2):
                        self.memset("dve", Hst[d][:], 0.0, [Hst[d].d()])
                        self.memset("dve", Hbf[d][:], 0.0, [Hbf[d].d()])
                    iters = [(d, orders[d][step]) for step in range(18) for d in range(2)]
                    hdf = lambda d: d * 16 + 8 * g + 4 * ip
                    for w0 in range(0, len(iters), W):
                        win = list(enumerate(iters[w0:w0 + W]))
                        pcum, pcb = {}, {}
                        for b, (d, c) in win:
                            hd = hdf(d)
                            self.tt("dve", rhsla[b][:], bc(tri(d), [128, NH, 128], 1), bc(la_tok[:, c, hd:hd + NH], [128, NH, 128], 2), ALU.mult,
                                    [la_tok.d(), self.cD], [rhsla[b].d()])
                        for b, (d, c) in win:
                            cs = slice(c * 128, (c + 1) * 128)
                            ps = self.psum()
                            pcb[b] = ps
                            self.mm(ps[:, 0:128], Bfm[:, cs], Cfm[:, cs], True, True, [Bfm.d(), Cfm.d()], [ps.d()])
                            ps = self.psum()
                            pcum[b] = ps
                            self.mm(ps[:, 0:512], onesf, rhsla[b][:].rearrange("p h l -> p (h l)"), True, True, [rhsla[b].d(), self.cD], [ps.d()])
                        cum3f = lambda b: pcum[b][:, 0:512].rearrange("p (h l) -> p h l", l=128)
                        for b, (d, c) in win:
                            hd = hdf(d)
                            ccb = bc(cumcol[:, c, hd:hd + NH], [128, NH, 128], 2)
                            self.tt("dve", cbm[b][:], pcb[b][:, 0:128], tri(d), ALU.mult, [pcb[b].d(), self.cD], [cbm[b].d()])
                            self.tt("dve", rhsla[b][:], cum3f(b), ccb, ALU.min, [pcum[b].d(), cumcol.d()], [rhsla[b].d()])
                            self.tt("dve", rhsla[b][:], rhsla[b][:], ccb, ALU.subtract, [rhsla[b].d(), cumcol.d()], [rhsla[b].d()])
                        for b, (d, c) in win:
                            self.act(E_[b][:], rhsla[b][:], AF.Exp, [rhsla[b].d()], [E_[b].d()])
                            self.act(Eb[b][:], cum3f(b), AF.Exp, [pcum[b].d()], [Eb[b].d()])
                        for b, (d, c) in win:
                            hd = hdf(d)
                            last = 127 if d == 0 else 0
                            cs = slice(c * 128, (c + 1) * 128)
                            self.tt("dve", E_[b][:], E_[b][:], bc(cbm[b][:], [128, NH, 128], 1), ALU.mult, [E_[b].d(), cbm[b].d()], [E_[b].d()])
                            self.tt("dve", Eb[b][:], Eb[b][:], bc(Cfm[:, cs], [128, NH, 128], 1), ALU.mult, [Eb[b].d(), Cfm.d()], [Eb[b].d()])
                            self.tt("dve", xdt[b][:], xstok[:, c, :].rearrange("p (h q) -> p h q", q=64),
                                    bc(dt_tok[:, c, hd:hd + NH], [128, NH, 64], 2), ALU.mult, [xstok.d(), dt_tok.d()], [xdt[b].d()])
                            self.tt("dve", wv[b][:], cum3f(b)[:, :, last], cumcol[:, c, hd:hd + NH], ALU.subtract, [pcum[b].d(), cumcol.d()], [wv[b].d()])
                        for b, (d, c) in win:
                            last = 127 if d == 0 else 0
                            self.act(wv[b][:], wv[b][:], AF.Exp, [wv[b].d()], [wv[b].d()])
                            self.act(dtot[b][:], cum3f(b)[:, :, last], AF.Exp, [pcum[b].d()], [dtot[b].d()])
                        for b, (d, c) in win:
                            self.tt("dve", xdtw[b][:], xdt[b][:], bc(wv[b][:], [128, NH, 64], 2), ALU.mult, [xdt[b].d(), wv[b].d()], [xdtw[b].d()])
                        for b, (d, c) in win:
                            cs = slice(c * 128, (c + 1) * 128)
                            ps2 = self.psum()
                            for hh in range(NH):
                                o_ap = ps2[(hh % 2) * 64:(hh % 2 + 1) * 64, (hh // 2) * 128:(hh // 2 + 1) * 128]
                                self.mm(o_ap, xdt[b][:, hh, :], E_[b][:, hh, :], True, False, [xdt[b].d(), E_[b].d()], [ps2.d()])
                                self.mm(o_ap, Hbf[d][:, hh, :], Eb[b][:, hh, :], False, True, [Hbf[d].d(), Eb[b].d()], [ps2.d()])
                            self.mm(ps2[:, 256:512], Btok[:, c, :], xdtw[b][:].rearrange("p h q -> p (h q)"), True, True, [Btok.d(), xdtw[b].d()], [ps2.d()])
                            self.tt("dve", mix[:, cx0:cx0 + 2, cs], mix[:, cx0:cx0 + 2, cs], ps2[:, 0:256].rearrange("p (e l) -> p e l", l=128), ALU.add,
                                    [mix.d(), ps2.d()], [mix.d()])
                            self.tt("dve", Hst[d][:], Hst[d][:], bc(dtot[b][:], [128, NH, 64], 2), ALU.mult, [Hst[d].d(), dtot[b].d()], [Hst[d].d()])
                            self.tt("dve", Hst[d][:], Hst[d][:], ps2[:, 256:512].rearrange("p (h q) -> p h q", q=64), ALU.add, [Hst[d].d(), ps2.d()], [Hst[d].d()])
                            self.cp("act", Hbf[d][:], Hst[d][:], [Hst[d].d()], [Hbf[d].d()])
                self.P.barrier()
            self.dump(mix, 0, 8, 8)
            wt = self.sb(st, "swt2", [128, 8, 128], BF16)
            szt = [self.sb(st, "szt%d" % i, [128, 512], BF16) for i in range(2)]
            for ch in range(8):
                self.dma("pool", wt[:], self.evin_d[j, ch], (), [wt.d()])
                for bi, (t0, n) in enumerate(BLKS):
                    ps = self.psum()
                    for k in range(8):
                        self.mm(ps[:, 0:n], wt[:, k, :], self.hx[:, k, t0:t0 + n], k == 0, k == 7, [wt.d(), self.hx.d()], [ps.d()])
                    z_ = szt[bi % 2]
                    self.act(z_[:, 0:n], ps[:, 0:n], AF.Silu, [ps.d()], [z_.d()])
                    self.tt("dve", mix[:, ch, t0:t0 + n], mix[:, ch, t0:t0 + n], z_[:, 0:n], ALU.mult, [mix.d(), z_.d()], [mix.d()])
            self.P.barrier()
        with contextlib.ExitStack() as st:
            go = po["sng%d" % j][0]
            self.rms_mod(st, mix, mix, lambda k, r: self.partile[:, go + k:go + k + 1], None, s)
            self.P.barrier()

    def prologue_lb(self):
        with contextlib.ExitStack() as st:
            lo = self.po["lbl"][0]
            L = self.partile[:, lo:lo + 24].rearrange("p (l h) -> p l h", h=6)
            mx = self.sb(st, "lbmx", [128, 6], F32)
            e = self.sb(st, "lbe", [128, 4, 6], F32)
            sm = self.sb(st, "lbs", [128, 6], F32)
            D = [self.cD]
            self.tt("dve", mx[:], L[:, 0, :], L[:, 1, :], ALU.max, D, D)
            self.tt("dve", mx[:], mx[:], L[:, 2, :], ALU.max, D, D)
            self.tt("dve", mx[:], mx[:], L[:, 3, :], ALU.max, D, D)
            self.tt("dve", e[:], L, mx[:].unsqueeze(1).to_broadcast([128, 4, 6]), ALU.subtract, D, D)
            self.act(e[:], e[:], AF.Exp, D, D)
            self.tt("dve", sm[:], e[:, 0, :], e[:, 1, :], ALU.add, D, D)
            self.tt("dve", sm[:], sm[:], e[:, 2, :], ALU.add, D, D)
            self.tt("dve", sm[:], sm[:], e[:, 3, :], ALU.add, D, D)
            self.P.op("dve", lambda en: en.reciprocal(out=sm[:], in_=sm[:]), D, D)
            self.tt("dve", e[:], e[:], sm[:].unsqueeze(1).to_broadcast([128, 4, 6]), ALU.mult, D, D)
            self.memset("dve", self.lbt[:, 0, :], 0.0, D)
            for l in range(1, 4):
                self.tt("dve", self.lbt[:, l, :], self.lbt[:, l - 1, :], e[:, l, :], ALU.add, D, D)
            self.P.barrier()

    def odd_mixer(self, l, s):
        j = l // 2
        NR = self.NR
        A = lambda k, r: self.amix[:, (l * 8 + k) * NR + r:(l * 8 + k) * NR + r + 1]
        B = lambda k, r: self.mods[:, (l * 48 + k) * NR + r:(l * 48 + k) * NR + r + 1]
        with contextlib.ExitStack() as st0:
            with contextlib.ExitStack() as st:
                self.rms_mod(st, self.x, self.hx, A, B, s)
            self.P.barrier()
            mix = self.sb(st0, "mixo", [128, 8, T], BF16)
            which = getattr(self, "odd_parts", ("hgrn", "s5"))
            if "hgrn" in which:
                self.hgrn(l, s, mix)
            else:
                self.memset("pool", mix[:, 0:6, :], 0.0, [mix.d()])
            if "s5" in which:
                self.s5(l, s, mix)
            else:
                self.memset("pool", mix[:, 6:8, :], 0.0, [mix.d()])
            self.dump(mix, 0, 8, 0)
            self.outproj(lambda oc: self.odout_d[j, oc], 8, mix, l, s, 16)

    def scan(self, out, d0, d1, init, reads, writes):
        return self.P.op("dve", lambda e: e.tensor_tensor_scan(out=out, data0=d0, data1=d1, initial=init, op0=ALU.mult, op1=ALU.add),
                         reads, writes)

    def hgrn(self, l, s, mix):
        j = l // 2
        po = self.po
        hgm = lambda d: self.cst[:, 512 + 128 * d:640 + 128 * d]
        with contextlib.ExitStack() as st:
            lbm = self.sb(st, "lbm", [128, 6, 2], F32)
            rmask_t = self.sb(st, "rmask", [128, T + 32], BF16)
            self.rmask = rmask_t
            self.memset("pool", rmask_t[:], 1.0, [self.cD])
            self.memset("pool", rmask_t[:].rearrange("p (c i) -> p c i", i=32)[:, :, 0:1], 0.0, [self.cD])
            wq = self.sb(st, "hwq", [128, 8, 128], BF16)
            wf = [self.sb(st, "hwf%d" % i, [128, 8, 128], BF16) for i in range(2)]
            wv = self.sb(st, "hwv", [128, 8, 128], BF16)
            vtok = self.sb(st, "vtok", [128, 18, 128], BF16)
            qt = [self.sb(st, "qt%d" % d, [128, T], BF16) for d in range(2)]
            kt = [self.sb(st, "kt%d" % d, [128, T], BF16) for d in range(2)]
            elast = [self.sb(st, "elast%d" % d, [128, 72], F32) for d in range(2)]
            eprev = [self.sb(st, "eprev%d" % d, [128, 72], F32) for d in range(2)]
            R = [self.sb(st, "R%d" % d, [128, 128], F32) for d in range(2)]
            Rb = [self.sb(st, "Rb%d" % d, [128, 128], BF16) for d in range(2)]
            qsl = self.sb(st, "qsl", [128, 512], F32)
            tA = [self.sb(st, "htA", [128, 512], F32)] * 2
            tB = [self.sb(st, "htB", [128, 512], F32)] * 2
            tC = [self.sb(st, "htC", [128, 512], F32)] * 2
            ktok = [self.sb(st, "ktok%d" % i, [128, 128], BF16) for i in range(2)]
            ktok2 = [self.sb(st, "ktokm%d" % i, [128, 128], BF16) for i in range(2)]
            attm = [self.sb(st, "attm%d" % i, [128, 128], BF16) for i in range(2)]
            self.ts("dve", lbm[:, :, 0], self.lbt[:, l, :], -1.0, 1.0, ALU.mult, ALU.add, [self.cD], [lbm.d()])
            self.ts("dve", lbm[:, :, 1], lbm[:, :, 0], -1.0, None, ALU.mult, None, [lbm.d()], [lbm.d()])
            self.memset("pool", mix[:, 0:6, :], 0.0, [mix.d()])
            it = 0
            for hh in range(6):
                oml = lbm[:, hh, 0:1]
                noml = lbm[:, hh, 1:2]
                lb = self.lbt[:, l, hh:hh + 1]
                self.dma("pool", wq[:], self.odin_d[j, hh], (), [wq.d()])
                self.dma("pool", wf[0][:], self.odin_d[j, 6 + hh], (), [wf[0].d()])
                self.dma("pool", wf[1][:], self.odin_d[j, 12 + hh], (), [wf[1].d()])
                self.dma("pool", wv[:], self.odv_d[j, :, :, hh * 128:(hh + 1) * 128], (), [wv.d()])
                c = 0
                while c < 18:
                    nq = min(4, 18 - c)
                    ps = self.psum()
                    for q in range(nq):
                        for k in range(8):
                            self.mm(ps[:, q * 128:(q + 1) * 128], self.hx[:, k, (c + q) * 128:(c + q + 1) * 128], wv[:, k, :], k == 0, k == 7,
                                    [self.hx.d(), wv.d()], [ps.d()])
                    self.cp("act", vtok[:, c:c + nq, :], ps[:, 0:nq * 128].rearrange("p (q f) -> p q f", f=128), [ps.d()], [vtok.d()])
                    c += nq
                for bi, (t0, n) in enumerate(BLKS):
                    ps = self.psum()
                    for k in range(8):
                        self.mm(ps[:, 0:n], wq[:, k, :], self.hx[:, k, t0:t0 + n], k == 0, k == 7, [wq.d(), self.hx.d()], [ps.d()])
                    self.act(qsl[:, 0:n], ps[:, 0:n], AF.Silu, [ps.d()], [qsl.d()])
                    for d in range(2):
                        a_, b_, c_ = tA[d], tB[d], tC[d]
                        ps = self.psum()
                        for k in range(8):
                            self.mm(ps[:, 0:n], wf[d][:, k, :], self.hx[:, k, t0:t0 + n], k == 0, k == 7, [wf[d].d(), self.hx.d()], [ps.d()])
                        self.act(a_[:, 0:n], ps[:, 0:n], AF.Sigmoid, [ps.d()], [a_.d()])
                        self.act(b_[:, 0:n], a_[:, 0:n], AF.Ln, [a_.d(), lbm.d(), self.cD], [b_.d()], scale=oml, bias=lb)
                        self.ts("dve", a_[:, 0:n], a_[:, 0:n], noml, oml, ALU.mult, ALU.add, [a_.d(), lbm.d()], [a_.d()])
                        if d == 0:
                            self.scan(c_[:, 0:n], self.rmask[:, t0:t0 + n], b_[:, 0:n], 0.0, [b_.d(), self.cD], [c_.d()])
                        else:
                            self.scan(rev(c_[:, 0:n]), rev(self.rmask[:, t0 + 1:t0 + n + 1]), rev(b_[:, 0:n]), 0.0, [b_.d(), self.cD], [c_.d()])
                        self.act(b_[:, 0:n], c_[:, 0:n], AF.Exp, [c_.d()], [b_.d()])
                        lastpos = 31 if d == 0 else 0
                        self.cp("dve", elast[d][:, t0 // 32:(t0 + n) // 32], b_[:, 0:n].rearrange("p (c i) -> p c i", i=32)[:, :, lastpos],
                                [b_.d()], [elast[d].d()])
                        self.tt("dve", qt[d][:, t0:t0 + n], qsl[:, 0:n], b_[:, 0:n], ALU.mult, [qsl.d(), b_.d()], [qt[d].d()])
                        self.act(c_[:, 0:n], c_[:, 0:n], AF.Exp, [c_.d()], [c_.d()], scale=-1.0)
                        self.tt("dve", kt[d][:, t0:t0 + n], a_[:, 0:n], c_[:, 0:n], ALU.mult, [a_.d(), c_.d()], [kt[d].d()])
                for d in range(2):
                    self.memset("dve", eprev[d][:], 1.0, [eprev[d].d()])
                    if d == 0:
                        self.cp("dve", eprev[d][:, 1:72], elast[d][:, 0:71], [elast[d].d()], [eprev[d].d()])
                    else:
                        self.cp("dve", eprev[d][:, 0:7], elast[d][:, 1:8], [elast[d].d()], [eprev[d].d()])
                        self.cp("dve", eprev[d][:, 8:71], elast[d][:, 9:72], [elast[d].d()], [eprev[d].d()])
                        self.cp("dve", eprev[d][:, 71:72], elast[d][:, 0:1], [elast[d].d()], [eprev[d].d()])
                    self.memset("pool", R[d][:], 0.0, [R[d].d()])
                    self.memset("pool", Rb[d][:], 0.0, [Rb[d].d()])
                blocks = [list(range(18)), [1, 0] + list(range(17, 1, -1))]
                for step in range(18):
                    for d in range(2):
                        bk = blocks[d][step]
                        b = it % 2
                        it += 1
                        ts_ = slice(bk * 128, (bk + 1) * 128)
                        ps_a = self.psum()
                        self.mm(ps_a[:, 0:128], kt[d][:, ts_], qt[d][:, ts_], True, True, [kt[d].d(), qt[d].d()], [ps_a.d()])
                        self.tt("dve", attm[b][:], ps_a[:, 0:128], hgm(d), ALU.mult, [ps_a.d(), self.cD], [attm[b].d()])
                        ps_t = self.psum()
                        pb = ps_t[:].bitcast(BF16)
                        self.pe_T(pb[:, 0:128], kt[d][:, ts_], [kt[d].d()], [ps_t.d()])
                        self.cp("act", ktok[b][:], pb[:, 0:128], [ps_t.d()], [ktok[b].d()])
                        self.cp("act", ktok2[b][64:128, :], pb[64:128, 0:128], [ps_t.d()], [ktok2[b].d()])
                        self.memset("pool", ktok2[b][64:96, :], 0.0, [ktok2[b].d()])
                        ps_o = self.psum()
                        self.mm(ps_o[:, 0:128], vtok[:, bk, :], attm[b][:], True, False, [vtok.d(), attm[b].d()], [ps_o.d()])
                        corder = [0, 1, 2, 3] if d == 0 else [3, 2, 1, 0]
                        pus = []
                        for cc in corder:
                            ps_u = self.psum()
                            pus.append(ps_u)
                            if cc < 3:
                                self.mm(ps_u[:, 0:128], ktok[b][cc * 32:(cc + 1) * 32, :], vtok[cc * 32:(cc + 1) * 32, bk, :], True, True,
                                        [ktok[b].d(), vtok.d()], [ps_u.d()])
                            else:
                                self.mm(ps_u[:, 0:128], ktok2[b][64:128, :], vtok[64:128, bk, :], True, True,
                                        [ktok2[b].d(), vtok.d()], [ps_u.d()])
                        for ci, cc in enumerate(corder):
                            cg = bk * 4 + cc
                            cs_ = slice(cg * 32, (cg + 1) * 32)
                            ps_u = pus[ci]
                            self.mm(ps_o[:, cc * 32:(cc + 1) * 32], Rb[d][:], qt[d][:, cs_], False, ci == 3, [Rb[d].d(), qt[d].d()], [ps_o.d()])
                            self.stt("dve", R[d][:], R[d][:], eprev[d][:, cg:cg + 1], ps_u[:, 0:128], ALU.mult, ALU.add,
                                     [R[d].d(), eprev[d].d(), ps_u.d()], [R[d].d()])
                            self.act(Rb[d][:], R[d][:], AF.Copy, [R[d].d(), elast[d].d()], [Rb[d].d()], scale=elast[d][:, cg:cg + 1])
                        self.tt("dve", mix[:, hh, ts_], mix[:, hh, ts_], ps_o[:, 0:128], ALU.add, [mix.d(), ps_o.d()], [mix.d()])
            self.P.barrier()
        with contextlib.ExitStack() as st:
            go = po["hgn%d" % j][0]
            self.rms_mod(st, mix, mix, lambda k, r: self.partile[:, go + k:go + k + 1], None, s, nch=6, pergroup=True)
            self.P.barrier()
        with contextlib.ExitStack() as st:
            wg_ = [self.sb(st, "hwg%d" % i, [128, 8, 128], BF16) for i in range(2)]
            sz = [self.sb(st, "hsz%d" % i, [128, 512], BF16) for i in range(2)]
            for hh in range(6):
                w_ = wg_[hh % 2]
                self.dma("pool", w_[:], self.odin_d[j, 18 + hh], (), [w_.d()])
                for bi, (t0, n) in enumerate(BLKS):
                    ps = self.psum()
                    for k in range(8):
                        self.mm(ps[:, 0:n], w_[:, k, :], self.hx[:, k, t0:t0 + n], k == 0, k == 7, [w_.d(), self.hx.d()], [ps.d()])
                    z_ = sz[bi % 2]
                    self.act(z_[:, 0:n], ps[:, 0:n], AF.Silu, [ps.d()], [z_.d()])
                    self.tt("dve", mix[:, hh, t0:t0 + n], mix[:, hh, t0:t0 + n], z_[:, 0:n], ALU.mult, [mix.d(), z_.d()], [mix.d()])
            self.P.barrier()

    def cexp_small(self, st, name, lr, li, ls, shape, D):
        mk = lambda n: self.sb(st, name + n, shape, F32)
        step, c, sn, mag, t1, t2 = mk("st"), mk("c"), mk("s"), mk("m"), mk("t1"), mk("t2")
        dd = [step.d()]
        self.act(step[:], ls, AF.Exp, D, dd)
        self.tt("dve", mag[:], lr, step[:], ALU.mult, D + dd, dd)
        self.act(mag[:], mag[:], AF.Exp, dd, dd)
        self.tt("dve", t1[:], li, step[:], ALU.mult, D + dd, dd)
        self.act(sn[:], t1[:], AF.Sin, dd, dd, scale=1.0 / 16.0)
        self.ts("dve", t2[:], t1[:], 1.0 / 16.0, 1.5707963267948966, ALU.mult, ALU.add, dd, dd)
        self.act(c[:], t2[:], AF.Sin, dd, dd)
        for _ in range(4):
            self.tt("dve", t1[:], c[:], c[:], ALU.mult, dd, dd)
            self.tt("dve", t2[:], sn[:], sn[:], ALU.mult, dd, dd)
            self.tt("dve", sn[:], sn[:], c[:], ALU.mult, dd, dd)
            self.ts("dve", sn[:], sn[:], 2.0, None, ALU.mult, None, dd, dd)
            self.tt("dve", c[:], t1[:], t2[:], ALU.subtract, dd, dd)
        for t_ in (c, sn, mag, t1, t2):
            t_.deps[None] = step.d()
        return c, sn, mag, step, t1, t2

    def s5(self, l, s, mix):
        j = l // 2
        po = self.po
        L = 256
        NTC = T // L
        D = [self.cD]
        with contextlib.ExitStack() as st:
            ufm = self.sb(st, "ufm", [128, 2, T], BF16)
            wu = self.sb(st, "s5wu", [128, 8, 128], BF16)
            for c in range(2):
                self.dma("pool", wu[:], self.odin_d[j, 24 + c], (), [wu.d()])
                for (t0, n) in BLKS:
                    ps = self.psum()
                    for k in range(8):
                        self.mm(ps[:, 0:n], wu[:, k, :], self.hx[:, k, t0:t0 + n], k == 0, k == 7, [wu.d(), self.hx.d()], [ps.d()])
                    self.cp("act", ufm[:, c, t0:t0 + n], ps[:, 0:n], [ps.d()], [ufm.d()])
            so = po["s5p%d" % j][0]
            pc, psn, pmag, _, _, _ = self.cexp_small(st, "sp", self.partile[:, so:so + 16], self.partile[:, so + 16:so + 32],
                                                    self.partile[:, so + 32:so + 48], [128, 16], D)
            pdep = [pc.d()]
            E = self.sb(st, "s5E", [128, 4, 2], F32)
            for d in range(2):
                for c in range(2):
                    with contextlib.ExitStack() as st2:
                        tabc = self.sb(st2, "tabc", [128, 4, L], F32)
                        tabs = self.sb(st2, "tabs", [128, 4, L], F32)
                        BM = self.sb(st2, "BM", [128, 4, 2, 128], BF16)
                        CM = self.sb(st2, "CM", [128, 4, 2, 128], BF16)
                        tdep = [tabc.d()]
                        tmp1 = self.sb(st2, "s5tm", [128, 128], F32)
                        for q4 in range(4):
                            q = c * 4 + q4
                            col = d * 8 + q
                            self.cp("dve", tabc[:, q4, 0:1], pc[:, col:col + 1], pdep, tdep)
                            self.cp("dve", tabs[:, q4, 0:1], psn[:, col:col + 1], pdep, tdep)
                            span = 1
                            while span < L:
                                cm_ = tabc[:, q4, span - 1:span]
                                sm_ = tabs[:, q4, span - 1:span]
                                lo, hi = slice(0, span), slice(span, 2 * span)
                                self.ts("dve", tmp1[:, 0:span], tabs[:, q4, lo], sm_, None, ALU.mult, None, tdep, [tmp1.d()])
                                self.stt("dve", tabc[:, q4, hi], tabc[:, q4, lo], cm_, tmp1[:, 0:span], ALU.mult, ALU.subtract, tdep + [tmp1.d()], tdep)
                                self.ts("dve", tmp1[:, 0:span], tabs[:, q4, lo], cm_, None, ALU.mult, None, tdep, [tmp1.d()])
                                self.stt("dve", tabs[:, q4, hi], tabc[:, q4, lo], sm_, tmp1[:, 0:span], ALU.mult, ALU.add, tdep + [tmp1.d()], tdep)
                                span *= 2
                            with contextlib.ExitStack() as st3:
                                rows = self.sb(st3, "s5rows", [128, 3, 128], F32)
                                bpad = self.sb(st3, "s5bp", [128, 2, 128], F32)
                                self.dma("pool", rows[:], self.s5row_d[j, d, :, q, :].unsqueeze(0).to_broadcast([128, 3, 128]), (), [rows.d()])
                                self.dma("act", bpad[:], self.s5b_d[j, q].rearrange("r p c -> p r c"), (), [bpad.d()])
                                self.dma("pool", CM[:, q4, :, :], self.s5c_d[j, d, q].rearrange("r p c -> p r c"), (), [CM.d()])
                                rd = [rows.d()]
                                rc, rs_, rmag, rstep, t1, t2 = self.cexp_small(st3, "sr", rows[:, 0, :], rows[:, 1, :], rows[:, 2, :], [128, 128], rd)
                                w = [rc.d()]
                                lr_, li_ = rows[:, 0, :], rows[:, 1, :]
                                ar, ai, den, zr, zi = rc, rs_, rstep, t1, t2
                                self.tt("dve", ar[:], rc[:], rmag[:], ALU.mult, w, w)
                                self.tt("dve", ai[:], rs_[:], rmag[:], ALU.mult, w, w)
                                self.tt("dve", den[:], lr_, lr_, ALU.mult, rd + w, w)
                                self.tt("dve", rmag[:], li_, li_, ALU.mult, rd + w, w)
                                self.tt("dve", den[:], den[:], rmag[:], ALU.add, w, w)
                                self.P.op("dve", (lambda t_: lambda e: e.reciprocal(out=t_[:], in_=t_[:]))(den), w, w)
                                self.ts("dve", ar[:], ar[:], -1.0, None, ALU.add, None, w, w)
                                self.tt("dve", zr[:], ar[:], lr_, ALU.mult, rd + w, w)
                                self.tt("dve", rmag[:], ai[:], li_, ALU.mult, rd + w, w)
                                self.tt("dve", zr[:], zr[:], rmag[:], ALU.add, w, w)
                                self.tt("dve", zr[:], zr[:], den[:], ALU.mult, w, w)
                                self.tt("dve", zi[:], ai[:], lr_, ALU.mult, rd + w, w)
                                self.tt("dve", rmag[:], ar[:], li_, ALU.mult, rd + w, w)
                                self.tt("dve", zi[:], zi[:], rmag[:], ALU.subtract, w, w)
                                self.tt("dve", zi[:], zi[:], den[:], ALU.mult, w, w)
                                bd = [bpad.d()]
                                self.tt("dve", ar[:], zr[:], bpad[:, 0, :], ALU.mult, w + bd, w)
                                self.tt("dve", ai[:], zi[:], bpad[:, 1, :], ALU.mult, w + bd, w)
                                self.tt("dve", BM[:, q4, 0, :], ar[:], ai[:], ALU.subtract, w, [BM.d()])
                                self.tt("dve", ar[:], zr[:], bpad[:, 1, :], ALU.mult, w + bd, w)
                                self.tt("dve", ai[:], zi[:], bpad[:, 0, :], ALU.mult, w + bd, w)
                                self.tt("dve", BM[:, q4, 1, :], ar[:], ai[:], ALU.add, w, [BM.d()])
                                self.ts("dve", CM[:, q4, 1, :], CM[:, q4, 1, :], -1.0, None, ALU.mult, None, [CM.d()], [CM.d()])
                                self.P.barrier()
                        self.memset("dve", E[:], 0.0, [E.d()])
                        zk = [self.sb(st2, "s5zk%d" % i, [128, 2, L], F32) for i in range(4)]
                        h32 = [self.sb(st2, "s5h%d" % i, [128, 2, L], F32) for i in range(4)]
                        tt_ = self.sb(st2, "s5t", [128, L], F32)
                        hb = [self.sb(st2, "s5hb%d" % i, [128, 4, 2, L], BF16) for i in range(2)]
                        order = list(range(NTC)) if d == 0 else [0] + list(range(NTC - 1, 0, -1))
                        fl = (lambda a: a) if d == 0 else rev
                        lastpos = L - 1 if d == 0 else 0
                        for ti, tc in enumerate(order):
                            t0 = tc * L
                            hbt = hb[ti % 2]
                            px = {}
                            for q4 in range(4):
                                ps_x = self.psum()
                                px[q4] = ps_x
                                self.mm(ps_x[:, 0:L], BM[:, q4, 0, :], ufm[:, c, t0:t0 + L], True, True, [BM.d(), ufm.d()], [ps_x.d()])
                                self.mm(ps_x[:, L:2 * L], BM[:, q4, 1, :], ufm[:, c, t0:t0 + L], True, True, [BM.d(), ufm.d()], [ps_x.d()])
                            td = [tt_.d()]
                            for q4 in range(4):
                                xr, xi = fl(px[q4][:, 0:L]), fl(px[q4][:, L:2 * L])
                                tcq, tsq = tabc[:, q4, :], tabs[:, q4, :]
                                z = zk[q4]
                                zd = [z.d()]
                                pd_ = [px[q4].d()]
                                self.tt("dve", z[:, 0, :], tcq, xr, ALU.mult, tdep + pd_, zd)
                                self.tt("dve", tt_[:], tsq, xi, ALU.mult, tdep + pd_, td)
                                self.tt("dve", z[:, 0, :], z[:, 0, :], tt_[:], ALU.add, zd + td, zd)
                                self.tt("dve", z[:, 1, :], tcq, xi, ALU.mult, tdep + pd_, zd)
                                self.tt("dve", tt_[:], tsq, xr, ALU.mult, tdep + pd_, td)
                                self.tt("dve", z[:, 1, :], z[:, 1, :], tt_[:], ALU.subtract, zd + td, zd)
                            for q4 in range(4):
                                col = d * 8 + c * 4 + q4
                                z = zk[q4]
                                zd = [z.d()]
                                mg = pmag[:, col:col + 1].to_broadcast([128, L])
                                self.scan(z[:, 0, :], mg, z[:, 0, :], E[:, q4, 0:1], zd + [E.d()] + pdep, zd)
                                self.scan(z[:, 1, :], mg, z[:, 1, :], E[:, q4, 1:2], zd + [E.d()] + pdep, zd)
                            for q4 in range(4):
                                tcq, tsq = tabc[:, q4, :], tabs[:, q4, :]
                                z, hh_ = zk[q4], h32[q4]
                                zd, hd_ = [z.d()], [hh_.d()]
                                self.tt("dve", fl(hh_[:, 0, :]), tcq, z[:, 0, :], ALU.mult, tdep + zd, hd_)
                                self.tt("dve", tt_[:], tsq, z[:, 1, :], ALU.mult, tdep + zd, td)
                                self.tt("dve", fl(hh_[:, 0, :]), fl(hh_[:, 0, :]), tt_[:], ALU.subtract, hd_ + td, hd_)
                                self.tt("dve", fl(hh_[:, 1, :]), tsq, z[:, 0, :], ALU.mult, tdep + zd, hd_)
                                self.tt("dve", tt_[:], tcq, z[:, 1, :], ALU.mult, tdep + zd, td)
                                self.tt("dve", fl(hh_[:, 1, :]), fl(hh_[:, 1, :]), tt_[:], ALU.add, hd_ + td, hd_)
                                self.cp("dve", E[:, q4, :], hh_[:, :, lastpos], hd_, [E.d()])
                            for q4 in range(4):
                                self.cp("act", hbt[:, q4, :, :], h32[q4][:, :, :], [h32[q4].d()], [hbt.d()])
                            ps_y = self.psum()
                            for q4 in range(4):
                                for ri in range(2):
                                    self.mm(ps_y[:, 0:L], CM[:, q4, ri, :], hbt[:, q4, ri, :], q4 == 0 and ri == 0, q4 == 3 and ri == 1,
                                            [CM.d(), hbt.d()], [ps_y.d()])
                            if d == 0:
                                self.cp("act", mix[:, 6 + c, t0:t0 + L], ps_y[:, 0:L], [ps_y.d()], [mix.d()])
                            else:
                                self.tt("dve", mix[:, 6 + c, t0:t0 + L], mix[:, 6 + c, t0:t0 + L], ps_y[:, 0:L], ALU.add, [mix.d(), ps_y.d()], [mix.d()])
                        self.P.barrier()
            do = po["s5d%d" % j][0]
            gb = po["glub%d" % j][0]
            wgl = self.sb(st, "wglu", [128, 2, 256], BF16)
            yt = [self.sb(st, "s5yt%d" % i, [128, 512], F32) for i in range(2)]
            self.dma("pool", wgl[:], self.gluw_d[j], (), [wgl.d()])
            for c in range(2):
                for bi, (t0, n) in enumerate(BLKS):
                    y_ = yt[bi % 2]
                    self.stt("dve", y_[:, 0:n], ufm[:, c, t0:t0 + n], self.partile[:, do + c:do + c + 1], mix[:, 6 + c, t0:t0 + n], ALU.mult, ALU.add,
                             [ufm.d(), mix.d(), self.cD], [y_.d()])
                    self.act(ufm[:, c, t0:t0 + n], y_[:, 0:n], AF.Gelu, [y_.d()], [ufm.d()])
            for c in range(2):
                for bi, (t0, n) in enumerate(BLKS):
                    ps = self.psum()
                    for k in range(2):
                        self.mm(ps[:, 0:n], wgl[:, k, c * 128:(c + 1) * 128], ufm[:, k, t0:t0 + n], k == 0, k == 1, [wgl.d(), ufm.d()], [ps.d()])
                    y_ = yt[bi % 2]
                    self.act(y_[:, 0:n], ps[:, 0:n], AF.Sigmoid, [ps.d(), self.cD], [y_.d()], bias=self.partile[:, gb + c:gb + c + 1])
                    self.tt("dve", mix[:, 6 + c, t0:t0 + n], ufm[:, c, t0:t0 + n], y_[:, 0:n], ALU.mult, [ufm.d(), y_.d()], [mix.d()])
            self.P.barrier()

    def final_out(self, s):
        NR = self.NR
        if self.final:
            go, _ = self.po["fng"]
            A = lambda k, r: self.partile[:, go + k:go + k + 1]
            with contextlib.ExitStack() as st:
                self.rms_mod(st, self.x, self.x, A, None, s)
                self.out_ops.append(self.dma("sp", self.yout[s, :, 0:4, :], self.x[:, 0:4, CTX:T], [self.x.d()], ()))
                self.out_ops.append(self.dma("act", self.yout[s, :, 4:8, :], self.x[:, 4:8, CTX:T], [self.x.d()], ()))
                self.P.barrier()
        else:
            self.out_ops.append(self.dma("sp", self.yout[s, :, 0:4, :], self.x[:, 0:4, CTX:T], [self.x.d()], ()))
            self.out_ops.append(self.dma("act", self.yout[s, :, 4:8, :], self.x[:, 4:8, CTX:T], [self.x.d()], ()))


def make_consts():
    c = np.zeros((128, 768), np.float32)
    c[:, 0:128] = np.eye(128, dtype=np.float32)
    i = np.arange(128)
    c[:, 128:256] = (i[:, None] <= i[None, :]).astype(np.float32)
    c[:, 256:384] = (i[:, None] >= i[None, :]).astype(np.float32)
    c[:, 384:512] = 1.0
    same = (i[:, None] // 32) == (i[None, :] // 32)
    c[:, 512:640] = (same & (i[:, None] <= i[None, :])).astype(np.float32)
    c[:, 640:768] = (same & (i[:, None] >= i[None, :])).astype(np.float32)
    return c


def kernel(**inputs):
    nseq = 4
    inp = {k: np.asarray(v) for k, v in inputs.items()}
    phases = []
    for l in range(4):
        phases += [("mix", l), ("ffn", l)]
    kb = K(nseq, phases, final=True)
    nc = kb.build()
    sh = host_prep(inp)
    sh["cst"] = make_consts()
    in_maps = []
    for c in range(NCORES):
        m = dict(sh)
        m.update(core_inputs(inp, c, nseq))
        in_maps.append(m)
    res = run_bass_kernel_spmd(nc, in_maps, core_ids=list(range(NCORES)))
    outs = []
    for c in range(NCORES):
        y = res.results[c]["yout"]
        outs.append(y.transpose(0, 3, 2, 1).reshape(nseq, 2048, 1024))
    return np.ascontiguousarray(np.concatenate(outs, axis=0)).astype(np.float32)
```
